# Optimizing a Trainium2 kernel written in Bass

```python
import math
import jax, jax.numpy as jnp
from jax import lax
import numpy as np

D_MODEL = 1024
BATCH = 2
SEQ = 8192
DEPTH = 1

EPS = 1e-6
PLE_DIM = 256
ATTN_HEADS = 16
HEAD_DIM = 64
ATTN_WIDTH = ATTN_HEADS * HEAD_DIM
MOBA_BLOCK = 256
MOBA_TOPK = 3
Q_CHUNK = 32
SSM_INNER = 2 * D_MODEL
SSM_HEADDIM = 64
SSM_HEADS = SSM_INNER // SSM_HEADDIM
SSM_GROUPS = 4
SSM_STATE = 128
SSM_CONV = 4
SSM_CHUNK = 128
CONV_DIM = SSM_INNER + 2 * SSM_GROUPS * SSM_STATE
DT_MIN = 0.001
DT_MAX = 0.1
D_FF = -(-8 * D_MODEL // (3 * 256)) * 256
IN_SPLITS = (ATTN_WIDTH, ATTN_WIDTH, ATTN_WIDTH, SSM_INNER, CONV_DIM, SSM_HEADS, D_MODEL, D_MODEL)
IN_DIM = sum(IN_SPLITS)
IN_OFFSETS = tuple(np.cumsum(IN_SPLITS)[:-1].tolist())

kernel_name = "hybrid_moba_mamba2_gated_block"


def rms_norm(x, g):
    xf = x.astype(jnp.float32)
    y = xf * lax.rsqrt(jnp.mean(xf * xf, axis=-1, keepdims=True) + EPS)
    return (y * g.astype(jnp.float32)).astype(x.dtype)


def split_heads(t, b_, s):
    return t.reshape(b_, s, ATTN_HEADS, HEAD_DIM).transpose(0, 2, 1, 3)


def moba_attention(q, k, v):
    b_, nh, s, dh = q.shape
    nb = -(-s // MOBA_BLOCK)
    pad = nb * MOBA_BLOCK - s
    kp = jnp.pad(k, ((0, 0), (0, 0), (0, pad), (0, 0)))
    vp = jnp.pad(v, ((0, 0), (0, 0), (0, pad), (0, 0)))
    kb = kp.reshape(b_, nh, nb, MOBA_BLOCK, dh)
    vb = vp.reshape(b_, nh, nb, MOBA_BLOCK, dh)
    k_mean = jnp.mean(kb.astype(jnp.float32), axis=3)
    topk = min(MOBA_TOPK, nb)
    scale = dh ** -0.5
    bidx = jnp.arange(b_)[:, None, None, None]
    hidx = jnp.arange(nh)[None, :, None, None]
    blk_ids = jnp.arange(nb)

    def chunk(ci):
        start = ci * Q_CHUNK
        qc = lax.dynamic_slice_in_dim(q, start, Q_CHUNK, axis=2)
        qpos = start + jnp.arange(Q_CHUNK)
        own = start // MOBA_BLOCK
        gate = jnp.einsum('bhqd,bhnd->bhqn', qc.astype(jnp.float32), k_mean)
        gate = jnp.where(blk_ids < own, gate, -jnp.inf)
        _, sel = lax.top_k(gate, topk)
        sel_ok = sel < own
        ks = kb[bidx, hidx, sel]
        vs = vb[bidx, hidx, sel]
        s_sel = jnp.einsum('bhqd,bhqtkd->bhqtk', qc, ks, preferred_element_type=jnp.float32) * scale
        s_sel = jnp.where(sel_ok[..., None], s_sel, -jnp.inf)
        ko = lax.dynamic_slice_in_dim(kp, own * MOBA_BLOCK, MOBA_BLOCK, axis=2)
        vo = lax.dynamic_slice_in_dim(vp, own * MOBA_BLOCK, MOBA_BLOCK, axis=2)
        s_own = jnp.einsum('bhqd,bhkd->bhqk', qc, ko, preferred_element_type=jnp.float32) * scale
        kpos = own * MOBA_BLOCK + jnp.arange(MOBA_BLOCK)
        s_own = jnp.where(kpos[None, :] <= qpos[:, None], s_own, -jnp.inf)
        scores = jnp.concatenate([s_sel.reshape(b_, nh, Q_CHUNK, topk * MOBA_BLOCK), s_own], axis=-1)
        probs = jax.nn.softmax(scores, axis=-1).astype(v.dtype)
        p_sel = probs[..., :topk * MOBA_BLOCK].reshape(b_, nh, Q_CHUNK, topk, MOBA_BLOCK)
        p_own = probs[..., topk * MOBA_BLOCK:]
        return (jnp.einsum('bhqtk,bhqtkd->bhqd', p_sel, vs)
                + jnp.einsum('bhqk,bhkd->bhqd', p_own, vo))

    out = lax.map(chunk, jnp.arange(s // Q_CHUNK))
    return out.transpose(1, 2, 0, 3, 4).reshape(b_, nh, s, dh)


def causal_depthwise_conv(u, w, b):
    ch = u.shape[-1]
    out = lax.conv_general_dilated(u, w[:, None, :].astype(u.dtype), (1,), [(SSM_CONV - 1, 0)],
                                   dimension_numbers=('NWC', 'WIO', 'NWC'), feature_group_count=ch)
    return out + b.astype(u.dtype)


def segsum(a):
    t = a.shape[-1]
    xr = jnp.broadcast_to(a[..., :, None], a.shape + (t,))
    xr = jnp.where(jnp.tril(jnp.ones((t, t), bool), -1), xr, 0.0)
    cs = jnp.cumsum(xr, axis=-2)
    return jnp.where(jnp.tril(jnp.ones((t, t), bool)), cs, -jnp.inf)


def ssd_chunked(xh, dt, a_head, bm, cm):
    b_, s, h, p = xh.shape
    g, n = bm.shape[-2], bm.shape[-1]
    j = h // g
    L = SSM_CHUNK
    c = s // L
    xdt = (xh * dt[..., None]).reshape(b_, c, L, g, j, p)
    a = (dt * a_head).reshape(b_, c, L, h).transpose(0, 3, 1, 2)
    a_cs = jnp.cumsum(a, axis=-1)
    bc = bm.reshape(b_, c, L, g, n)
    cc = cm.reshape(b_, c, L, g, n)
    decay_in = jnp.exp(segsum(a)).reshape(b_, g, j, c, L, L)
    cb = jnp.einsum('bclgn,bcsgn->bgcls', cc, bc)
    y_diag = jnp.einsum('bgjcls,bcsgjp->bclgjp', cb[:, :, None] * decay_in, xdt)
    w_end = jnp.exp(a_cs[..., -1:] - a_cs).transpose(0, 2, 3, 1).reshape(b_, c, L, g, j)
    chunk_states = jnp.einsum('bclgn,bclgjp->bcgjpn', bc, xdt * w_end[..., None])
    chunk_decay = jnp.exp(a_cs[..., -1]).reshape(b_, g, j, c).transpose(3, 0, 1, 2)

    def step(state, inp):
        dec, new = inp
        return state * dec[..., None, None] + new, state

    init = jnp.zeros((b_, g, j, p, n), jnp.float32)
    _, prev = lax.scan(step, init, (chunk_decay, chunk_states.transpose(1, 0, 2, 3, 4, 5)))
    prev = prev.transpose(1, 0, 2, 3, 4, 5)
    w_start = jnp.exp(a_cs).transpose(0, 2, 3, 1).reshape(b_, c, L, g, j)
    y_off = jnp.einsum('bclgn,bcgjpn->bclgjp', cc, prev) * w_start[..., None]
    return (y_diag + y_off).reshape(b_, s, h, p)


def mamba2_mixer(z, xbc, dt_raw, conv_w, conv_b, dt_bias, a_log, d_skip, norm_g):
    b_, s, _ = z.shape
    f32 = jnp.float32
    xbc = jax.nn.silu(causal_depthwise_conv(xbc, conv_w, conv_b))
    xs, bm, cm = jnp.split(xbc, [SSM_INNER, SSM_INNER + SSM_GROUPS * SSM_STATE], axis=-1)
    dt = jax.nn.softplus(dt_raw.astype(f32) + dt_bias.astype(f32))
    a_head = -jnp.exp(a_log.astype(f32))
    xh = xs.astype(f32).reshape(b_, s, SSM_HEADS, SSM_HEADDIM)
    y = ssd_chunked(xh, dt, a_head,
                    bm.astype(f32).reshape(b_, s, SSM_GROUPS, SSM_STATE),
                    cm.astype(f32).reshape(b_, s, SSM_GROUPS, SSM_STATE))
    y = y + d_skip.astype(f32)[:, None] * xh
    y = y.reshape(b_, s, SSM_INNER) * jax.nn.silu(z.astype(f32))
    y = y.reshape(b_, s, SSM_GROUPS, SSM_INNER // SSM_GROUPS)
    y = y * lax.rsqrt(jnp.mean(y * y, axis=-1, keepdims=True) + EPS)
    return (y.reshape(b_, s, SSM_INNER) * norm_g.astype(f32)).astype(z.dtype)


def setup_inputs(seed: int = 0) -> dict:
    key = jax.random.key(seed)
    ks = jax.random.split(key, 24)
    f32 = jnp.float32

    def normal(k, shape, scale):
        return jax.random.normal(k, shape, f32) * scale

    def gain(k, shape):
        return 1.0 + 0.02 * jax.random.normal(k, shape, f32)

    dt0 = jnp.exp(jax.random.uniform(ks[9], (DEPTH, SSM_HEADS), f32, math.log(DT_MIN), math.log(DT_MAX)))
    return {
        "x": normal(ks[0], (BATCH, SEQ, D_MODEL), 1.0),
        "p": normal(ks[1], (DEPTH, BATCH, SEQ, PLE_DIM), 1.0),
        "ln1_g": gain(ks[2], (DEPTH, D_MODEL)),
        "w_in": normal(ks[3], (DEPTH, D_MODEL, IN_DIM), D_MODEL ** -0.5),
        "q_norm_g": gain(ks[4], (DEPTH, HEAD_DIM)),
        "k_norm_g": gain(ks[5], (DEPTH, HEAD_DIM)),
        "w_o_attn": normal(ks[6], (DEPTH, ATTN_WIDTH, D_MODEL), ATTN_WIDTH ** -0.5),
        "conv_w": normal(ks[7], (DEPTH, SSM_CONV, CONV_DIM), SSM_CONV ** -0.5),
        "conv_b": normal(ks[8], (DEPTH, CONV_DIM), 0.02),
        "dt_bias": dt0 + jnp.log(-jnp.expm1(-dt0)),
        "a_log": jnp.log(jax.random.uniform(ks[10], (DEPTH, SSM_HEADS), f32, 1.0, 16.0)),
        "d_skip": gain(ks[11], (DEPTH, SSM_HEADS)),
        "ssm_norm_g": gain(ks[12], (DEPTH, SSM_INNER)),
        "w_o_ssm": normal(ks[13], (DEPTH, SSM_INNER, D_MODEL), SSM_INNER ** -0.5),
        "w_out": normal(ks[14], (DEPTH, D_MODEL, D_MODEL), D_MODEL ** -0.5),
        "ln2_g": gain(ks[15], (DEPTH, D_MODEL)),
        "w_gate_up": normal(ks[16], (DEPTH, D_MODEL, 2 * D_FF), D_MODEL ** -0.5),
        "w_down": normal(ks[17], (DEPTH, D_FF, D_MODEL), D_FF ** -0.5),
        "ln3_g": gain(ks[18], (DEPTH, D_MODEL)),
        "w_ple_gate": normal(ks[19], (DEPTH, D_MODEL, D_MODEL), D_MODEL ** -0.5),
        "w_ple_proj": normal(ks[20], (DEPTH, PLE_DIM, D_MODEL), PLE_DIM ** -0.5),
    }


def reference(x, p, ln1_g, w_in, q_norm_g, k_norm_g, w_o_attn, conv_w, conv_b, dt_bias, a_log,
              d_skip, ssm_norm_g, w_o_ssm, w_out, ln2_g, w_gate_up, w_down, ln3_g, w_ple_gate,
              w_ple_proj):
    b_, s, _ = x.shape
    for i in range(DEPTH):
        h = rms_norm(x, ln1_g[i])
        proj = h @ w_in[i]
        q, k, v, z, xbc, dt_raw, gate_a, gate_b = jnp.split(proj, IN_OFFSETS, axis=-1)
        qh = rms_norm(split_heads(q, b_, s), q_norm_g[i])
        kh = rms_norm(split_heads(k, b_, s), k_norm_g[i])
        vh = split_heads(v, b_, s)
        att = moba_attention(qh, kh, vh)
        y_a = att.transpose(0, 2, 1, 3).reshape(b_, s, ATTN_WIDTH) @ w_o_attn[i]
        y_b = mamba2_mixer(z, xbc, dt_raw, conv_w[i], conv_b[i], dt_bias[i], a_log[i],
                           d_skip[i], ssm_norm_g[i]) @ w_o_ssm[i]
        merged = jax.nn.sigmoid(gate_a) * y_a + jax.nn.sigmoid(gate_b) * y_b
        x = x + merged @ w_out[i]
        h2 = rms_norm(x, ln2_g[i])
        g_ff, u_ff = jnp.split(h2 @ w_gate_up[i], 2, axis=-1)
        x = x + (jax.nn.silu(g_ff) * u_ff) @ w_down[i]
        h3 = rms_norm(x, ln3_g[i])
        x = x + (p[i] @ w_ple_proj[i]) * jax.nn.sigmoid(h3 @ w_ple_gate[i])
    return x
```

```python
import numpy as np
from contextlib import ExitStack
import ml_dtypes
import concourse.bass as bass
import concourse.mybir as mybir
from concourse.bass_utils import run_bass_kernel_spmd

F32 = mybir.dt.float32
BF16 = mybir.dt.bfloat16
AF = mybir.ActivationFunctionType
ALU = mybir.AluOpType
AX = mybir.AxisListType

EPS = 1e-6
NEG = -30000.0
SAME_ENGINE_SYNC = True
STAGES = {}


class T:
    __slots__ = ("ap", "w", "r", "name", "excl")

    def __init__(self, ap=None, name="", excl=False):
        self.ap = ap
        self.w = None
        self.r = []
        self.name = name
        self.excl = excl

    def __getitem__(self, k):
        return self.ap[k]


class TV(T):
    __slots__ = ("parent",)

    def __init__(self, parent, ap):
        self.parent = parent
        self.ap = ap
        self.name = parent.name
        self.excl = parent.excl

    @property
    def w(self):
        return self.parent.w

    @w.setter
    def w(self, v):
        self.parent.w = v

    @property
    def r(self):
        return self.parent.r

    @r.setter
    def r(self, v):
        self.parent.r = v


class Op:
    __slots__ = ("eng", "fn", "deps", "dma", "inc", "sem", "ticket", "idx", "prev_same_sem")

    def __init__(self, eng, fn, dma):
        self.eng = eng
        self.fn = fn
        self.dma = dma
        self.deps = []
        self.inc = False
        self.sem = None
        self.ticket = 0
        self.prev_same_sem = None


class Prog:
    ENGS = ("pe", "act", "dve", "pool", "sp")
    NDS = 8

    def __init__(self, nc):
        self.nc = nc
        self.ops = {e: [] for e in self.ENGS}
        self.all = []
        self.bar = {}

    def op(self, eng, fn, reads=(), writes=(), dma=False):
        o = Op(eng, fn, dma)
        deps = []
        for t in reads:
            if t.w is not None:
                deps.append(t.w)
            if t.excl:
                deps.extend(x for x in t.r if x.eng != eng)
        for t in writes:
            if t.w is not None:
                deps.append(t.w)
            deps.extend(t.r)
        b = self.bar.pop(eng, None)
        if b:
            deps.extend(b)
        seen = set()
        for d in deps:
            if d is o or id(d) in seen:
                continue
            seen.add(id(d))
            if (not d.dma) and d.eng == eng and (eng == "pe" or not SAME_ENGINE_SYNC):
                continue
            o.deps.append(d)
            d.inc = True
        for t in reads:
            if dma:
                t.r.append(o)
            else:
                t.r = [x for x in t.r if x.dma or x.eng != eng] + [o]
        for t in writes:
            t.w = o
            t.r = []
        if dma:
            o.inc = True
        self.ops[eng].append(o)
        self.all.append(o)
        return o

    def barrier(self):
        last = []
        for e in self.ENGS:
            ops = self.ops[e]
            if ops:
                last.append(ops[-1])
            last.extend([o for o in ops if o.dma][-self.NDS:])
        for e in self.ENGS:
            self.bar[e] = list(last)

    def emit(self, final_ops):
        nc = self.nc
        with ExitStack() as es:
            SEM_CAP = 1000
            nsem = {e: sum(1 for o in self.ops[e] if o.inc and not o.dma) // SEM_CAP + 1 for e in ("pe", "act", "dve", "pool")}
            csem = {e: [es.enter_context(nc.semaphore("cs_%s%d" % (e, i))) for i in range(nsem[e])]
                    for e in ("pe", "act", "dve", "pool")}
            dsem = {e: [es.enter_context(nc.semaphore("ds_%s%d" % (e, i))) for i in range(self.NDS)]
                    for e in self.ENGS}
            ccount = {e: 0 for e in self.ENGS}
            dcount = {e: [0] * self.NDS for e in self.ENGS}
            drr = {e: 0 for e in self.ENGS}
            dlast = {e: [None] * self.NDS for e in self.ENGS}
            for e in self.ENGS:
                for o in self.ops[e]:
                    if o.dma:
                        k = drr[e] % self.NDS
                        drr[e] += 1
                        dcount[e][k] += 16
                        o.sem = dsem[e][k]
                        o.ticket = dcount[e][k]
                        o.prev_same_sem = dlast[e][k]
                        dlast[e][k] = o
                    elif o.inc:
                        o.sem = csem[e][ccount[e] // SEM_CAP]
                        o.ticket = ccount[e] % SEM_CAP + 1
                        ccount[e] += 1

            def run(ename, eng):
                seen = {}

                def wait(d):
                    key = id(d.sem)
                    if seen.get(key, 0) < d.ticket:
                        eng.wait_ge(d.sem, d.ticket)
                        seen[key] = d.ticket

                for o in self.ops[ename]:
                    for d in o.deps:
                        wait(d)
                    if o.dma and o.prev_same_sem is not None:
                        wait(o.prev_same_sem)
                    ins = o.fn(eng)
                    if o.sem is not None:
                        ins.then_inc(o.sem, 16 if o.dma else 1)
                if ename == "sp":
                    for d in final_ops:
                        wait(d)

            with nc.Block() as block:
                @block.tensor
                def _(eng):
                    run("pe", eng)

                @block.scalar
                def _(eng):
                    run("act", eng)

                @block.vector
                def _(eng):
                    run("dve", eng)

                @block.gpsimd
                def _(eng):
                    run("pool", eng)

                @block.sync
                def _(eng):
                    run("sp", eng)


class Ctx:
    N = 0

    def __init__(self, nc, es, P):
        self.nc, self.es, self.P = nc, es, P
        self.n = 0

    def sb(self, shape, dt, name=None):
        Ctx.N += 1
        h = self.es.enter_context(self.nc.sbuf_tensor("%s_%d" % (name or "sb", Ctx.N), list(shape), dt))
        return h

    def sbT(self, shape, dt, name=None):
        h = self.sb(shape, dt, name)
        return T(h[tuple(slice(None) for _ in shape)], name or "")

    def dram(self, name, shape, dt, kind):
        return self.nc.dram_tensor(name, list(shape), dt, kind=kind).ap()


class Ring:
    def __init__(self, tiles):
        self.tiles = tiles
        self.i = 0

    def next(self):
        t = self.tiles[self.i % len(self.tiles)]
        self.i += 1
        return t


P2W = {
    "wg": (1024, 2048), "woa": (1024, 1024), "wos": (2048, 1024), "wout": (1024, 1024),
    "wgu": (1024, 5632), "wd": (2816, 1024), "wpg": (1024, 1024), "wpp": (256, 1024),
}
P2W_GAIN = {"wg": "g1", "wgu": "g2", "wpg": "g3"}
WB_COLS = 256


def wblocks(name):
    K, N = P2W[name]
    nkc = K // 128
    kbs = []
    k0 = 0
    while k0 < nkc:
        nk = min(8, nkc - k0)
        kbs.append((k0, nk))
        k0 += nk
    return kbs, N // WB_COLS


def build_precast(P, C, din, wscr, wT, consts):
    nc = P.nc
    stg = Ring([C.sbT([128, 8, WB_COLS], F32, "pc_stg") for _ in range(2)])
    wbf = Ring([C.sbT([128, 8, WB_COLS], BF16, "pc_bf") for _ in range(2)])
    for name in P2W:
        kbs, ncb = wblocks(name)
        src = din[name]
        dst = wscr[name]
        gain = consts.get(P2W_GAIN.get(name))
        for cb in range(ncb):
            for (k0, nk) in kbs:
                s, w = stg.next(), wbf.next()
                sv = src[k0 * 128:(k0 + nk) * 128, cb * WB_COLS:(cb + 1) * WB_COLS].rearrange("(k p) n -> p k n", p=128)
                dv = dst[k0 * 128:(k0 + nk) * 128, cb * WB_COLS:(cb + 1) * WB_COLS].rearrange("(k p) n -> p k n", p=128)
                P.op("sp", lambda e, s=s, sv=sv, nk=nk: e.dma_start(out=s[:, 0:nk, :], in_=sv), writes=[s], dma=True)
                if gain is not None:
                    gv = gain[:, k0:k0 + nk].unsqueeze(2).to_broadcast([128, nk, WB_COLS])
                    P.op("pool", lambda e, s=s, w=w, gv=gv, nk=nk: e.tensor_tensor(
                        out=w[:, 0:nk, :], in0=s[:, 0:nk, :], in1=gv, op=ALU.mult), reads=[s, gain], writes=[w])
                else:
                    P.op("pool", lambda e, s=s, w=w, nk=nk: e.tensor_copy(out=w[:, 0:nk, :], in_=s[:, 0:nk, :]),
                         reads=[s], writes=[w])
                t = T(None, "wscr")
                wT[(name, cb, k0)] = t
                P.op("pool", lambda e, w=w, dv=dv, nk=nk: e.dma_start(out=dv, in_=w[:, 0:nk, :]),
                     reads=[w], writes=[t], dma=True)


def build_phase2(P, C, din, wscr, wT, consts, exch, dout, NTOK=2048, TT=1024):
    nc = P.nc
    NTG = TT // 512
    NSUB = TT // 128
    ident_b = consts["ident_b"]
    ones_b = consts["ones_b"]
    xT = [C.sbT([128, TT], F32, "xT") for _ in range(8)]
    hT = [C.sbT([128, TT], BF16, "hT") for _ in range(8)]
    big = C.sb([128, 24, TT], BF16, "big")
    attT = [T(big[:, c, :], "att") for c in range(8)]
    ybT = [T(big[:, 8 + c, :], "yb") for c in range(16)]
    mg = [C.sbT([128, TT], BF16, "mg") for _ in range(8)]
    pT = [C.sbT([128, TT], BF16, "pT") for _ in range(2)]
    xin = Ring([C.sbT([128, 1024], F32, "xin") for _ in range(2)])
    xhi = Ring([C.sbT([128, 1024], BF16, "xhi") for _ in range(2)])
    xlo = Ring([C.sbT([128, 1024], BF16, "xlo") for _ in range(2)])
    xres = Ring([C.sbT([128, 1024], F32, "xres") for _ in range(1)])
    pin = Ring([C.sbT([128, 256], F32, "pin") for _ in range(2)])
    pinb = Ring([C.sbT([128, 256], BF16, "pinb") for _ in range(2)])
    wring = Ring([C.sbT([128, 8, WB_COLS], BF16, "wr") for _ in range(8)])
    tmpf = Ring([C.sbT([128, 512], F32, "tmpf") for _ in range(8)])
    sqb = Ring([C.sbT([128, 512], BF16, "sqb") for _ in range(3)])
    rstd = [C.sbT([128, 512], F32, "rstd") for _ in range(NTG)]
    psum = Ring(consts["psum"])

    def load_w(name, cb, k0, nk):
        w = wring.next()
        sv = wscr[name][k0 * 128:(k0 + nk) * 128, cb * WB_COLS:(cb + 1) * WB_COLS].rearrange("(k p) n -> p k n", p=128)
        P.op("sp", lambda e, w=w, sv=sv, nk=nk: e.dma_start(out=w[:, 0:nk, :], in_=sv),
             reads=[wT[(name, cb, k0)]], writes=[w], dma=True)
        return w

    def mm_group(ps, wlist, X, tg, oc_in_blk):
        n = sum(nk for _, _, nk in wlist)
        i = 0
        for (w, k0, nk) in wlist:
            for k in range(nk):
                st, sp_ = (i == 0), (i == n - 1)
                xk = X[k0 + k]
                P.op("pe", lambda e, ps=ps, w=w, k=k, xk=xk, st=st, sp_=sp_: e.matmul(
                    ps[:, :], lhsT=w[:, k, oc_in_blk * 128:(oc_in_blk + 1) * 128],
                    rhs=xk[:, tg * 512:(tg + 1) * 512], start=st, stop=sp_),
                    reads=[w, xk], writes=[ps])
                i += 1

    def rmsnorm():
        for tg in range(NTG):
            ps = psum.next()
            for kc in range(8):
                sq = sqb.next()
                P.op("act", lambda e, sq=sq, kc=kc, tg=tg: e.activation(
                    out=sq[:, :], in_=xT[kc][:, tg * 512:(tg + 1) * 512], func=AF.Square), reads=[xT[kc]], writes=[sq])
                P.op("pe", lambda e, ps=ps, sq=sq, kc=kc: e.matmul(
                    ps[:, :], lhsT=ones_b[:, :], rhs=sq[:, :], start=(kc == 0), stop=(kc == 7)),
                    reads=[sq, ones_b], writes=[ps])
            r = rstd[tg]
            P.op("act", lambda e, r=r, ps=ps: e.activation(
                out=r[:, :], in_=ps[:, :], func=AF.Sqrt, scale=1.0 / 1024.0, bias=EPS), reads=[ps], writes=[r])
            P.op("dve", lambda e, r=r: e.reciprocal(out=r[:, :], in_=r[:, :]), reads=[r], writes=[r])
        for kc in range(8):
            for tg in range(NTG):
                P.op("dve", lambda e, kc=kc, tg=tg: e.tensor_tensor(
                    out=hT[kc][:, tg * 512:(tg + 1) * 512], in0=xT[kc][:, tg * 512:(tg + 1) * 512],
                    in1=rstd[tg][:, :], op=ALU.mult), reads=[xT[kc], rstd[tg]], writes=[hT[kc]])

    out_ops = []
    for tt in range(NTOK // TT):
        t0 = tt * TT
        for s in range(NSUB):
            xi = xin.next()
            P.op("sp", lambda e, xi=xi, s=s, t0=t0: e.dma_start(out=xi[:, :], in_=din["x2"][t0 + s * 128:t0 + (s + 1) * 128, :]),
                 writes=[xi], dma=True)
            xh, xl, xr = xhi.next(), xlo.next(), xres.next()
            P.op("act", lambda e, xi=xi, xh=xh: e.copy(out=xh[:, :], in_=xi[:, :]), reads=[xi], writes=[xh])
            P.op("dve", lambda e, xi=xi, xh=xh, xr=xr: e.tensor_tensor(out=xr[:, :], in0=xi[:, :], in1=xh[:, :], op=ALU.subtract),
                 reads=[xi, xh], writes=[xr])
            P.op("pool", lambda e, xr=xr, xl=xl: e.tensor_copy(out=xl[:, :], in_=xr[:, :]), reads=[xr], writes=[xl])
            for half in range(2):
                ps = psum.next()
                for j in range(4):
                    kc = half * 4 + j
                    P.op("pe", lambda e, ps=ps, xh=xh, kc=kc, j=j: e.matmul(
                        ps[:, j * 128:(j + 1) * 128], lhsT=xh[:, kc * 128:(kc + 1) * 128], rhs=ident_b[:, :], start=True, stop=False),
                        reads=[xh, ident_b], writes=[ps])
                    P.op("pe", lambda e, ps=ps, xl=xl, kc=kc, j=j: e.matmul(
                        ps[:, j * 128:(j + 1) * 128], lhsT=xl[:, kc * 128:(kc + 1) * 128], rhs=ident_b[:, :], start=False, stop=True),
                        reads=[xl, ident_b], writes=[ps])
                for j in range(4):
                    kc = half * 4 + j
                    eng = "act" if j % 2 == 0 else "dve"
                    if eng == "act":
                        P.op("act", lambda e, ps=ps, kc=kc, j=j, s=s: e.copy(
                            out=xT[kc][:, s * 128:(s + 1) * 128], in_=ps[:, j * 128:(j + 1) * 128]),
                            reads=[ps], writes=[xT[kc]])
                    else:
                        P.op("dve", lambda e, ps=ps, kc=kc, j=j, s=s: e.tensor_copy(
                            out=xT[kc][:, s * 128:(s + 1) * 128], in_=ps[:, j * 128:(j + 1) * 128]),
                            reads=[ps], writes=[xT[kc]])
            pi, pb = pin.next(), pinb.next()
            P.op("sp", lambda e, pi=pi, s=s, t0=t0: e.dma_start(out=pi[:, :], in_=din["p2"][t0 + s * 128:t0 + (s + 1) * 128, :]),
                 writes=[pi], dma=True)
            P.op("pool", lambda e, pi=pi, pb=pb: e.tensor_copy(out=pb[:, :], in_=pi[:, :]), reads=[pi], writes=[pb])
            ps = psum.next()
            psb = ps.ap.bitcast(BF16)
            for j in range(2):
                P.op("pe", lambda e, psb=psb, pb=pb, j=j: e.transpose(
                    psb[:, j * 128:(j + 1) * 128], pb[:, j * 128:(j + 1) * 128], ident_b[:, :]),
                    reads=[pb, ident_b], writes=[ps])
            for j in range(2):
                P.op("act", lambda e, psb=psb, j=j, s=s: e.copy(
                    out=pT[j][:, s * 128:(s + 1) * 128], in_=psb[:, j * 128:(j + 1) * 128]), reads=[ps], writes=[pT[j]])
        for g in range(4):
            for c in range(2):
                tl = attT[2 * g + c]
                P.op("sp", lambda e, tl=tl, g=g, c=c, t0=t0: e.dma_start(out=tl[:, :], in_=exch["att"][g, c, :, t0:t0 + TT]),
                     reads=[exch["att_T"]], writes=[tl], dma=True)
            for c in range(4):
                tl = ybT[4 * g + c]
                P.op("sp", lambda e, tl=tl, g=g, c=c, t0=t0: e.dma_start(out=tl[:, :], in_=exch["yb"][g, c, :, t0:t0 + TT]),
                     reads=[exch["yb_T"]], writes=[tl], dma=True)
        p2s = STAGES.get('p2s', 'BXCD')
        if 'B' in p2s:
            rmsnorm()
        for j in (range(4) if 'B' in p2s else []):
            wga = load_w("wg", j, 0, 8)
            wgb = load_w("wg", 4 + j, 0, 8)
            wa = load_w("woa", j, 0, 8)
            ws0 = load_w("wos", j, 0, 8)
            ws1 = load_w("wos", j, 8, 8)
            for o2 in range(2):
                oc = 2 * j + o2
                for tg in range(NTG):
                    pga, pgb, pya, pyb = psum.next(), psum.next(), psum.next(), psum.next()
                    mm_group(pga, [(wga, 0, 8)], hT, tg, o2)
                    mm_group(pgb, [(wgb, 0, 8)], hT, tg, o2)
                    mm_group(pya, [(wa, 0, 8)], attT, tg, o2)
                    mm_group(pyb, [(ws0, 0, 8), (ws1, 8, 8)], ybT, tg, o2)
                    sa, sb_, m1, m2 = tmpf.next(), tmpf.next(), tmpf.next(), tmpf.next()
                    P.op("act", lambda e, sa=sa, pga=pga: e.activation(out=sa[:, :], in_=pga[:, :], func=AF.Sigmoid),
                         reads=[pga], writes=[sa])
                    P.op("act", lambda e, sb_=sb_, pgb=pgb: e.activation(out=sb_[:, :], in_=pgb[:, :], func=AF.Sigmoid),
                         reads=[pgb], writes=[sb_])
                    P.op("dve", lambda e, m1=m1, sa=sa, pya=pya: e.tensor_tensor(
                        out=m1[:, :], in0=sa[:, :], in1=pya[:, :], op=ALU.mult), reads=[sa, pya], writes=[m1])
                    P.op("dve", lambda e, m2=m2, sb_=sb_, pyb=pyb: e.tensor_tensor(
                        out=m2[:, :], in0=sb_[:, :], in1=pyb[:, :], op=ALU.mult), reads=[sb_, pyb], writes=[m2])
                    P.op("pool", lambda e, m1=m1, m2=m2, oc=oc, tg=tg: e.tensor_tensor(
                        out=mg[oc][:, tg * 512:(tg + 1) * 512], in0=m1[:, :], in1=m2[:, :], op=ALU.add),
                        reads=[m1, m2], writes=[mg[oc]])
        for j in (range(4) if 'X' in p2s else []):
            w = load_w("wout", j, 0, 8)
            for o2 in range(2):
                oc = 2 * j + o2
                for tg in range(NTG):
                    ps = psum.next()
                    mm_group(ps, [(w, 0, 8)], mg, tg, o2)
                    P.op("dve", lambda e, ps=ps, oc=oc, tg=tg: e.tensor_tensor(
                        out=xT[oc][:, tg * 512:(tg + 1) * 512], in0=ps[:, :], in1=xT[oc][:, tg * 512:(tg + 1) * 512],
                        op=ALU.add), reads=[ps, xT[oc]], writes=[xT[oc]])
        if 'C' in p2s:
            rmsnorm()
        actT = [T(big[:, f, :], "act") for f in range(22)]
        for f in range(22):
            old = attT[f] if f < 8 else ybT[f - 8]
            actT[f].w, actT[f].r = old.w, old.r
        for j in (range(11) if 'C' in p2s else []):
            wgt = load_w("wgu", j, 0, 8)
            wup = load_w("wgu", 11 + j, 0, 8)
            for o2 in range(2):
                f = 2 * j + o2
                for tg in range(NTG):
                    pg, pu = psum.next(), psum.next()
                    mm_group(pg, [(wgt, 0, 8)], hT, tg, o2)
                    mm_group(pu, [(wup, 0, 8)], hT, tg, o2)
                    sg = tmpf.next()
                    P.op("act", lambda e, sg=sg, pg=pg: e.activation(out=sg[:, :], in_=pg[:, :], func=AF.Silu),
                         reads=[pg], writes=[sg])
                    P.op("dve", lambda e, sg=sg, pu=pu, f=f, tg=tg: e.tensor_tensor(
                        out=actT[f][:, tg * 512:(tg + 1) * 512], in0=sg[:, :], in1=pu[:, :], op=ALU.mult),
                        reads=[sg, pu], writes=[actT[f]])
        for j in (range(4) if 'C' in p2s else []):
            w0 = load_w("wd", j, 0, 8)
            w1 = load_w("wd", j, 8, 8)
            w2 = load_w("wd", j, 16, 6)
            for o2 in range(2):
                oc = 2 * j + o2
                for tg in range(NTG):
                    ps = psum.next()
                    mm_group(ps, [(w0, 0, 8), (w1, 8, 8), (w2, 16, 6)], actT, tg, o2)
                    P.op("dve", lambda e, ps=ps, oc=oc, tg=tg: e.tensor_tensor(
                        out=xT[oc][:, tg * 512:(tg + 1) * 512], in0=ps[:, :], in1=xT[oc][:, tg * 512:(tg + 1) * 512],
                        op=ALU.add), reads=[ps, xT[oc]], writes=[xT[oc]])
        for f in range(22):
            old = attT[f] if f < 8 else ybT[f - 8]
            old.w, old.r = actT[f].w, actT[f].r
        if 'D' in p2s:
            rmsnorm()
        for j in (range(4) if 'D' in p2s else []):
            wpg = load_w("wpg", j, 0, 8)
            wpp = load_w("wpp", j, 0, 2)
            for o2 in range(2):
                oc = 2 * j + o2
                for tg in range(NTG):
                    pg, pp = psum.next(), psum.next()
                    mm_group(pg, [(wpg, 0, 8)], hT, tg, o2)
                    mm_group(pp, [(wpp, 0, 2)], pT, tg, o2)
                    sg, m = tmpf.next(), tmpf.next()
                    P.op("act", lambda e, sg=sg, pg=pg: e.activation(out=sg[:, :], in_=pg[:, :], func=AF.Sigmoid),
                         reads=[pg], writes=[sg])
                    P.op("dve", lambda e, m=m, sg=sg, pp=pp: e.tensor_tensor(
                        out=m[:, :], in0=sg[:, :], in1=pp[:, :], op=ALU.mult), reads=[sg, pp], writes=[m])
                    P.op("dve", lambda e, m=m, oc=oc, tg=tg: e.tensor_tensor(
                        out=xT[oc][:, tg * 512:(tg + 1) * 512], in0=m[:, :], in1=xT[oc][:, tg * 512:(tg + 1) * 512],
                        op=ALU.add), reads=[m, xT[oc]], writes=[xT[oc]])
        for kc in range(8):
            P.op("act", lambda e, kc=kc: e.copy(out=hT[kc][:, :], in_=xT[kc][:, :]), reads=[xT[kc], hT[kc]], writes=[hT[kc]])
            P.op("dve", lambda e, kc=kc: e.tensor_tensor(out=xT[kc][:, :], in0=xT[kc][:, :], in1=hT[kc][:, :], op=ALU.subtract),
                 reads=[xT[kc], hT[kc]], writes=[xT[kc]])
            P.op("pool", lambda e, kc=kc: e.tensor_copy(out=mg[kc][:, :], in_=xT[kc][:, :]), reads=[xT[kc], mg[kc]], writes=[mg[kc]])
        for s in range(NSUB):
            xo = xin.next()
            for half in range(2):
                ps = psum.next()
                for j in range(4):
                    kc = half * 4 + j
                    P.op("pe", lambda e, ps=ps, kc=kc, j=j, s=s: e.matmul(
                        ps[:, j * 128:(j + 1) * 128], lhsT=hT[kc][:, s * 128:(s + 1) * 128], rhs=ident_b[:, :], start=True, stop=False),
                        reads=[hT[kc], ident_b], writes=[ps])
                    P.op("pe", lambda e, ps=ps, kc=kc, j=j, s=s: e.matmul(
                        ps[:, j * 128:(j + 1) * 128], lhsT=mg[kc][:, s * 128:(s + 1) * 128], rhs=ident_b[:, :], start=False, stop=True),
                        reads=[mg[kc], ident_b], writes=[ps])
                if half == 0:
                    P.op("act", lambda e, ps=ps, xo=xo: e.copy(out=xo[:, 0:512], in_=ps[:, :]), reads=[ps], writes=[xo])
                else:
                    P.op("dve", lambda e, ps=ps, xo=xo: e.tensor_copy(out=xo[:, 512:1024], in_=ps[:, :]),
                         reads=[ps, xo], writes=[xo])
            out_ops.append(P.op("pool", lambda e, xo=xo, s=s, t0=t0: e.dma_start(
                out=dout[t0 + s * 128:t0 + (s + 1) * 128, :], in_=xo[:, :]), reads=[xo], dma=True))
    return out_ops


W1A = 1288
W1B = 768


def load_cast_weight(P, C, src, w, ncols, gain, stg):
    c0 = 0
    while c0 < ncols:
        n = min(WB_COLS, ncols - c0)
        s = stg.next()
        sv = src[:, c0:c0 + n].rearrange("(k p) n -> p k n", p=128)
        P.op("sp", lambda e, s=s, sv=sv, n=n: e.dma_start(out=s[:, :, 0:n], in_=sv), writes=[s], dma=True)
        gv = gain[:, 0:8].unsqueeze(2).to_broadcast([128, 8, n])
        P.op("pool", lambda e, s=s, gv=gv, n=n, c0=c0: e.tensor_tensor(
            out=w[:, :, c0:c0 + n], in0=s[:, :, 0:n], in1=gv, op=ALU.mult), reads=[s, gain], writes=[w])
        c0 += n


def build_prologue(P, C, din, cst, hT_scr, hT_T, NTOKW):
    xin = Ring([C.sbT([128, 1024], F32, "pxin") for _ in range(3)])
    junk = C.sbT([128, 1024], BF16, "pjunk")
    hb = Ring([C.sbT([128, 1024], BF16, "phb") for _ in range(2)])
    ssr = Ring([C.sbT([128, 2], F32, "pss") for _ in range(4)])
    hst = Ring([C.sbT([128, 8, 512], BF16, "phst") for _ in range(2)])
    psum = cst["psring"]
    ident_b = cst["ident_b"]
    for m in range(NTOKW // 512):
        ht = hst.next()
        for s in range(4):
            t0 = m * 512 + s * 128
            xi, ss, h = xin.next(), ssr.next(), hb.next()
            P.op("sp", lambda e, xi=xi, t0=t0: e.dma_start(out=xi[:, :], in_=din["xw"][t0:t0 + 128, :]), writes=[xi], dma=True)
            P.op("act", lambda e, xi=xi, ss=ss: e.activation(out=junk[:, :], in_=xi[:, :], func=AF.Square, accum_out=ss[:, 0:1]),
                 reads=[xi], writes=[junk, ss])
            P.op("act", lambda e, ss=ss: e.activation(out=ss[:, 1:2], in_=ss[:, 0:1], func=AF.Sqrt, scale=1.0 / 1024.0, bias=EPS),
                 reads=[ss], writes=[ss])
            P.op("dve", lambda e, ss=ss: e.reciprocal(out=ss[:, 1:2], in_=ss[:, 1:2]), reads=[ss], writes=[ss])
            P.op("dve", lambda e, xi=xi, ss=ss, h=h: e.tensor_scalar(
                out=h[:, :], in0=xi[:, :], scalar1=ss[:, 1:2], scalar2=None, op0=ALU.mult), reads=[xi, ss], writes=[h])
            ps = psum.next()
            psb = ps.ap.bitcast(BF16)
            for kc in range(8):
                P.op("pe", lambda e, psb=psb, h=h, kc=kc: e.transpose(
                    psb[:, kc * 128:(kc + 1) * 128], h[:, kc * 128:(kc + 1) * 128], ident_b[:, :]),
                    reads=[h, ident_b], writes=[ps])
            P.op("act", lambda e, psb=psb, ht=ht, s=s: e.copy(
                out=ht[:, 0:4, s * 128:(s + 1) * 128], in_=psb[:, 0:512].rearrange("p (k n) -> p k n", k=4)),
                reads=[ps], writes=[ht])
            P.op("dve", lambda e, psb=psb, ht=ht, s=s: e.tensor_copy(
                out=ht[:, 4:8, s * 128:(s + 1) * 128], in_=psb[:, 512:1024].rearrange("p (k n) -> p k n", k=4)),
                reads=[ps, ht], writes=[ht])
        t = T(None, "hTscr")
        hT_T.append(t)
        P.op("pool", lambda e, ht=ht, m=m: e.dma_start(out=hT_scr[:, :, m * 512:(m + 1) * 512], in_=ht[:, :, :]),
             reads=[ht], writes=[t], dma=True)


def build_p1a(P, C, din, g, cst, hT_scr, hT_T, e_yb, e_yb_T, NTOKW, OWN0):
    psum = cst["psring"]
    ident_b, ones_b, U, T1 = cst["ident_b"], cst["ones_b"], cst["U"], cst["T1"]
    stg = Ring([C.sbT([128, 8, WB_COLS], F32, "a_stg") for _ in range(2)])
    w1a = C.sbT([128, 8, W1A], BF16, "w1a")
    load_cast_weight(P, C, din["w1a"][g], w1a, W1A, cst["g1"], stg)
    small = {}
    for nm, shp in (("convw", [128, 6, 4]), ("convb", [128, 6]), ("dtb", [128, 8]), ("alog", [128, 8]),
                    ("dsk", [128, 8]), ("sng", [128, 512])):
        t = C.sbT(shp, F32, "a_" + nm)
        P.op("sp", lambda e, t=t, nm=nm: e.dma_start(out=t.ap, in_=din[nm][g]), writes=[t], dma=True)
        small[nm] = t
    cw, cb, dtb, alog, dsk, sng = (small[k] for k in ("convw", "convb", "dtb", "alog", "dsk", "sng"))
    tokmask = cst["tokmask"]
    Abc = C.sbT([128, 8], F32, "Abc")
    P.op("act", lambda e: e.activation(out=Abc[:, :], in_=alog[:, :], func=AF.Exp), reads=[alog], writes=[Abc])
    P.op("dve", lambda e: e.tensor_scalar(out=Abc[:, :], in0=Abc[:, :], scalar1=-1.0, scalar2=None, op0=ALU.mult),
         reads=[Abc], writes=[Abc])
    S = C.sbT([128, 512], F32, "S")
    Sbf = C.sbT([128, 512], BF16, "Sbf")
    xbc = C.sbT([128, 6, 515], F32, "xbc")
    P.op("pool", lambda e: e.memset(S[:, :], 0.0), writes=[S])
    P.op("pool", lambda e: e.memset(Sbf[:, :], 0.0), writes=[Sbf])
    P.op("pool", lambda e: e.memset(xbc[:, :, 0:3], 0.0), writes=[xbc])
    hring = Ring([C.sbT([128, 8, 512], BF16, "a_hT") for _ in range(2)])
    cring = Ring([C.sbT([128, 6, 512], BF16, "a_co") for _ in range(2)])
    accr = Ring([C.sbT([128, 512], F32, "a_acc") for _ in range(5)])
    f512 = Ring([C.sbT([128, 512], F32, "a_f512") for _ in range(6)])
    b512 = Ring([C.sbT([128, 512], BF16, "a_b512") for _ in range(8)])
    s8 = Ring([C.sbT([128, 8], F32, "a_s8") for _ in range(24)])
    s2 = Ring([C.sbT([128, 2], F32, "a_s2") for _ in range(4)])
    Rr = Ring([C.sbT([128, 3, 8, 128], BF16, "a_R") for _ in range(2)])
    a3r = Ring([C.sbT([128, 3, 8], BF16, "a_a3") for _ in range(3)])
    Lr = Ring([C.sbT([128, 8, 128], BF16, "a_L") for _ in range(2)])
    Mr = Ring([C.sbT([128, 8, 128], BF16, "a_M") for _ in range(2)])
    cbm = Ring([C.sbT([128, 128], BF16, "a_cbm") for _ in range(2)])
    ybst = Ring([C.sbT([128, 4, 512], BF16, "a_ybst") for _ in range(2)])
    junk = C.sbT([128, 512], BF16, "a_junk")
    pss = cst["ps_small"]
    i_eng = 0
    for m in range(NTOKW // 512):
        tok0 = m * 512
        own = tok0 >= OWN0
        hT = hring.next()
        P.op("sp", lambda e, hT=hT, tok0=tok0: e.dma_start(out=hT[:, :, :], in_=hT_scr[:, :, tok0:tok0 + 512]),
             reads=[hT_T[m]], writes=[hT], dma=True)
        for c in range(6):
            ps = psum.next()
            for kc in range(8):
                P.op("pe", lambda e, ps=ps, kc=kc, c=c, hT=hT: e.matmul(
                    ps[:, :], lhsT=w1a[:, kc, c * 128:(c + 1) * 128], rhs=hT[:, kc, :], start=(kc == 0), stop=(kc == 7)),
                    reads=[w1a, hT], writes=[ps])
            if c % 2 == 0:
                P.op("act", lambda e, ps=ps, c=c: e.copy(out=xbc[:, c, 3:515], in_=ps[:, :]), reads=[ps, xbc], writes=[xbc])
            else:
                P.op("dve", lambda e, ps=ps, c=c: e.tensor_copy(out=xbc[:, c, 3:515], in_=ps[:, :]), reads=[ps, xbc], writes=[xbc])
        co = cring.next()
        for c in range(6):
            eng = "pool" if c in (1, 4) else "dve"
            acc = accr.next()
            P.op(eng, lambda e, acc=acc, c=c: e.tensor_scalar(
                out=acc[:, :], in0=xbc[:, c, 0:512], scalar1=cw[:, c, 0:1], scalar2=None, op0=ALU.mult),
                reads=[xbc, cw], writes=[acc])
            for k in range(1, 4):
                if eng == "dve":
                    P.op(eng, lambda e, acc=acc, c=c, k=k: e.scalar_tensor_tensor(
                        out=acc[:, :], in0=xbc[:, c, k:k + 512], scalar=cw[:, c, k:k + 1], in1=acc[:, :],
                        op0=ALU.mult, op1=ALU.add), reads=[xbc, cw, acc], writes=[acc])
                else:
                    tmpc = accr.next()
                    P.op(eng, lambda e, tmpc=tmpc, c=c, k=k: e.tensor_scalar(
                        out=tmpc[:, :], in0=xbc[:, c, k:k + 512], scalar1=cw[:, c, k:k + 1], scalar2=None, op0=ALU.mult),
                        reads=[xbc, cw], writes=[tmpc])
                    P.op(eng, lambda e, tmpc=tmpc, acc=acc: e.tensor_tensor(out=acc[:, :], in0=acc[:, :], in1=tmpc[:, :], op=ALU.add),
                         reads=[acc, tmpc], writes=[acc])
            P.op("act", lambda e, acc=acc, c=c, co=co: e.activation(
                out=co[:, c, :], in_=acc[:, :], func=AF.Silu, bias=cb[:, c:c + 1], scale=1.0), reads=[acc, cb, co], writes=[co])
        P.op("pool", lambda e: e.tensor_copy(out=xbc[:, :, 0:3], in_=xbc[:, :, 512:515]), reads=[xbc], writes=[xbc])
        yst = ybst.next() if own else None
        for s in range(4):
            sub = slice(s * 128, (s + 1) * 128)
            tile_idx = m * 4 + s
            pdt = pss["dt"]
            for kc in range(8):
                P.op("pe", lambda e, kc=kc, hT=hT, sub=sub: e.matmul(
                    pdt[:, :], lhsT=hT[:, kc, sub], rhs=w1a[:, kc, 1280:1288], start=(kc == 0), stop=(kc == 7)),
                    reads=[w1a, hT], writes=[pdt])
            dtr, ax, ee, dt_, a_ = s8.next(), s8.next(), s8.next(), s8.next(), s8.next()
            P.op("dve", lambda e, dtr=dtr: e.tensor_tensor(out=dtr[:, :], in0=pdt[:, :], in1=dtb[:, :], op=ALU.add),
                 reads=[pdt, dtb], writes=[dtr])
            P.op("act", lambda e, dtr=dtr, ax=ax: e.activation(out=ax[:, :], in_=dtr[:, :], func=AF.Abs),
                 reads=[dtr], writes=[ax])
            P.op("act", lambda e, ax=ax, ee=ee: e.activation(out=ee[:, :], in_=ax[:, :], func=AF.Exp, scale=-1.0),
                 reads=[ax], writes=[ee])
            P.op("act", lambda e, ee=ee: e.activation(out=ee[:, :], in_=ee[:, :], func=AF.Ln, bias=1.0, scale=1.0),
                 reads=[ee], writes=[ee])
            P.op("dve", lambda e, dtr=dtr, ee=ee, dt_=dt_: e.scalar_tensor_tensor(
                out=dt_[:, :], in0=dtr[:, :], scalar=0.0, in1=ee[:, :], op0=ALU.max, op1=ALU.add),
                reads=[dtr, ee], writes=[dt_])
            P.op("dve", lambda e, dt_=dt_, tile_idx=tile_idx: e.tensor_scalar(
                out=dt_[:, :], in0=dt_[:, :], scalar1=tokmask[:, tile_idx:tile_idx + 1], scalar2=None, op0=ALU.mult),
                reads=[dt_, tokmask], writes=[dt_])
            P.op("dve", lambda e, dt_=dt_, a_=a_: e.tensor_tensor(out=a_[:, :], in0=dt_[:, :], in1=Abc[:, :], op=ALU.mult),
                 reads=[dt_, Abc], writes=[a_])
            a3 = a3r.next()
            ar1, ar2 = s8.next(), s8.next()
            P.op("act", lambda e, a_=a_, a3=a3: e.copy(out=a3[:, 0, :], in_=a_[:, :]), reads=[a_, a3], writes=[a3])
            P.op("dve", lambda e, a_=a_, a3=a3, ar1=ar1: e.tensor_tensor(out=ar1[:, :], in0=a_[:, :], in1=a3[:, 0, :], op=ALU.subtract),
                 reads=[a_, a3], writes=[ar1])
            P.op("act", lambda e, ar1=ar1, a3=a3: e.copy(out=a3[:, 1, :], in_=ar1[:, :]), reads=[ar1, a3], writes=[a3])
            P.op("dve", lambda e, ar1=ar1, a3=a3, ar2=ar2: e.tensor_tensor(out=ar2[:, :], in0=ar1[:, :], in1=a3[:, 1, :], op=ALU.subtract),
                 reads=[ar1, a3], writes=[ar2])
            P.op("act", lambda e, ar2=ar2, a3=a3: e.copy(out=a3[:, 2, :], in_=ar2[:, :]), reads=[ar2, a3], writes=[a3])
            pac = pss["acs"]
            for i3 in range(3):
                P.op("pe", lambda e, a3=a3, i3=i3: e.matmul(pac[:, 0:8], lhsT=U[:, :], rhs=a3[:, i3, :], start=(i3 == 0), stop=(i3 == 2)),
                     reads=[U, a3], writes=[pac])
            for i3 in range(3):
                P.op("pe", lambda e, a3=a3, i3=i3: e.matmul(pac[:, 8:16], lhsT=ones_b[:, :], rhs=a3[:, i3, :], start=(i3 == 0), stop=(i3 == 2)),
                     reads=[ones_b, a3], writes=[pac])
            acs, wst, wend, cdec = s8.next(), s8.next(), s8.next(), s8.next()
            P.op("act", lambda e, acs=acs: e.copy(out=acs[:, :], in_=pac[:, 0:8]), reads=[pac], writes=[acs])
            P.op("act", lambda e, wst=wst: e.activation(out=wst[:, :], in_=pac[:, 0:8], func=AF.Exp), reads=[pac], writes=[wst])
            P.op("act", lambda e, cdec=cdec: e.activation(out=cdec[:, :], in_=pac[:, 8:16], func=AF.Exp), reads=[pac], writes=[cdec])
            P.op("dve", lambda e, wend=wend, acs=acs: e.tensor_tensor(out=wend[:, :], in0=pac[:, 8:16], in1=acs[:, :], op=ALU.subtract),
                 reads=[pac, acs], writes=[wend])
            P.op("act", lambda e, wend=wend: e.activation(out=wend[:, :], in_=wend[:, :], func=AF.Exp), reads=[wend], writes=[wend])
            pxs = psum.next()
            pxb = pxs.ap.bitcast(BF16)
            for c in range(5):
                P.op("pe", lambda e, pxb=pxb, c=c, co=co, sub=sub: e.transpose(
                    pxb[:, c * 128:(c + 1) * 128], co[:, c, sub], ident_b[:, :]), reads=[co, ident_b], writes=[pxs])
            xs_tm, Btm, xdt, xdtw = b512.next(), b512.next(), b512.next(), b512.next()
            P.op("act", lambda e, pxb=pxb, xs_tm=xs_tm: e.copy(out=xs_tm[:, :], in_=pxb[:, 0:512]), reads=[pxs], writes=[xs_tm])
            P.op("act", lambda e, pxb=pxb, Btm=Btm: e.copy(out=Btm[:, 0:128], in_=pxb[:, 512:640]), reads=[pxs], writes=[Btm])
            P.op("pool", lambda e, xs_tm=xs_tm, xdt=xdt, dt_=dt_: e.tensor_tensor(
                out=xdt[:, :].rearrange("p (h d) -> p h d", h=8), in0=xs_tm[:, :].rearrange("p (h d) -> p h d", h=8),
                in1=dt_[:, :].unsqueeze(2).to_broadcast([128, 8, 64]), op=ALU.mult), reads=[xs_tm, dt_], writes=[xdt])
            P.op("pool", lambda e, xdt=xdt, xdtw=xdtw, wend=wend: e.tensor_tensor(
                out=xdtw[:, :].rearrange("p (h d) -> p h d", h=8), in0=xdt[:, :].rearrange("p (h d) -> p h d", h=8),
                in1=wend[:, :].unsqueeze(2).to_broadcast([128, 8, 64]), op=ALU.mult), reads=[xdt, wend], writes=[xdtw])
            if own:
                R, L, Mh, cbt = Rr.next(), Lr.next(), Mr.next(), cbm.next()
                for i3 in range(3):
                    P.op("dve" if i3 != 1 else "pool", lambda e, R=R, a3=a3, i3=i3: e.tensor_tensor(
                        out=R[:, i3, :, :], in0=U[:, :].unsqueeze(1).to_broadcast([128, 8, 128]),
                        in1=a3[:, i3, :].unsqueeze(2).to_broadcast([128, 8, 128]), op=ALU.mult), reads=[U, a3, R], writes=[R])
                for hh in range(2):
                    pD = psum.next()
                    for i3 in range(3):
                        P.op("pe", lambda e, pD=pD, R=R, hh=hh, i3=i3: e.matmul(
                            pD[:, :], lhsT=T1[:, :], rhs=R[:, i3, hh * 4:(hh + 1) * 4, :].rearrange("p h l -> p (h l)"),
                            start=(i3 == 0), stop=(i3 == 2)), reads=[T1, R], writes=[pD])
                    P.op("act", lambda e, pD=pD, L=L, hh=hh: e.activation(
                        out=L[:, hh * 4:(hh + 1) * 4, :].rearrange("p h l -> p (h l)"), in_=pD[:, :], func=AF.Exp),
                        reads=[pD, L], writes=[L])
                pcb = pss["cb"]
                P.op("pe", lambda e, co=co, sub=sub: e.matmul(
                    pcb[:, :], lhsT=co[:, 4, sub], rhs=co[:, 5, sub], start=True, stop=True), reads=[co], writes=[pcb])
                P.op("dve", lambda e, cbt=cbt: e.tensor_tensor(out=cbt[:, :], in0=pcb[:, :], in1=U[:, :], op=ALU.mult),
                     reads=[pcb, U], writes=[cbt])
                P.op("pool", lambda e, Mh=Mh, L=L, cbt=cbt: e.tensor_tensor(
                    out=Mh[:, :, :], in0=L[:, :, :], in1=cbt[:, :].unsqueeze(1).to_broadcast([128, 8, 128]), op=ALU.mult),
                    reads=[L, cbt], writes=[Mh])
                xsD = b512.next()
                P.op("pool", lambda e, xs_tm=xs_tm, xsD=xsD: e.tensor_tensor(
                    out=xsD[:, :].rearrange("p (h d) -> p h d", h=8), in0=xs_tm[:, :].rearrange("p (h d) -> p h d", h=8),
                    in1=dsk[:, :].unsqueeze(2).to_broadcast([128, 8, 64]), op=ALU.mult), reads=[xs_tm, dsk], writes=[xsD])
                pz = psum.next()
                for kc in range(8):
                    P.op("pe", lambda e, pz=pz, kc=kc, hT=hT, sub=sub: e.matmul(
                        pz[:, :], lhsT=hT[:, kc, sub], rhs=w1a[:, kc, 768:1280], start=(kc == 0), stop=(kc == 7)),
                        reads=[w1a, hT], writes=[pz])
                sz = f512.next()
                P.op("act", lambda e, pz=pz, sz=sz: e.activation(out=sz[:, :], in_=pz[:, :], func=AF.Silu), reads=[pz], writes=[sz])
                pyo, py = psum.next(), psum.next()
                P.op("pe", lambda e, pyo=pyo, co=co, sub=sub: e.matmul(
                    pyo[:, :], lhsT=co[:, 5, sub], rhs=Sbf[:, :], start=True, stop=True), reads=[co, Sbf], writes=[pyo])
                P.op("pe", lambda e, py=py, xsD=xsD: e.matmul(py[:, :], lhsT=ident_b[:, :], rhs=xsD[:, :], start=True, stop=False),
                     reads=[ident_b, xsD], writes=[py])
                for h in range(8):
                    P.op("pe", lambda e, py=py, Mh=Mh, xdt=xdt, h=h: e.matmul(
                        py[:, h * 64:(h + 1) * 64], lhsT=Mh[:, h, :], rhs=xdt[:, h * 64:(h + 1) * 64], start=False, stop=(h == 7)),
                        reads=[Mh, xdt], writes=[py])
                y1, y2, y3 = f512.next(), f512.next(), f512.next()
                P.op("dve", lambda e, pyo=pyo, y1=y1, wst=wst: e.tensor_tensor(
                    out=y1[:, :].rearrange("p (h d) -> p h d", h=8), in0=pyo[:, :].rearrange("p (h d) -> p h d", h=8),
                    in1=wst[:, :].unsqueeze(2).to_broadcast([128, 8, 64]), op=ALU.mult), reads=[pyo, wst], writes=[y1])
                P.op("dve", lambda e, y1=y1, y2=y2, py=py: e.tensor_tensor(out=y2[:, :], in0=y1[:, :], in1=py[:, :], op=ALU.add),
                     reads=[y1, py], writes=[y2])
                P.op("pool", lambda e, y2=y2, y3=y3, sz=sz: e.tensor_tensor(out=y3[:, :], in0=y2[:, :], in1=sz[:, :], op=ALU.mult),
                     reads=[y2, sz], writes=[y3])
                ss = s2.next()
                P.op("act", lambda e, y3=y3, ss=ss: e.activation(out=junk[:, :], in_=y3[:, :], func=AF.Square, accum_out=ss[:, 0:1]),
                     reads=[y3], writes=[junk, ss])
                P.op("act", lambda e, ss=ss: e.activation(out=ss[:, 1:2], in_=ss[:, 0:1], func=AF.Sqrt, scale=1.0 / 512.0, bias=EPS),
                     reads=[ss], writes=[ss])
                P.op("dve", lambda e, ss=ss: e.reciprocal(out=ss[:, 1:2], in_=ss[:, 1:2]), reads=[ss], writes=[ss])
                yn = b512.next()
                P.op("dve", lambda e, y3=y3, ss=ss, yn=yn: e.scalar_tensor_tensor(
                    out=yn[:, :], in0=y3[:, :], scalar=ss[:, 1:2], in1=sng[:, :], op0=ALU.mult, op1=ALU.mult),
                    reads=[y3, ss, sng], writes=[yn])
                pyt = psum.next()
                pytb = pyt.ap.bitcast(BF16)
                for c in range(4):
                    P.op("pe", lambda e, pytb=pytb, yn=yn, c=c: e.transpose(
                        pytb[:, c * 128:(c + 1) * 128], yn[:, c * 128:(c + 1) * 128], ident_b[:, :]),
                        reads=[yn, ident_b], writes=[pyt])
                P.op("act", lambda e, pytb=pytb, yst=yst, sub=sub: e.copy(
                    out=yst[:, :, sub], in_=pytb[:, 0:512].rearrange("p (c n) -> p c n", c=4)), reads=[pyt, yst], writes=[yst])
            pst = psum.next()
            P.op("pe", lambda e, pst=pst, Btm=Btm, xdtw=xdtw: e.matmul(
                pst[:, :], lhsT=Btm[:, 0:128], rhs=xdtw[:, :], start=True, stop=True), reads=[Btm, xdtw], writes=[pst])
            P.op("pool", lambda e, cdec=cdec: e.tensor_tensor(
                out=S[:, :].rearrange("p (h d) -> p h d", h=8), in0=S[:, :].rearrange("p (h d) -> p h d", h=8),
                in1=cdec[:, :].unsqueeze(2).to_broadcast([128, 8, 64]), op=ALU.mult), reads=[S, cdec], writes=[S])
            P.op("dve", lambda e, pst=pst: e.tensor_tensor(out=S[:, :], in0=S[:, :], in1=pst[:, :], op=ALU.add),
                 reads=[S, pst], writes=[S])
            P.op("act", lambda e: e.copy(out=Sbf[:, :], in_=S[:, :]), reads=[S, Sbf], writes=[Sbf])
        if own:
            o0 = tok0 - OWN0
            P.op("pool", lambda e, yst=yst, o0=o0: e.dma_start(
                out=e_yb[g, :, :, o0:o0 + 512].rearrange("c p n -> p c n"), in_=yst[:, :, :]),
                reads=[yst], writes=[e_yb_T], dma=True)


def build_p1_init(P, C, din, cst, NTOKW):
    KT = C.sb([96, 4, NTOKW], BF16, "KT")
    VA = C.sb([128, NTOKW // 128, 2, 3, 64], BF16, "VA")
    kmT = C.sbT([64, 4, 32], BF16, "kmT")
    Mpad = [C.sbT([128, 4, 96], BF16, "Mpad") for _ in range(2)]
    for h in range(4):
        P.op("sp", lambda e, h=h: e.dma_start(out=KT[64:96, h, :], in_=din["kind"]), dma=True)
    P.op("pool", lambda e: e.memset(VA[:, :, :, 1, :], 1.0))
    for mp in Mpad:
        P.op("pool", lambda e, mp=mp: e.memset(mp[:, :, :], 0.0), writes=[mp])
    P.op("pool", lambda e: e.memset(kmT[:, :, :], 0.0), writes=[kmT])
    G = C.sbT([128, 512], F32, "G")
    gq, gk = cst["gq"], cst["gk"]
    for h in range(4):
        P.op("dve", lambda e, h=h: e.tensor_scalar(out=G[:, h * 64:(h + 1) * 64], in0=gq[:, :], scalar1=0.125, scalar2=None,
                                                   op0=ALU.mult), reads=[gq, G], writes=[G])
        P.op("dve", lambda e, h=h: e.tensor_copy(out=G[:, 256 + h * 64:256 + (h + 1) * 64], in_=gk[:, :]), reads=[gk, G], writes=[G])
    bb4 = C.sbT([128, 128], F32, "bb4")
    for h in range(4):
        P.op("dve", lambda e, h=h: e.tensor_copy(out=bb4[:, h * 32:(h + 1) * 32], in_=cst["blkbias"][:, :]),
             reads=[cst["blkbias"], bb4], writes=[bb4])
    P.barrier()
    nm = NTOKW // 512
    return dict(KT=KT, VA=VA, kmT=kmT, Mpad=Ring(Mpad), G=G, bb4=bb4,
                KT_T=[T(None, "KT%d" % i) for i in range(nm)], VA_T=[T(None, "VA%d" % i) for i in range(nm)])


def build_p1b(P, C, din, g, cst, A, hT_scr, hT_T, e_att, e_att_T, NTOKW, OWN0):
    psum = cst["psring"]
    po_ring = cst["po_ring"]
    pss = cst["ps_small"]
    ident_b, negm = cst["ident_b"], cst["negm"]
    KT, VA, kmT, G, bb4 = A["KT"], A["VA"], A["kmT"], A["G"], A["bb4"]
    KT_T, VA_T = A["KT_T"], A["VA_T"]
    stg = Ring([C.sbT([128, 8, WB_COLS], F32, "b_stg") for _ in range(2)])
    w1b = C.sbT([128, 8, W1B], BF16, "w1b")
    load_cast_weight(P, C, din["w1b"][g], w1b, W1B, cst["g1"], stg)
    hring = Ring([C.sbT([128, 8, 512], BF16, "b_hT") for _ in range(2)])
    f512 = Ring([C.sbT([128, 512], F32, "b_f512") for _ in range(4)])
    b512 = Ring([C.sbT([128, 512], BF16, "b_b512") for _ in range(3)])
    ptr = Ring([C.sbT([128, 512], BF16, "b_pt") for _ in range(4)])
    s8 = Ring([C.sbT([128, 8], F32, "b_s8") for _ in range(6)])
    g128 = Ring([C.sbT([128, 128], F32, "b_g128") for _ in range(6)])
    t8r = Ring([C.sbT([128, 32], F32, "b_t8") for _ in range(2)])
    kmf = C.sbT([64, 4, 2], F32, "b_kmf")
    QTr = Ring([C.sbT([96, 4, 512], BF16, "b_QT") for _ in range(2)])
    ast = [Ring([C.sbT([128, 512], BF16, "b_ast") for _ in range(2)]) for _ in range(2)]
    rdr = Ring([C.sbT([128, 512], F32, "b_rd") for _ in range(2)])
    outs = []
    for m in range(NTOKW // 512):
        tok0 = m * 512
        own = tok0 >= OWN0
        c0 = 0 if own else 256
        h0 = 0 if own else 4
        hT = hring.next()
        P.op("sp", lambda e, hT=hT, tok0=tok0: e.dma_start(out=hT[:, :, :], in_=hT_scr[:, :, tok0:tok0 + 512]),
             reads=[hT_T[m]], writes=[hT], dma=True)
        QT = QTr.next() if own else None
        for s in range(4):
            sub = slice(s * 128, (s + 1) * 128)
            kt = m * 4 + s
            pqk, pv = psum.next(), psum.next()
            for kc in range(8):
                P.op("pe", lambda e, pqk=pqk, kc=kc, hT=hT, sub=sub, c0=c0: e.matmul(
                    pqk[:, c0:512], lhsT=hT[:, kc, sub], rhs=w1b[:, kc, c0:512], start=(kc == 0), stop=(kc == 7)),
                    reads=[w1b, hT], writes=[pqk])
            for kc in range(8):
                P.op("pe", lambda e, pv=pv, kc=kc, hT=hT, sub=sub: e.matmul(
                    pv[:, 0:256], lhsT=hT[:, kc, sub], rhs=w1b[:, kc, 512:768], start=(kc == 0), stop=(kc == 7)),
                    reads=[w1b, hT], writes=[pv])
            sq, ssum, tt = f512.next(), s8.next(), f512.next()
            P.op("act", lambda e, pqk=pqk, sq=sq, c0=c0: e.activation(out=sq[:, c0:512], in_=pqk[:, c0:512], func=AF.Square),
                 reads=[pqk], writes=[sq])
            P.op("dve", lambda e, sq=sq, ssum=ssum, c0=c0, h0=h0: e.tensor_reduce(
                out=ssum[:, h0:8], in_=sq[:, c0:512].rearrange("p (h d) -> p h d", d=64), axis=AX.X, op=ALU.add),
                reads=[sq], writes=[ssum])
            P.op("act", lambda e, ssum=ssum, h0=h0: e.activation(
                out=ssum[:, h0:8], in_=ssum[:, h0:8], func=AF.Sqrt, scale=1.0 / 64.0, bias=EPS), reads=[ssum], writes=[ssum])
            P.op("dve", lambda e, ssum=ssum, h0=h0: e.reciprocal(out=ssum[:, h0:8], in_=ssum[:, h0:8]), reads=[ssum], writes=[ssum])
            P.op("dve", lambda e, pqk=pqk, tt=tt, ssum=ssum, c0=c0, h0=h0: e.tensor_tensor(
                out=tt[:, c0:512].rearrange("p (h d) -> p h d", d=64), in0=pqk[:, c0:512].rearrange("p (h d) -> p h d", d=64),
                in1=ssum[:, h0:8].unsqueeze(2).to_broadcast([128, 8 - h0, 64]), op=ALU.mult), reads=[pqk, ssum], writes=[tt])
            qkn = b512.next()
            P.op("pool", lambda e, tt=tt, qkn=qkn, c0=c0: e.tensor_tensor(
                out=qkn[:, c0:512], in0=tt[:, c0:512], in1=G[:, c0:512], op=ALU.mult), reads=[tt, G], writes=[qkn])
            pkt = psum.next()
            pktb = pkt.ap.bitcast(BF16)
            for h in range(4):
                P.op("pe", lambda e, pktb=pktb, qkn=qkn, h=h: e.transpose(
                    pktb[0:64, h * 128:(h + 1) * 128], qkn[:, 256 + h * 64:256 + (h + 1) * 64], ident_b[:, :]),
                    reads=[qkn, ident_b], writes=[pkt])
            P.op("act", lambda e, pktb=pktb, tok0=tok0, s=s: e.copy(
                out=KT[0:64, :, tok0 + s * 128:tok0 + (s + 1) * 128], in_=pktb[0:64, 0:512].rearrange("p (h n) -> p h n", h=4)),
                reads=[pkt, KT_T[m]], writes=[KT_T[m]])
            P.op("act", lambda e, pv=pv, kt=kt: e.copy(
                out=VA[:, kt, :, 0, :], in_=pv[:, 0:256].rearrange("p (a b d) -> p a b d", a=2, b=2)[:, :, 0, :]),
                reads=[pv, VA_T[m]], writes=[VA_T[m]])
            P.op("dve", lambda e, pv=pv, kt=kt: e.tensor_copy(
                out=VA[:, kt, :, 2, :], in_=pv[:, 0:256].rearrange("p (a b d) -> p a b d", a=2, b=2)[:, :, 1, :]),
                reads=[pv, VA_T[m]], writes=[VA_T[m]])
            if own:
                pqt = psum.next()
                pqtb = pqt.ap.bitcast(BF16)
                for h in range(4):
                    P.op("pe", lambda e, pqtb=pqtb, qkn=qkn, h=h: e.transpose(
                        pqtb[0:64, h * 128:(h + 1) * 128], qkn[:, h * 64:(h + 1) * 64], ident_b[:, :]),
                        reads=[qkn, ident_b], writes=[pqt])
                P.op("dve", lambda e, pqtb=pqtb, QT=QT, sub=sub: e.tensor_copy(
                    out=QT[0:64, :, sub], in_=pqtb[0:64, 0:512].rearrange("p (h n) -> p h n", h=4)),
                    reads=[pqt, QT], writes=[QT])
        P.op("dve", lambda e, tok0=tok0: e.tensor_reduce(
            out=kmf[:, :, :], in_=KT[0:64, :, tok0:tok0 + 512].rearrange("p h (b k) -> p h b k", b=2), axis=AX.X, op=ALU.add),
            reads=[KT_T[m]], writes=[kmf])
        P.op("dve", lambda e, m=m: e.tensor_scalar(out=kmT[:, :, 2 * m:2 * m + 2], in0=kmf[:, :, :], scalar1=1.0 / 256.0,
                                                   scalar2=None, op0=ALU.mult), reads=[kmf, kmT], writes=[kmT])
        if not own:
            continue
        for s in range(4):
            sub = slice(s * 128, (s + 1) * 128)
            ownblk = 2 * m + s // 2
            pg = pss["gate"]
            for h in range(4):
                P.op("pe", lambda e, h=h, QT=QT, sub=sub: e.matmul(
                    pg[:, h * 32:(h + 1) * 32], lhsT=QT[0:64, h, sub], rhs=kmT[0:64, h, :], start=True, stop=True),
                    reads=[QT, kmT], writes=[pg])
            gm, m1, m2, t8 = g128.next(), g128.next(), g128.next(), t8r.next()
            P.op("dve", lambda e, gm=gm: e.tensor_tensor(out=gm[:, :], in0=pg[:, :], in1=bb4[:, :], op=ALU.add),
                 reads=[pg, bb4], writes=[gm])
            P.op("pool", lambda e, gm=gm, ownblk=ownblk: e.memset(
                gm[:, :].rearrange("p (h b) -> p h b", h=4)[:, :, ownblk:32], NEG), reads=[gm], writes=[gm])
            for h in range(4):
                P.op("dve", lambda e, gm=gm, t8=t8, h=h: e.max(out=t8[:, h * 8:(h + 1) * 8], in_=gm[:, h * 32:(h + 1) * 32]),
                     reads=[gm, t8], writes=[t8])
            P.op("dve", lambda e, gm=gm, m1=m1, t8=t8: e.tensor_tensor(
                out=m1[:, :].rearrange("p (h b) -> p h b", h=4), in0=gm[:, :].rearrange("p (h b) -> p h b", h=4),
                in1=t8[:, :].rearrange("p (h k) -> p h k", h=4)[:, :, 2:3].to_broadcast([128, 4, 32]), op=ALU.is_lt),
                reads=[gm, t8], writes=[m1])
            P.op("dve", lambda e, gm=gm, m2=m2: e.tensor_scalar(
                out=m2[:, :], in0=gm[:, :], scalar1=NEG / 2, scalar2=NEG, op0=ALU.is_lt, op1=ALU.mult), reads=[gm], writes=[m2])
            Mp = A["Mpad"].next()
            P.op("dve", lambda e, Mp=Mp, m1=m1, m2=m2: e.scalar_tensor_tensor(
                out=Mp[:, :, 64:96], in0=m1[:, :].rearrange("p (h b) -> p h b", h=4), scalar=NEG,
                in1=m2[:, :].rearrange("p (h b) -> p h b", h=4), op0=ALU.mult, op1=ALU.min), reads=[m1, m2, Mp], writes=[Mp])
            P.op("pool", lambda e, Mp=Mp, ownblk=ownblk: e.memset(Mp[:, :, 64 + ownblk:65 + ownblk], 0.0), reads=[Mp], writes=[Mp])
            pmt = psum.next()
            pmtb = pmt.ap.bitcast(BF16)
            for h in range(4):
                P.op("pe", lambda e, pmtb=pmtb, Mp=Mp, h=h: e.transpose(
                    pmtb[0:96, h * 128:(h + 1) * 128], Mp[:, h, :], ident_b[:, :]), reads=[Mp, ident_b], writes=[pmt])
            P.op("act", lambda e, pmtb=pmtb, QT=QT, sub=sub: e.copy(
                out=QT[64:96, :, sub], in_=pmtb[64:96, 0:512].rearrange("p (h n) -> p h n", h=4)), reads=[pmt, QT], writes=[QT])
        nkt = (2 * m + 2) * 2
        o0 = tok0 - OWN0
        for h in range(4):
            pair, hb = h // 2, h % 2
            po = po_ring.next()
            for kt in range(nkt):
                blk = kt // 2
                mk = kt // 4
                if blk < 2 * m:
                    a0, a1, cz = 0, 512, None
                elif blk == 2 * m:
                    a0, a1, cz = 0, 512, 0
                else:
                    a0, a1, cz = 256, 512, 256
                ps = psum.next()
                P.op("pe", lambda e, ps=ps, h=h, kt=kt, QT=QT, a0=a0, a1=a1, cz=cz: e.matmul(
                    ps[:, a0:a1], lhsT=KT[0:96, h, kt * 128:(kt + 1) * 128], rhs=QT[0:96, h, a0:a1],
                    start=True, stop=(cz is None)), reads=[KT_T[mk], QT], writes=[ps])
                if cz is not None:
                    P.op("pe", lambda e, ps=ps, kt=kt, cz=cz: e.matmul(
                        ps[:, cz:cz + 256], lhsT=ident_b[:, :], rhs=negm[:, kt % 2, :], start=False, stop=True),
                        reads=[ident_b, negm], writes=[ps])
                pt = ptr.next()
                P.op("act", lambda e, ps=ps, pt=pt, a0=a0, a1=a1: e.activation(out=pt[:, a0:a1], in_=ps[:, a0:a1], func=AF.Exp),
                     reads=[ps], writes=[pt])
                P.op("pe", lambda e, po=po, pt=pt, kt=kt, pair=pair, hb=hb, a0=a0, a1=a1, nkt=nkt: e.matmul(
                    po[:, a0:a1], lhsT=VA[:, kt, pair, hb:hb + 2, :].rearrange("p a d -> p (a d)"), rhs=pt[:, a0:a1],
                    start=(kt == 0), stop=(kt == nkt - 1), skip_group_check=True), reads=[VA_T[mk], pt], writes=[po])
            nr = slice(0, 64) if hb == 0 else slice(64, 128)
            dr = slice(64, 128) if hb == 0 else slice(0, 64)
            rd = rdr.next()
            if hb == 0:
                at_ = ast[pair].next()
                ast_cur = at_
            else:
                at_ = ast_cur
            P.op("dve", lambda e, po=po, rd=rd, nr=nr, dr=dr: e.reciprocal(out=rd[nr, :], in_=po[dr, :]), reads=[po], writes=[rd])
            P.op("dve", lambda e, po=po, rd=rd, nr=nr, at_=at_: e.tensor_tensor(
                out=at_[nr, :], in0=po[nr, :], in1=rd[nr, :], op=ALU.mult), reads=[po, rd, at_], writes=[at_])
            if hb == 1:
                outs.append(P.op("pool", lambda e, at_=at_, pair=pair, o0=o0: e.dma_start(
                    out=e_att[g, pair, :, o0:o0 + 512], in_=at_[:, :]), reads=[at_], writes=[e_att_T], dma=True))
    return outs


def load_consts(P, C, din, names_shapes):
    out = {}
    for name, shape, dt in names_shapes:
        t = C.sbT(shape, dt, name)
        P.op("sp", lambda e, t=t, name=name: e.dma_start(out=t.ap, in_=din[name]), writes=[t], dma=True)
        out[name] = t
    return out


def build_program(mode, NTOKW=8192, OWN0=0, NG=1):
    nc = bass.Bass("TRN2", target_bir_lowering=False)
    P = Prog(nc)
    with ExitStack() as es:
        C = Ctx(nc, es, P)
        din = {}

        def inp(name, shape, dt=F32):
            din[name] = C.dram(name, shape, dt, "ExternalInput")

        psum = [T(es.enter_context(nc.psum_tensor("ps%d" % i, [128, 512], F32))[:, :], "ps%d" % i, excl=True) for i in range(8)]
        final = []
        NOWN = NTOKW - OWN0
        if mode in ("p1", "fused"):
            inp("xw", [NTOKW, 1024])
            inp("w1a", [NG, 1024, W1A])
            inp("w1b", [NG, 1024, W1B])
            inp("convw", [NG, 128, 6, 4])
            inp("convb", [NG, 128, 6])
            for nm in ("dtb", "alog", "dsk"):
                inp(nm, [NG, 128, 8])
            inp("sng", [NG, 128, 512])
            inp("kind", [32, NTOKW], BF16)
            shapes1 = [("gq", [128, 64], F32), ("gk", [128, 64], F32), ("g1", [128, 8], F32),
                       ("tokmask", [128, NTOKW // 128], F32), ("blkbias", [128, 32], F32),
                       ("ident_b", [128, 128], BF16), ("ones_b", [128, 128], BF16), ("U", [128, 128], BF16),
                       ("T1", [128, 128], BF16), ("negm", [128, 2, 256], BF16)]
            for nm, shp, dt in shapes1:
                if nm not in din:
                    inp(nm, shp, dt)
            cst = load_consts(P, C, din, shapes1)
            cst["psring"] = Ring(psum[0:5])
            cst["po_ring"] = Ring(psum[5:7])
            cst["ps_small"] = {"dt": TV(psum[5], psum[5].ap[:, 0:8]), "acs": TV(psum[5], psum[5].ap[:, 16:32]),
                               "cb": TV(psum[6], psum[6].ap[:, 0:128]), "gate": TV(psum[7], psum[7].ap[:, 0:128])}
            kind_e = "ExternalOutput" if mode == "p1" else "Internal"
            e_att = C.dram("e_att", [NG, 2, 128, NOWN], BF16, kind_e)
            e_yb = C.dram("e_yb", [NG, 4, 128, NOWN], BF16, kind_e)
            e_att_T, e_yb_T = T(None, "e_att"), T(None, "e_yb")
            hT_scr = C.dram("hT_scr", [128, 8, NTOKW], BF16, "Internal")
            hT_T = []
            with ExitStack() as es1:
                C1 = Ctx(nc, es1, P)
                if STAGES.get("pro", True):
                    build_prologue(P, C1, din, cst, hT_scr, hT_T, NTOKW)
            P.barrier()
            with ExitStack() as es1:
                C1 = Ctx(nc, es1, P)
                for g in range(NG):
                    if STAGES.get("a", True):
                        with ExitStack() as es2:
                            build_p1a(P, Ctx(nc, es2, P), din, g, cst, hT_scr, hT_T, e_yb, e_yb_T, NTOKW, OWN0)
                        P.barrier()
                    if STAGES.get("b", True):
                        with ExitStack() as es2:
                            C2b = Ctx(nc, es2, P)
                            A = build_p1_init(P, C2b, din, cst, NTOKW)
                            build_p1b(P, C2b, din, g, cst, A, hT_scr, hT_T, e_att, e_att_T, NTOKW, OWN0)
                        P.barrier()
            if mode == "p1":
                final = [o for o in P.ops["pool"] if o.dma][-8:]
        if mode in ("p2", "fused"):
            inp("x2", [2048, 1024])
            inp("p2", [2048, 256])
            for name, (K, N) in P2W.items():
                inp(name, [K, N])
            shapes2 = [("g1", [128, 8], F32), ("g2", [128, 8], F32), ("g3", [128, 8], F32),
                       ("ident_b", [128, 128], BF16), ("ones_b", [128, 128], BF16)]
            for nm, shp, dt in shapes2:
                if nm not in din:
                    inp(nm, shp, dt)
            consts = load_consts(P, C, din, shapes2)
            consts["psum"] = psum
            wscr = {name: C.dram("scr_" + name, [K, N], BF16, "Internal") for name, (K, N) in P2W.items()}
            wT = {}
            with ExitStack() as es2:
                C2 = Ctx(nc, es2, P)
                if STAGES.get("precast", True):
                    build_precast(P, C2, din, wscr, wT, consts)
            P.barrier()
            if mode == "p2":
                inp("e_att", [4, 2, 128, 2048], BF16)
                inp("e_yb", [4, 4, 128, 2048], BF16)
                exch = {"att": din["e_att"], "yb": din["e_yb"], "att_T": T(None), "yb_T": T(None)}
            else:
                exch = {"att": e_att, "yb": e_yb, "att_T": e_att_T, "yb_T": e_yb_T}
            dout = C.dram("out", [2048, 1024], F32, "ExternalOutput")
            if STAGES.get("p2", True):
                with ExitStack() as es3:
                    C3 = Ctx(nc, es3, P)
                    final = build_phase2(P, C3, din, wscr, wT, consts, exch, dout, NTOK=STAGES.get('ntok', 2048))
            else:
                final = [o for o in P.ops["pool"] if o.dma][-8:]
        P.emit(final)
    return nc


BF = ml_dtypes.bfloat16


def host_consts():
    return {
        "ident_b": np.eye(128, dtype=np.float32).astype(BF),
        "ones_b": np.ones((128, 128), dtype=np.float32).astype(BF),
    }


def host_consts1(NTOKW):
    i = np.arange(128)
    U = (i[:, None] <= i[None, :]).astype(np.float32)
    T1 = (i[:, None] > i[None, :]).astype(np.float32)
    q = np.arange(256)
    negm = np.stack([np.where((kt * 128 + i[:, None]) <= q[None, :], 0.0, NEG) for kt in range(2)], 1).astype(np.float32)
    kind = (np.arange(NTOKW)[None, :] // 256 == np.arange(32)[:, None]).astype(np.float32)
    return {"ident_b": np.eye(128, dtype=np.float32).astype(BF), "ones_b": np.ones((128, 128), np.float32).astype(BF),
            "U": U.astype(BF), "T1": T1.astype(BF), "negm": negm.astype(BF), "kind": kind.astype(BF)}


def gain_layout(g):
    return np.ascontiguousarray(g.reshape(8, 128).T)


def bc(v):
    return np.ascontiguousarray(np.broadcast_to(v[None, :], (128, v.shape[0]))).astype(np.float32)


def p1_inputs(inputs, b, groups, xw, tokmask, blkbias, NTOKW):
    w_in = inputs["w_in"][0]
    cw, cbias = inputs["conv_w"][0], inputs["conv_b"][0]
    w1a, w1b, convw, convb, dtb, alog, dsk, sng = [], [], [], [], [], [], [], []
    for g in groups:
        cols_a = np.concatenate([np.arange(5120 + 512 * g, 5120 + 512 * (g + 1)), np.arange(7168 + 128 * g, 7168 + 128 * (g + 1)),
                                 np.arange(7680 + 128 * g, 7680 + 128 * (g + 1)), np.arange(3072 + 512 * g, 3072 + 512 * (g + 1)),
                                 np.arange(8192 + 8 * g, 8192 + 8 * (g + 1))])
        cols_b = np.concatenate([np.arange(256 * g, 256 * (g + 1)), np.arange(1024 + 256 * g, 1024 + 256 * (g + 1)),
                                 np.arange(2048 + 256 * g, 2048 + 256 * (g + 1))])
        w1a.append(w_in[:, cols_a])
        w1b.append(w_in[:, cols_b])
        ch = np.concatenate([np.arange(512 * g, 512 * (g + 1)), np.arange(2048 + 128 * g, 2048 + 128 * (g + 1)),
                             np.arange(2560 + 128 * g, 2560 + 128 * (g + 1))])
        convw.append(cw[:, ch].T.reshape(6, 128, 4).transpose(1, 0, 2))
        convb.append(cbias[ch].reshape(6, 128).T)
        dtb.append(bc(inputs["dt_bias"][0][8 * g:8 * g + 8]))
        alog.append(bc(inputs["a_log"][0][8 * g:8 * g + 8]))
        dsk.append(bc(inputs["d_skip"][0][8 * g:8 * g + 8]))
        sng.append(bc(inputs["ssm_norm_g"][0][512 * g:512 * g + 512]))
    m = {"xw": np.ascontiguousarray(xw), "w1a": np.ascontiguousarray(np.stack(w1a)), "w1b": np.ascontiguousarray(np.stack(w1b)),
         "convw": np.ascontiguousarray(np.stack(convw)), "convb": np.ascontiguousarray(np.stack(convb)),
         "dtb": np.stack(dtb), "alog": np.stack(alog), "dsk": np.stack(dsk), "sng": np.stack(sng),
         "gq": bc(inputs["q_norm_g"][0]), "gk": bc(inputs["k_norm_g"][0]), "g1": gain_layout(inputs["ln1_g"][0]),
         "tokmask": np.ascontiguousarray(tokmask.reshape(-1, 128).T).astype(np.float32), "blkbias": bc(blkbias)}
    m.update(host_consts1(NTOKW))
    return m


def p2_inputs(inputs, core, e_att=None, e_yb=None):
    b, t = core // 4, core % 4
    sl = slice(t * 2048, (t + 1) * 2048)
    m = {
        "x2": np.ascontiguousarray(inputs["x"][b, sl]),
        "p2": np.ascontiguousarray(inputs["p"][0, b, sl]),
        "wg": np.ascontiguousarray(inputs["w_in"][0][:, 8224:10272]),
        "woa": inputs["w_o_attn"][0], "wos": inputs["w_o_ssm"][0], "wout": inputs["w_out"][0],
        "wgu": inputs["w_gate_up"][0], "wd": inputs["w_down"][0], "wpg": inputs["w_ple_gate"][0],
        "wpp": inputs["w_ple_proj"][0],
        "g1": gain_layout(inputs["ln1_g"][0]), "g2": gain_layout(inputs["ln2_g"][0]),
        "g3": gain_layout(inputs["ln3_g"][0]),
    }
    m.update(host_consts())
    if e_att is not None:
        m["e_att"] = e_att
        m["e_yb"] = e_yb
    return m


MODE = "fused"


def kernel(**inputs):
    inputs = {k: np.asarray(v) for k, v in inputs.items()}
    x = inputs["x"]
    out = np.zeros(x.shape, np.float32)
    if MODE == "two":
        nc1 = build_program("p1", NTOKW=8192, OWN0=0, NG=1)
        maps1 = []
        for core in range(8):
            b, g = core // 4, core % 4
            maps1.append(p1_inputs(inputs, b, [g], x[b], np.ones(8192, np.float32), np.zeros(32, np.float32), 8192))
        r1 = run_bass_kernel_spmd(nc1, maps1, core_ids=list(range(8))).results
        nc2 = build_program("p2")
        maps2 = []
        for core in range(8):
            b, t = core // 4, core % 4
            sl = slice(t * 2048, (t + 1) * 2048)
            ea = np.stack([np.asarray(r1[b * 4 + g]["e_att"])[0][:, :, sl] for g in range(4)])
            ey = np.stack([np.asarray(r1[b * 4 + g]["e_yb"])[0][:, :, sl] for g in range(4)])
            maps2.append(p2_inputs(inputs, core, np.ascontiguousarray(ea), np.ascontiguousarray(ey)))
        r2 = run_bass_kernel_spmd(nc2, maps2, core_ids=list(range(8))).results
        for core in range(8):
            b, t = core // 4, core % 4
            out[b, t * 2048:(t + 1) * 2048] = np.asarray(r2[core]["out"])
        return out
    nc = build_program("fused", NTOKW=8192, OWN0=6144, NG=4)
    maps = []
    for core in range(8):
        b, t = core // 4, core % 4
        npad = (3 - t) * 2048
        xw = np.concatenate([np.zeros((npad, 1024), np.float32), x[b, :(t + 1) * 2048]], 0)
        tokmask = np.concatenate([np.zeros(npad, np.float32), np.ones(8192 - npad, np.float32)])
        blkbias = np.where(np.arange(32) < npad // 256, NEG, 0.0).astype(np.float32)
        m = p1_inputs(inputs, b, [0, 1, 2, 3], xw, tokmask, blkbias, 8192)
        m.update(p2_inputs(inputs, core))
        maps.append(m)
    r = run_bass_kernel_spmd(nc, maps, core_ids=list(range(8))).results
    for core in range(8):
        b, t = core // 4, core % 4
        out[b, t * 2048:(t + 1) * 2048] = np.asarray(r[core]["out"])
    return out
```

```python
import numpy as np
from contextlib import ExitStack
import ml_dtypes
import concourse.bass as bass
import concourse.mybir as mybir
from concourse.bass_utils import run_bass_kernel_spmd

F32 = mybir.dt.float32
BF16 = mybir.dt.bfloat16
AF = mybir.ActivationFunctionType
ALU = mybir.AluOpType
AX = mybir.AxisListType

EPS = 1e-6
NEG = -30000.0
SAME_ENGINE_SYNC = True
STAGES = {}


class T:
    __slots__ = ("ap", "w", "r", "name", "excl")

    def __init__(self, ap=None, name="", excl=False):
        self.ap = ap
        self.w = None
        self.r = []
        self.name = name
        self.excl = excl

    def __getitem__(self, k):
        return self.ap[k]


class TV(T):
    __slots__ = ("parent",)

    def __init__(self, parent, ap):
        self.parent = parent
        self.ap = ap
        self.name = parent.name
        self.excl = parent.excl

    @property
    def w(self):
        return self.parent.w

    @w.setter
    def w(self, v):
        self.parent.w = v

    @property
    def r(self):
        return self.parent.r

    @r.setter
    def r(self, v):
        self.parent.r = v


class Op:
    __slots__ = ("eng", "fn", "deps", "dma", "inc", "sem", "ticket", "idx", "prev_same_sem")

    def __init__(self, eng, fn, dma):
        self.eng = eng
        self.fn = fn
        self.dma = dma
        self.deps = []
        self.inc = False
        self.sem = None
        self.ticket = 0
        self.prev_same_sem = None


class Prog:
    ENGS = ("pe", "act", "dve", "pool", "sp")
    NDS = 8

    def __init__(self, nc):
        self.nc = nc
        self.ops = {e: [] for e in self.ENGS}
        self.all = []
        self.bar = {}

    def op(self, eng, fn, reads=(), writes=(), dma=False):
        o = Op(eng, fn, dma)
        deps = []
        for t in reads:
            if t.w is not None:
                deps.append(t.w)
            if t.excl:
                deps.extend(x for x in t.r if x.eng != eng)
        for t in writes:
            if t.w is not None:
                deps.append(t.w)
            deps.extend(t.r)
        b = self.bar.pop(eng, None)
        if b:
            deps.extend(b)
        seen = set()
        for d in deps:
            if d is o or id(d) in seen:
                continue
            seen.add(id(d))
            if (not d.dma) and d.eng == eng and (eng == "pe" or not SAME_ENGINE_SYNC):
                continue
            o.deps.append(d)
            d.inc = True
        for t in reads:
            if dma:
                t.r.append(o)
            else:
                t.r = [x for x in t.r if x.dma or x.eng != eng] + [o]
        for t in writes:
            t.w = o
            t.r = []
        if dma:
            o.inc = True
        self.ops[eng].append(o)
        self.all.append(o)
        return o

    def barrier(self):
        last = []
        for e in self.ENGS:
            ops = self.ops[e]
            if ops:
                last.append(ops[-1])
            last.extend([o for o in ops if o.dma][-self.NDS:])
        for e in self.ENGS:
            self.bar[e] = list(last)

    def emit(self, final_ops):
        nc = self.nc
        with ExitStack() as es:
            SEM_CAP = 1000
            nsem = {e: sum(1 for o in self.ops[e] if o.inc and not o.dma) // SEM_CAP + 1 for e in ("pe", "act", "dve", "pool")}
            csem = {e: [es.enter_context(nc.semaphore("cs_%s%d" % (e, i))) for i in range(nsem[e])]
                    for e in ("pe", "act", "dve", "pool")}
            dsem = {e: [es.enter_context(nc.semaphore("ds_%s%d" % (e, i))) for i in range(self.NDS)]
                    for e in self.ENGS}
            ccount = {e: 0 for e in self.ENGS}
            dcount = {e: [0] * self.NDS for e in self.ENGS}
            drr = {e: 0 for e in self.ENGS}
            dlast = {e: [None] * self.NDS for e in self.ENGS}
            for e in self.ENGS:
                for o in self.ops[e]:
                    if o.dma:
                        k = drr[e] % self.NDS
                        drr[e] += 1
                        dcount[e][k] += 16
                        o.sem = dsem[e][k]
                        o.ticket = dcount[e][k]
                        o.prev_same_sem = dlast[e][k]
                        dlast[e][k] = o
                    elif o.inc:
                        o.sem = csem[e][ccount[e] // SEM_CAP]
                        o.ticket = ccount[e] % SEM_CAP + 1
                        ccount[e] += 1

            def run(ename, eng):
                seen = {}

                def wait(d):
                    key = id(d.sem)
                    if seen.get(key, 0) < d.ticket:
                        eng.wait_ge(d.sem, d.ticket)
                        seen[key] = d.ticket

                for o in self.ops[ename]:
                    for d in o.deps:
                        wait(d)
                    if o.dma and o.prev_same_sem is not None:
                        wait(o.prev_same_sem)
                    ins = o.fn(eng)
                    if o.sem is not None:
                        ins.then_inc(o.sem, 16 if o.dma else 1)
                if ename == "sp":
                    for d in final_ops:
                        wait(d)

            with nc.Block() as block:
                @block.tensor
                def _(eng):
                    run("pe", eng)

                @block.scalar
                def _(eng):
                    run("act", eng)

                @block.vector
                def _(eng):
                    run("dve", eng)

                @block.gpsimd
                def _(eng):
                    run("pool", eng)

                @block.sync
                def _(eng):
                    run("sp", eng)


class Ctx:
    N = 0

    def __init__(self, nc, es, P):
        self.nc, self.es, self.P = nc, es, P
        self.n = 0

    def sb(self, shape, dt, name=None):
        Ctx.N += 1
        h = self.es.enter_context(self.nc.sbuf_tensor("%s_%d" % (name or "sb", Ctx.N), list(shape), dt))
        return h

    def sbT(self, shape, dt, name=None):
        h = self.sb(shape, dt, name)
        return T(h[tuple(slice(None) for _ in shape)], name or "")

    def dram(self, name, shape, dt, kind):
        return self.nc.dram_tensor(name, list(shape), dt, kind=kind).ap()


class Ring:
    def __init__(self, tiles):
        self.tiles = tiles
        self.i = 0

    def next(self):
        t = self.tiles[self.i % len(self.tiles)]
        self.i += 1
        return t


P2W = {
    "wg": (1024, 2048), "woa": (1024, 1024), "wos": (2048, 1024), "wout": (1024, 1024),
    "wgu": (1024, 5632), "wd": (2816, 1024), "wpg": (1024, 1024), "wpp": (256, 1024),
}
P2W_GAIN = {"wg": "g1", "wgu": "g2", "wpg": "g3"}
WB_COLS = 256


def wblocks(name):
    K, N = P2W[name]
    nkc = K // 128
    kbs = []
    k0 = 0
    while k0 < nkc:
        nk = min(8, nkc - k0)
        kbs.append((k0, nk))
        k0 += nk
    return kbs, N // WB_COLS


def build_precast(P, C, din, wscr, wT, consts):
    nc = P.nc
    stg = Ring([C.sbT([128, 8, WB_COLS], F32, "pc_stg") for _ in range(2)])
    wbf = Ring([C.sbT([128, 8, WB_COLS], BF16, "pc_bf") for _ in range(2)])
    for name in P2W:
        kbs, ncb = wblocks(name)
        src = din[name]
        dst = wscr[name]
        gain = consts.get(P2W_GAIN.get(name))
        for cb in range(ncb):
            for (k0, nk) in kbs:
                s, w = stg.next(), wbf.next()
                sv = src[k0 * 128:(k0 + nk) * 128, cb * WB_COLS:(cb + 1) * WB_COLS].rearrange("(k p) n -> p k n", p=128)
                dv = dst[k0 * 128:(k0 + nk) * 128, cb * WB_COLS:(cb + 1) * WB_COLS].rearrange("(k p) n -> p k n", p=128)
                P.op("sp", lambda e, s=s, sv=sv, nk=nk: e.dma_start(out=s[:, 0:nk, :], in_=sv), writes=[s], dma=True)
                if gain is not None:
                    gv = gain[:, k0:k0 + nk].unsqueeze(2).to_broadcast([128, nk, WB_COLS])
                    P.op("pool", lambda e, s=s, w=w, gv=gv, nk=nk: e.tensor_tensor(
                        out=w[:, 0:nk, :], in0=s[:, 0:nk, :], in1=gv, op=ALU.mult), reads=[s, gain], writes=[w])
                else:
                    P.op("pool", lambda e, s=s, w=w, nk=nk: e.tensor_copy(out=w[:, 0:nk, :], in_=s[:, 0:nk, :]),
                         reads=[s], writes=[w])
                t = T(None, "wscr")
                wT[(name, cb, k0)] = t
                P.op("pool", lambda e, w=w, dv=dv, nk=nk: e.dma_start(out=dv, in_=w[:, 0:nk, :]),
                     reads=[w], writes=[t], dma=True)


def build_phase2(P, C, din, wscr, wT, consts, exch, dout, NTOK=2048, TT=1024):
    nc = P.nc
    NTG = TT // 512
    NSUB = TT // 128
    ident_b = consts["ident_b"]
    ones_b = consts["ones_b"]
    xT = [C.sbT([128, TT], F32, "xT") for _ in range(8)]
    hT = [C.sbT([128, TT], BF16, "hT") for _ in range(8)]
    big = C.sb([128, 24, TT], BF16, "big")
    attT = [T(big[:, c, :], "att") for c in range(8)]
    ybT = [T(big[:, 8 + c, :], "yb") for c in range(16)]
    mg = [C.sbT([128, TT], BF16, "mg") for _ in range(8)]
    pT = [C.sbT([128, TT], BF16, "pT") for _ in range(2)]
    xin = Ring([C.sbT([128, 1024], F32, "xin") for _ in range(2)])
    xhi = Ring([C.sbT([128, 1024], BF16, "xhi") for _ in range(2)])
    xlo = Ring([C.sbT([128, 1024], BF16, "xlo") for _ in range(2)])
    xres = Ring([C.sbT([128, 1024], F32, "xres") for _ in range(1)])
    pin = Ring([C.sbT([128, 256], F32, "pin") for _ in range(2)])
    pinb = Ring([C.sbT([128, 256], BF16, "pinb") for _ in range(2)])
    wring = Ring([C.sbT([128, 8, WB_COLS], BF16, "wr") for _ in range(8)])
    tmpf = Ring([C.sbT([128, 512], F32, "tmpf") for _ in range(8)])
    sqb = Ring([C.sbT([128, 512], BF16, "sqb") for _ in range(3)])
    rstd = [C.sbT([128, 512], F32, "rstd") for _ in range(NTG)]
    psum = Ring(consts["psum"])

    def load_w(name, cb, k0, nk):
        w = wring.next()
        sv = wscr[name][k0 * 128:(k0 + nk) * 128, cb * WB_COLS:(cb + 1) * WB_COLS].rearrange("(k p) n -> p k n", p=128)
        P.op("sp", lambda e, w=w, sv=sv, nk=nk: e.dma_start(out=w[:, 0:nk, :], in_=sv),
             reads=[wT[(name, cb, k0)]], writes=[w], dma=True)
        return w

    def mm_group(ps, wlist, X, tg, oc_in_blk):
        n = sum(nk for _, _, nk in wlist)
        i = 0
        for (w, k0, nk) in wlist:
            for k in range(nk):
                st, sp_ = (i == 0), (i == n - 1)
                xk = X[k0 + k]
                P.op("pe", lambda e, ps=ps, w=w, k=k, xk=xk, st=st, sp_=sp_: e.matmul(
                    ps[:, :], lhsT=w[:, k, oc_in_blk * 128:(oc_in_blk + 1) * 128],
                    rhs=xk[:, tg * 512:(tg + 1) * 512], start=st, stop=sp_),
                    reads=[w, xk], writes=[ps])
                i += 1

    def rmsnorm():
        for tg in range(NTG):
            ps = psum.next()
            for kc in range(8):
                sq = sqb.next()
                P.op("act", lambda e, sq=sq, kc=kc, tg=tg: e.activation(
                    out=sq[:, :], in_=xT[kc][:, tg * 512:(tg + 1) * 512], func=AF.Square), reads=[xT[kc]], writes=[sq])
                P.op("pe", lambda e, ps=ps, sq=sq, kc=kc: e.matmul(
                    ps[:, :], lhsT=ones_b[:, :], rhs=sq[:, :], start=(kc == 0), stop=(kc == 7)),
                    reads=[sq, ones_b], writes=[ps])
            r = rstd[tg]
            P.op("act", lambda e, r=r, ps=ps: e.activation(
                out=r[:, :], in_=ps[:, :], func=AF.Sqrt, scale=1.0 / 1024.0, bias=EPS), reads=[ps], writes=[r])
            P.op("dve", lambda e, r=r: e.reciprocal(out=r[:, :], in_=r[:, :]), reads=[r], writes=[r])
        for kc in range(8):
            for tg in range(NTG):
                P.op("dve", lambda e, kc=kc, tg=tg: e.tensor_tensor(
                    out=hT[kc][:, tg * 512:(tg + 1) * 512], in0=xT[kc][:, tg * 512:(tg + 1) * 512],
                    in1=rstd[tg][:, :], op=ALU.mult), reads=[xT[kc], rstd[tg]], writes=[hT[kc]])

    out_ops = []
    for tt in range(NTOK // TT):
        t0 = tt * TT
        for s in range(NSUB):
            xi = xin.next()
            P.op("sp", lambda e, xi=xi, s=s, t0=t0: e.dma_start(out=xi[:, :], in_=din["x2"][t0 + s * 128:t0 + (s + 1) * 128, :]),
                 writes=[xi], dma=True)
            xh, xl, xr = xhi.next(), xlo.next(), xres.next()
            P.op("act", lambda e, xi=xi, xh=xh: e.copy(out=xh[:, :], in_=xi[:, :]), reads=[xi], writes=[xh])
            P.op("dve", lambda e, xi=xi, xh=xh, xr=xr: e.tensor_tensor(out=xr[:, :], in0=xi[:, :], in1=xh[:, :], op=ALU.subtract),
                 reads=[xi, xh], writes=[xr])
            P.op("pool", lambda e, xr=xr, xl=xl: e.tensor_copy(out=xl[:, :], in_=xr[:, :]), reads=[xr], writes=[xl])
            for half in range(2):
                ps = psum.next()
                for j in range(4):
                    kc = half * 4 + j
                    P.op("pe", lambda e, ps=ps, xh=xh, kc=kc, j=j: e.matmul(
                        ps[:, j * 128:(j + 1) * 128], lhsT=xh[:, kc * 128:(kc + 1) * 128], rhs=ident_b[:, :], start=True, stop=False),
                        reads=[xh, ident_b], writes=[ps])
                    P.op("pe", lambda e, ps=ps, xl=xl, kc=kc, j=j: e.matmul(
                        ps[:, j * 128:(j + 1) * 128], lhsT=xl[:, kc * 128:(kc + 1) * 128], rhs=ident_b[:, :], start=False, stop=True),
                        reads=[xl, ident_b], writes=[ps])
                for j in range(4):
                    kc = half * 4 + j
                    eng = "act" if j % 2 == 0 else "dve"
                    if eng == "act":
                        P.op("act", lambda e, ps=ps, kc=kc, j=j, s=s: e.copy(
                            out=xT[kc][:, s * 128:(s + 1) * 128], in_=ps[:, j * 128:(j + 1) * 128]),
                            reads=[ps], writes=[xT[kc]])
                    else:
                        P.op("dve", lambda e, ps=ps, kc=kc, j=j, s=s: e.tensor_copy(
                            out=xT[kc][:, s * 128:(s + 1) * 128], in_=ps[:, j * 128:(j + 1) * 128]),
                            reads=[ps], writes=[xT[kc]])
            pi, pb = pin.next(), pinb.next()
            P.op("sp", lambda e, pi=pi, s=s, t0=t0: e.dma_start(out=pi[:, :], in_=din["p2"][t0 + s * 128:t0 + (s + 1) * 128, :]),
                 writes=[pi], dma=True)
            P.op("pool", lambda e, pi=pi, pb=pb: e.tensor_copy(out=pb[:, :], in_=pi[:, :]), reads=[pi], writes=[pb])
            ps = psum.next()
            psb = ps.ap.bitcast(BF16)
            for j in range(2):
                P.op("pe", lambda e, psb=psb, pb=pb, j=j: e.transpose(
                    psb[:, j * 128:(j + 1) * 128], pb[:, j * 128:(j + 1) * 128], ident_b[:, :]),
                    reads=[pb, ident_b], writes=[ps])
            for j in range(2):
                P.op("act", lambda e, psb=psb, j=j, s=s: e.copy(
                    out=pT[j][:, s * 128:(s + 1) * 128], in_=psb[:, j * 128:(j + 1) * 128]), reads=[ps], writes=[pT[j]])
        for g in range(4):
            for c in range(2):
                tl = attT[2 * g + c]
                P.op("sp", lambda e, tl=tl, g=g, c=c, t0=t0: e.dma_start(out=tl[:, :], in_=exch["att"][g, c, :, t0:t0 + TT]),
                     reads=[exch["att_T"]], writes=[tl], dma=True)
            for c in range(4):
                tl = ybT[4 * g + c]
                P.op("sp", lambda e, tl=tl, g=g, c=c, t0=t0: e.dma_start(out=tl[:, :], in_=exch["yb"][g, c, :, t0:t0 + TT]),
                     reads=[exch["yb_T"]], writes=[tl], dma=True)
        p2s = STAGES.get('p2s', 'BXCD')
        if 'B' in p2s:
            rmsnorm()
        for j in (range(4) if 'B' in p2s else []):
            wga = load_w("wg", j, 0, 8)
            wgb = load_w("wg", 4 + j, 0, 8)
            wa = load_w("woa", j, 0, 8)
            ws0 = load_w("wos", j, 0, 8)
            ws1 = load_w("wos", j, 8, 8)
            for o2 in range(2):
                oc = 2 * j + o2
                for tg in range(NTG):
                    pga, pgb, pya, pyb = psum.next(), psum.next(), psum.next(), psum.next()
                    mm_group(pga, [(wga, 0, 8)], hT, tg, o2)
                    mm_group(pgb, [(wgb, 0, 8)], hT, tg, o2)
                    mm_group(pya, [(wa, 0, 8)], attT, tg, o2)
                    mm_group(pyb, [(ws0, 0, 8), (ws1, 8, 8)], ybT, tg, o2)
                    sa, sb_, m1, m2 = tmpf.next(), tmpf.next(), tmpf.next(), tmpf.next()
                    P.op("act", lambda e, sa=sa, pga=pga: e.activation(out=sa[:, :], in_=pga[:, :], func=AF.Sigmoid),
                         reads=[pga], writes=[sa])
                    P.op("act", lambda e, sb_=sb_, pgb=pgb: e.activation(out=sb_[:, :], in_=pgb[:, :], func=AF.Sigmoid),
                         reads=[pgb], writes=[sb_])
                    P.op("dve", lambda e, m1=m1, sa=sa, pya=pya: e.tensor_tensor(
                        out=m1[:, :], in0=sa[:, :], in1=pya[:, :], op=ALU.mult), reads=[sa, pya], writes=[m1])
                    P.op("dve", lambda e, m2=m2, sb_=sb_, pyb=pyb: e.tensor_tensor(
                        out=m2[:, :], in0=sb_[:, :], in1=pyb[:, :], op=ALU.mult), reads=[sb_, pyb], writes=[m2])
                    P.op("pool", lambda e, m1=m1, m2=m2, oc=oc, tg=tg: e.tensor_tensor(
                        out=mg[oc][:, tg * 512:(tg + 1) * 512], in0=m1[:, :], in1=m2[:, :], op=ALU.add),
                        reads=[m1, m2], writes=[mg[oc]])
        for j in (range(4) if 'X' in p2s else []):
            w = load_w("wout", j, 0, 8)
            for o2 in range(2):
                oc = 2 * j + o2
                for tg in range(NTG):
                    ps = psum.next()
                    mm_group(ps, [(w, 0, 8)], mg, tg, o2)
                    P.op("dve", lambda e, ps=ps, oc=oc, tg=tg: e.tensor_tensor(
                        out=xT[oc][:, tg * 512:(tg + 1) * 512], in0=ps[:, :], in1=xT[oc][:, tg * 512:(tg + 1) * 512],
                        op=ALU.add), reads=[ps, xT[oc]], writes=[xT[oc]])
        if 'C' in p2s:
            rmsnorm()
        actT = [T(big[:, f, :], "act") for f in range(22)]
        for f in range(22):
            old = attT[f] if f < 8 else ybT[f - 8]
            actT[f].w, actT[f].r = old.w, old.r
        for j in (range(11) if 'C' in p2s else []):
            wgt = load_w("wgu", j, 0, 8)
            wup = load_w("wgu", 11 + j, 0, 8)
            for o2 in range(2):
                f = 2 * j + o2
                for tg in range(NTG):
                    pg, pu = psum.next(), psum.next()
                    mm_group(pg, [(wgt, 0, 8)], hT, tg, o2)
                    mm_group(pu, [(wup, 0, 8)], hT, tg, o2)
                    sg = tmpf.next()
                    P.op("act", lambda e, sg=sg, pg=pg: e.activation(out=sg[:, :], in_=pg[:, :], func=AF.Silu),
                         reads=[pg], writes=[sg])
                    P.op("dve", lambda e, sg=sg, pu=pu, f=f, tg=tg: e.tensor_tensor(
                        out=actT[f][:, tg * 512:(tg + 1) * 512], in0=sg[:, :], in1=pu[:, :], op=ALU.mult),
                        reads=[sg, pu], writes=[actT[f]])
        for j in (range(4) if 'C' in p2s else []):
            w0 = load_w("wd", j, 0, 8)
            w1 = load_w("wd", j, 8, 8)
            w2 = load_w("wd", j, 16, 6)
            for o2 in range(2):
                oc = 2 * j + o2
                for tg in range(NTG):
                    ps = psum.next()
                    mm_group(ps, [(w0, 0, 8), (w1, 8, 8), (w2, 16, 6)], actT, tg, o2)
                    P.op("dve", lambda e, ps=ps, oc=oc, tg=tg: e.tensor_tensor(
                        out=xT[oc][:, tg * 512:(tg + 1) * 512], in0=ps[:, :], in1=xT[oc][:, tg * 512:(tg + 1) * 512],
                        op=ALU.add), reads=[ps, xT[oc]], writes=[xT[oc]])
        for f in range(22):
            old = attT[f] if f < 8 else ybT[f - 8]
            old.w, old.r = actT[f].w, actT[f].r
        if 'D' in p2s:
            rmsnorm()
        for j in (range(4) if 'D' in p2s else []):
            wpg = load_w("wpg", j, 0, 8)
            wpp = load_w("wpp", j, 0, 2)
            for o2 in range(2):
                oc = 2 * j + o2
                for tg in range(NTG):
                    pg, pp = psum.next(), psum.next()
                    mm_group(pg, [(wpg, 0, 8)], hT, tg, o2)
                    mm_group(pp, [(wpp, 0, 2)], pT, tg, o2)
                    sg, m = tmpf.next(), tmpf.next()
                    P.op("act", lambda e, sg=sg, pg=pg: e.activation(out=sg[:, :], in_=pg[:, :], func=AF.Sigmoid),
                         reads=[pg], writes=[sg])
                    P.op("dve", lambda e, m=m, sg=sg, pp=pp: e.tensor_tensor(
                        out=m[:, :], in0=sg[:, :], in1=pp[:, :], op=ALU.mult), reads=[sg, pp], writes=[m])
                    P.op("dve", lambda e, m=m, oc=oc, tg=tg: e.tensor_tensor(
                        out=xT[oc][:, tg * 512:(tg + 1) * 512], in0=m[:, :], in1=xT[oc][:, tg * 512:(tg + 1) * 512],
                        op=ALU.add), reads=[m, xT[oc]], writes=[xT[oc]])
        for kc in range(8):
            P.op("act", lambda e, kc=kc: e.copy(out=hT[kc][:, :], in_=xT[kc][:, :]), reads=[xT[kc], hT[kc]], writes=[hT[kc]])
            P.op("dve", lambda e, kc=kc: e.tensor_tensor(out=xT[kc][:, :], in0=xT[kc][:, :], in1=hT[kc][:, :], op=ALU.subtract),
                 reads=[xT[kc], hT[kc]], writes=[xT[kc]])
            P.op("pool", lambda e, kc=kc: e.tensor_copy(out=mg[kc][:, :], in_=xT[kc][:, :]), reads=[xT[kc], mg[kc]], writes=[mg[kc]])
        for s in range(NSUB):
            xo = xin.next()
            for half in range(2):
                ps = psum.next()
                for j in range(4):
                    kc = half * 4 + j
                    P.op("pe", lambda e, ps=ps, kc=kc, j=j, s=s: e.matmul(
                        ps[:, j * 128:(j + 1) * 128], lhsT=hT[kc][:, s * 128:(s + 1) * 128], rhs=ident_b[:, :], start=True, stop=False),
                        reads=[hT[kc], ident_b], writes=[ps])
                    P.op("pe", lambda e, ps=ps, kc=kc, j=j, s=s: e.matmul(
                        ps[:, j * 128:(j + 1) * 128], lhsT=mg[kc][:, s * 128:(s + 1) * 128], rhs=ident_b[:, :], start=False, stop=True),
                        reads=[mg[kc], ident_b], writes=[ps])
                if half == 0:
                    P.op("act", lambda e, ps=ps, xo=xo: e.copy(out=xo[:, 0:512], in_=ps[:, :]), reads=[ps], writes=[xo])
                else:
                    P.op("dve", lambda e, ps=ps, xo=xo: e.tensor_copy(out=xo[:, 512:1024], in_=ps[:, :]),
                         reads=[ps, xo], writes=[xo])
            out_ops.append(P.op("pool", lambda e, xo=xo, s=s, t0=t0: e.dma_start(
                out=dout[t0 + s * 128:t0 + (s + 1) * 128, :], in_=xo[:, :]), reads=[xo], dma=True))
    return out_ops


W1A = 1288
W1B = 768


def load_cast_weight(P, C, src, w, ncols, gain, stg):
    c0 = 0
    while c0 < ncols:
        n = min(WB_COLS, ncols - c0)
        s = stg.next()
        sv = src[:, c0:c0 + n].rearrange("(k p) n -> p k n", p=128)
        P.op("sp", lambda e, s=s, sv=sv, n=n: e.dma_start(out=s[:, :, 0:n], in_=sv), writes=[s], dma=True)
        gv = gain[:, 0:8].unsqueeze(2).to_broadcast([128, 8, n])
        P.op("pool", lambda e, s=s, gv=gv, n=n, c0=c0: e.tensor_tensor(
            out=w[:, :, c0:c0 + n], in0=s[:, :, 0:n], in1=gv, op=ALU.mult), reads=[s, gain], writes=[w])
        c0 += n


def build_prologue(P, C, din, cst, hT_scr, hT_T, NTOKW):
    xin = Ring([C.sbT([128, 1024], F32, "pxin") for _ in range(3)])
    junk = C.sbT([128, 1024], BF16, "pjunk")
    hb = Ring([C.sbT([128, 1024], BF16, "phb") for _ in range(2)])
    ssr = Ring([C.sbT([128, 2], F32, "pss") for _ in range(4)])
    hst = Ring([C.sbT([128, 8, 512], BF16, "phst") for _ in range(2)])
    psum = cst["psring"]
    ident_b = cst["ident_b"]
    for m in range(NTOKW // 512):
        ht = hst.next()
        for s in range(4):
            t0 = m * 512 + s * 128
            xi, ss, h = xin.next(), ssr.next(), hb.next()
            P.op("sp", lambda e, xi=xi, t0=t0: e.dma_start(out=xi[:, :], in_=din["xw"][t0:t0 + 128, :]), writes=[xi], dma=True)
            P.op("act", lambda e, xi=xi, ss=ss: e.activation(out=junk[:, :], in_=xi[:, :], func=AF.Square, accum_out=ss[:, 0:1]),
                 reads=[xi], writes=[junk, ss])
            P.op("act", lambda e, ss=ss: e.activation(out=ss[:, 1:2], in_=ss[:, 0:1], func=AF.Sqrt, scale=1.0 / 1024.0, bias=EPS),
                 reads=[ss], writes=[ss])
            P.op("dve", lambda e, ss=ss: e.reciprocal(out=ss[:, 1:2], in_=ss[:, 1:2]), reads=[ss], writes=[ss])
            P.op("dve", lambda e, xi=xi, ss=ss, h=h: e.tensor_scalar(
                out=h[:, :], in0=xi[:, :], scalar1=ss[:, 1:2], scalar2=None, op0=ALU.mult), reads=[xi, ss], writes=[h])
            ps = psum.next()
            psb = ps.ap.bitcast(BF16)
            for kc in range(8):
                P.op("pe", lambda e, psb=psb, h=h, kc=kc: e.transpose(
                    psb[:, kc * 128:(kc + 1) * 128], h[:, kc * 128:(kc + 1) * 128], ident_b[:, :]),
                    reads=[h, ident_b], writes=[ps])
            P.op("act", lambda e, psb=psb, ht=ht, s=s: e.copy(
                out=ht[:, 0:4, s * 128:(s + 1) * 128], in_=psb[:, 0:512].rearrange("p (k n) -> p k n", k=4)),
                reads=[ps], writes=[ht])
            P.op("dve", lambda e, psb=psb, ht=ht, s=s: e.tensor_copy(
                out=ht[:, 4:8, s * 128:(s + 1) * 128], in_=psb[:, 512:1024].rearrange("p (k n) -> p k n", k=4)),
                reads=[ps, ht], writes=[ht])
        t = T(None, "hTscr")
        hT_T.append(t)
        P.op("pool", lambda e, ht=ht, m=m: e.dma_start(out=hT_scr[:, :, m * 512:(m + 1) * 512], in_=ht[:, :, :]),
             reads=[ht], writes=[t], dma=True)


def build_p1a(P, C, din, g, cst, hT_scr, hT_T, e_yb, e_yb_T, NTOKW, OWN0):
    psum = cst["psring"]
    ident_b, ones_b, U, T1 = cst["ident_b"], cst["ones_b"], cst["U"], cst["T1"]
    stg = Ring([C.sbT([128, 8, WB_COLS], F32, "a_stg") for _ in range(2)])
    w1a = C.sbT([128, 8, W1A], BF16, "w1a")
    load_cast_weight(P, C, din["w1a"][g], w1a, W1A, cst["g1"], stg)
    small = {}
    for nm, shp in (("convw", [128, 6, 4]), ("convb", [128, 6]), ("dtb", [128, 8]), ("alog", [128, 8]),
                    ("dsk", [128, 8]), ("sng", [128, 512])):
        t = C.sbT(shp, F32, "a_" + nm)
        P.op("sp", lambda e, t=t, nm=nm: e.dma_start(out=t.ap, in_=din[nm][g]), writes=[t], dma=True)
        small[nm] = t
    cw, cb, dtb, alog, dsk, sng = (small[k] for k in ("convw", "convb", "dtb", "alog", "dsk", "sng"))
    tokmask = cst["tokmask"]
    Abc = C.sbT([128, 8], F32, "Abc")
    P.op("act", lambda e: e.activation(out=Abc[:, :], in_=alog[:, :], func=AF.Exp), reads=[alog], writes=[Abc])
    P.op("dve", lambda e: e.tensor_scalar(out=Abc[:, :], in0=Abc[:, :], scalar1=-1.0, scalar2=None, op0=ALU.mult),
         reads=[Abc], writes=[Abc])
    S = C.sbT([128, 512], F32, "S")
    Sbf = C.sbT([128, 512], BF16, "Sbf")
    xbc = C.sbT([128, 6, 515], F32, "xbc")
    P.op("pool", lambda e: e.memset(S[:, :], 0.0), writes=[S])
    P.op("pool", lambda e: e.memset(Sbf[:, :], 0.0), writes=[Sbf])
    P.op("pool", lambda e: e.memset(xbc[:, :, 0:3], 0.0), writes=[xbc])
    hring = Ring([C.sbT([128, 8, 512], BF16, "a_hT") for _ in range(2)])
    cring = Ring([C.sbT([128, 6, 512], BF16, "a_co") for _ in range(2)])
    accr = Ring([C.sbT([128, 512], F32, "a_acc") for _ in range(5)])
    f512 = Ring([C.sbT([128, 512], F32, "a_f512") for _ in range(6)])
    szr = Ring([C.sbT([128, 512], F32, "a_sz") for _ in range(5)])
    b512 = Ring([C.sbT([128, 512], BF16, "a_b512") for _ in range(28)])
    s8 = Ring([C.sbT([128, 8], F32, "a_s8") for _ in range(64)])
    s2 = Ring([C.sbT([128, 2], F32, "a_s2") for _ in range(6)])
    Rr = Ring([C.sbT([128, 3, 8, 128], BF16, "a_R") for _ in range(4)])
    a3r = Ring([C.sbT([128, 3, 8], BF16, "a_a3") for _ in range(6)])
    Lr = Ring([C.sbT([128, 8, 128], BF16, "a_L") for _ in range(5)])
    Mr = Ring([C.sbT([128, 8, 128], BF16, "a_M") for _ in range(5)])
    cbm = Ring([C.sbT([128, 128], BF16, "a_cbm") for _ in range(5)])
    ybst = Ring([C.sbT([128, 4, 512], BF16, "a_ybst") for _ in range(2)])
    junk = C.sbT([128, 512], BF16, "a_junk")
    pss = cst["ps_small"]
    i_eng = 0
    for m in range(NTOKW // 512):
        tok0 = m * 512
        own = tok0 >= OWN0
        hT = hring.next()
        P.op("sp", lambda e, hT=hT, tok0=tok0: e.dma_start(out=hT[:, :, :], in_=hT_scr[:, :, tok0:tok0 + 512]),
             reads=[hT_T[m]], writes=[hT], dma=True)
        for c in range(6):
            ps = psum.next()
            for kc in range(8):
                P.op("pe", lambda e, ps=ps, kc=kc, c=c, hT=hT: e.matmul(
                    ps[:, :], lhsT=w1a[:, kc, c * 128:(c + 1) * 128], rhs=hT[:, kc, :], start=(kc == 0), stop=(kc == 7)),
                    reads=[w1a, hT], writes=[ps])
            if c % 2 == 0:
                P.op("act", lambda e, ps=ps, c=c: e.copy(out=xbc[:, c, 3:515], in_=ps[:, :]), reads=[ps, xbc], writes=[xbc])
            else:
                P.op("dve", lambda e, ps=ps, c=c: e.tensor_copy(out=xbc[:, c, 3:515], in_=ps[:, :]), reads=[ps, xbc], writes=[xbc])
        co = cring.next()
        for c in range(6):
            eng = "pool" if c in (1, 4) else "dve"
            acc = accr.next()
            P.op(eng, lambda e, acc=acc, c=c: e.tensor_scalar(
                out=acc[:, :], in0=xbc[:, c, 0:512], scalar1=cw[:, c, 0:1], scalar2=None, op0=ALU.mult),
                reads=[xbc, cw], writes=[acc])
            for k in range(1, 4):
                if eng == "dve":
                    P.op(eng, lambda e, acc=acc, c=c, k=k: e.scalar_tensor_tensor(
                        out=acc[:, :], in0=xbc[:, c, k:k + 512], scalar=cw[:, c, k:k + 1], in1=acc[:, :],
                        op0=ALU.mult, op1=ALU.add), reads=[xbc, cw, acc], writes=[acc])
                else:
                    tmpc = accr.next()
                    P.op(eng, lambda e, tmpc=tmpc, c=c, k=k: e.tensor_scalar(
                        out=tmpc[:, :], in0=xbc[:, c, k:k + 512], scalar1=cw[:, c, k:k + 1], scalar2=None, op0=ALU.mult),
                        reads=[xbc, cw], writes=[tmpc])
                    P.op(eng, lambda e, tmpc=tmpc, acc=acc: e.tensor_tensor(out=acc[:, :], in0=acc[:, :], in1=tmpc[:, :], op=ALU.add),
                         reads=[acc, tmpc], writes=[acc])
            P.op("act", lambda e, acc=acc, c=c, co=co: e.activation(
                out=co[:, c, :], in_=acc[:, :], func=AF.Silu, bias=cb[:, c:c + 1], scale=1.0), reads=[acc, cb, co], writes=[co])
        P.op("pool", lambda e: e.tensor_copy(out=xbc[:, :, 0:3], in_=xbc[:, :, 512:515]), reads=[xbc], writes=[xbc])
        yst = ybst.next() if own else None
        ctx = {}

        def pre(s, m=m, hT=hT, co=co, own=own):
            sub = slice(s * 128, (s + 1) * 128)
            tile_idx = m * 4 + s
            pdt = pss["dt%d" % s]
            for kc in range(8):
                P.op("pe", lambda e, kc=kc, hT=hT, sub=sub: e.matmul(
                    pdt[:, :], lhsT=hT[:, kc, sub], rhs=w1a[:, kc, 1280:1288], start=(kc == 0), stop=(kc == 7)),
                    reads=[w1a, hT], writes=[pdt])
            yield
            dtr, ax, ee, dt_, a_ = s8.next(), s8.next(), s8.next(), s8.next(), s8.next()
            P.op("dve", lambda e, dtr=dtr: e.tensor_tensor(out=dtr[:, :], in0=pdt[:, :], in1=dtb[:, :], op=ALU.add),
                 reads=[pdt, dtb], writes=[dtr])
            P.op("act", lambda e, dtr=dtr, ax=ax: e.activation(out=ax[:, :], in_=dtr[:, :], func=AF.Abs),
                 reads=[dtr], writes=[ax])
            P.op("act", lambda e, ax=ax, ee=ee: e.activation(out=ee[:, :], in_=ax[:, :], func=AF.Exp, scale=-1.0),
                 reads=[ax], writes=[ee])
            P.op("act", lambda e, ee=ee: e.activation(out=ee[:, :], in_=ee[:, :], func=AF.Ln, bias=1.0, scale=1.0),
                 reads=[ee], writes=[ee])
            P.op("dve", lambda e, dtr=dtr, ee=ee, dt_=dt_: e.scalar_tensor_tensor(
                out=dt_[:, :], in0=dtr[:, :], scalar=0.0, in1=ee[:, :], op0=ALU.max, op1=ALU.add),
                reads=[dtr, ee], writes=[dt_])
            P.op("dve", lambda e, dt_=dt_, tile_idx=tile_idx: e.tensor_scalar(
                out=dt_[:, :], in0=dt_[:, :], scalar1=tokmask[:, tile_idx:tile_idx + 1], scalar2=None, op0=ALU.mult),
                reads=[dt_, tokmask], writes=[dt_])
            P.op("dve", lambda e, dt_=dt_, a_=a_: e.tensor_tensor(out=a_[:, :], in0=dt_[:, :], in1=Abc[:, :], op=ALU.mult),
                 reads=[dt_, Abc], writes=[a_])
            a3 = a3r.next()
            ar1, ar2 = s8.next(), s8.next()
            P.op("act", lambda e, a_=a_, a3=a3: e.copy(out=a3[:, 0, :], in_=a_[:, :]), reads=[a_, a3], writes=[a3])
            P.op("dve", lambda e, a_=a_, a3=a3, ar1=ar1: e.tensor_tensor(out=ar1[:, :], in0=a_[:, :], in1=a3[:, 0, :], op=ALU.subtract),
                 reads=[a_, a3], writes=[ar1])
            P.op("act", lambda e, ar1=ar1, a3=a3: e.copy(out=a3[:, 1, :], in_=ar1[:, :]), reads=[ar1, a3], writes=[a3])
            P.op("dve", lambda e, ar1=ar1, a3=a3, ar2=ar2: e.tensor_tensor(out=ar2[:, :], in0=ar1[:, :], in1=a3[:, 1, :], op=ALU.subtract),
                 reads=[ar1, a3], writes=[ar2])
            P.op("act", lambda e, ar2=ar2, a3=a3: e.copy(out=a3[:, 2, :], in_=ar2[:, :]), reads=[ar2, a3], writes=[a3])
            yield
            pac = pss["acs%d" % s]
            for i3 in range(3):
                P.op("pe", lambda e, a3=a3, i3=i3: e.matmul(pac[:, 0:8], lhsT=U[:, :], rhs=a3[:, i3, :], start=(i3 == 0), stop=(i3 == 2)),
                     reads=[U, a3], writes=[pac])
            for i3 in range(3):
                P.op("pe", lambda e, a3=a3, i3=i3: e.matmul(pac[:, 8:16], lhsT=ones_b[:, :], rhs=a3[:, i3, :], start=(i3 == 0), stop=(i3 == 2)),
                     reads=[ones_b, a3], writes=[pac])
            yield
            acs, wst, wend, cdec = s8.next(), s8.next(), s8.next(), s8.next()
            P.op("act", lambda e, acs=acs: e.copy(out=acs[:, :], in_=pac[:, 0:8]), reads=[pac], writes=[acs])
            P.op("act", lambda e, wst=wst: e.activation(out=wst[:, :], in_=pac[:, 0:8], func=AF.Exp), reads=[pac], writes=[wst])
            P.op("act", lambda e, cdec=cdec: e.activation(out=cdec[:, :], in_=pac[:, 8:16], func=AF.Exp), reads=[pac], writes=[cdec])
            P.op("dve", lambda e, wend=wend, acs=acs: e.tensor_tensor(out=wend[:, :], in0=pac[:, 8:16], in1=acs[:, :], op=ALU.subtract),
                 reads=[pac, acs], writes=[wend])
            P.op("act", lambda e, wend=wend: e.activation(out=wend[:, :], in_=wend[:, :], func=AF.Exp), reads=[wend], writes=[wend])
            yield
            pxs = psum.next()
            pxb = pxs.ap.bitcast(BF16)
            for c in range(5):
                P.op("pe", lambda e, pxb=pxb, c=c, co=co, sub=sub: e.transpose(
                    pxb[:, c * 128:(c + 1) * 128], co[:, c, sub], ident_b[:, :]), reads=[co, ident_b], writes=[pxs])
            xs_tm, Btm, xdt, xdtw = b512.next(), b512.next(), b512.next(), b512.next()
            P.op("act", lambda e, pxb=pxb, xs_tm=xs_tm: e.copy(out=xs_tm[:, :], in_=pxb[:, 0:512]), reads=[pxs], writes=[xs_tm])
            P.op("act", lambda e, pxb=pxb, Btm=Btm: e.copy(out=Btm[:, 0:128], in_=pxb[:, 512:640]), reads=[pxs], writes=[Btm])
            P.op("pool", lambda e, xs_tm=xs_tm, xdt=xdt, dt_=dt_: e.tensor_tensor(
                out=xdt[:, :].rearrange("p (h d) -> p h d", h=8), in0=xs_tm[:, :].rearrange("p (h d) -> p h d", h=8),
                in1=dt_[:, :].unsqueeze(2).to_broadcast([128, 8, 64]), op=ALU.mult), reads=[xs_tm, dt_], writes=[xdt])
            P.op("pool", lambda e, xdt=xdt, xdtw=xdtw, wend=wend: e.tensor_tensor(
                out=xdtw[:, :].rearrange("p (h d) -> p h d", h=8), in0=xdt[:, :].rearrange("p (h d) -> p h d", h=8),
                in1=wend[:, :].unsqueeze(2).to_broadcast([128, 8, 64]), op=ALU.mult), reads=[xdt, wend], writes=[xdtw])
            yield
            if own:
                R, L, Mh, cbt = Rr.next(), Lr.next(), Mr.next(), cbm.next()
                for i3 in range(3):
                    P.op("dve" if i3 != 1 else "pool", lambda e, R=R, a3=a3, i3=i3: e.tensor_tensor(
                        out=R[:, i3, :, :], in0=U[:, :].unsqueeze(1).to_broadcast([128, 8, 128]),
                        in1=a3[:, i3, :].unsqueeze(2).to_broadcast([128, 8, 128]), op=ALU.mult), reads=[U, a3, R], writes=[R])
                for hh in range(2):
                    pD = psum.next()
                    for i3 in range(3):
                        P.op("pe", lambda e, pD=pD, R=R, hh=hh, i3=i3: e.matmul(
                            pD[:, :], lhsT=T1[:, :], rhs=R[:, i3, hh * 4:(hh + 1) * 4, :].rearrange("p h l -> p (h l)"),
                            start=(i3 == 0), stop=(i3 == 2)), reads=[T1, R], writes=[pD])
                    P.op("act", lambda e, pD=pD, L=L, hh=hh: e.activation(
                        out=L[:, hh * 4:(hh + 1) * 4, :].rearrange("p h l -> p (h l)"), in_=pD[:, :], func=AF.Exp),
                        reads=[pD, L], writes=[L])
                yield
                pcb = pss["cb%d" % s]
                P.op("pe", lambda e, co=co, sub=sub: e.matmul(
                    pcb[:, :], lhsT=co[:, 4, sub], rhs=co[:, 5, sub], start=True, stop=True), reads=[co], writes=[pcb])
                P.op("dve", lambda e, cbt=cbt: e.tensor_tensor(out=cbt[:, :], in0=pcb[:, :], in1=U[:, :], op=ALU.mult),
                     reads=[pcb, U], writes=[cbt])
                P.op("pool", lambda e, Mh=Mh, L=L, cbt=cbt: e.tensor_tensor(
                    out=Mh[:, :, :], in0=L[:, :, :], in1=cbt[:, :].unsqueeze(1).to_broadcast([128, 8, 128]), op=ALU.mult),
                    reads=[L, cbt], writes=[Mh])
                xsD = b512.next()
                P.op("pool", lambda e, xs_tm=xs_tm, xsD=xsD: e.tensor_tensor(
                    out=xsD[:, :].rearrange("p (h d) -> p h d", h=8), in0=xs_tm[:, :].rearrange("p (h d) -> p h d", h=8),
                    in1=dsk[:, :].unsqueeze(2).to_broadcast([128, 8, 64]), op=ALU.mult), reads=[xs_tm, dsk], writes=[xsD])
                yield
                pz = psum.next()
                for kc in range(8):
                    P.op("pe", lambda e, pz=pz, kc=kc, hT=hT, sub=sub: e.matmul(
                        pz[:, :], lhsT=hT[:, kc, sub], rhs=w1a[:, kc, 768:1280], start=(kc == 0), stop=(kc == 7)),
                        reads=[w1a, hT], writes=[pz])
                sz = szr.next()
                P.op("act", lambda e, pz=pz, sz=sz: e.activation(out=sz[:, :], in_=pz[:, :], func=AF.Silu), reads=[pz], writes=[sz])
            ctx[s] = dict(locals())
            yield

        def seq(s, m=m, hT=hT, co=co, own=own, yst=yst):
            L_ = ctx[s]
            sub = L_["sub"]
            wst, cdec, xdt, xdtw, Btm = L_["wst"], L_["cdec"], L_["xdt"], L_["xdtw"], L_["Btm"]
            if own:
                Mh, xsD, sz = L_["Mh"], L_["xsD"], L_["sz"]
                pyo, py = psum.next(), psum.next()
                P.op("pe", lambda e, pyo=pyo, co=co, sub=sub: e.matmul(
                    pyo[:, :], lhsT=co[:, 5, sub], rhs=Sbf[:, :], start=True, stop=True), reads=[co, Sbf], writes=[pyo])
                P.op("pe", lambda e, py=py, xsD=xsD: e.matmul(py[:, :], lhsT=ident_b[:, :], rhs=xsD[:, :], start=True, stop=False),
                     reads=[ident_b, xsD], writes=[py])
                for h in range(8):
                    P.op("pe", lambda e, py=py, Mh=Mh, xdt=xdt, h=h: e.matmul(
                        py[:, h * 64:(h + 1) * 64], lhsT=Mh[:, h, :], rhs=xdt[:, h * 64:(h + 1) * 64], start=False, stop=(h == 7)),
                        reads=[Mh, xdt], writes=[py])
                y1, y2, y3 = f512.next(), f512.next(), f512.next()
                P.op("dve", lambda e, pyo=pyo, y1=y1, wst=wst: e.tensor_tensor(
                    out=y1[:, :].rearrange("p (h d) -> p h d", h=8), in0=pyo[:, :].rearrange("p (h d) -> p h d", h=8),
                    in1=wst[:, :].unsqueeze(2).to_broadcast([128, 8, 64]), op=ALU.mult), reads=[pyo, wst], writes=[y1])
                P.op("dve", lambda e, y1=y1, y2=y2, py=py: e.tensor_tensor(out=y2[:, :], in0=y1[:, :], in1=py[:, :], op=ALU.add),
                     reads=[y1, py], writes=[y2])
                P.op("pool", lambda e, y2=y2, y3=y3, sz=sz: e.tensor_tensor(out=y3[:, :], in0=y2[:, :], in1=sz[:, :], op=ALU.mult),
                     reads=[y2, sz], writes=[y3])
                ss = s2.next()
                P.op("act", lambda e, y3=y3, ss=ss: e.activation(out=junk[:, :], in_=y3[:, :], func=AF.Square, accum_out=ss[:, 0:1]),
                     reads=[y3], writes=[junk, ss])
                P.op("act", lambda e, ss=ss: e.activation(out=ss[:, 1:2], in_=ss[:, 0:1], func=AF.Sqrt, scale=1.0 / 512.0, bias=EPS),
                     reads=[ss], writes=[ss])
                P.op("dve", lambda e, ss=ss: e.reciprocal(out=ss[:, 1:2], in_=ss[:, 1:2]), reads=[ss], writes=[ss])
                yn = b512.next()
                P.op("dve", lambda e, y3=y3, ss=ss, yn=yn: e.scalar_tensor_tensor(
                    out=yn[:, :], in0=y3[:, :], scalar=ss[:, 1:2], in1=sng[:, :], op0=ALU.mult, op1=ALU.mult),
                    reads=[y3, ss, sng], writes=[yn])
                pyt = psum.next()
                pytb = pyt.ap.bitcast(BF16)
                for c in range(4):
                    P.op("pe", lambda e, pytb=pytb, yn=yn, c=c: e.transpose(
                        pytb[:, c * 128:(c + 1) * 128], yn[:, c * 128:(c + 1) * 128], ident_b[:, :]),
                        reads=[yn, ident_b], writes=[pyt])
                P.op("act", lambda e, pytb=pytb, yst=yst, sub=sub: e.copy(
                    out=yst[:, :, sub], in_=pytb[:, 0:512].rearrange("p (c n) -> p c n", c=4)), reads=[pyt, yst], writes=[yst])
            pst = psum.next()
            P.op("pe", lambda e, pst=pst, Btm=Btm, xdtw=xdtw: e.matmul(
                pst[:, :], lhsT=Btm[:, 0:128], rhs=xdtw[:, :], start=True, stop=True), reads=[Btm, xdtw], writes=[pst])
            P.op("pool", lambda e, cdec=cdec: e.tensor_tensor(
                out=S[:, :].rearrange("p (h d) -> p h d", h=8), in0=S[:, :].rearrange("p (h d) -> p h d", h=8),
                in1=cdec[:, :].unsqueeze(2).to_broadcast([128, 8, 64]), op=ALU.mult), reads=[S, cdec], writes=[S])
            P.op("dve", lambda e, pst=pst: e.tensor_tensor(out=S[:, :], in0=S[:, :], in1=pst[:, :], op=ALU.add),
                 reads=[S, pst], writes=[S])
            P.op("act", lambda e: e.copy(out=Sbf[:, :], in_=S[:, :]), reads=[S, Sbf], writes=[Sbf])

        gens = [pre(s) for s in range(4)]
        while gens:
            for g_ in list(gens):
                try:
                    next(g_)
                except StopIteration:
                    gens.remove(g_)
        for s in range(4):
            seq(s)
        if own:
            o0 = tok0 - OWN0
            P.op("pool", lambda e, yst=yst, o0=o0: e.dma_start(
                out=e_yb[g, :, :, o0:o0 + 512].rearrange("c p n -> p c n"), in_=yst[:, :, :]),
                reads=[yst], writes=[e_yb_T], dma=True)


def build_p1_init(P, C, din, cst, NTOKW):
    KT = C.sb([96, 4, NTOKW], BF16, "KT")
    VA = C.sb([128, NTOKW // 128, 2, 3, 64], BF16, "VA")
    kmT = C.sbT([64, 4, 32], BF16, "kmT")
    Mpad = [C.sbT([128, 4, 96], BF16, "Mpad") for _ in range(2)]
    for h in range(4):
        P.op("sp", lambda e, h=h: e.dma_start(out=KT[64:96, h, :], in_=din["kind"]), dma=True)
    P.op("pool", lambda e: e.memset(VA[:, :, :, 1, :], 1.0))
    for mp in Mpad:
        P.op("pool", lambda e, mp=mp: e.memset(mp[:, :, :], 0.0), writes=[mp])
    P.op("pool", lambda e: e.memset(kmT[:, :, :], 0.0), writes=[kmT])
    G = C.sbT([128, 512], F32, "G")
    gq, gk = cst["gq"], cst["gk"]
    for h in range(4):
        P.op("dve", lambda e, h=h: e.tensor_scalar(out=G[:, h * 64:(h + 1) * 64], in0=gq[:, :], scalar1=0.125, scalar2=None,
                                                   op0=ALU.mult), reads=[gq, G], writes=[G])
        P.op("dve", lambda e, h=h: e.tensor_copy(out=G[:, 256 + h * 64:256 + (h + 1) * 64], in_=gk[:, :]), reads=[gk, G], writes=[G])
    bb4 = C.sbT([128, 128], F32, "bb4")
    for h in range(4):
        P.op("dve", lambda e, h=h: e.tensor_copy(out=bb4[:, h * 32:(h + 1) * 32], in_=cst["blkbias"][:, :]),
             reads=[cst["blkbias"], bb4], writes=[bb4])
    P.barrier()
    nm = NTOKW // 512
    return dict(KT=KT, VA=VA, kmT=kmT, Mpad=Ring(Mpad), G=G, bb4=bb4,
                KT_T=[T(None, "KT%d" % i) for i in range(nm)], VA_T=[T(None, "VA%d" % i) for i in range(nm)])


def build_p1b(P, C, din, g, cst, A, hT_scr, hT_T, e_att, e_att_T, NTOKW, OWN0):
    psum = cst["psring"]
    po_ring = cst["po_ring"]
    pss = cst["ps_small"]
    ident_b, negm = cst["ident_b"], cst["negm"]
    KT, VA, kmT, G, bb4 = A["KT"], A["VA"], A["kmT"], A["G"], A["bb4"]
    KT_T, VA_T = A["KT_T"], A["VA_T"]
    stg = Ring([C.sbT([128, 8, WB_COLS], F32, "b_stg") for _ in range(2)])
    w1b = C.sbT([128, 8, W1B], BF16, "w1b")
    load_cast_weight(P, C, din["w1b"][g], w1b, W1B, cst["g1"], stg)
    hring = Ring([C.sbT([128, 8, 512], BF16, "b_hT") for _ in range(2)])
    f512 = Ring([C.sbT([128, 512], F32, "b_f512") for _ in range(4)])
    b512 = Ring([C.sbT([128, 512], BF16, "b_b512") for _ in range(3)])
    ptr = Ring([C.sbT([128, 512], BF16, "b_pt") for _ in range(4)])
    s8 = Ring([C.sbT([128, 8], F32, "b_s8") for _ in range(6)])
    g128 = Ring([C.sbT([128, 128], F32, "b_g128") for _ in range(6)])
    t8r = Ring([C.sbT([128, 32], F32, "b_t8") for _ in range(2)])
    kmf = C.sbT([64, 4, 2], F32, "b_kmf")
    QTr = Ring([C.sbT([96, 4, 512], BF16, "b_QT") for _ in range(2)])
    ast = [Ring([C.sbT([128, 512], BF16, "b_ast") for _ in range(2)]) for _ in range(2)]
    rdr = Ring([C.sbT([128, 512], F32, "b_rd") for _ in range(2)])
    outs = []
    for m in range(NTOKW // 512):
        tok0 = m * 512
        own = tok0 >= OWN0
        c0 = 0 if own else 256
        h0 = 0 if own else 4
        hT = hring.next()
        P.op("sp", lambda e, hT=hT, tok0=tok0: e.dma_start(out=hT[:, :, :], in_=hT_scr[:, :, tok0:tok0 + 512]),
             reads=[hT_T[m]], writes=[hT], dma=True)
        QT = QTr.next() if own else None
        for s in range(4):
            sub = slice(s * 128, (s + 1) * 128)
            kt = m * 4 + s
            pqk, pv = psum.next(), psum.next()
            for kc in range(8):
                P.op("pe", lambda e, pqk=pqk, kc=kc, hT=hT, sub=sub, c0=c0: e.matmul(
                    pqk[:, c0:512], lhsT=hT[:, kc, sub], rhs=w1b[:, kc, c0:512], start=(kc == 0), stop=(kc == 7)),
                    reads=[w1b, hT], writes=[pqk])
            for kc in range(8):
                P.op("pe", lambda e, pv=pv, kc=kc, hT=hT, sub=sub: e.matmul(
                    pv[:, 0:256], lhsT=hT[:, kc, sub], rhs=w1b[:, kc, 512:768], start=(kc == 0), stop=(kc == 7)),
                    reads=[w1b, hT], writes=[pv])
            sq, ssum, tt = f512.next(), s8.next(), f512.next()
            P.op("act", lambda e, pqk=pqk, sq=sq, c0=c0: e.activation(out=sq[:, c0:512], in_=pqk[:, c0:512], func=AF.Square),
                 reads=[pqk], writes=[sq])
            P.op("dve", lambda e, sq=sq, ssum=ssum, c0=c0, h0=h0: e.tensor_reduce(
                out=ssum[:, h0:8], in_=sq[:, c0:512].rearrange("p (h d) -> p h d", d=64), axis=AX.X, op=ALU.add),
                reads=[sq], writes=[ssum])
            P.op("act", lambda e, ssum=ssum, h0=h0: e.activation(
                out=ssum[:, h0:8], in_=ssum[:, h0:8], func=AF.Sqrt, scale=1.0 / 64.0, bias=EPS), reads=[ssum], writes=[ssum])
            P.op("dve", lambda e, ssum=ssum, h0=h0: e.reciprocal(out=ssum[:, h0:8], in_=ssum[:, h0:8]), reads=[ssum], writes=[ssum])
            P.op("dve", lambda e, pqk=pqk, tt=tt, ssum=ssum, c0=c0, h0=h0: e.tensor_tensor(
                out=tt[:, c0:512].rearrange("p (h d) -> p h d", d=64), in0=pqk[:, c0:512].rearrange("p (h d) -> p h d", d=64),
                in1=ssum[:, h0:8].unsqueeze(2).to_broadcast([128, 8 - h0, 64]), op=ALU.mult), reads=[pqk, ssum], writes=[tt])
            qkn = b512.next()
            P.op("pool", lambda e, tt=tt, qkn=qkn, c0=c0: e.tensor_tensor(
                out=qkn[:, c0:512], in0=tt[:, c0:512], in1=G[:, c0:512], op=ALU.mult), reads=[tt, G], writes=[qkn])
            pkt = psum.next()
            pktb = pkt.ap.bitcast(BF16)
            for h in range(4):
                P.op("pe", lambda e, pktb=pktb, qkn=qkn, h=h: e.transpose(
                    pktb[0:64, h * 128:(h + 1) * 128], qkn[:, 256 + h * 64:256 + (h + 1) * 64], ident_b[:, :]),
                    reads=[qkn, ident_b], writes=[pkt])
            P.op("act", lambda e, pktb=pktb, tok0=tok0, s=s: e.copy(
                out=KT[0:64, :, tok0 + s * 128:tok0 + (s + 1) * 128], in_=pktb[0:64, 0:512].rearrange("p (h n) -> p h n", h=4)),
                reads=[pkt, KT_T[m]], writes=[KT_T[m]])
            P.op("act", lambda e, pv=pv, kt=kt: e.copy(
                out=VA[:, kt, :, 0, :], in_=pv[:, 0:256].rearrange("p (a b d) -> p a b d", a=2, b=2)[:, :, 0, :]),
                reads=[pv, VA_T[m]], writes=[VA_T[m]])
            P.op("dve", lambda e, pv=pv, kt=kt: e.tensor_copy(
                out=VA[:, kt, :, 2, :], in_=pv[:, 0:256].rearrange("p (a b d) -> p a b d", a=2, b=2)[:, :, 1, :]),
                reads=[pv, VA_T[m]], writes=[VA_T[m]])
            if own:
                pqt = psum.next()
                pqtb = pqt.ap.bitcast(BF16)
                for h in range(4):
                    P.op("pe", lambda e, pqtb=pqtb, qkn=qkn, h=h: e.transpose(
                        pqtb[0:64, h * 128:(h + 1) * 128], qkn[:, h * 64:(h + 1) * 64], ident_b[:, :]),
                        reads=[qkn, ident_b], writes=[pqt])
                P.op("dve", lambda e, pqtb=pqtb, QT=QT, sub=sub: e.tensor_copy(
                    out=QT[0:64, :, sub], in_=pqtb[0:64, 0:512].rearrange("p (h n) -> p h n", h=4)),
                    reads=[pqt, QT], writes=[QT])
        P.op("dve", lambda e, tok0=tok0: e.tensor_reduce(
            out=kmf[:, :, :], in_=KT[0:64, :, tok0:tok0 + 512].rearrange("p h (b k) -> p h b k", b=2), axis=AX.X, op=ALU.add),
            reads=[KT_T[m]], writes=[kmf])
        P.op("dve", lambda e, m=m: e.tensor_scalar(out=kmT[:, :, 2 * m:2 * m + 2], in0=kmf[:, :, :], scalar1=1.0 / 256.0,
                                                   scalar2=None, op0=ALU.mult), reads=[kmf, kmT], writes=[kmT])
        if not own:
            continue
        for s in range(4):
            sub = slice(s * 128, (s + 1) * 128)
            ownblk = 2 * m + s // 2
            pg = pss["gate"]
            for h in range(4):
                P.op("pe", lambda e, h=h, QT=QT, sub=sub: e.matmul(
                    pg[:, h * 32:(h + 1) * 32], lhsT=QT[0:64, h, sub], rhs=kmT[0:64, h, :], start=True, stop=True),
                    reads=[QT, kmT], writes=[pg])
            gm, m1, m2, t8 = g128.next(), g128.next(), g128.next(), t8r.next()
            P.op("dve", lambda e, gm=gm: e.tensor_tensor(out=gm[:, :], in0=pg[:, :], in1=bb4[:, :], op=ALU.add),
                 reads=[pg, bb4], writes=[gm])
            P.op("pool", lambda e, gm=gm, ownblk=ownblk: e.memset(
                gm[:, :].rearrange("p (h b) -> p h b", h=4)[:, :, ownblk:32], NEG), reads=[gm], writes=[gm])
            for h in range(4):
                P.op("dve", lambda e, gm=gm, t8=t8, h=h: e.max(out=t8[:, h * 8:(h + 1) * 8], in_=gm[:, h * 32:(h + 1) * 32]),
                     reads=[gm, t8], writes=[t8])
            P.op("dve", lambda e, gm=gm, m1=m1, t8=t8: e.tensor_tensor(
                out=m1[:, :].rearrange("p (h b) -> p h b", h=4), in0=gm[:, :].rearrange("p (h b) -> p h b", h=4),
                in1=t8[:, :].rearrange("p (h k) -> p h k", h=4)[:, :, 2:3].to_broadcast([128, 4, 32]), op=ALU.is_lt),
                reads=[gm, t8], writes=[m1])
            P.op("dve", lambda e, gm=gm, m2=m2: e.tensor_scalar(
                out=m2[:, :], in0=gm[:, :], scalar1=NEG / 2, scalar2=NEG, op0=ALU.is_lt, op1=ALU.mult), reads=[gm], writes=[m2])
            Mp = A["Mpad"].next()
            P.op("dve", lambda e, Mp=Mp, m1=m1, m2=m2: e.scalar_tensor_tensor(
                out=Mp[:, :, 64:96], in0=m1[:, :].rearrange("p (h b) -> p h b", h=4), scalar=NEG,
                in1=m2[:, :].rearrange("p (h b) -> p h b", h=4), op0=ALU.mult, op1=ALU.min), reads=[m1, m2, Mp], writes=[Mp])
            P.op("pool", lambda e, Mp=Mp, ownblk=ownblk: e.memset(Mp[:, :, 64 + ownblk:65 + ownblk], 0.0), reads=[Mp], writes=[Mp])
            pmt = psum.next()
            pmtb = pmt.ap.bitcast(BF16)
            for h in range(4):
                P.op("pe", lambda e, pmtb=pmtb, Mp=Mp, h=h: e.transpose(
                    pmtb[0:96, h * 128:(h + 1) * 128], Mp[:, h, :], ident_b[:, :]), reads=[Mp, ident_b], writes=[pmt])
            P.op("act", lambda e, pmtb=pmtb, QT=QT, sub=sub: e.copy(
                out=QT[64:96, :, sub], in_=pmtb[64:96, 0:512].rearrange("p (h n) -> p h n", h=4)), reads=[pmt, QT], writes=[QT])
        nkt = (2 * m + 2) * 2
        o0 = tok0 - OWN0
        for h in range(4):
            pair, hb = h // 2, h % 2
            po = po_ring.next()
            def tile_cols(kt):
                blk = kt // 2
                if blk < 2 * m:
                    return 0, 512, None
                if blk == 2 * m:
                    return 0, 512, 0
                return 256, 512, 256

            def emit_s(kt):
                a0, a1, cz = tile_cols(kt)
                mk = kt // 4
                ps = psum.next()
                P.op("pe", lambda e, ps=ps, h=h, kt=kt, QT=QT, a0=a0, a1=a1, cz=cz: e.matmul(
                    ps[:, a0:a1], lhsT=KT[0:96, h, kt * 128:(kt + 1) * 128], rhs=QT[0:96, h, a0:a1],
                    start=True, stop=(cz is None)), reads=[KT_T[mk], QT], writes=[ps])
                if cz is not None:
                    P.op("pe", lambda e, ps=ps, kt=kt, cz=cz: e.matmul(
                        ps[:, cz:cz + 256], lhsT=ident_b[:, :], rhs=negm[:, kt % 2, :], start=False, stop=True),
                        reads=[ident_b, negm], writes=[ps])
                return ps

            def emit_pv(kt, ps):
                a0, a1, cz = tile_cols(kt)
                mk = kt // 4
                pt = ptr.next()
                P.op("act", lambda e, ps=ps, pt=pt, a0=a0, a1=a1: e.activation(out=pt[:, a0:a1], in_=ps[:, a0:a1], func=AF.Exp),
                     reads=[ps], writes=[pt])
                P.op("pe", lambda e, po=po, pt=pt, kt=kt, pair=pair, hb=hb, a0=a0, a1=a1, nkt=nkt: e.matmul(
                    po[:, a0:a1], lhsT=VA[:, kt, pair, hb:hb + 2, :].rearrange("p a d -> p (a d)"), rhs=pt[:, a0:a1],
                    start=(kt == 0), stop=(kt == nkt - 1), skip_group_check=True), reads=[VA_T[mk], pt], writes=[po])

            LOOK = 2
            pend = []
            for kt in range(nkt):
                pend.append((kt, emit_s(kt)))
                if len(pend) > LOOK:
                    emit_pv(*pend.pop(0))
            while pend:
                emit_pv(*pend.pop(0))
            nr = slice(0, 64) if hb == 0 else slice(64, 128)
            dr = slice(64, 128) if hb == 0 else slice(0, 64)
            rd = rdr.next()
            if hb == 0:
                at_ = ast[pair].next()
                ast_cur = at_
            else:
                at_ = ast_cur
            P.op("dve", lambda e, po=po, rd=rd, nr=nr, dr=dr: e.reciprocal(out=rd[nr, :], in_=po[dr, :]), reads=[po], writes=[rd])
            P.op("dve", lambda e, po=po, rd=rd, nr=nr, at_=at_: e.tensor_tensor(
                out=at_[nr, :], in0=po[nr, :], in1=rd[nr, :], op=ALU.mult), reads=[po, rd, at_], writes=[at_])
            if hb == 1:
                outs.append(P.op("pool", lambda e, at_=at_, pair=pair, o0=o0: e.dma_start(
                    out=e_att[g, pair, :, o0:o0 + 512], in_=at_[:, :]), reads=[at_], writes=[e_att_T], dma=True))
    return outs


def load_consts(P, C, din, names_shapes):
    out = {}
    for name, shape, dt in names_shapes:
        t = C.sbT(shape, dt, name)
        P.op("sp", lambda e, t=t, name=name: e.dma_start(out=t.ap, in_=din[name]), writes=[t], dma=True)
        out[name] = t
    return out


def build_program(mode, NTOKW=8192, OWN0=0, NG=1):
    nc = bass.Bass("TRN2", target_bir_lowering=False)
    P = Prog(nc)
    with ExitStack() as es:
        C = Ctx(nc, es, P)
        din = {}

        def inp(name, shape, dt=F32):
            din[name] = C.dram(name, shape, dt, "ExternalInput")

        psum = [T(es.enter_context(nc.psum_tensor("ps%d" % i, [128, 512], F32))[:, :], "ps%d" % i, excl=True) for i in range(8)]
        final = []
        NOWN = NTOKW - OWN0
        if mode in ("p1", "fused"):
            inp("xw", [NTOKW, 1024])
            inp("w1a", [NG, 1024, W1A])
            inp("w1b", [NG, 1024, W1B])
            inp("convw", [NG, 128, 6, 4])
            inp("convb", [NG, 128, 6])
            for nm in ("dtb", "alog", "dsk"):
                inp(nm, [NG, 128, 8])
            inp("sng", [NG, 128, 512])
            inp("kind", [32, NTOKW], BF16)
            shapes1 = [("gq", [128, 64], F32), ("gk", [128, 64], F32), ("g1", [128, 8], F32),
                       ("tokmask", [128, NTOKW // 128], F32), ("blkbias", [128, 32], F32),
                       ("ident_b", [128, 128], BF16), ("ones_b", [128, 128], BF16), ("U", [128, 128], BF16),
                       ("T1", [128, 128], BF16), ("negm", [128, 2, 256], BF16)]
            for nm, shp, dt in shapes1:
                if nm not in din:
                    inp(nm, shp, dt)
            cst = load_consts(P, C, din, shapes1)
            cst["psring"] = Ring(psum[0:5])
            cst["po_ring"] = Ring(psum[5:7])
            cst["ps_small"] = {"gate": TV(psum[7], psum[7].ap[:, 0:128])}
            for s_ in range(4):
                cst["ps_small"]["dt%d" % s_] = TV(psum[5], psum[5].ap[:, 8 * s_:8 * s_ + 8])
                cst["ps_small"]["acs%d" % s_] = TV(psum[5], psum[5].ap[:, 64 + 16 * s_:64 + 16 * s_ + 16])
                cst["ps_small"]["cb%d" % s_] = TV(psum[6], psum[6].ap[:, 128 * s_:128 * s_ + 128])
            kind_e = "ExternalOutput" if mode == "p1" else "Internal"
            e_att = C.dram("e_att", [NG, 2, 128, NOWN], BF16, kind_e)
            e_yb = C.dram("e_yb", [NG, 4, 128, NOWN], BF16, kind_e)
            e_att_T, e_yb_T = T(None, "e_att"), T(None, "e_yb")
            hT_scr = C.dram("hT_scr", [128, 8, NTOKW], BF16, "Internal")
            hT_T = []
            with ExitStack() as es1:
                C1 = Ctx(nc, es1, P)
                if STAGES.get("pro", True):
                    build_prologue(P, C1, din, cst, hT_scr, hT_T, NTOKW)
            P.barrier()
            with ExitStack() as es1:
                C1 = Ctx(nc, es1, P)
                for g in range(NG):
                    if STAGES.get("a", True):
                        with ExitStack() as es2:
                            build_p1a(P, Ctx(nc, es2, P), din, g, cst, hT_scr, hT_T, e_yb, e_yb_T, NTOKW, OWN0)
                        P.barrier()
                    if STAGES.get("b", True):
                        with ExitStack() as es2:
                            C2b = Ctx(nc, es2, P)
                            A = build_p1_init(P, C2b, din, cst, NTOKW)
                            build_p1b(P, C2b, din, g, cst, A, hT_scr, hT_T, e_att, e_att_T, NTOKW, OWN0)
                        P.barrier()
            if mode == "p1":
                final = [o for o in P.ops["pool"] if o.dma][-8:]
        if mode in ("p2", "fused"):
            inp("x2", [2048, 1024])
            inp("p2", [2048, 256])
            for name, (K, N) in P2W.items():
                inp(name, [K, N])
            shapes2 = [("g1", [128, 8], F32), ("g2", [128, 8], F32), ("g3", [128, 8], F32),
                       ("ident_b", [128, 128], BF16), ("ones_b", [128, 128], BF16)]
            for nm, shp, dt in shapes2:
                if nm not in din:
                    inp(nm, shp, dt)
            consts = load_consts(P, C, din, shapes2)
            consts["psum"] = psum
            wscr = {name: C.dram("scr_" + name, [K, N], BF16, "Internal") for name, (K, N) in P2W.items()}
            wT = {}
            with ExitStack() as es2:
                C2 = Ctx(nc, es2, P)
                if STAGES.get("precast", True):
                    build_precast(P, C2, din, wscr, wT, consts)
            P.barrier()
            if mode == "p2":
                inp("e_att", [4, 2, 128, 2048], BF16)
                inp("e_yb", [4, 4, 128, 2048], BF16)
                exch = {"att": din["e_att"], "yb": din["e_yb"], "att_T": T(None), "yb_T": T(None)}
            else:
                exch = {"att": e_att, "yb": e_yb, "att_T": e_att_T, "yb_T": e_yb_T}
            dout = C.dram("out", [2048, 1024], F32, "ExternalOutput")
            if STAGES.get("p2", True):
                with ExitStack() as es3:
                    C3 = Ctx(nc, es3, P)
                    final = build_phase2(P, C3, din, wscr, wT, consts, exch, dout, NTOK=STAGES.get('ntok', 2048))
            else:
                final = [o for o in P.ops["pool"] if o.dma][-8:]
        P.emit(final)
    return nc


BF = ml_dtypes.bfloat16


def host_consts():
    return {
        "ident_b": np.eye(128, dtype=np.float32).astype(BF),
        "ones_b": np.ones((128, 128), dtype=np.float32).astype(BF),
    }


def host_consts1(NTOKW):
    i = np.arange(128)
    U = (i[:, None] <= i[None, :]).astype(np.float32)
    T1 = (i[:, None] > i[None, :]).astype(np.float32)
    q = np.arange(256)
    negm = np.stack([np.where((kt * 128 + i[:, None]) <= q[None, :], 0.0, NEG) for kt in range(2)], 1).astype(np.float32)
    kind = (np.arange(NTOKW)[None, :] // 256 == np.arange(32)[:, None]).astype(np.float32)
    return {"ident_b": np.eye(128, dtype=np.float32).astype(BF), "ones_b": np.ones((128, 128), np.float32).astype(BF),
            "U": U.astype(BF), "T1": T1.astype(BF), "negm": negm.astype(BF), "kind": kind.astype(BF)}


def gain_layout(g):
    return np.ascontiguousarray(g.reshape(8, 128).T)


def bc(v):
    return np.ascontiguousarray(np.broadcast_to(v[None, :], (128, v.shape[0]))).astype(np.float32)


def p1_inputs(inputs, b, groups, xw, tokmask, blkbias, NTOKW):
    w_in = inputs["w_in"][0]
    cw, cbias = inputs["conv_w"][0], inputs["conv_b"][0]
    w1a, w1b, convw, convb, dtb, alog, dsk, sng = [], [], [], [], [], [], [], []
    for g in groups:
        cols_a = np.concatenate([np.arange(5120 + 512 * g, 5120 + 512 * (g + 1)), np.arange(7168 + 128 * g, 7168 + 128 * (g + 1)),
                                 np.arange(7680 + 128 * g, 7680 + 128 * (g + 1)), np.arange(3072 + 512 * g, 3072 + 512 * (g + 1)),
                                 np.arange(8192 + 8 * g, 8192 + 8 * (g + 1))])
        cols_b = np.concatenate([np.arange(256 * g, 256 * (g + 1)), np.arange(1024 + 256 * g, 1024 + 256 * (g + 1)),
                                 np.arange(2048 + 256 * g, 2048 + 256 * (g + 1))])
        w1a.append(w_in[:, cols_a])
        w1b.append(w_in[:, cols_b])
        ch = np.concatenate([np.arange(512 * g, 512 * (g + 1)), np.arange(2048 + 128 * g, 2048 + 128 * (g + 1)),
                             np.arange(2560 + 128 * g, 2560 + 128 * (g + 1))])
        convw.append(cw[:, ch].T.reshape(6, 128, 4).transpose(1, 0, 2))
        convb.append(cbias[ch].reshape(6, 128).T)
        dtb.append(bc(inputs["dt_bias"][0][8 * g:8 * g + 8]))
        alog.append(bc(inputs["a_log"][0][8 * g:8 * g + 8]))
        dsk.append(bc(inputs["d_skip"][0][8 * g:8 * g + 8]))
        sng.append(bc(inputs["ssm_norm_g"][0][512 * g:512 * g + 512]))
    m = {"xw": np.ascontiguousarray(xw), "w1a": np.ascontiguousarray(np.stack(w1a)), "w1b": np.ascontiguousarray(np.stack(w1b)),
         "convw": np.ascontiguousarray(np.stack(convw)), "convb": np.ascontiguousarray(np.stack(convb)),
         "dtb": np.stack(dtb), "alog": np.stack(alog), "dsk": np.stack(dsk), "sng": np.stack(sng),
         "gq": bc(inputs["q_norm_g"][0]), "gk": bc(inputs["k_norm_g"][0]), "g1": gain_layout(inputs["ln1_g"][0]),
         "tokmask": np.ascontiguousarray(tokmask.reshape(-1, 128).T).astype(np.float32), "blkbias": bc(blkbias)}
    m.update(host_consts1(NTOKW))
    return m


def p2_inputs(inputs, core, e_att=None, e_yb=None):
    b, t = core // 4, core % 4
    sl = slice(t * 2048, (t + 1) * 2048)
    m = {
        "x2": np.ascontiguousarray(inputs["x"][b, sl]),
        "p2": np.ascontiguousarray(inputs["p"][0, b, sl]),
        "wg": np.ascontiguousarray(inputs["w_in"][0][:, 8224:10272]),
        "woa": inputs["w_o_attn"][0], "wos": inputs["w_o_ssm"][0], "wout": inputs["w_out"][0],
        "wgu": inputs["w_gate_up"][0], "wd": inputs["w_down"][0], "wpg": inputs["w_ple_gate"][0],
        "wpp": inputs["w_ple_proj"][0],
        "g1": gain_layout(inputs["ln1_g"][0]), "g2": gain_layout(inputs["ln2_g"][0]),
        "g3": gain_layout(inputs["ln3_g"][0]),
    }
    m.update(host_consts())
    if e_att is not None:
        m["e_att"] = e_att
        m["e_yb"] = e_yb
    return m


MODE = "fused"


def kernel(**inputs):
    inputs = {k: np.asarray(v) for k, v in inputs.items()}
    x = inputs["x"]
    out = np.zeros(x.shape, np.float32)
    if MODE == "two":
        nc1 = build_program("p1", NTOKW=8192, OWN0=0, NG=1)
        maps1 = []
        for core in range(8):
            b, g = core // 4, core % 4
            maps1.append(p1_inputs(inputs, b, [g], x[b], np.ones(8192, np.float32), np.zeros(32, np.float32), 8192))
        r1 = run_bass_kernel_spmd(nc1, maps1, core_ids=list(range(8))).results
        nc2 = build_program("p2")
        maps2 = []
        for core in range(8):
            b, t = core // 4, core % 4
            sl = slice(t * 2048, (t + 1) * 2048)
            ea = np.stack([np.asarray(r1[b * 4 + g]["e_att"])[0][:, :, sl] for g in range(4)])
            ey = np.stack([np.asarray(r1[b * 4 + g]["e_yb"])[0][:, :, sl] for g in range(4)])
            maps2.append(p2_inputs(inputs, core, np.ascontiguousarray(ea), np.ascontiguousarray(ey)))
        r2 = run_bass_kernel_spmd(nc2, maps2, core_ids=list(range(8))).results
        for core in range(8):
            b, t = core // 4, core % 4
            out[b, t * 2048:(t + 1) * 2048] = np.asarray(r2[core]["out"])
        return out
    nc = build_program("fused", NTOKW=8192, OWN0=6144, NG=4)
    maps = []
    for core in range(8):
        b, t = core // 4, core % 4
        npad = (3 - t) * 2048
        xw = np.concatenate([np.zeros((npad, 1024), np.float32), x[b, :(t + 1) * 2048]], 0)
        tokmask = np.concatenate([np.zeros(npad, np.float32), np.ones(8192 - npad, np.float32)])
        blkbias = np.where(np.arange(32) < npad // 256, NEG, 0.0).astype(np.float32)
        m = p1_inputs(inputs, b, [0, 1, 2, 3], xw, tokmask, blkbias, 8192)
        m.update(p2_inputs(inputs, core))
        maps.append(m)
    r = run_bass_kernel_spmd(nc, maps, core_ids=list(range(8))).results
    for core in range(8):
        b, t = core // 4, core % 4
        out[b, t * 2048:(t + 1) * 2048] = np.asarray(r[core]["out"])
    return out
```

```python
import numpy as np
from contextlib import ExitStack
import ml_dtypes
import concourse.bass as bass
import concourse.mybir as mybir
from concourse.bass_utils import run_bass_kernel_spmd

F32 = mybir.dt.float32
BF16 = mybir.dt.bfloat16
AF = mybir.ActivationFunctionType
ALU = mybir.AluOpType
AX = mybir.AxisListType

EPS = 1e-6
NEG = -30000.0
SAME_ENGINE_SYNC = True
STAGES = {}


class T:
    __slots__ = ("ap", "w", "r", "name", "excl")

    def __init__(self, ap=None, name="", excl=False):
        self.ap = ap
        self.w = None
        self.r = []
        self.name = name
        self.excl = excl

    def __getitem__(self, k):
        return self.ap[k]


class TV(T):
    __slots__ = ("parent",)

    def __init__(self, parent, ap):
        self.parent = parent
        self.ap = ap
        self.name = parent.name
        self.excl = parent.excl

    @property
    def w(self):
        return self.parent.w

    @w.setter
    def w(self, v):
        self.parent.w = v

    @property
    def r(self):
        return self.parent.r

    @r.setter
    def r(self, v):
        self.parent.r = v


class Op:
    __slots__ = ("eng", "fn", "deps", "dma", "inc", "sem", "ticket", "idx", "prev_same_sem", "vc")

    def __init__(self, eng, fn, dma):
        self.eng = eng
        self.fn = fn
        self.dma = dma
        self.deps = []
        self.inc = False
        self.sem = None
        self.ticket = 0
        self.prev_same_sem = None


class Prog:
    ENGS = ("pe", "act", "dve", "pool", "sp")
    NDS = 8

    def __init__(self, nc):
        self.nc = nc
        self.ops = {e: [] for e in self.ENGS}
        self.all = []
        self.bar = {}

    def op(self, eng, fn, reads=(), writes=(), dma=False):
        o = Op(eng, fn, dma)
        deps = []
        for t in reads:
            if t.w is not None:
                deps.append(t.w)
            if t.excl:
                deps.extend(x for x in t.r if x.eng != eng)
        for t in writes:
            if t.w is not None:
                deps.append(t.w)
            deps.extend(t.r)
        b = self.bar.pop(eng, None)
        if b:
            deps.extend(b)
        seen = set()
        for d in deps:
            if d is o or id(d) in seen:
                continue
            seen.add(id(d))
            if (not d.dma) and d.eng == eng and (eng == "pe" or not SAME_ENGINE_SYNC):
                continue
            o.deps.append(d)
            d.inc = True
        for t in reads:
            if dma:
                t.r.append(o)
            else:
                t.r = [x for x in t.r if x.dma or x.eng != eng] + [o]
        for t in writes:
            t.w = o
            t.r = []
        if dma:
            o.inc = True
        self.ops[eng].append(o)
        self.all.append(o)
        return o

    def barrier(self):
        last = []
        for e in self.ENGS:
            ops = self.ops[e]
            if ops:
                last.append(ops[-1])
            last.extend([o for o in ops if o.dma][-self.NDS:])
        for e in self.ENGS:
            self.bar[e] = list(last)

    def emit(self, final_ops):
        nc = self.nc
        with ExitStack() as es:
            SEM_CAP = 1000
            nsem = {e: sum(1 for o in self.ops[e] if o.inc and not o.dma) // SEM_CAP + 1 for e in ("pe", "act", "dve", "pool")}
            csem = {e: [es.enter_context(nc.semaphore("cs_%s%d" % (e, i))) for i in range(nsem[e])]
                    for e in ("pe", "act", "dve", "pool")}
            dsem = {e: [es.enter_context(nc.semaphore("ds_%s%d" % (e, i))) for i in range(self.NDS)]
                    for e in self.ENGS}
            ccount = {e: 0 for e in self.ENGS}
            dcount = {e: [0] * self.NDS for e in self.ENGS}
            drr = {e: 0 for e in self.ENGS}
            dlast = {e: [None] * self.NDS for e in self.ENGS}
            for e in self.ENGS:
                for o in self.ops[e]:
                    if o.dma:
                        k = drr[e] % self.NDS
                        drr[e] += 1
                        dcount[e][k] += 16
                        o.sem = dsem[e][k]
                        o.ticket = dcount[e][k]
                        o.prev_same_sem = dlast[e][k]
                        dlast[e][k] = o
                    elif o.inc:
                        o.sem = csem[e][ccount[e] // SEM_CAP]
                        o.ticket = ccount[e] % SEM_CAP + 1
                        ccount[e] += 1

            know = {e: {} for e in self.ENGS}
            plan = {}
            for o in self.all:
                kn = know[o.eng]
                waits = []
                dl = list(o.deps)
                if o.dma and o.prev_same_sem is not None:
                    dl.append(o.prev_same_sem)
                for d in dl:
                    key = id(d.sem)
                    if kn.get(key, 0) < d.ticket:
                        waits.append(d)
                        for kk, vv in d.vc.items():
                            if kn.get(kk, 0) < vv:
                                kn[kk] = vv
                plan[id(o)] = waits
                o.vc = dict(kn)
                if o.sem is not None:
                    o.vc[id(o.sem)] = o.ticket
                    if not o.dma:
                        kn[id(o.sem)] = max(kn.get(id(o.sem), 0), 0)

            def run(ename, eng):
                for o in self.ops[ename]:
                    for d in plan[id(o)]:
                        eng.wait_ge(d.sem, d.ticket)
                    ins = o.fn(eng)
                    if o.sem is not None:
                        ins.then_inc(o.sem, 16 if o.dma else 1)
                if ename == "sp":
                    kn = know["sp"]
                    for d in final_ops:
                        if kn.get(id(d.sem), 0) < d.ticket:
                            eng.wait_ge(d.sem, d.ticket)
                            kn[id(d.sem)] = d.ticket

            with nc.Block() as block:
                @block.tensor
                def _(eng):
                    run("pe", eng)

                @block.scalar
                def _(eng):
                    run("act", eng)

                @block.vector
                def _(eng):
                    run("dve", eng)

                @block.gpsimd
                def _(eng):
                    run("pool", eng)

                @block.sync
                def _(eng):
                    run("sp", eng)


class Ctx:
    N = 0

    def __init__(self, nc, es, P):
        self.nc, self.es, self.P = nc, es, P
        self.n = 0

    def sb(self, shape, dt, name=None):
        Ctx.N += 1
        h = self.es.enter_context(self.nc.sbuf_tensor("%s_%d" % (name or "sb", Ctx.N), list(shape), dt))
        return h

    def sbT(self, shape, dt, name=None):
        h = self.sb(shape, dt, name)
        return T(h[tuple(slice(None) for _ in shape)], name or "")

    def dram(self, name, shape, dt, kind):
        return self.nc.dram_tensor(name, list(shape), dt, kind=kind).ap()


class Ring:
    def __init__(self, tiles):
        self.tiles = tiles
        self.i = 0

    def next(self):
        t = self.tiles[self.i % len(self.tiles)]
        self.i += 1
        return t


P2W = {
    "wg": (1024, 2048), "woa": (1024, 1024), "wos": (2048, 1024), "wout": (1024, 1024),
    "wgu": (1024, 5632), "wd": (2816, 1024), "wpg": (1024, 1024), "wpp": (256, 1024),
}
P2W_GAIN = {"wg": "g1", "wgu": "g2", "wpg": "g3"}
WB_COLS = 256


def wblocks(name):
    K, N = P2W[name]
    nkc = K // 128
    kbs = []
    k0 = 0
    while k0 < nkc:
        nk = min(8, nkc - k0)
        kbs.append((k0, nk))
        k0 += nk
    return kbs, N // WB_COLS


def build_precast(P, C, din, wscr, wT, consts):
    nc = P.nc
    stg = Ring([C.sbT([128, 8, WB_COLS], F32, "pc_stg") for _ in range(2)])
    wbf = Ring([C.sbT([128, 8, WB_COLS], BF16, "pc_bf") for _ in range(2)])
    for name in P2W:
        kbs, ncb = wblocks(name)
        src = din[name]
        dst = wscr[name]
        gain = consts.get(P2W_GAIN.get(name))
        for cb in range(ncb):
            for (k0, nk) in kbs:
                s, w = stg.next(), wbf.next()
                sv = src[k0 * 128:(k0 + nk) * 128, cb * WB_COLS:(cb + 1) * WB_COLS].rearrange("(k p) n -> p k n", p=128)
                dv = dst[k0 * 128:(k0 + nk) * 128, cb * WB_COLS:(cb + 1) * WB_COLS].rearrange("(k p) n -> p k n", p=128)
                P.op("sp", lambda e, s=s, sv=sv, nk=nk: e.dma_start(out=s[:, 0:nk, :], in_=sv), writes=[s], dma=True)
                if gain is not None:
                    gv = gain[:, k0:k0 + nk].unsqueeze(2).to_broadcast([128, nk, WB_COLS])
                    P.op("pool", lambda e, s=s, w=w, gv=gv, nk=nk: e.tensor_tensor(
                        out=w[:, 0:nk, :], in0=s[:, 0:nk, :], in1=gv, op=ALU.mult), reads=[s, gain], writes=[w])
                else:
                    P.op("pool", lambda e, s=s, w=w, nk=nk: e.tensor_copy(out=w[:, 0:nk, :], in_=s[:, 0:nk, :]),
                         reads=[s], writes=[w])
                t = T(None, "wscr")
                wT[(name, cb, k0)] = t
                P.op("pool", lambda e, w=w, dv=dv, nk=nk: e.dma_start(out=dv, in_=w[:, 0:nk, :]),
                     reads=[w], writes=[t], dma=True)


def build_phase2(P, C, din, wscr, wT, consts, exch, dout, NTOK=2048, TT=1024):
    nc = P.nc
    NTG = TT // 512
    NSUB = TT // 128
    ident_b = consts["ident_b"]
    ones_b = consts["ones_b"]
    xT = [C.sbT([128, TT], F32, "xT") for _ in range(8)]
    hT = [C.sbT([128, TT], BF16, "hT") for _ in range(8)]
    big = C.sb([128, 24, TT], BF16, "big")
    attT = [T(big[:, c, :], "att") for c in range(8)]
    ybT = [T(big[:, 8 + c, :], "yb") for c in range(16)]
    mg = [C.sbT([128, TT], BF16, "mg") for _ in range(8)]
    pT = [C.sbT([128, TT], BF16, "pT") for _ in range(2)]
    xin = Ring([C.sbT([128, 1024], F32, "xin") for _ in range(2)])
    xhi = Ring([C.sbT([128, 1024], BF16, "xhi") for _ in range(2)])
    xlo = Ring([C.sbT([128, 1024], BF16, "xlo") for _ in range(2)])
    xres = Ring([C.sbT([128, 1024], F32, "xres") for _ in range(1)])
    pin = Ring([C.sbT([128, 256], F32, "pin") for _ in range(2)])
    pinb = Ring([C.sbT([128, 256], BF16, "pinb") for _ in range(2)])
    wring = Ring([C.sbT([128, 8, WB_COLS], BF16, "wr") for _ in range(8)])
    tmpf = Ring([C.sbT([128, 512], F32, "tmpf") for _ in range(8)])
    sqb = Ring([C.sbT([128, 512], BF16, "sqb") for _ in range(3)])
    rstd = [C.sbT([128, 512], F32, "rstd") for _ in range(NTG)]
    psum = Ring(consts["psum"])

    def load_w(name, cb, k0, nk):
        w = wring.next()
        sv = wscr[name][k0 * 128:(k0 + nk) * 128, cb * WB_COLS:(cb + 1) * WB_COLS].rearrange("(k p) n -> p k n", p=128)
        P.op("sp", lambda e, w=w, sv=sv, nk=nk: e.dma_start(out=w[:, 0:nk, :], in_=sv),
             reads=[wT[(name, cb, k0)]], writes=[w], dma=True)
        return w

    def mm_group(ps, wlist, X, tg, oc_in_blk):
        n = sum(nk for _, _, nk in wlist)
        i = 0
        for (w, k0, nk) in wlist:
            for k in range(nk):
                st, sp_ = (i == 0), (i == n - 1)
                xk = X[k0 + k]
                P.op("pe", lambda e, ps=ps, w=w, k=k, xk=xk, st=st, sp_=sp_: e.matmul(
                    ps[:, :], lhsT=w[:, k, oc_in_blk * 128:(oc_in_blk + 1) * 128],
                    rhs=xk[:, tg * 512:(tg + 1) * 512], start=st, stop=sp_),
                    reads=[w, xk], writes=[ps])
                i += 1

    def rmsnorm():
        for tg in range(NTG):
            ps = psum.next()
            for kc in range(8):
                sq = sqb.next()
                P.op("act", lambda e, sq=sq, kc=kc, tg=tg: e.activation(
                    out=sq[:, :], in_=xT[kc][:, tg * 512:(tg + 1) * 512], func=AF.Square), reads=[xT[kc]], writes=[sq])
                P.op("pe", lambda e, ps=ps, sq=sq, kc=kc: e.matmul(
                    ps[:, :], lhsT=ones_b[:, :], rhs=sq[:, :], start=(kc == 0), stop=(kc == 7)),
                    reads=[sq, ones_b], writes=[ps])
            r = rstd[tg]
            P.op("act", lambda e, r=r, ps=ps: e.activation(
                out=r[:, :], in_=ps[:, :], func=AF.Sqrt, scale=1.0 / 1024.0, bias=EPS), reads=[ps], writes=[r])
            P.op("dve", lambda e, r=r: e.reciprocal(out=r[:, :], in_=r[:, :]), reads=[r], writes=[r])
        for kc in range(8):
            for tg in range(NTG):
                P.op("dve", lambda e, kc=kc, tg=tg: e.tensor_tensor(
                    out=hT[kc][:, tg * 512:(tg + 1) * 512], in0=xT[kc][:, tg * 512:(tg + 1) * 512],
                    in1=rstd[tg][:, :], op=ALU.mult), reads=[xT[kc], rstd[tg]], writes=[hT[kc]])

    out_ops = []
    for tt in range(NTOK // TT):
        t0 = tt * TT
        for s in range(NSUB):
            xi = xin.next()
            P.op("sp", lambda e, xi=xi, s=s, t0=t0: e.dma_start(out=xi[:, :], in_=din["x2"][t0 + s * 128:t0 + (s + 1) * 128, :]),
                 writes=[xi], dma=True)
            xh, xl, xr = xhi.next(), xlo.next(), xres.next()
            P.op("act", lambda e, xi=xi, xh=xh: e.copy(out=xh[:, :], in_=xi[:, :]), reads=[xi], writes=[xh])
            P.op("dve", lambda e, xi=xi, xh=xh, xr=xr: e.tensor_tensor(out=xr[:, :], in0=xi[:, :], in1=xh[:, :], op=ALU.subtract),
                 reads=[xi, xh], writes=[xr])
            P.op("pool", lambda e, xr=xr, xl=xl: e.tensor_copy(out=xl[:, :], in_=xr[:, :]), reads=[xr], writes=[xl])
            for half in range(2):
                ps = psum.next()
                for j in range(4):
                    kc = half * 4 + j
                    P.op("pe", lambda e, ps=ps, xh=xh, kc=kc, j=j: e.matmul(
                        ps[:, j * 128:(j + 1) * 128], lhsT=xh[:, kc * 128:(kc + 1) * 128], rhs=ident_b[:, :], start=True, stop=False),
                        reads=[xh, ident_b], writes=[ps])
                    P.op("pe", lambda e, ps=ps, xl=xl, kc=kc, j=j: e.matmul(
                        ps[:, j * 128:(j + 1) * 128], lhsT=xl[:, kc * 128:(kc + 1) * 128], rhs=ident_b[:, :], start=False, stop=True),
                        reads=[xl, ident_b], writes=[ps])
                for j in range(4):
                    kc = half * 4 + j
                    eng = "act" if j % 2 == 0 else "dve"
                    if eng == "act":
                        P.op("act", lambda e, ps=ps, kc=kc, j=j, s=s: e.copy(
                            out=xT[kc][:, s * 128:(s + 1) * 128], in_=ps[:, j * 128:(j + 1) * 128]),
                            reads=[ps], writes=[xT[kc]])
                    else:
                        P.op("dve", lambda e, ps=ps, kc=kc, j=j, s=s: e.tensor_copy(
                            out=xT[kc][:, s * 128:(s + 1) * 128], in_=ps[:, j * 128:(j + 1) * 128]),
                            reads=[ps], writes=[xT[kc]])
            pi, pb = pin.next(), pinb.next()
            P.op("sp", lambda e, pi=pi, s=s, t0=t0: e.dma_start(out=pi[:, :], in_=din["p2"][t0 + s * 128:t0 + (s + 1) * 128, :]),
                 writes=[pi], dma=True)
            P.op("pool", lambda e, pi=pi, pb=pb: e.tensor_copy(out=pb[:, :], in_=pi[:, :]), reads=[pi], writes=[pb])
            ps = psum.next()
            psb = ps.ap.bitcast(BF16)
            for j in range(2):
                P.op("pe", lambda e, psb=psb, pb=pb, j=j: e.transpose(
                    psb[:, j * 128:(j + 1) * 128], pb[:, j * 128:(j + 1) * 128], ident_b[:, :]),
                    reads=[pb, ident_b], writes=[ps])
            for j in range(2):
                P.op("act", lambda e, psb=psb, j=j, s=s: e.copy(
                    out=pT[j][:, s * 128:(s + 1) * 128], in_=psb[:, j * 128:(j + 1) * 128]), reads=[ps], writes=[pT[j]])
        for g in range(4):
            for c in range(2):
                tl = attT[2 * g + c]
                P.op("sp", lambda e, tl=tl, g=g, c=c, t0=t0: e.dma_start(out=tl[:, :], in_=exch["att"][g, c, :, t0:t0 + TT]),
                     reads=[exch["att_T"]], writes=[tl], dma=True)
            for c in range(4):
                tl = ybT[4 * g + c]
                P.op("sp", lambda e, tl=tl, g=g, c=c, t0=t0: e.dma_start(out=tl[:, :], in_=exch["yb"][g, c, :, t0:t0 + TT]),
                     reads=[exch["yb_T"]], writes=[tl], dma=True)
        p2s = STAGES.get('p2s', 'BXCD')
        if 'B' in p2s:
            rmsnorm()
        for j in (range(4) if 'B' in p2s else []):
            wga = load_w("wg", j, 0, 8)
            wgb = load_w("wg", 4 + j, 0, 8)
            wa = load_w("woa", j, 0, 8)
            ws0 = load_w("wos", j, 0, 8)
            ws1 = load_w("wos", j, 8, 8)
            for o2 in range(2):
                oc = 2 * j + o2
                for tg in range(NTG):
                    pga, pgb, pya, pyb = psum.next(), psum.next(), psum.next(), psum.next()
                    mm_group(pga, [(wga, 0, 8)], hT, tg, o2)
                    mm_group(pgb, [(wgb, 0, 8)], hT, tg, o2)
                    mm_group(pya, [(wa, 0, 8)], attT, tg, o2)
                    mm_group(pyb, [(ws0, 0, 8), (ws1, 8, 8)], ybT, tg, o2)
                    sa, sb_, m1, m2 = tmpf.next(), tmpf.next(), tmpf.next(), tmpf.next()
                    P.op("act", lambda e, sa=sa, pga=pga: e.activation(out=sa[:, :], in_=pga[:, :], func=AF.Sigmoid),
                         reads=[pga], writes=[sa])
                    P.op("act", lambda e, sb_=sb_, pgb=pgb: e.activation(out=sb_[:, :], in_=pgb[:, :], func=AF.Sigmoid),
                         reads=[pgb], writes=[sb_])
                    P.op("dve", lambda e, m1=m1, sa=sa, pya=pya: e.tensor_tensor(
                        out=m1[:, :], in0=sa[:, :], in1=pya[:, :], op=ALU.mult), reads=[sa, pya], writes=[m1])
                    P.op("dve", lambda e, m2=m2, sb_=sb_, pyb=pyb: e.tensor_tensor(
                        out=m2[:, :], in0=sb_[:, :], in1=pyb[:, :], op=ALU.mult), reads=[sb_, pyb], writes=[m2])
                    P.op("pool", lambda e, m1=m1, m2=m2, oc=oc, tg=tg: e.tensor_tensor(
                        out=mg[oc][:, tg * 512:(tg + 1) * 512], in0=m1[:, :], in1=m2[:, :], op=ALU.add),
                        reads=[m1, m2], writes=[mg[oc]])
        for j in (range(4) if 'X' in p2s else []):
            w = load_w("wout", j, 0, 8)
            for o2 in range(2):
                oc = 2 * j + o2
                for tg in range(NTG):
                    ps = psum.next()
                    mm_group(ps, [(w, 0, 8)], mg, tg, o2)
                    P.op("dve", lambda e, ps=ps, oc=oc, tg=tg: e.tensor_tensor(
                        out=xT[oc][:, tg * 512:(tg + 1) * 512], in0=ps[:, :], in1=xT[oc][:, tg * 512:(tg + 1) * 512],
                        op=ALU.add), reads=[ps, xT[oc]], writes=[xT[oc]])
        if 'C' in p2s:
            rmsnorm()
        actT = [T(big[:, f, :], "act") for f in range(22)]
        for f in range(22):
            old = attT[f] if f < 8 else ybT[f - 8]
            actT[f].w, actT[f].r = old.w, old.r
        for j in (range(11) if 'C' in p2s else []):
            wgt = load_w("wgu", j, 0, 8)
            wup = load_w("wgu", 11 + j, 0, 8)
            for o2 in range(2):
                f = 2 * j + o2
                for tg in range(NTG):
                    pg, pu = psum.next(), psum.next()
                    mm_group(pg, [(wgt, 0, 8)], hT, tg, o2)
                    mm_group(pu, [(wup, 0, 8)], hT, tg, o2)
                    sg = tmpf.next()
                    P.op("act", lambda e, sg=sg, pg=pg: e.activation(out=sg[:, :], in_=pg[:, :], func=AF.Silu),
                         reads=[pg], writes=[sg])
                    P.op("dve", lambda e, sg=sg, pu=pu, f=f, tg=tg: e.tensor_tensor(
                        out=actT[f][:, tg * 512:(tg + 1) * 512], in0=sg[:, :], in1=pu[:, :], op=ALU.mult),
                        reads=[sg, pu], writes=[actT[f]])
        for j in (range(4) if 'C' in p2s else []):
            w0 = load_w("wd", j, 0, 8)
            w1 = load_w("wd", j, 8, 8)
            w2 = load_w("wd", j, 16, 6)
            for o2 in range(2):
                oc = 2 * j + o2
                for tg in range(NTG):
                    ps = psum.next()
                    mm_group(ps, [(w0, 0, 8), (w1, 8, 8), (w2, 16, 6)], actT, tg, o2)
                    P.op("dve", lambda e, ps=ps, oc=oc, tg=tg: e.tensor_tensor(
                        out=xT[oc][:, tg * 512:(tg + 1) * 512], in0=ps[:, :], in1=xT[oc][:, tg * 512:(tg + 1) * 512],
                        op=ALU.add), reads=[ps, xT[oc]], writes=[xT[oc]])
        for f in range(22):
            old = attT[f] if f < 8 else ybT[f - 8]
            old.w, old.r = actT[f].w, actT[f].r
        if 'D' in p2s:
            rmsnorm()
        for j in (range(4) if 'D' in p2s else []):
            wpg = load_w("wpg", j, 0, 8)
            wpp = load_w("wpp", j, 0, 2)
            for o2 in range(2):
                oc = 2 * j + o2
                for tg in range(NTG):
                    pg, pp = psum.next(), psum.next()
                    mm_group(pg, [(wpg, 0, 8)], hT, tg, o2)
                    mm_group(pp, [(wpp, 0, 2)], pT, tg, o2)
                    sg, m = tmpf.next(), tmpf.next()
                    P.op("act", lambda e, sg=sg, pg=pg: e.activation(out=sg[:, :], in_=pg[:, :], func=AF.Sigmoid),
                         reads=[pg], writes=[sg])
                    P.op("dve", lambda e, m=m, sg=sg, pp=pp: e.tensor_tensor(
                        out=m[:, :], in0=sg[:, :], in1=pp[:, :], op=ALU.mult), reads=[sg, pp], writes=[m])
                    P.op("dve", lambda e, m=m, oc=oc, tg=tg: e.tensor_tensor(
                        out=xT[oc][:, tg * 512:(tg + 1) * 512], in0=m[:, :], in1=xT[oc][:, tg * 512:(tg + 1) * 512],
                        op=ALU.add), reads=[m, xT[oc]], writes=[xT[oc]])
        for kc in range(8):
            P.op("act", lambda e, kc=kc: e.copy(out=hT[kc][:, :], in_=xT[kc][:, :]), reads=[xT[kc], hT[kc]], writes=[hT[kc]])
            P.op("dve", lambda e, kc=kc: e.tensor_tensor(out=xT[kc][:, :], in0=xT[kc][:, :], in1=hT[kc][:, :], op=ALU.subtract),
                 reads=[xT[kc], hT[kc]], writes=[xT[kc]])
            P.op("pool", lambda e, kc=kc: e.tensor_copy(out=mg[kc][:, :], in_=xT[kc][:, :]), reads=[xT[kc], mg[kc]], writes=[mg[kc]])
        for s in range(NSUB):
            xo = xin.next()
            for half in range(2):
                ps = psum.next()
                for j in range(4):
                    kc = half * 4 + j
                    P.op("pe", lambda e, ps=ps, kc=kc, j=j, s=s: e.matmul(
                        ps[:, j * 128:(j + 1) * 128], lhsT=hT[kc][:, s * 128:(s + 1) * 128], rhs=ident_b[:, :], start=True, stop=False),
                        reads=[hT[kc], ident_b], writes=[ps])
                    P.op("pe", lambda e, ps=ps, kc=kc, j=j, s=s: e.matmul(
                        ps[:, j * 128:(j + 1) * 128], lhsT=mg[kc][:, s * 128:(s + 1) * 128], rhs=ident_b[:, :], start=False, stop=True),
                        reads=[mg[kc], ident_b], writes=[ps])
                if half == 0:
                    P.op("act", lambda e, ps=ps, xo=xo: e.copy(out=xo[:, 0:512], in_=ps[:, :]), reads=[ps], writes=[xo])
                else:
                    P.op("dve", lambda e, ps=ps, xo=xo: e.tensor_copy(out=xo[:, 512:1024], in_=ps[:, :]),
                         reads=[ps, xo], writes=[xo])
            out_ops.append(P.op("pool", lambda e, xo=xo, s=s, t0=t0: e.dma_start(
                out=dout[t0 + s * 128:t0 + (s + 1) * 128, :], in_=xo[:, :]), reads=[xo], dma=True))
    return out_ops


W1A = 1288
W1B = 768


def load_cast_weight(P, C, src, w, ncols, gain, stg):
    c0 = 0
    while c0 < ncols:
        n = min(WB_COLS, ncols - c0)
        s = stg.next()
        sv = src[:, c0:c0 + n].rearrange("(k p) n -> p k n", p=128)
        P.op("sp", lambda e, s=s, sv=sv, n=n: e.dma_start(out=s[:, :, 0:n], in_=sv), writes=[s], dma=True)
        gv = gain[:, 0:8].unsqueeze(2).to_broadcast([128, 8, n])
        P.op("pool", lambda e, s=s, gv=gv, n=n, c0=c0: e.tensor_tensor(
            out=w[:, :, c0:c0 + n], in0=s[:, :, 0:n], in1=gv, op=ALU.mult), reads=[s, gain], writes=[w])
        c0 += n


def build_prologue(P, C, din, cst, hT_scr, hT_T, NTOKW):
    xin = Ring([C.sbT([128, 1024], F32, "pxin") for _ in range(3)])
    junk = C.sbT([128, 1024], BF16, "pjunk")
    hb = Ring([C.sbT([128, 1024], BF16, "phb") for _ in range(2)])
    ssr = Ring([C.sbT([128, 2], F32, "pss") for _ in range(4)])
    hst = Ring([C.sbT([128, 8, 512], BF16, "phst") for _ in range(2)])
    psum = cst["psring"]
    ident_b = cst["ident_b"]
    for m in range(NTOKW // 512):
        ht = hst.next()
        for s in range(4):
            t0 = m * 512 + s * 128
            xi, ss, h = xin.next(), ssr.next(), hb.next()
            P.op("sp", lambda e, xi=xi, t0=t0: e.dma_start(out=xi[:, :], in_=din["xw"][t0:t0 + 128, :]), writes=[xi], dma=True)
            P.op("act", lambda e, xi=xi, ss=ss: e.activation(out=junk[:, :], in_=xi[:, :], func=AF.Square, accum_out=ss[:, 0:1]),
                 reads=[xi], writes=[junk, ss])
            P.op("act", lambda e, ss=ss: e.activation(out=ss[:, 1:2], in_=ss[:, 0:1], func=AF.Sqrt, scale=1.0 / 1024.0, bias=EPS),
                 reads=[ss], writes=[ss])
            P.op("dve", lambda e, ss=ss: e.reciprocal(out=ss[:, 1:2], in_=ss[:, 1:2]), reads=[ss], writes=[ss])
            P.op("dve", lambda e, xi=xi, ss=ss, h=h: e.tensor_scalar(
                out=h[:, :], in0=xi[:, :], scalar1=ss[:, 1:2], scalar2=None, op0=ALU.mult), reads=[xi, ss], writes=[h])
            ps = psum.next()
            psb = ps.ap.bitcast(BF16)
            for kc in range(8):
                P.op("pe", lambda e, psb=psb, h=h, kc=kc: e.transpose(
                    psb[:, kc * 128:(kc + 1) * 128], h[:, kc * 128:(kc + 1) * 128], ident_b[:, :]),
                    reads=[h, ident_b], writes=[ps])
            P.op("act", lambda e, psb=psb, ht=ht, s=s: e.copy(
                out=ht[:, 0:4, s * 128:(s + 1) * 128], in_=psb[:, 0:512].rearrange("p (k n) -> p k n", k=4)),
                reads=[ps], writes=[ht])
            P.op("dve", lambda e, psb=psb, ht=ht, s=s: e.tensor_copy(
                out=ht[:, 4:8, s * 128:(s + 1) * 128], in_=psb[:, 512:1024].rearrange("p (k n) -> p k n", k=4)),
                reads=[ps, ht], writes=[ht])
        t = T(None, "hTscr")
        hT_T.append(t)
        P.op("pool", lambda e, ht=ht, m=m: e.dma_start(out=hT_scr[:, :, m * 512:(m + 1) * 512], in_=ht[:, :, :]),
             reads=[ht], writes=[t], dma=True)


def build_p1a(P, C, din, g, cst, hT_scr, hT_T, e_yb, e_yb_T, NTOKW, OWN0):
    psum = cst["psring"]
    ident_b, ones_b, U, T1 = cst["ident_b"], cst["ones_b"], cst["U"], cst["T1"]
    stg = Ring([C.sbT([128, 8, WB_COLS], F32, "a_stg") for _ in range(2)])
    w1a = C.sbT([128, 8, W1A], BF16, "w1a")
    load_cast_weight(P, C, din["w1a"][g], w1a, W1A, cst["g1"], stg)
    small = {}
    for nm, shp in (("convw", [128, 6, 4]), ("convb", [128, 6]), ("dtb", [128, 8]), ("alog", [128, 8]),
                    ("dsk", [128, 8]), ("sng", [128, 512])):
        t = C.sbT(shp, F32, "a_" + nm)
        P.op("sp", lambda e, t=t, nm=nm: e.dma_start(out=t.ap, in_=din[nm][g]), writes=[t], dma=True)
        small[nm] = t
    cw, cb, dtb, alog, dsk, sng = (small[k] for k in ("convw", "convb", "dtb", "alog", "dsk", "sng"))
    tokmask = cst["tokmask"]
    Abc = C.sbT([128, 8], F32, "Abc")
    P.op("act", lambda e: e.activation(out=Abc[:, :], in_=alog[:, :], func=AF.Exp), reads=[alog], writes=[Abc])
    P.op("dve", lambda e: e.tensor_scalar(out=Abc[:, :], in0=Abc[:, :], scalar1=-1.0, scalar2=None, op0=ALU.mult),
         reads=[Abc], writes=[Abc])
    S = C.sbT([128, 512], F32, "S")
    Sbf = C.sbT([128, 512], BF16, "Sbf")
    xbc = C.sbT([128, 6, 515], F32, "xbc")
    P.op("pool", lambda e: e.memset(S[:, :], 0.0), writes=[S])
    P.op("pool", lambda e: e.memset(Sbf[:, :], 0.0), writes=[Sbf])
    P.op("pool", lambda e: e.memset(xbc[:, :, 0:3], 0.0), writes=[xbc])
    hring = Ring([C.sbT([128, 8, 512], BF16, "a_hT") for _ in range(2)])
    cring = Ring([C.sbT([128, 6, 512], BF16, "a_co") for _ in range(2)])
    accr = Ring([C.sbT([128, 512], F32, "a_acc") for _ in range(5)])
    f512 = Ring([C.sbT([128, 512], F32, "a_f512") for _ in range(6)])
    szr = Ring([C.sbT([128, 512], F32, "a_sz") for _ in range(5)])
    b512 = Ring([C.sbT([128, 512], BF16, "a_b512") for _ in range(28)])
    s8 = Ring([C.sbT([128, 8], F32, "a_s8") for _ in range(64)])
    s2 = Ring([C.sbT([128, 2], F32, "a_s2") for _ in range(6)])
    Rr = Ring([C.sbT([128, 3, 8, 128], BF16, "a_R") for _ in range(4)])
    a3r = Ring([C.sbT([128, 3, 8], BF16, "a_a3") for _ in range(6)])
    Lr = Ring([C.sbT([128, 8, 128], BF16, "a_L") for _ in range(5)])
    Mr = Ring([C.sbT([128, 8, 128], BF16, "a_M") for _ in range(5)])
    cbm = Ring([C.sbT([128, 128], BF16, "a_cbm") for _ in range(5)])
    ybst = Ring([C.sbT([128, 4, 512], BF16, "a_ybst") for _ in range(2)])
    junk = C.sbT([128, 512], BF16, "a_junk")
    pss = cst["ps_small"]
    i_eng = 0
    for m in range(NTOKW // 512):
        tok0 = m * 512
        own = tok0 >= OWN0
        hT = hring.next()
        P.op("sp", lambda e, hT=hT, tok0=tok0: e.dma_start(out=hT[:, :, :], in_=hT_scr[:, :, tok0:tok0 + 512]),
             reads=[hT_T[m]], writes=[hT], dma=True)
        for c in range(6):
            ps = psum.next()
            for kc in range(8):
                P.op("pe", lambda e, ps=ps, kc=kc, c=c, hT=hT: e.matmul(
                    ps[:, :], lhsT=w1a[:, kc, c * 128:(c + 1) * 128], rhs=hT[:, kc, :], start=(kc == 0), stop=(kc == 7)),
                    reads=[w1a, hT], writes=[ps])
            if c % 2 == 0:
                P.op("act", lambda e, ps=ps, c=c: e.copy(out=xbc[:, c, 3:515], in_=ps[:, :]), reads=[ps, xbc], writes=[xbc])
            else:
                P.op("dve", lambda e, ps=ps, c=c: e.tensor_copy(out=xbc[:, c, 3:515], in_=ps[:, :]), reads=[ps, xbc], writes=[xbc])
        co = cring.next()
        for c in range(6):
            eng = "pool" if c in (1, 4) else "dve"
            acc = accr.next()
            P.op(eng, lambda e, acc=acc, c=c: e.tensor_scalar(
                out=acc[:, :], in0=xbc[:, c, 0:512], scalar1=cw[:, c, 0:1], scalar2=None, op0=ALU.mult),
                reads=[xbc, cw], writes=[acc])
            for k in range(1, 4):
                if eng == "dve":
                    P.op(eng, lambda e, acc=acc, c=c, k=k: e.scalar_tensor_tensor(
                        out=acc[:, :], in0=xbc[:, c, k:k + 512], scalar=cw[:, c, k:k + 1], in1=acc[:, :],
                        op0=ALU.mult, op1=ALU.add), reads=[xbc, cw, acc], writes=[acc])
                else:
                    tmpc = accr.next()
                    P.op(eng, lambda e, tmpc=tmpc, c=c, k=k: e.tensor_scalar(
                        out=tmpc[:, :], in0=xbc[:, c, k:k + 512], scalar1=cw[:, c, k:k + 1], scalar2=None, op0=ALU.mult),
                        reads=[xbc, cw], writes=[tmpc])
                    P.op(eng, lambda e, tmpc=tmpc, acc=acc: e.tensor_tensor(out=acc[:, :], in0=acc[:, :], in1=tmpc[:, :], op=ALU.add),
                         reads=[acc, tmpc], writes=[acc])
            P.op("act", lambda e, acc=acc, c=c, co=co: e.activation(
                out=co[:, c, :], in_=acc[:, :], func=AF.Silu, bias=cb[:, c:c + 1], scale=1.0), reads=[acc, cb, co], writes=[co])
        P.op("pool", lambda e: e.tensor_copy(out=xbc[:, :, 0:3], in_=xbc[:, :, 512:515]), reads=[xbc], writes=[xbc])
        yst = ybst.next() if own else None
        ctx = {}

        def pre(s, m=m, hT=hT, co=co, own=own):
            sub = slice(s * 128, (s + 1) * 128)
            tile_idx = m * 4 + s
            pdt = pss["dt%d" % s]
            for kc in range(8):
                P.op("pe", lambda e, kc=kc, hT=hT, sub=sub: e.matmul(
                    pdt[:, :], lhsT=hT[:, kc, sub], rhs=w1a[:, kc, 1280:1288], start=(kc == 0), stop=(kc == 7)),
                    reads=[w1a, hT], writes=[pdt])
            yield
            dtr, ax, ee, dt_, a_ = s8.next(), s8.next(), s8.next(), s8.next(), s8.next()
            P.op("dve", lambda e, dtr=dtr: e.tensor_tensor(out=dtr[:, :], in0=pdt[:, :], in1=dtb[:, :], op=ALU.add),
                 reads=[pdt, dtb], writes=[dtr])
            P.op("act", lambda e, dtr=dtr, ax=ax: e.activation(out=ax[:, :], in_=dtr[:, :], func=AF.Abs),
                 reads=[dtr], writes=[ax])
            P.op("act", lambda e, ax=ax, ee=ee: e.activation(out=ee[:, :], in_=ax[:, :], func=AF.Exp, scale=-1.0),
                 reads=[ax], writes=[ee])
            P.op("act", lambda e, ee=ee: e.activation(out=ee[:, :], in_=ee[:, :], func=AF.Ln, bias=1.0, scale=1.0),
                 reads=[ee], writes=[ee])
            P.op("dve", lambda e, dtr=dtr, ee=ee, dt_=dt_: e.scalar_tensor_tensor(
                out=dt_[:, :], in0=dtr[:, :], scalar=0.0, in1=ee[:, :], op0=ALU.max, op1=ALU.add),
                reads=[dtr, ee], writes=[dt_])
            P.op("dve", lambda e, dt_=dt_, tile_idx=tile_idx: e.tensor_scalar(
                out=dt_[:, :], in0=dt_[:, :], scalar1=tokmask[:, tile_idx:tile_idx + 1], scalar2=None, op0=ALU.mult),
                reads=[dt_, tokmask], writes=[dt_])
            P.op("dve", lambda e, dt_=dt_, a_=a_: e.tensor_tensor(out=a_[:, :], in0=dt_[:, :], in1=Abc[:, :], op=ALU.mult),
                 reads=[dt_, Abc], writes=[a_])
            a3 = a3r.next()
            ar1, ar2 = s8.next(), s8.next()
            P.op("act", lambda e, a_=a_, a3=a3: e.copy(out=a3[:, 0, :], in_=a_[:, :]), reads=[a_, a3], writes=[a3])
            P.op("dve", lambda e, a_=a_, a3=a3, ar1=ar1: e.tensor_tensor(out=ar1[:, :], in0=a_[:, :], in1=a3[:, 0, :], op=ALU.subtract),
                 reads=[a_, a3], writes=[ar1])
            P.op("act", lambda e, ar1=ar1, a3=a3: e.copy(out=a3[:, 1, :], in_=ar1[:, :]), reads=[ar1, a3], writes=[a3])
            P.op("dve", lambda e, ar1=ar1, a3=a3, ar2=ar2: e.tensor_tensor(out=ar2[:, :], in0=ar1[:, :], in1=a3[:, 1, :], op=ALU.subtract),
                 reads=[ar1, a3], writes=[ar2])
            P.op("act", lambda e, ar2=ar2, a3=a3: e.copy(out=a3[:, 2, :], in_=ar2[:, :]), reads=[ar2, a3], writes=[a3])
            yield
            pac = pss["acs%d" % s]
            for i3 in range(3):
                P.op("pe", lambda e, a3=a3, i3=i3: e.matmul(pac[:, 0:8], lhsT=U[:, :], rhs=a3[:, i3, :], start=(i3 == 0), stop=(i3 == 2)),
                     reads=[U, a3], writes=[pac])
            for i3 in range(3):
                P.op("pe", lambda e, a3=a3, i3=i3: e.matmul(pac[:, 8:16], lhsT=ones_b[:, :], rhs=a3[:, i3, :], start=(i3 == 0), stop=(i3 == 2)),
                     reads=[ones_b, a3], writes=[pac])
            yield
            acs, wst, wend, cdec = s8.next(), s8.next(), s8.next(), s8.next()
            P.op("act", lambda e, acs=acs: e.copy(out=acs[:, :], in_=pac[:, 0:8]), reads=[pac], writes=[acs])
            P.op("act", lambda e, wst=wst: e.activation(out=wst[:, :], in_=pac[:, 0:8], func=AF.Exp), reads=[pac], writes=[wst])
            P.op("act", lambda e, cdec=cdec: e.activation(out=cdec[:, :], in_=pac[:, 8:16], func=AF.Exp), reads=[pac], writes=[cdec])
            P.op("dve", lambda e, wend=wend, acs=acs: e.tensor_tensor(out=wend[:, :], in0=pac[:, 8:16], in1=acs[:, :], op=ALU.subtract),
                 reads=[pac, acs], writes=[wend])
            P.op("act", lambda e, wend=wend: e.activation(out=wend[:, :], in_=wend[:, :], func=AF.Exp), reads=[wend], writes=[wend])
            yield
            pxs = psum.next()
            pxb = pxs.ap.bitcast(BF16)
            for c in range(5):
                P.op("pe", lambda e, pxb=pxb, c=c, co=co, sub=sub: e.transpose(
                    pxb[:, c * 128:(c + 1) * 128], co[:, c, sub], ident_b[:, :]), reads=[co, ident_b], writes=[pxs])
            xs_tm, Btm, xdt, xdtw = b512.next(), b512.next(), b512.next(), b512.next()
            P.op("act", lambda e, pxb=pxb, xs_tm=xs_tm: e.copy(out=xs_tm[:, :], in_=pxb[:, 0:512]), reads=[pxs], writes=[xs_tm])
            P.op("act", lambda e, pxb=pxb, Btm=Btm: e.copy(out=Btm[:, 0:128], in_=pxb[:, 512:640]), reads=[pxs], writes=[Btm])
            P.op("pool", lambda e, xs_tm=xs_tm, xdt=xdt, dt_=dt_: e.tensor_tensor(
                out=xdt[:, :].rearrange("p (h d) -> p h d", h=8), in0=xs_tm[:, :].rearrange("p (h d) -> p h d", h=8),
                in1=dt_[:, :].unsqueeze(2).to_broadcast([128, 8, 64]), op=ALU.mult), reads=[xs_tm, dt_], writes=[xdt])
            P.op("pool", lambda e, xdt=xdt, xdtw=xdtw, wend=wend: e.tensor_tensor(
                out=xdtw[:, :].rearrange("p (h d) -> p h d", h=8), in0=xdt[:, :].rearrange("p (h d) -> p h d", h=8),
                in1=wend[:, :].unsqueeze(2).to_broadcast([128, 8, 64]), op=ALU.mult), reads=[xdt, wend], writes=[xdtw])
            yield
            if own:
                R, L, Mh, cbt = Rr.next(), Lr.next(), Mr.next(), cbm.next()
                for i3 in range(3):
                    P.op("dve" if i3 != 1 else "pool", lambda e, R=R, a3=a3, i3=i3: e.tensor_tensor(
                        out=R[:, i3, :, :], in0=U[:, :].unsqueeze(1).to_broadcast([128, 8, 128]),
                        in1=a3[:, i3, :].unsqueeze(2).to_broadcast([128, 8, 128]), op=ALU.mult), reads=[U, a3, R], writes=[R])
                for hh in range(2):
                    pD = psum.next()
                    for i3 in range(3):
                        P.op("pe", lambda e, pD=pD, R=R, hh=hh, i3=i3: e.matmul(
                            pD[:, :], lhsT=T1[:, :], rhs=R[:, i3, hh * 4:(hh + 1) * 4, :].rearrange("p h l -> p (h l)"),
                            start=(i3 == 0), stop=(i3 == 2)), reads=[T1, R], writes=[pD])
                    P.op("act", lambda e, pD=pD, L=L, hh=hh: e.activation(
                        out=L[:, hh * 4:(hh + 1) * 4, :].rearrange("p h l -> p (h l)"), in_=pD[:, :], func=AF.Exp),
                        reads=[pD, L], writes=[L])
                yield
                pcb = pss["cb%d" % s]
                P.op("pe", lambda e, co=co, sub=sub: e.matmul(
                    pcb[:, :], lhsT=co[:, 4, sub], rhs=co[:, 5, sub], start=True, stop=True), reads=[co], writes=[pcb])
                P.op("dve", lambda e, cbt=cbt: e.tensor_tensor(out=cbt[:, :], in0=pcb[:, :], in1=U[:, :], op=ALU.mult),
                     reads=[pcb, U], writes=[cbt])
                P.op("pool", lambda e, Mh=Mh, L=L, cbt=cbt: e.tensor_tensor(
                    out=Mh[:, :, :], in0=L[:, :, :], in1=cbt[:, :].unsqueeze(1).to_broadcast([128, 8, 128]), op=ALU.mult),
                    reads=[L, cbt], writes=[Mh])
                xsD = b512.next()
                P.op("pool", lambda e, xs_tm=xs_tm, xsD=xsD: e.tensor_tensor(
                    out=xsD[:, :].rearrange("p (h d) -> p h d", h=8), in0=xs_tm[:, :].rearrange("p (h d) -> p h d", h=8),
                    in1=dsk[:, :].unsqueeze(2).to_broadcast([128, 8, 64]), op=ALU.mult), reads=[xs_tm, dsk], writes=[xsD])
                yield
                pz = psum.next()
                for kc in range(8):
                    P.op("pe", lambda e, pz=pz, kc=kc, hT=hT, sub=sub: e.matmul(
                        pz[:, :], lhsT=hT[:, kc, sub], rhs=w1a[:, kc, 768:1280], start=(kc == 0), stop=(kc == 7)),
                        reads=[w1a, hT], writes=[pz])
                sz = szr.next()
                P.op("act", lambda e, pz=pz, sz=sz: e.activation(out=sz[:, :], in_=pz[:, :], func=AF.Silu), reads=[pz], writes=[sz])
            ctx[s] = dict(locals())
            yield

        def seq(s, m=m, hT=hT, co=co, own=own, yst=yst):
            L_ = ctx[s]
            sub = L_["sub"]
            wst, cdec, xdt, xdtw, Btm = L_["wst"], L_["cdec"], L_["xdt"], L_["xdtw"], L_["Btm"]
            if own:
                Mh, xsD, sz = L_["Mh"], L_["xsD"], L_["sz"]
                pyo, py = psum.next(), psum.next()
                P.op("pe", lambda e, pyo=pyo, co=co, sub=sub: e.matmul(
                    pyo[:, :], lhsT=co[:, 5, sub], rhs=Sbf[:, :], start=True, stop=True), reads=[co, Sbf], writes=[pyo])
                P.op("pe", lambda e, py=py, xsD=xsD: e.matmul(py[:, :], lhsT=ident_b[:, :], rhs=xsD[:, :], start=True, stop=False),
                     reads=[ident_b, xsD], writes=[py])
                for h in range(8):
                    P.op("pe", lambda e, py=py, Mh=Mh, xdt=xdt, h=h: e.matmul(
                        py[:, h * 64:(h + 1) * 64], lhsT=Mh[:, h, :], rhs=xdt[:, h * 64:(h + 1) * 64], start=False, stop=(h == 7)),
                        reads=[Mh, xdt], writes=[py])
                y1, y2, y3 = f512.next(), f512.next(), f512.next()
                P.op("dve", lambda e, pyo=pyo, y1=y1, wst=wst: e.tensor_tensor(
                    out=y1[:, :].rearrange("p (h d) -> p h d", h=8), in0=pyo[:, :].rearrange("p (h d) -> p h d", h=8),
                    in1=wst[:, :].unsqueeze(2).to_broadcast([128, 8, 64]), op=ALU.mult), reads=[pyo, wst], writes=[y1])
                P.op("dve", lambda e, y1=y1, y2=y2, py=py: e.tensor_tensor(out=y2[:, :], in0=y1[:, :], in1=py[:, :], op=ALU.add),
                     reads=[y1, py], writes=[y2])
                P.op("pool", lambda e, y2=y2, y3=y3, sz=sz: e.tensor_tensor(out=y3[:, :], in0=y2[:, :], in1=sz[:, :], op=ALU.mult),
                     reads=[y2, sz], writes=[y3])
                ss = s2.next()
                P.op("act", lambda e, y3=y3, ss=ss: e.activation(out=junk[:, :], in_=y3[:, :], func=AF.Square, accum_out=ss[:, 0:1]),
                     reads=[y3], writes=[junk, ss])
                P.op("act", lambda e, ss=ss: e.activation(out=ss[:, 1:2], in_=ss[:, 0:1], func=AF.Sqrt, scale=1.0 / 512.0, bias=EPS),
                     reads=[ss], writes=[ss])
                P.op("dve", lambda e, ss=ss: e.reciprocal(out=ss[:, 1:2], in_=ss[:, 1:2]), reads=[ss], writes=[ss])
                yn = b512.next()
                P.op("dve", lambda e, y3=y3, ss=ss, yn=yn: e.scalar_tensor_tensor(
                    out=yn[:, :], in0=y3[:, :], scalar=ss[:, 1:2], in1=sng[:, :], op0=ALU.mult, op1=ALU.mult),
                    reads=[y3, ss, sng], writes=[yn])
                pyt = psum.next()
                pytb = pyt.ap.bitcast(BF16)
                for c in range(4):
                    P.op("pe", lambda e, pytb=pytb, yn=yn, c=c: e.transpose(
                        pytb[:, c * 128:(c + 1) * 128], yn[:, c * 128:(c + 1) * 128], ident_b[:, :]),
                        reads=[yn, ident_b], writes=[pyt])
                P.op("act", lambda e, pytb=pytb, yst=yst, sub=sub: e.copy(
                    out=yst[:, :, sub], in_=pytb[:, 0:512].rearrange("p (c n) -> p c n", c=4)), reads=[pyt, yst], writes=[yst])
            pst = psum.next()
            P.op("pe", lambda e, pst=pst, Btm=Btm, xdtw=xdtw: e.matmul(
                pst[:, :], lhsT=Btm[:, 0:128], rhs=xdtw[:, :], start=True, stop=True), reads=[Btm, xdtw], writes=[pst])
            P.op("pool", lambda e, cdec=cdec: e.tensor_tensor(
                out=S[:, :].rearrange("p (h d) -> p h d", h=8), in0=S[:, :].rearrange("p (h d) -> p h d", h=8),
                in1=cdec[:, :].unsqueeze(2).to_broadcast([128, 8, 64]), op=ALU.mult), reads=[S, cdec], writes=[S])
            P.op("dve", lambda e, pst=pst: e.tensor_tensor(out=S[:, :], in0=S[:, :], in1=pst[:, :], op=ALU.add),
                 reads=[S, pst], writes=[S])
            P.op("act", lambda e: e.copy(out=Sbf[:, :], in_=S[:, :]), reads=[S, Sbf], writes=[Sbf])

        gens = [pre(s) for s in range(4)]
        while gens:
            for g_ in list(gens):
                try:
                    next(g_)
                except StopIteration:
                    gens.remove(g_)
        for s in range(4):
            seq(s)
        if own:
            o0 = tok0 - OWN0
            P.op("pool", lambda e, yst=yst, o0=o0: e.dma_start(
                out=e_yb[g, :, :, o0:o0 + 512].rearrange("c p n -> p c n"), in_=yst[:, :, :]),
                reads=[yst], writes=[e_yb_T], dma=True)


def build_p1_init(P, C, din, cst, NTOKW):
    KT = C.sb([96, 4, NTOKW], BF16, "KT")
    VA = C.sb([128, NTOKW // 128, 2, 3, 64], BF16, "VA")
    kmT = C.sbT([64, 4, 32], BF16, "kmT")
    Mpad = [C.sbT([128, 4, 96], BF16, "Mpad") for _ in range(2)]
    for h in range(4):
        P.op("sp", lambda e, h=h: e.dma_start(out=KT[64:96, h, :], in_=din["kind"]), dma=True)
    P.op("pool", lambda e: e.memset(VA[:, :, :, 1, :], 1.0))
    for mp in Mpad:
        P.op("pool", lambda e, mp=mp: e.memset(mp[:, :, :], 0.0), writes=[mp])
    P.op("pool", lambda e: e.memset(kmT[:, :, :], 0.0), writes=[kmT])
    G = C.sbT([128, 512], F32, "G")
    gq, gk = cst["gq"], cst["gk"]
    for h in range(4):
        P.op("dve", lambda e, h=h: e.tensor_scalar(out=G[:, h * 64:(h + 1) * 64], in0=gq[:, :], scalar1=0.125, scalar2=None,
                                                   op0=ALU.mult), reads=[gq, G], writes=[G])
        P.op("dve", lambda e, h=h: e.tensor_copy(out=G[:, 256 + h * 64:256 + (h + 1) * 64], in_=gk[:, :]), reads=[gk, G], writes=[G])
    bb4 = C.sbT([128, 128], F32, "bb4")
    for h in range(4):
        P.op("dve", lambda e, h=h: e.tensor_copy(out=bb4[:, h * 32:(h + 1) * 32], in_=cst["blkbias"][:, :]),
             reads=[cst["blkbias"], bb4], writes=[bb4])
    P.barrier()
    nm = NTOKW // 512
    return dict(KT=KT, VA=VA, kmT=kmT, Mpad=Ring(Mpad), G=G, bb4=bb4,
                KT_T=[T(None, "KT%d" % i) for i in range(nm)], VA_T=[T(None, "VA%d" % i) for i in range(nm)])


def build_p1b(P, C, din, g, cst, A, hT_scr, hT_T, e_att, e_att_T, NTOKW, OWN0):
    psum = cst["psring"]
    po_ring = cst["po_ring"]
    pss = cst["ps_small"]
    ident_b, negm = cst["ident_b"], cst["negm"]
    KT, VA, kmT, G, bb4 = A["KT"], A["VA"], A["kmT"], A["G"], A["bb4"]
    KT_T, VA_T = A["KT_T"], A["VA_T"]
    stg = Ring([C.sbT([128, 8, WB_COLS], F32, "b_stg") for _ in range(2)])
    w1b = C.sbT([128, 8, W1B], BF16, "w1b")
    load_cast_weight(P, C, din["w1b"][g], w1b, W1B, cst["g1"], stg)
    hring = Ring([C.sbT([128, 8, 512], BF16, "b_hT") for _ in range(2)])
    f512 = Ring([C.sbT([128, 512], F32, "b_f512") for _ in range(4)])
    b512 = Ring([C.sbT([128, 512], BF16, "b_b512") for _ in range(3)])
    ptr = Ring([C.sbT([128, 512], BF16, "b_pt") for _ in range(4)])
    s8 = Ring([C.sbT([128, 8], F32, "b_s8") for _ in range(6)])
    g128 = Ring([C.sbT([128, 128], F32, "b_g128") for _ in range(6)])
    t8r = Ring([C.sbT([128, 32], F32, "b_t8") for _ in range(2)])
    kmf = C.sbT([64, 4, 2], F32, "b_kmf")
    QTr = Ring([C.sbT([96, 4, 512], BF16, "b_QT") for _ in range(2)])
    ast = [Ring([C.sbT([128, 512], BF16, "b_ast") for _ in range(2)]) for _ in range(2)]
    rdr = Ring([C.sbT([128, 512], F32, "b_rd") for _ in range(2)])
    outs = []
    for m in range(NTOKW // 512):
        tok0 = m * 512
        own = tok0 >= OWN0
        c0 = 0 if own else 256
        h0 = 0 if own else 4
        hT = hring.next()
        P.op("sp", lambda e, hT=hT, tok0=tok0: e.dma_start(out=hT[:, :, :], in_=hT_scr[:, :, tok0:tok0 + 512]),
             reads=[hT_T[m]], writes=[hT], dma=True)
        QT = QTr.next() if own else None
        for s in range(4):
            sub = slice(s * 128, (s + 1) * 128)
            kt = m * 4 + s
            pqk, pv = psum.next(), psum.next()
            for kc in range(8):
                P.op("pe", lambda e, pqk=pqk, kc=kc, hT=hT, sub=sub, c0=c0: e.matmul(
                    pqk[:, c0:512], lhsT=hT[:, kc, sub], rhs=w1b[:, kc, c0:512], start=(kc == 0), stop=(kc == 7)),
                    reads=[w1b, hT], writes=[pqk])
            for kc in range(8):
                P.op("pe", lambda e, pv=pv, kc=kc, hT=hT, sub=sub: e.matmul(
                    pv[:, 0:256], lhsT=hT[:, kc, sub], rhs=w1b[:, kc, 512:768], start=(kc == 0), stop=(kc == 7)),
                    reads=[w1b, hT], writes=[pv])
            sq, ssum, tt = f512.next(), s8.next(), f512.next()
            P.op("act", lambda e, pqk=pqk, sq=sq, c0=c0: e.activation(out=sq[:, c0:512], in_=pqk[:, c0:512], func=AF.Square),
                 reads=[pqk], writes=[sq])
            P.op("dve", lambda e, sq=sq, ssum=ssum, c0=c0, h0=h0: e.tensor_reduce(
                out=ssum[:, h0:8], in_=sq[:, c0:512].rearrange("p (h d) -> p h d", d=64), axis=AX.X, op=ALU.add),
                reads=[sq], writes=[ssum])
            P.op("act", lambda e, ssum=ssum, h0=h0: e.activation(
                out=ssum[:, h0:8], in_=ssum[:, h0:8], func=AF.Sqrt, scale=1.0 / 64.0, bias=EPS), reads=[ssum], writes=[ssum])
            P.op("dve", lambda e, ssum=ssum, h0=h0: e.reciprocal(out=ssum[:, h0:8], in_=ssum[:, h0:8]), reads=[ssum], writes=[ssum])
            P.op("dve", lambda e, pqk=pqk, tt=tt, ssum=ssum, c0=c0, h0=h0: e.tensor_tensor(
                out=tt[:, c0:512].rearrange("p (h d) -> p h d", d=64), in0=pqk[:, c0:512].rearrange("p (h d) -> p h d", d=64),
                in1=ssum[:, h0:8].unsqueeze(2).to_broadcast([128, 8 - h0, 64]), op=ALU.mult), reads=[pqk, ssum], writes=[tt])
            qkn = b512.next()
            P.op("pool", lambda e, tt=tt, qkn=qkn, c0=c0: e.tensor_tensor(
                out=qkn[:, c0:512], in0=tt[:, c0:512], in1=G[:, c0:512], op=ALU.mult), reads=[tt, G], writes=[qkn])
            pkt = psum.next()
            pktb = pkt.ap.bitcast(BF16)
            for h in range(4):
                P.op("pe", lambda e, pktb=pktb, qkn=qkn, h=h: e.transpose(
                    pktb[0:64, h * 128:(h + 1) * 128], qkn[:, 256 + h * 64:256 + (h + 1) * 64], ident_b[:, :]),
                    reads=[qkn, ident_b], writes=[pkt])
            P.op("act", lambda e, pktb=pktb, tok0=tok0, s=s: e.copy(
                out=KT[0:64, :, tok0 + s * 128:tok0 + (s + 1) * 128], in_=pktb[0:64, 0:512].rearrange("p (h n) -> p h n", h=4)),
                reads=[pkt, KT_T[m]], writes=[KT_T[m]])
            P.op("act", lambda e, pv=pv, kt=kt: e.copy(
                out=VA[:, kt, :, 0, :], in_=pv[:, 0:256].rearrange("p (a b d) -> p a b d", a=2, b=2)[:, :, 0, :]),
                reads=[pv, VA_T[m]], writes=[VA_T[m]])
            P.op("dve", lambda e, pv=pv, kt=kt: e.tensor_copy(
                out=VA[:, kt, :, 2, :], in_=pv[:, 0:256].rearrange("p (a b d) -> p a b d", a=2, b=2)[:, :, 1, :]),
                reads=[pv, VA_T[m]], writes=[VA_T[m]])
            if own:
                pqt = psum.next()
                pqtb = pqt.ap.bitcast(BF16)
                for h in range(4):
                    P.op("pe", lambda e, pqtb=pqtb, qkn=qkn, h=h: e.transpose(
                        pqtb[0:64, h * 128:(h + 1) * 128], qkn[:, h * 64:(h + 1) * 64], ident_b[:, :]),
                        reads=[qkn, ident_b], writes=[pqt])
                P.op("dve", lambda e, pqtb=pqtb, QT=QT, sub=sub: e.tensor_copy(
                    out=QT[0:64, :, sub], in_=pqtb[0:64, 0:512].rearrange("p (h n) -> p h n", h=4)),
                    reads=[pqt, QT], writes=[QT])
        P.op("dve", lambda e, tok0=tok0: e.tensor_reduce(
            out=kmf[:, :, :], in_=KT[0:64, :, tok0:tok0 + 512].rearrange("p h (b k) -> p h b k", b=2), axis=AX.X, op=ALU.add),
            reads=[KT_T[m]], writes=[kmf])
        P.op("dve", lambda e, m=m: e.tensor_scalar(out=kmT[:, :, 2 * m:2 * m + 2], in0=kmf[:, :, :], scalar1=1.0 / 256.0,
                                                   scalar2=None, op0=ALU.mult), reads=[kmf, kmT], writes=[kmT])
        if not own:
            continue
        for s in range(4):
            sub = slice(s * 128, (s + 1) * 128)
            ownblk = 2 * m + s // 2
            pg = pss["gate"]
            for h in range(4):
                P.op("pe", lambda e, h=h, QT=QT, sub=sub: e.matmul(
                    pg[:, h * 32:(h + 1) * 32], lhsT=QT[0:64, h, sub], rhs=kmT[0:64, h, :], start=True, stop=True),
                    reads=[QT, kmT], writes=[pg])
            gm, m1, m2, t8 = g128.next(), g128.next(), g128.next(), t8r.next()
            P.op("dve", lambda e, gm=gm: e.tensor_tensor(out=gm[:, :], in0=pg[:, :], in1=bb4[:, :], op=ALU.add),
                 reads=[pg, bb4], writes=[gm])
            P.op("pool", lambda e, gm=gm, ownblk=ownblk: e.memset(
                gm[:, :].rearrange("p (h b) -> p h b", h=4)[:, :, ownblk:32], NEG), reads=[gm], writes=[gm])
            for h in range(4):
                P.op("dve", lambda e, gm=gm, t8=t8, h=h: e.max(out=t8[:, h * 8:(h + 1) * 8], in_=gm[:, h * 32:(h + 1) * 32]),
                     reads=[gm, t8], writes=[t8])
            P.op("dve", lambda e, gm=gm, m1=m1, t8=t8: e.tensor_tensor(
                out=m1[:, :].rearrange("p (h b) -> p h b", h=4), in0=gm[:, :].rearrange("p (h b) -> p h b", h=4),
                in1=t8[:, :].rearrange("p (h k) -> p h k", h=4)[:, :, 2:3].to_broadcast([128, 4, 32]), op=ALU.is_lt),
                reads=[gm, t8], writes=[m1])
            P.op("dve", lambda e, gm=gm, m2=m2: e.tensor_scalar(
                out=m2[:, :], in0=gm[:, :], scalar1=NEG / 2, scalar2=NEG, op0=ALU.is_lt, op1=ALU.mult), reads=[gm], writes=[m2])
            Mp = A["Mpad"].next()
            P.op("dve", lambda e, Mp=Mp, m1=m1, m2=m2: e.scalar_tensor_tensor(
                out=Mp[:, :, 64:96], in0=m1[:, :].rearrange("p (h b) -> p h b", h=4), scalar=NEG,
                in1=m2[:, :].rearrange("p (h b) -> p h b", h=4), op0=ALU.mult, op1=ALU.min), reads=[m1, m2, Mp], writes=[Mp])
            P.op("pool", lambda e, Mp=Mp, ownblk=ownblk: e.memset(Mp[:, :, 64 + ownblk:65 + ownblk], 0.0), reads=[Mp], writes=[Mp])
            pmt = psum.next()
            pmtb = pmt.ap.bitcast(BF16)
            for h in range(4):
                P.op("pe", lambda e, pmtb=pmtb, Mp=Mp, h=h: e.transpose(
                    pmtb[0:96, h * 128:(h + 1) * 128], Mp[:, h, :], ident_b[:, :]), reads=[Mp, ident_b], writes=[pmt])
            P.op("act", lambda e, pmtb=pmtb, QT=QT, sub=sub: e.copy(
                out=QT[64:96, :, sub], in_=pmtb[64:96, 0:512].rearrange("p (h n) -> p h n", h=4)), reads=[pmt, QT], writes=[QT])
        nkt = (2 * m + 2) * 2
        o0 = tok0 - OWN0
        for h in range(4):
            pair, hb = h // 2, h % 2
            po = po_ring.next()
            def tile_cols(kt):
                blk = kt // 2
                if blk < 2 * m:
                    return 0, 512, None
                if blk == 2 * m:
                    return 0, 512, 0
                return 256, 512, 256

            def emit_s(kt):
                a0, a1, cz = tile_cols(kt)
                mk = kt // 4
                ps = psum.next()
                P.op("pe", lambda e, ps=ps, h=h, kt=kt, QT=QT, a0=a0, a1=a1, cz=cz: e.matmul(
                    ps[:, a0:a1], lhsT=KT[0:96, h, kt * 128:(kt + 1) * 128], rhs=QT[0:96, h, a0:a1],
                    start=True, stop=(cz is None)), reads=[KT_T[mk], QT], writes=[ps])
                if cz is not None:
                    P.op("pe", lambda e, ps=ps, kt=kt, cz=cz: e.matmul(
                        ps[:, cz:cz + 256], lhsT=ident_b[:, :], rhs=negm[:, kt % 2, :], start=False, stop=True),
                        reads=[ident_b, negm], writes=[ps])
                return ps

            def emit_pv(kt, ps):
                a0, a1, cz = tile_cols(kt)
                mk = kt // 4
                pt = ptr.next()
                P.op("act", lambda e, ps=ps, pt=pt, a0=a0, a1=a1: e.activation(out=pt[:, a0:a1], in_=ps[:, a0:a1], func=AF.Exp),
                     reads=[ps], writes=[pt])
                P.op("pe", lambda e, po=po, pt=pt, kt=kt, pair=pair, hb=hb, a0=a0, a1=a1, nkt=nkt: e.matmul(
                    po[:, a0:a1], lhsT=VA[:, kt, pair, hb:hb + 2, :].rearrange("p a d -> p (a d)"), rhs=pt[:, a0:a1],
                    start=(kt == 0), stop=(kt == nkt - 1), skip_group_check=True), reads=[VA_T[mk], pt], writes=[po])

            LOOK = 2
            pend = []
            for kt in range(nkt):
                pend.append((kt, emit_s(kt)))
                if len(pend) > LOOK:
                    emit_pv(*pend.pop(0))
            while pend:
                emit_pv(*pend.pop(0))
            nr = slice(0, 64) if hb == 0 else slice(64, 128)
            dr = slice(64, 128) if hb == 0 else slice(0, 64)
            rd = rdr.next()
            if hb == 0:
                at_ = ast[pair].next()
                ast_cur = at_
            else:
                at_ = ast_cur
            P.op("dve", lambda e, po=po, rd=rd, nr=nr, dr=dr: e.reciprocal(out=rd[nr, :], in_=po[dr, :]), reads=[po], writes=[rd])
            P.op("dve", lambda e, po=po, rd=rd, nr=nr, at_=at_: e.tensor_tensor(
                out=at_[nr, :], in0=po[nr, :], in1=rd[nr, :], op=ALU.mult), reads=[po, rd, at_], writes=[at_])
            if hb == 1:
                outs.append(P.op("pool", lambda e, at_=at_, pair=pair, o0=o0: e.dma_start(
                    out=e_att[g, pair, :, o0:o0 + 512], in_=at_[:, :]), reads=[at_], writes=[e_att_T], dma=True))
    return outs


def load_consts(P, C, din, names_shapes):
    out = {}
    for name, shape, dt in names_shapes:
        t = C.sbT(shape, dt, name)
        P.op("sp", lambda e, t=t, name=name: e.dma_start(out=t.ap, in_=din[name]), writes=[t], dma=True)
        out[name] = t
    return out


def build_program(mode, NTOKW=8192, OWN0=0, NG=1):
    nc = bass.Bass("TRN2", target_bir_lowering=False)
    P = Prog(nc)
    with ExitStack() as es:
        C = Ctx(nc, es, P)
        din = {}

        def inp(name, shape, dt=F32):
            din[name] = C.dram(name, shape, dt, "ExternalInput")

        psum = [T(es.enter_context(nc.psum_tensor("ps%d" % i, [128, 512], F32))[:, :], "ps%d" % i, excl=True) for i in range(8)]
        final = []
        NOWN = NTOKW - OWN0
        if mode in ("p1", "fused"):
            inp("xw", [NTOKW, 1024])
            inp("w1a", [NG, 1024, W1A])
            inp("w1b", [NG, 1024, W1B])
            inp("convw", [NG, 128, 6, 4])
            inp("convb", [NG, 128, 6])
            for nm in ("dtb", "alog", "dsk"):
                inp(nm, [NG, 128, 8])
            inp("sng", [NG, 128, 512])
            inp("kind", [32, NTOKW], BF16)
            shapes1 = [("gq", [128, 64], F32), ("gk", [128, 64], F32), ("g1", [128, 8], F32),
                       ("tokmask", [128, NTOKW // 128], F32), ("blkbias", [128, 32], F32),
                       ("ident_b", [128, 128], BF16), ("ones_b", [128, 128], BF16), ("U", [128, 128], BF16),
                       ("T1", [128, 128], BF16), ("negm", [128, 2, 256], BF16)]
            for nm, shp, dt in shapes1:
                if nm not in din:
                    inp(nm, shp, dt)
            cst = load_consts(P, C, din, shapes1)
            cst["psring"] = Ring(psum[0:5])
            cst["po_ring"] = Ring(psum[5:7])
            cst["ps_small"] = {"gate": TV(psum[7], psum[7].ap[:, 0:128])}
            for s_ in range(4):
                cst["ps_small"]["dt%d" % s_] = TV(psum[5], psum[5].ap[:, 8 * s_:8 * s_ + 8])
                cst["ps_small"]["acs%d" % s_] = TV(psum[5], psum[5].ap[:, 64 + 16 * s_:64 + 16 * s_ + 16])
                cst["ps_small"]["cb%d" % s_] = TV(psum[6], psum[6].ap[:, 128 * s_:128 * s_ + 128])
            kind_e = "ExternalOutput" if mode == "p1" else "Internal"
            e_att = C.dram("e_att", [NG, 2, 128, NOWN], BF16, kind_e)
            e_yb = C.dram("e_yb", [NG, 4, 128, NOWN], BF16, kind_e)
            e_att_T, e_yb_T = T(None, "e_att"), T(None, "e_yb")
            hT_scr = C.dram("hT_scr", [128, 8, NTOKW], BF16, "Internal")
            hT_T = []
            with ExitStack() as es1:
                C1 = Ctx(nc, es1, P)
                if STAGES.get("pro", True):
                    build_prologue(P, C1, din, cst, hT_scr, hT_T, NTOKW)
            P.barrier()
            with ExitStack() as es1:
                C1 = Ctx(nc, es1, P)
                for g in range(NG):
                    if STAGES.get("a", True):
                        with ExitStack() as es2:
                            build_p1a(P, Ctx(nc, es2, P), din, g, cst, hT_scr, hT_T, e_yb, e_yb_T, NTOKW, OWN0)
                        P.barrier()
                    if STAGES.get("b", True):
                        with ExitStack() as es2:
                            C2b = Ctx(nc, es2, P)
                            A = build_p1_init(P, C2b, din, cst, NTOKW)
                            build_p1b(P, C2b, din, g, cst, A, hT_scr, hT_T, e_att, e_att_T, NTOKW, OWN0)
                        P.barrier()
            if mode == "p1":
                final = [o for o in P.ops["pool"] if o.dma][-8:]
        if mode in ("p2", "fused"):
            inp("x2", [2048, 1024])
            inp("p2", [2048, 256])
            for name, (K, N) in P2W.items():
                inp(name, [K, N])
            shapes2 = [("g1", [128, 8], F32), ("g2", [128, 8], F32), ("g3", [128, 8], F32),
                       ("ident_b", [128, 128], BF16), ("ones_b", [128, 128], BF16)]
            for nm, shp, dt in shapes2:
                if nm not in din:
                    inp(nm, shp, dt)
            consts = load_consts(P, C, din, shapes2)
            consts["psum"] = psum
            wscr = {name: C.dram("scr_" + name, [K, N], BF16, "Internal") for name, (K, N) in P2W.items()}
            wT = {}
            with ExitStack() as es2:
                C2 = Ctx(nc, es2, P)
                if STAGES.get("precast", True):
                    build_precast(P, C2, din, wscr, wT, consts)
            P.barrier()
            if mode == "p2":
                inp("e_att", [4, 2, 128, 2048], BF16)
                inp("e_yb", [4, 4, 128, 2048], BF16)
                exch = {"att": din["e_att"], "yb": din["e_yb"], "att_T": T(None), "yb_T": T(None)}
            else:
                exch = {"att": e_att, "yb": e_yb, "att_T": e_att_T, "yb_T": e_yb_T}
            dout = C.dram("out", [2048, 1024], F32, "ExternalOutput")
            if STAGES.get("p2", True):
                with ExitStack() as es3:
                    C3 = Ctx(nc, es3, P)
                    final = build_phase2(P, C3, din, wscr, wT, consts, exch, dout, NTOK=STAGES.get('ntok', 2048))
            else:
                final = [o for o in P.ops["pool"] if o.dma][-8:]
        P.emit(final)
    return nc


BF = ml_dtypes.bfloat16


def host_consts():
    return {
        "ident_b": np.eye(128, dtype=np.float32).astype(BF),
        "ones_b": np.ones((128, 128), dtype=np.float32).astype(BF),
    }


def host_consts1(NTOKW):
    i = np.arange(128)
    U = (i[:, None] <= i[None, :]).astype(np.float32)
    T1 = (i[:, None] > i[None, :]).astype(np.float32)
    q = np.arange(256)
    negm = np.stack([np.where((kt * 128 + i[:, None]) <= q[None, :], 0.0, NEG) for kt in range(2)], 1).astype(np.float32)
    kind = (np.arange(NTOKW)[None, :] // 256 == np.arange(32)[:, None]).astype(np.float32)
    return {"ident_b": np.eye(128, dtype=np.float32).astype(BF), "ones_b": np.ones((128, 128), np.float32).astype(BF),
            "U": U.astype(BF), "T1": T1.astype(BF), "negm": negm.astype(BF), "kind": kind.astype(BF)}


def gain_layout(g):
    return np.ascontiguousarray(g.reshape(8, 128).T)


def bc(v):
    return np.ascontiguousarray(np.broadcast_to(v[None, :], (128, v.shape[0]))).astype(np.float32)


def p1_inputs(inputs, b, groups, xw, tokmask, blkbias, NTOKW):
    w_in = inputs["w_in"][0]
    cw, cbias = inputs["conv_w"][0], inputs["conv_b"][0]
    w1a, w1b, convw, convb, dtb, alog, dsk, sng = [], [], [], [], [], [], [], []
    for g in groups:
        cols_a = np.concatenate([np.arange(5120 + 512 * g, 5120 + 512 * (g + 1)), np.arange(7168 + 128 * g, 7168 + 128 * (g + 1)),
                                 np.arange(7680 + 128 * g, 7680 + 128 * (g + 1)), np.arange(3072 + 512 * g, 3072 + 512 * (g + 1)),
                                 np.arange(8192 + 8 * g, 8192 + 8 * (g + 1))])
        cols_b = np.concatenate([np.arange(256 * g, 256 * (g + 1)), np.arange(1024 + 256 * g, 1024 + 256 * (g + 1)),
                                 np.arange(2048 + 256 * g, 2048 + 256 * (g + 1))])
        w1a.append(w_in[:, cols_a])
        w1b.append(w_in[:, cols_b])
        ch = np.concatenate([np.arange(512 * g, 512 * (g + 1)), np.arange(2048 + 128 * g, 2048 + 128 * (g + 1)),
                             np.arange(2560 + 128 * g, 2560 + 128 * (g + 1))])
        convw.append(cw[:, ch].T.reshape(6, 128, 4).transpose(1, 0, 2))
        convb.append(cbias[ch].reshape(6, 128).T)
        dtb.append(bc(inputs["dt_bias"][0][8 * g:8 * g + 8]))
        alog.append(bc(inputs["a_log"][0][8 * g:8 * g + 8]))
        dsk.append(bc(inputs["d_skip"][0][8 * g:8 * g + 8]))
        sng.append(bc(inputs["ssm_norm_g"][0][512 * g:512 * g + 512]))
    m = {"xw": np.ascontiguousarray(xw), "w1a": np.ascontiguousarray(np.stack(w1a)), "w1b": np.ascontiguousarray(np.stack(w1b)),
         "convw": np.ascontiguousarray(np.stack(convw)), "convb": np.ascontiguousarray(np.stack(convb)),
         "dtb": np.stack(dtb), "alog": np.stack(alog), "dsk": np.stack(dsk), "sng": np.stack(sng),
         "gq": bc(inputs["q_norm_g"][0]), "gk": bc(inputs["k_norm_g"][0]), "g1": gain_layout(inputs["ln1_g"][0]),
         "tokmask": np.ascontiguousarray(tokmask.reshape(-1, 128).T).astype(np.float32), "blkbias": bc(blkbias)}
    m.update(host_consts1(NTOKW))
    return m


def p2_inputs(inputs, core, e_att=None, e_yb=None):
    b, t = core // 4, core % 4
    sl = slice(t * 2048, (t + 1) * 2048)
    m = {
        "x2": np.ascontiguousarray(inputs["x"][b, sl]),
        "p2": np.ascontiguousarray(inputs["p"][0, b, sl]),
        "wg": np.ascontiguousarray(inputs["w_in"][0][:, 8224:10272]),
        "woa": inputs["w_o_attn"][0], "wos": inputs["w_o_ssm"][0], "wout": inputs["w_out"][0],
        "wgu": inputs["w_gate_up"][0], "wd": inputs["w_down"][0], "wpg": inputs["w_ple_gate"][0],
        "wpp": inputs["w_ple_proj"][0],
        "g1": gain_layout(inputs["ln1_g"][0]), "g2": gain_layout(inputs["ln2_g"][0]),
        "g3": gain_layout(inputs["ln3_g"][0]),
    }
    m.update(host_consts())
    if e_att is not None:
        m["e_att"] = e_att
        m["e_yb"] = e_yb
    return m


MODE = "fused"


def kernel(**inputs):
    inputs = {k: np.asarray(v) for k, v in inputs.items()}
    x = inputs["x"]
    out = np.zeros(x.shape, np.float32)
    if MODE == "two":
        nc1 = build_program("p1", NTOKW=8192, OWN0=0, NG=1)
        maps1 = []
        for core in range(8):
            b, g = core // 4, core % 4
            maps1.append(p1_inputs(inputs, b, [g], x[b], np.ones(8192, np.float32), np.zeros(32, np.float32), 8192))
        r1 = run_bass_kernel_spmd(nc1, maps1, core_ids=list(range(8))).results
        nc2 = build_program("p2")
        maps2 = []
        for core in range(8):
            b, t = core // 4, core % 4
            sl = slice(t * 2048, (t + 1) * 2048)
            ea = np.stack([np.asarray(r1[b * 4 + g]["e_att"])[0][:, :, sl] for g in range(4)])
            ey = np.stack([np.asarray(r1[b * 4 + g]["e_yb"])[0][:, :, sl] for g in range(4)])
            maps2.append(p2_inputs(inputs, core, np.ascontiguousarray(ea), np.ascontiguousarray(ey)))
        r2 = run_bass_kernel_spmd(nc2, maps2, core_ids=list(range(8))).results
        for core in range(8):
            b, t = core // 4, core % 4
            out[b, t * 2048:(t + 1) * 2048] = np.asarray(r2[core]["out"])
        return out
    nc = build_program("fused", NTOKW=8192, OWN0=6144, NG=4)
    maps = []
    for core in range(8):
        b, t = core // 4, core % 4
        npad = (3 - t) * 2048
        xw = np.concatenate([np.zeros((npad, 1024), np.float32), x[b, :(t + 1) * 2048]], 0)
        tokmask = np.concatenate([np.zeros(npad, np.float32), np.ones(8192 - npad, np.float32)])
        blkbias = np.where(np.arange(32) < npad // 256, NEG, 0.0).astype(np.float32)
        m = p1_inputs(inputs, b, [0, 1, 2, 3], xw, tokmask, blkbias, 8192)
        m.update(p2_inputs(inputs, core))
        maps.append(m)
    r = run_bass_kernel_spmd(nc, maps, core_ids=list(range(8))).results
    for core in range(8):
        b, t = core // 4, core % 4
        out[b, t * 2048:(t + 1) * 2048] = np.asarray(r[core]["out"])
    return out
```

```python
import numpy as np
from contextlib import ExitStack
import ml_dtypes
import concourse.bass as bass
import concourse.mybir as mybir
from concourse.bass_utils import run_bass_kernel_spmd

F32 = mybir.dt.float32
BF16 = mybir.dt.bfloat16
AF = mybir.ActivationFunctionType
ALU = mybir.AluOpType
AX = mybir.AxisListType

EPS = 1e-6
NEG = -30000.0
SAME_ENGINE_SYNC = True
STAGES = {}


class T:
    __slots__ = ("ap", "w", "r", "name", "excl")

    def __init__(self, ap=None, name="", excl=False):
        self.ap = ap
        self.w = None
        self.r = []
        self.name = name
        self.excl = excl

    def __getitem__(self, k):
        return self.ap[k]


class TV(T):
    __slots__ = ("parent",)

    def __init__(self, parent, ap):
        self.parent = parent
        self.ap = ap
        self.name = parent.name
        self.excl = parent.excl

    @property
    def w(self):
        return self.parent.w

    @w.setter
    def w(self, v):
        self.parent.w = v

    @property
    def r(self):
        return self.parent.r

    @r.setter
    def r(self, v):
        self.parent.r = v


class Op:
    __slots__ = ("eng", "fn", "deps", "dma", "inc", "sem", "ticket", "idx", "prev_same_sem", "vc")

    def __init__(self, eng, fn, dma):
        self.eng = eng
        self.fn = fn
        self.dma = dma
        self.deps = []
        self.inc = False
        self.sem = None
        self.ticket = 0
        self.prev_same_sem = None


class Prog:
    ENGS = ("pe", "act", "dve", "pool", "sp")
    NDS = 8

    def __init__(self, nc):
        self.nc = nc
        self.ops = {e: [] for e in self.ENGS}
        self.all = []
        self.bar = {}

    def op(self, eng, fn, reads=(), writes=(), dma=False):
        o = Op(eng, fn, dma)
        deps = []
        for t in reads:
            if t.w is not None:
                deps.append(t.w)
            if t.excl:
                deps.extend(x for x in t.r if x.eng != eng)
        for t in writes:
            if t.w is not None:
                deps.append(t.w)
            deps.extend(t.r)
        b = self.bar.pop(eng, None)
        if b:
            deps.extend(b)
        seen = set()
        for d in deps:
            if d is o or id(d) in seen:
                continue
            seen.add(id(d))
            if (not d.dma) and d.eng == eng and (eng == "pe" or not SAME_ENGINE_SYNC):
                continue
            o.deps.append(d)
            d.inc = True
        for t in reads:
            if dma:
                t.r.append(o)
            else:
                t.r = [x for x in t.r if x.dma or x.eng != eng] + [o]
        for t in writes:
            t.w = o
            t.r = []
        if dma:
            o.inc = True
        self.ops[eng].append(o)
        self.all.append(o)
        return o

    def barrier(self):
        last = []
        for e in self.ENGS:
            ops = self.ops[e]
            if ops:
                last.append(ops[-1])
            last.extend([o for o in ops if o.dma][-self.NDS:])
        for e in self.ENGS:
            self.bar[e] = list(last)

    def emit(self, final_ops):
        nc = self.nc
        with ExitStack() as es:
            SEM_CAP = 1000
            nsem = {e: sum(1 for o in self.ops[e] if o.inc and not o.dma) // SEM_CAP + 1 for e in ("pe", "act", "dve", "pool")}
            csem = {e: [es.enter_context(nc.semaphore("cs_%s%d" % (e, i))) for i in range(nsem[e])]
                    for e in ("pe", "act", "dve", "pool")}
            dsem = {e: [es.enter_context(nc.semaphore("ds_%s%d" % (e, i))) for i in range(self.NDS)]
                    for e in self.ENGS}
            ccount = {e: 0 for e in self.ENGS}
            dcount = {e: [0] * self.NDS for e in self.ENGS}
            drr = {e: 0 for e in self.ENGS}
            dlast = {e: [None] * self.NDS for e in self.ENGS}
            for e in self.ENGS:
                for o in self.ops[e]:
                    if o.dma:
                        k = drr[e] % self.NDS
                        drr[e] += 1
                        dcount[e][k] += 16
                        o.sem = dsem[e][k]
                        o.ticket = dcount[e][k]
                        o.prev_same_sem = dlast[e][k]
                        dlast[e][k] = o
                    elif o.inc:
                        o.sem = csem[e][ccount[e] // SEM_CAP]
                        o.ticket = ccount[e] % SEM_CAP + 1
                        ccount[e] += 1

            know = {e: {} for e in self.ENGS}
            plan = {}
            for o in self.all:
                kn = know[o.eng]
                waits = []
                dl = list(o.deps)
                if o.dma and o.prev_same_sem is not None:
                    dl.append(o.prev_same_sem)
                for d in dl:
                    key = id(d.sem)
                    if kn.get(key, 0) < d.ticket:
                        waits.append(d)
                        for kk, vv in d.vc.items():
                            if kn.get(kk, 0) < vv:
                                kn[kk] = vv
                plan[id(o)] = waits
                o.vc = dict(kn)
                if o.sem is not None:
                    o.vc[id(o.sem)] = o.ticket
                    if not o.dma:
                        kn[id(o.sem)] = max(kn.get(id(o.sem), 0), 0)

            def run(ename, eng):
                for o in self.ops[ename]:
                    for d in plan[id(o)]:
                        eng.wait_ge(d.sem, d.ticket)
                    ins = o.fn(eng)
                    if o.sem is not None:
                        ins.then_inc(o.sem, 16 if o.dma else 1)
                if ename == "sp":
                    kn = know["sp"]
                    for d in final_ops:
                        if kn.get(id(d.sem), 0) < d.ticket:
                            eng.wait_ge(d.sem, d.ticket)
                            kn[id(d.sem)] = d.ticket

            with nc.Block() as block:
                @block.tensor
                def _(eng):
                    run("pe", eng)

                @block.scalar
                def _(eng):
                    run("act", eng)

                @block.vector
                def _(eng):
                    run("dve", eng)

                @block.gpsimd
                def _(eng):
                    run("pool", eng)

                @block.sync
                def _(eng):
                    run("sp", eng)


class Ctx:
    N = 0

    def __init__(self, nc, es, P):
        self.nc, self.es, self.P = nc, es, P
        self.n = 0

    def sb(self, shape, dt, name=None):
        Ctx.N += 1
        h = self.es.enter_context(self.nc.sbuf_tensor("%s_%d" % (name or "sb", Ctx.N), list(shape), dt))
        return h

    def sbT(self, shape, dt, name=None):
        h = self.sb(shape, dt, name)
        return T(h[tuple(slice(None) for _ in shape)], name or "")

    def dram(self, name, shape, dt, kind):
        return self.nc.dram_tensor(name, list(shape), dt, kind=kind).ap()


class Ring:
    def __init__(self, tiles):
        self.tiles = tiles
        self.i = 0

    def next(self):
        t = self.tiles[self.i % len(self.tiles)]
        self.i += 1
        return t


P2W = {
    "wg": (1024, 2048), "woa": (1024, 1024), "wos": (2048, 1024), "wout": (1024, 1024),
    "wgu": (1024, 5632), "wd": (2816, 1024), "wpg": (1024, 1024), "wpp": (256, 1024),
}
P2W_GAIN = {"wg": "g1", "wgu": "g2", "wpg": "g3"}
WB_COLS = 256


def wblocks(name):
    K, N = P2W[name]
    nkc = K // 128
    kbs = []
    k0 = 0
    while k0 < nkc:
        nk = min(8, nkc - k0)
        kbs.append((k0, nk))
        k0 += nk
    return kbs, N // WB_COLS


def build_precast(P, C, din, wscr, wT, consts):
    nc = P.nc
    stg = Ring([C.sbT([128, 8, WB_COLS], F32, "pc_stg") for _ in range(2)])
    wbf = Ring([C.sbT([128, 8, WB_COLS], BF16, "pc_bf") for _ in range(2)])
    for name in P2W:
        kbs, ncb = wblocks(name)
        src = din[name]
        dst = wscr[name]
        gain = consts.get(P2W_GAIN.get(name))
        for cb in range(ncb):
            for (k0, nk) in kbs:
                s, w = stg.next(), wbf.next()
                sv = src[k0 * 128:(k0 + nk) * 128, cb * WB_COLS:(cb + 1) * WB_COLS].rearrange("(k p) n -> p k n", p=128)
                dv = dst[k0 * 128:(k0 + nk) * 128, cb * WB_COLS:(cb + 1) * WB_COLS].rearrange("(k p) n -> p k n", p=128)
                P.op("sp", lambda e, s=s, sv=sv, nk=nk: e.dma_start(out=s[:, 0:nk, :], in_=sv), writes=[s], dma=True)
                if gain is not None:
                    gv = gain[:, k0:k0 + nk].unsqueeze(2).to_broadcast([128, nk, WB_COLS])
                    P.op("pool", lambda e, s=s, w=w, gv=gv, nk=nk: e.tensor_tensor(
                        out=w[:, 0:nk, :], in0=s[:, 0:nk, :], in1=gv, op=ALU.mult), reads=[s, gain], writes=[w])
                else:
                    P.op("pool", lambda e, s=s, w=w, nk=nk: e.tensor_copy(out=w[:, 0:nk, :], in_=s[:, 0:nk, :]),
                         reads=[s], writes=[w])
                t = T(None, "wscr")
                wT[(name, cb, k0)] = t
                P.op("pool", lambda e, w=w, dv=dv, nk=nk: e.dma_start(out=dv, in_=w[:, 0:nk, :]),
                     reads=[w], writes=[t], dma=True)


def build_phase2(P, C, din, wscr, wT, consts, exch, dout, NTOK=2048, TT=1024):
    nc = P.nc
    NTG = TT // 512
    NSUB = TT // 128
    ident_b = consts["ident_b"]
    ones_b = consts["ones_b"]
    xT = [C.sbT([128, TT], F32, "xT") for _ in range(8)]
    hT = [C.sbT([128, TT], BF16, "hT") for _ in range(8)]
    big = C.sb([128, 24, TT], BF16, "big")
    attT = [T(big[:, c, :], "att") for c in range(8)]
    ybT = [T(big[:, 8 + c, :], "yb") for c in range(16)]
    mg = [C.sbT([128, TT], BF16, "mg") for _ in range(8)]
    pT = [C.sbT([128, TT], BF16, "pT") for _ in range(2)]
    xin = Ring([C.sbT([128, 1024], F32, "xin") for _ in range(2)])
    xhi = Ring([C.sbT([128, 1024], BF16, "xhi") for _ in range(2)])
    xlo = Ring([C.sbT([128, 1024], BF16, "xlo") for _ in range(2)])
    xres = Ring([C.sbT([128, 1024], F32, "xres") for _ in range(1)])
    pin = Ring([C.sbT([128, 256], F32, "pin") for _ in range(2)])
    pinb = Ring([C.sbT([128, 256], BF16, "pinb") for _ in range(2)])
    wring = Ring([C.sbT([128, 8, WB_COLS], BF16, "wr") for _ in range(8)])
    tmpf = Ring([C.sbT([128, 512], F32, "tmpf") for _ in range(8)])
    sqb = Ring([C.sbT([128, 512], BF16, "sqb") for _ in range(3)])
    rstd = [C.sbT([128, 512], F32, "rstd") for _ in range(NTG)]
    psum = Ring(consts["psum"])

    def load_w(name, cb, k0, nk):
        w = wring.next()
        sv = wscr[name][k0 * 128:(k0 + nk) * 128, cb * WB_COLS:(cb + 1) * WB_COLS].rearrange("(k p) n -> p k n", p=128)
        P.op("sp", lambda e, w=w, sv=sv, nk=nk: e.dma_start(out=w[:, 0:nk, :], in_=sv),
             reads=[wT[(name, cb, k0)]], writes=[w], dma=True)
        return w

    def mm_group(ps, wlist, X, tg, oc_in_blk):
        n = sum(nk for _, _, nk in wlist)
        i = 0
        for (w, k0, nk) in wlist:
            for k in range(nk):
                st, sp_ = (i == 0), (i == n - 1)
                xk = X[k0 + k]
                P.op("pe", lambda e, ps=ps, w=w, k=k, xk=xk, st=st, sp_=sp_: e.matmul(
                    ps[:, :], lhsT=w[:, k, oc_in_blk * 128:(oc_in_blk + 1) * 128],
                    rhs=xk[:, tg * 512:(tg + 1) * 512], start=st, stop=sp_),
                    reads=[w, xk], writes=[ps])
                i += 1

    def rmsnorm():
        for tg in range(NTG):
            ps = psum.next()
            for kc in range(8):
                sq = sqb.next()
                P.op("act", lambda e, sq=sq, kc=kc, tg=tg: e.activation(
                    out=sq[:, :], in_=xT[kc][:, tg * 512:(tg + 1) * 512], func=AF.Square), reads=[xT[kc]], writes=[sq])
                P.op("pe", lambda e, ps=ps, sq=sq, kc=kc: e.matmul(
                    ps[:, :], lhsT=ones_b[:, :], rhs=sq[:, :], start=(kc == 0), stop=(kc == 7)),
                    reads=[sq, ones_b], writes=[ps])
            r = rstd[tg]
            P.op("act", lambda e, r=r, ps=ps: e.activation(
                out=r[:, :], in_=ps[:, :], func=AF.Sqrt, scale=1.0 / 1024.0, bias=EPS), reads=[ps], writes=[r])
            P.op("dve", lambda e, r=r: e.reciprocal(out=r[:, :], in_=r[:, :]), reads=[r], writes=[r])
        for kc in range(8):
            for tg in range(NTG):
                P.op("dve", lambda e, kc=kc, tg=tg: e.tensor_tensor(
                    out=hT[kc][:, tg * 512:(tg + 1) * 512], in0=xT[kc][:, tg * 512:(tg + 1) * 512],
                    in1=rstd[tg][:, :], op=ALU.mult), reads=[xT[kc], rstd[tg]], writes=[hT[kc]])

    out_ops = []
    for tt in range(NTOK // TT):
        t0 = tt * TT
        for s in range(NSUB):
            xi = xin.next()
            P.op("sp", lambda e, xi=xi, s=s, t0=t0: e.dma_start(out=xi[:, :], in_=din["x2"][t0 + s * 128:t0 + (s + 1) * 128, :]),
                 writes=[xi], dma=True)
            xh, xl, xr = xhi.next(), xlo.next(), xres.next()
            P.op("act", lambda e, xi=xi, xh=xh: e.copy(out=xh[:, :], in_=xi[:, :]), reads=[xi], writes=[xh])
            P.op("dve", lambda e, xi=xi, xh=xh, xr=xr: e.tensor_tensor(out=xr[:, :], in0=xi[:, :], in1=xh[:, :], op=ALU.subtract),
                 reads=[xi, xh], writes=[xr])
            P.op("pool", lambda e, xr=xr, xl=xl: e.tensor_copy(out=xl[:, :], in_=xr[:, :]), reads=[xr], writes=[xl])
            for half in range(2):
                ps = psum.next()
                for j in range(4):
                    kc = half * 4 + j
                    P.op("pe", lambda e, ps=ps, xh=xh, kc=kc, j=j: e.matmul(
                        ps[:, j * 128:(j + 1) * 128], lhsT=xh[:, kc * 128:(kc + 1) * 128], rhs=ident_b[:, :], start=True, stop=False),
                        reads=[xh, ident_b], writes=[ps])
                    P.op("pe", lambda e, ps=ps, xl=xl, kc=kc, j=j: e.matmul(
                        ps[:, j * 128:(j + 1) * 128], lhsT=xl[:, kc * 128:(kc + 1) * 128], rhs=ident_b[:, :], start=False, stop=True),
                        reads=[xl, ident_b], writes=[ps])
                for j in range(4):
                    kc = half * 4 + j
                    eng = "act" if j % 2 == 0 else "dve"
                    if eng == "act":
                        P.op("act", lambda e, ps=ps, kc=kc, j=j, s=s: e.copy(
                            out=xT[kc][:, s * 128:(s + 1) * 128], in_=ps[:, j * 128:(j + 1) * 128]),
                            reads=[ps], writes=[xT[kc]])
                    else:
                        P.op("dve", lambda e, ps=ps, kc=kc, j=j, s=s: e.tensor_copy(
                            out=xT[kc][:, s * 128:(s + 1) * 128], in_=ps[:, j * 128:(j + 1) * 128]),
                            reads=[ps], writes=[xT[kc]])
            pi, pb = pin.next(), pinb.next()
            P.op("sp", lambda e, pi=pi, s=s, t0=t0: e.dma_start(out=pi[:, :], in_=din["p2"][t0 + s * 128:t0 + (s + 1) * 128, :]),
                 writes=[pi], dma=True)
            P.op("pool", lambda e, pi=pi, pb=pb: e.tensor_copy(out=pb[:, :], in_=pi[:, :]), reads=[pi], writes=[pb])
            ps = psum.next()
            psb = ps.ap.bitcast(BF16)
            for j in range(2):
                P.op("pe", lambda e, psb=psb, pb=pb, j=j: e.transpose(
                    psb[:, j * 128:(j + 1) * 128], pb[:, j * 128:(j + 1) * 128], ident_b[:, :]),
                    reads=[pb, ident_b], writes=[ps])
            for j in range(2):
                P.op("act", lambda e, psb=psb, j=j, s=s: e.copy(
                    out=pT[j][:, s * 128:(s + 1) * 128], in_=psb[:, j * 128:(j + 1) * 128]), reads=[ps], writes=[pT[j]])
        for g in range(4):
            for c in range(2):
                tl = attT[2 * g + c]
                P.op("sp", lambda e, tl=tl, g=g, c=c, t0=t0: e.dma_start(out=tl[:, :], in_=exch["att"][g, c, :, t0:t0 + TT]),
                     reads=[exch["att_T"]], writes=[tl], dma=True)
            for c in range(4):
                tl = ybT[4 * g + c]
                P.op("sp", lambda e, tl=tl, g=g, c=c, t0=t0: e.dma_start(out=tl[:, :], in_=exch["yb"][g, c, :, t0:t0 + TT]),
                     reads=[exch["yb_T"]], writes=[tl], dma=True)
        p2s = STAGES.get('p2s', 'BXCD')
        if 'B' in p2s:
            rmsnorm()
        for j in (range(4) if 'B' in p2s else []):
            wga = load_w("wg", j, 0, 8)
            wgb = load_w("wg", 4 + j, 0, 8)
            wa = load_w("woa", j, 0, 8)
            ws0 = load_w("wos", j, 0, 8)
            ws1 = load_w("wos", j, 8, 8)
            for o2 in range(2):
                oc = 2 * j + o2
                for tg in range(NTG):
                    pga, pgb, pya, pyb = psum.next(), psum.next(), psum.next(), psum.next()
                    mm_group(pga, [(wga, 0, 8)], hT, tg, o2)
                    mm_group(pgb, [(wgb, 0, 8)], hT, tg, o2)
                    mm_group(pya, [(wa, 0, 8)], attT, tg, o2)
                    mm_group(pyb, [(ws0, 0, 8), (ws1, 8, 8)], ybT, tg, o2)
                    sa, sb_, m1, m2 = tmpf.next(), tmpf.next(), tmpf.next(), tmpf.next()
                    P.op("act", lambda e, sa=sa, pga=pga: e.activation(out=sa[:, :], in_=pga[:, :], func=AF.Sigmoid),
                         reads=[pga], writes=[sa])
                    P.op("act", lambda e, sb_=sb_, pgb=pgb: e.activation(out=sb_[:, :], in_=pgb[:, :], func=AF.Sigmoid),
                         reads=[pgb], writes=[sb_])
                    P.op("dve", lambda e, m1=m1, sa=sa, pya=pya: e.tensor_tensor(
                        out=m1[:, :], in0=sa[:, :], in1=pya[:, :], op=ALU.mult), reads=[sa, pya], writes=[m1])
                    P.op("dve", lambda e, m2=m2, sb_=sb_, pyb=pyb: e.tensor_tensor(
                        out=m2[:, :], in0=sb_[:, :], in1=pyb[:, :], op=ALU.mult), reads=[sb_, pyb], writes=[m2])
                    P.op("pool", lambda e, m1=m1, m2=m2, oc=oc, tg=tg: e.tensor_tensor(
                        out=mg[oc][:, tg * 512:(tg + 1) * 512], in0=m1[:, :], in1=m2[:, :], op=ALU.add),
                        reads=[m1, m2], writes=[mg[oc]])
        for j in (range(4) if 'X' in p2s else []):
            w = load_w("wout", j, 0, 8)
            for o2 in range(2):
                oc = 2 * j + o2
                for tg in range(NTG):
                    ps = psum.next()
                    mm_group(ps, [(w, 0, 8)], mg, tg, o2)
                    P.op("dve", lambda e, ps=ps, oc=oc, tg=tg: e.tensor_tensor(
                        out=xT[oc][:, tg * 512:(tg + 1) * 512], in0=ps[:, :], in1=xT[oc][:, tg * 512:(tg + 1) * 512],
                        op=ALU.add), reads=[ps, xT[oc]], writes=[xT[oc]])
        if 'C' in p2s:
            rmsnorm()
        actT = [T(big[:, f, :], "act") for f in range(22)]
        for f in range(22):
            old = attT[f] if f < 8 else ybT[f - 8]
            actT[f].w, actT[f].r = old.w, old.r
        for j in (range(11) if 'C' in p2s else []):
            wgt = load_w("wgu", j, 0, 8)
            wup = load_w("wgu", 11 + j, 0, 8)
            for o2 in range(2):
                f = 2 * j + o2
                for tg in range(NTG):
                    pg, pu = psum.next(), psum.next()
                    mm_group(pg, [(wgt, 0, 8)], hT, tg, o2)
                    mm_group(pu, [(wup, 0, 8)], hT, tg, o2)
                    sg = tmpf.next()
                    P.op("act", lambda e, sg=sg, pg=pg: e.activation(out=sg[:, :], in_=pg[:, :], func=AF.Silu),
                         reads=[pg], writes=[sg])
                    P.op("dve", lambda e, sg=sg, pu=pu, f=f, tg=tg: e.tensor_tensor(
                        out=actT[f][:, tg * 512:(tg + 1) * 512], in0=sg[:, :], in1=pu[:, :], op=ALU.mult),
                        reads=[sg, pu], writes=[actT[f]])
        for j in (range(4) if 'C' in p2s else []):
            w0 = load_w("wd", j, 0, 8)
            w1 = load_w("wd", j, 8, 8)
            w2 = load_w("wd", j, 16, 6)
            for o2 in range(2):
                oc = 2 * j + o2
                for tg in range(NTG):
                    ps = psum.next()
                    mm_group(ps, [(w0, 0, 8), (w1, 8, 8), (w2, 16, 6)], actT, tg, o2)
                    P.op("dve", lambda e, ps=ps, oc=oc, tg=tg: e.tensor_tensor(
                        out=xT[oc][:, tg * 512:(tg + 1) * 512], in0=ps[:, :], in1=xT[oc][:, tg * 512:(tg + 1) * 512],
                        op=ALU.add), reads=[ps, xT[oc]], writes=[xT[oc]])
        for f in range(22):
            old = attT[f] if f < 8 else ybT[f - 8]
            old.w, old.r = actT[f].w, actT[f].r
        if 'D' in p2s:
            rmsnorm()
        for j in (range(4) if 'D' in p2s else []):
            wpg = load_w("wpg", j, 0, 8)
            wpp = load_w("wpp", j, 0, 2)
            for o2 in range(2):
                oc = 2 * j + o2
                for tg in range(NTG):
                    pg, pp = psum.next(), psum.next()
                    mm_group(pg, [(wpg, 0, 8)], hT, tg, o2)
                    mm_group(pp, [(wpp, 0, 2)], pT, tg, o2)
                    sg, m = tmpf.next(), tmpf.next()
                    P.op("act", lambda e, sg=sg, pg=pg: e.activation(out=sg[:, :], in_=pg[:, :], func=AF.Sigmoid),
                         reads=[pg], writes=[sg])
                    P.op("dve", lambda e, m=m, sg=sg, pp=pp: e.tensor_tensor(
                        out=m[:, :], in0=sg[:, :], in1=pp[:, :], op=ALU.mult), reads=[sg, pp], writes=[m])
                    P.op("dve", lambda e, m=m, oc=oc, tg=tg: e.tensor_tensor(
                        out=xT[oc][:, tg * 512:(tg + 1) * 512], in0=m[:, :], in1=xT[oc][:, tg * 512:(tg + 1) * 512],
                        op=ALU.add), reads=[m, xT[oc]], writes=[xT[oc]])
        for kc in range(8):
            P.op("act", lambda e, kc=kc: e.copy(out=hT[kc][:, :], in_=xT[kc][:, :]), reads=[xT[kc], hT[kc]], writes=[hT[kc]])
            P.op("dve", lambda e, kc=kc: e.tensor_tensor(out=xT[kc][:, :], in0=xT[kc][:, :], in1=hT[kc][:, :], op=ALU.subtract),
                 reads=[xT[kc], hT[kc]], writes=[xT[kc]])
            P.op("pool", lambda e, kc=kc: e.tensor_copy(out=mg[kc][:, :], in_=xT[kc][:, :]), reads=[xT[kc], mg[kc]], writes=[mg[kc]])
        for s in range(NSUB):
            xo = xin.next()
            for half in range(2):
                ps = psum.next()
                for j in range(4):
                    kc = half * 4 + j
                    P.op("pe", lambda e, ps=ps, kc=kc, j=j, s=s: e.matmul(
                        ps[:, j * 128:(j + 1) * 128], lhsT=hT[kc][:, s * 128:(s + 1) * 128], rhs=ident_b[:, :], start=True, stop=False),
                        reads=[hT[kc], ident_b], writes=[ps])
                    P.op("pe", lambda e, ps=ps, kc=kc, j=j, s=s: e.matmul(
                        ps[:, j * 128:(j + 1) * 128], lhsT=mg[kc][:, s * 128:(s + 1) * 128], rhs=ident_b[:, :], start=False, stop=True),
                        reads=[mg[kc], ident_b], writes=[ps])
                if half == 0:
                    P.op("act", lambda e, ps=ps, xo=xo: e.copy(out=xo[:, 0:512], in_=ps[:, :]), reads=[ps], writes=[xo])
                else:
                    P.op("dve", lambda e, ps=ps, xo=xo: e.tensor_copy(out=xo[:, 512:1024], in_=ps[:, :]),
                         reads=[ps, xo], writes=[xo])
            out_ops.append(P.op("pool", lambda e, xo=xo, s=s, t0=t0: e.dma_start(
                out=dout[t0 + s * 128:t0 + (s + 1) * 128, :], in_=xo[:, :]), reads=[xo], dma=True))
    return out_ops


W1A = 1288
W1B = 768


def load_cast_weight(P, C, src, w, ncols, gain, stg):
    c0 = 0
    while c0 < ncols:
        n = min(WB_COLS, ncols - c0)
        s = stg.next()
        sv = src[:, c0:c0 + n].rearrange("(k p) n -> p k n", p=128)
        P.op("sp", lambda e, s=s, sv=sv, n=n: e.dma_start(out=s[:, :, 0:n], in_=sv), writes=[s], dma=True)
        gv = gain[:, 0:8].unsqueeze(2).to_broadcast([128, 8, n])
        P.op("pool", lambda e, s=s, gv=gv, n=n, c0=c0: e.tensor_tensor(
            out=w[:, :, c0:c0 + n], in0=s[:, :, 0:n], in1=gv, op=ALU.mult), reads=[s, gain], writes=[w])
        c0 += n


def build_prologue(P, C, din, cst, hT_scr, hT_T, NTOKW):
    xin = Ring([C.sbT([128, 1024], F32, "pxin") for _ in range(3)])
    junk = C.sbT([128, 1024], BF16, "pjunk")
    hb = Ring([C.sbT([128, 1024], BF16, "phb") for _ in range(2)])
    ssr = Ring([C.sbT([128, 2], F32, "pss") for _ in range(4)])
    hst = Ring([C.sbT([128, 8, 512], BF16, "phst") for _ in range(2)])
    psum = cst["psring"]
    ident_b = cst["ident_b"]
    for m in range(NTOKW // 512):
        ht = hst.next()
        for s in range(4):
            t0 = m * 512 + s * 128
            xi, ss, h = xin.next(), ssr.next(), hb.next()
            P.op("sp", lambda e, xi=xi, t0=t0: e.dma_start(out=xi[:, :], in_=din["xw"][t0:t0 + 128, :]), writes=[xi], dma=True)
            P.op("act", lambda e, xi=xi, ss=ss: e.activation(out=junk[:, :], in_=xi[:, :], func=AF.Square, accum_out=ss[:, 0:1]),
                 reads=[xi], writes=[junk, ss])
            P.op("act", lambda e, ss=ss: e.activation(out=ss[:, 1:2], in_=ss[:, 0:1], func=AF.Sqrt, scale=1.0 / 1024.0, bias=EPS),
                 reads=[ss], writes=[ss])
            P.op("dve", lambda e, ss=ss: e.reciprocal(out=ss[:, 1:2], in_=ss[:, 1:2]), reads=[ss], writes=[ss])
            P.op("dve", lambda e, xi=xi, ss=ss, h=h: e.tensor_scalar(
                out=h[:, :], in0=xi[:, :], scalar1=ss[:, 1:2], scalar2=None, op0=ALU.mult), reads=[xi, ss], writes=[h])
            ps = psum.next()
            psb = ps.ap.bitcast(BF16)
            for kc in range(8):
                P.op("pe", lambda e, psb=psb, h=h, kc=kc: e.transpose(
                    psb[:, kc * 128:(kc + 1) * 128], h[:, kc * 128:(kc + 1) * 128], ident_b[:, :]),
                    reads=[h, ident_b], writes=[ps])
            P.op("act", lambda e, psb=psb, ht=ht, s=s: e.copy(
                out=ht[:, 0:4, s * 128:(s + 1) * 128], in_=psb[:, 0:512].rearrange("p (k n) -> p k n", k=4)),
                reads=[ps], writes=[ht])
            P.op("dve", lambda e, psb=psb, ht=ht, s=s: e.tensor_copy(
                out=ht[:, 4:8, s * 128:(s + 1) * 128], in_=psb[:, 512:1024].rearrange("p (k n) -> p k n", k=4)),
                reads=[ps, ht], writes=[ht])
        t = T(None, "hTscr")
        hT_T.append(t)
        P.op("pool", lambda e, ht=ht, m=m: e.dma_start(out=hT_scr[:, :, m * 512:(m + 1) * 512], in_=ht[:, :, :]),
             reads=[ht], writes=[t], dma=True)


def build_p1a(P, C, din, g, cst, hT_scr, hT_T, e_yb, e_yb_T, NTOKW, OWN0):
    psum = cst["psring"]
    ident_b, ones_b, U, T1 = cst["ident_b"], cst["ones_b"], cst["U"], cst["T1"]
    stg = Ring([C.sbT([128, 8, WB_COLS], F32, "a_stg") for _ in range(2)])
    w1a = C.sbT([128, 8, W1A], BF16, "w1a")
    load_cast_weight(P, C, din["w1a"][g], w1a, W1A, cst["g1"], stg)
    small = {}
    for nm, shp in (("convw", [128, 6, 4]), ("convb", [128, 6]), ("dtb", [128, 8]), ("alog", [128, 8]),
                    ("dsk", [128, 8]), ("sng", [128, 512])):
        t = C.sbT(shp, F32, "a_" + nm)
        P.op("sp", lambda e, t=t, nm=nm: e.dma_start(out=t.ap, in_=din[nm][g]), writes=[t], dma=True)
        small[nm] = t
    cw, cb, dtb, alog, dsk, sng = (small[k] for k in ("convw", "convb", "dtb", "alog", "dsk", "sng"))
    tokmask = cst["tokmask"]
    Abc = C.sbT([128, 8], F32, "Abc")
    P.op("act", lambda e: e.activation(out=Abc[:, :], in_=alog[:, :], func=AF.Exp), reads=[alog], writes=[Abc])
    P.op("dve", lambda e: e.tensor_scalar(out=Abc[:, :], in0=Abc[:, :], scalar1=-1.0, scalar2=None, op0=ALU.mult),
         reads=[Abc], writes=[Abc])
    dtb4 = C.sbT([128, 32], F32, "dtb4")
    Abc4 = C.sbT([128, 32], F32, "Abc4")
    for s_ in range(4):
        P.op("dve", lambda e, s_=s_: e.tensor_copy(out=dtb4[:, 8 * s_:8 * s_ + 8], in_=dtb[:, :]), reads=[dtb, dtb4], writes=[dtb4])
        P.op("dve", lambda e, s_=s_: e.tensor_copy(out=Abc4[:, 8 * s_:8 * s_ + 8], in_=Abc[:, :]), reads=[Abc, Abc4], writes=[Abc4])
    s32 = Ring([C.sbT([128, 32], F32, "a_s32") for _ in range(21)])
    a3mr = Ring([C.sbT([128, 3, 32], BF16, "a_a3m") for _ in range(3)])
    S = C.sbT([128, 512], F32, "S")
    Sbf = C.sbT([128, 512], BF16, "Sbf")
    xbc = C.sbT([128, 6, 515], F32, "xbc")
    P.op("pool", lambda e: e.memset(S[:, :], 0.0), writes=[S])
    P.op("pool", lambda e: e.memset(Sbf[:, :], 0.0), writes=[Sbf])
    P.op("pool", lambda e: e.memset(xbc[:, :, 0:3], 0.0), writes=[xbc])
    hring = Ring([C.sbT([128, 8, 512], BF16, "a_hT") for _ in range(2)])
    cring = Ring([C.sbT([128, 6, 512], BF16, "a_co") for _ in range(2)])
    accr = Ring([C.sbT([128, 512], F32, "a_acc") for _ in range(5)])
    f512 = Ring([C.sbT([128, 512], F32, "a_f512") for _ in range(6)])
    szr = Ring([C.sbT([128, 512], F32, "a_sz") for _ in range(5)])
    b512 = Ring([C.sbT([128, 512], BF16, "a_b512") for _ in range(28)])
    s8 = Ring([C.sbT([128, 8], F32, "a_s8") for _ in range(64)])
    s2 = Ring([C.sbT([128, 2], F32, "a_s2") for _ in range(6)])
    Rr = Ring([C.sbT([128, 3, 8, 128], BF16, "a_R") for _ in range(4)])
    a3r = Ring([C.sbT([128, 3, 8], BF16, "a_a3") for _ in range(6)])
    Lr = Ring([C.sbT([128, 8, 128], BF16, "a_L") for _ in range(5)])
    Mr = Ring([C.sbT([128, 8, 128], BF16, "a_M") for _ in range(5)])
    cbm = Ring([C.sbT([128, 128], BF16, "a_cbm") for _ in range(5)])
    ybst = Ring([C.sbT([128, 4, 512], BF16, "a_ybst") for _ in range(2)])
    junk = C.sbT([128, 512], BF16, "a_junk")
    pss = cst["ps_small"]
    i_eng = 0
    for m in range(NTOKW // 512):
        tok0 = m * 512
        own = tok0 >= OWN0
        hT = hring.next()
        P.op("sp", lambda e, hT=hT, tok0=tok0: e.dma_start(out=hT[:, :, :], in_=hT_scr[:, :, tok0:tok0 + 512]),
             reads=[hT_T[m]], writes=[hT], dma=True)
        for c in range(6):
            ps = psum.next()
            for kc in range(8):
                P.op("pe", lambda e, ps=ps, kc=kc, c=c, hT=hT: e.matmul(
                    ps[:, :], lhsT=w1a[:, kc, c * 128:(c + 1) * 128], rhs=hT[:, kc, :], start=(kc == 0), stop=(kc == 7)),
                    reads=[w1a, hT], writes=[ps])
            if c % 2 == 0:
                P.op("act", lambda e, ps=ps, c=c: e.copy(out=xbc[:, c, 3:515], in_=ps[:, :]), reads=[ps, xbc], writes=[xbc])
            else:
                P.op("dve", lambda e, ps=ps, c=c: e.tensor_copy(out=xbc[:, c, 3:515], in_=ps[:, :]), reads=[ps, xbc], writes=[xbc])
        co = cring.next()
        for c in range(6):
            eng = "pool" if c in (1, 4) else "dve"
            acc = accr.next()
            P.op(eng, lambda e, acc=acc, c=c: e.tensor_scalar(
                out=acc[:, :], in0=xbc[:, c, 0:512], scalar1=cw[:, c, 0:1], scalar2=None, op0=ALU.mult),
                reads=[xbc, cw], writes=[acc])
            for k in range(1, 4):
                if eng == "dve":
                    P.op(eng, lambda e, acc=acc, c=c, k=k: e.scalar_tensor_tensor(
                        out=acc[:, :], in0=xbc[:, c, k:k + 512], scalar=cw[:, c, k:k + 1], in1=acc[:, :],
                        op0=ALU.mult, op1=ALU.add), reads=[xbc, cw, acc], writes=[acc])
                else:
                    tmpc = accr.next()
                    P.op(eng, lambda e, tmpc=tmpc, c=c, k=k: e.tensor_scalar(
                        out=tmpc[:, :], in0=xbc[:, c, k:k + 512], scalar1=cw[:, c, k:k + 1], scalar2=None, op0=ALU.mult),
                        reads=[xbc, cw], writes=[tmpc])
                    P.op(eng, lambda e, tmpc=tmpc, acc=acc: e.tensor_tensor(out=acc[:, :], in0=acc[:, :], in1=tmpc[:, :], op=ALU.add),
                         reads=[acc, tmpc], writes=[acc])
            P.op("act", lambda e, acc=acc, c=c, co=co: e.activation(
                out=co[:, c, :], in_=acc[:, :], func=AF.Silu, bias=cb[:, c:c + 1], scale=1.0), reads=[acc, cb, co], writes=[co])
        P.op("pool", lambda e: e.tensor_copy(out=xbc[:, :, 0:3], in_=xbc[:, :, 512:515]), reads=[xbc], writes=[xbc])
        yst = ybst.next() if own else None
        ctx = {}

        def pre(s, m=m, hT=hT, co=co, own=own):
            sub = slice(s * 128, (s + 1) * 128)
            tile_idx = m * 4 + s
            dt_ = TV(dtm, dtm.ap[:, 8 * s:8 * s + 8])
            a3 = TV(a3m, a3m.ap[:, :, 8 * s:8 * s + 8])
            yield
            pac = pss["acs%d" % s]
            for i3 in range(3):
                P.op("pe", lambda e, a3=a3, i3=i3: e.matmul(pac[:, 0:8], lhsT=U[:, :], rhs=a3[:, i3, :], start=(i3 == 0), stop=(i3 == 2)),
                     reads=[U, a3], writes=[pac])
            for i3 in range(3):
                P.op("pe", lambda e, a3=a3, i3=i3: e.matmul(pac[:, 8:16], lhsT=ones_b[:, :], rhs=a3[:, i3, :], start=(i3 == 0), stop=(i3 == 2)),
                     reads=[ones_b, a3], writes=[pac])
            yield
            acs, wst, wend, cdec = s8.next(), s8.next(), s8.next(), s8.next()
            P.op("act", lambda e, acs=acs: e.copy(out=acs[:, :], in_=pac[:, 0:8]), reads=[pac], writes=[acs])
            P.op("act", lambda e, wst=wst: e.activation(out=wst[:, :], in_=pac[:, 0:8], func=AF.Exp), reads=[pac], writes=[wst])
            P.op("act", lambda e, cdec=cdec: e.activation(out=cdec[:, :], in_=pac[:, 8:16], func=AF.Exp), reads=[pac], writes=[cdec])
            P.op("dve", lambda e, wend=wend, acs=acs: e.tensor_tensor(out=wend[:, :], in0=pac[:, 8:16], in1=acs[:, :], op=ALU.subtract),
                 reads=[pac, acs], writes=[wend])
            P.op("act", lambda e, wend=wend: e.activation(out=wend[:, :], in_=wend[:, :], func=AF.Exp), reads=[wend], writes=[wend])
            yield
            pxs = psum.next()
            pxb = pxs.ap.bitcast(BF16)
            for c in range(5):
                P.op("pe", lambda e, pxb=pxb, c=c, co=co, sub=sub: e.transpose(
                    pxb[:, c * 128:(c + 1) * 128], co[:, c, sub], ident_b[:, :]), reads=[co, ident_b], writes=[pxs])
            xs_tm, Btm, xdt, xdtw = b512.next(), b512.next(), b512.next(), b512.next()
            P.op("act", lambda e, pxb=pxb, xs_tm=xs_tm: e.copy(out=xs_tm[:, :], in_=pxb[:, 0:512]), reads=[pxs], writes=[xs_tm])
            P.op("act", lambda e, pxb=pxb, Btm=Btm: e.copy(out=Btm[:, 0:128], in_=pxb[:, 512:640]), reads=[pxs], writes=[Btm])
            P.op("pool", lambda e, xs_tm=xs_tm, xdt=xdt, dt_=dt_: e.tensor_tensor(
                out=xdt[:, :].rearrange("p (h d) -> p h d", h=8), in0=xs_tm[:, :].rearrange("p (h d) -> p h d", h=8),
                in1=dt_[:, :].unsqueeze(2).to_broadcast([128, 8, 64]), op=ALU.mult), reads=[xs_tm, dt_], writes=[xdt])
            P.op("pool", lambda e, xdt=xdt, xdtw=xdtw, wend=wend: e.tensor_tensor(
                out=xdtw[:, :].rearrange("p (h d) -> p h d", h=8), in0=xdt[:, :].rearrange("p (h d) -> p h d", h=8),
                in1=wend[:, :].unsqueeze(2).to_broadcast([128, 8, 64]), op=ALU.mult), reads=[xdt, wend], writes=[xdtw])
            yield
            if own:
                R, L, Mh, cbt = Rr.next(), Lr.next(), Mr.next(), cbm.next()
                for i3 in range(3):
                    P.op("dve" if i3 != 1 else "pool", lambda e, R=R, a3=a3, i3=i3: e.tensor_tensor(
                        out=R[:, i3, :, :], in0=U[:, :].unsqueeze(1).to_broadcast([128, 8, 128]),
                        in1=a3[:, i3, :].unsqueeze(2).to_broadcast([128, 8, 128]), op=ALU.mult), reads=[U, a3, R], writes=[R])
                for hh in range(2):
                    pD = psum.next()
                    for i3 in range(3):
                        P.op("pe", lambda e, pD=pD, R=R, hh=hh, i3=i3: e.matmul(
                            pD[:, :], lhsT=T1[:, :], rhs=R[:, i3, hh * 4:(hh + 1) * 4, :].rearrange("p h l -> p (h l)"),
                            start=(i3 == 0), stop=(i3 == 2)), reads=[T1, R], writes=[pD])
                    P.op("act", lambda e, pD=pD, L=L, hh=hh: e.activation(
                        out=L[:, hh * 4:(hh + 1) * 4, :].rearrange("p h l -> p (h l)"), in_=pD[:, :], func=AF.Exp),
                        reads=[pD, L], writes=[L])
                yield
                pcb = pss["cb%d" % s]
                P.op("pe", lambda e, co=co, sub=sub: e.matmul(
                    pcb[:, :], lhsT=co[:, 4, sub], rhs=co[:, 5, sub], start=True, stop=True), reads=[co], writes=[pcb])
                P.op("dve", lambda e, cbt=cbt: e.tensor_tensor(out=cbt[:, :], in0=pcb[:, :], in1=U[:, :], op=ALU.mult),
                     reads=[pcb, U], writes=[cbt])
                P.op("pool", lambda e, Mh=Mh, L=L, cbt=cbt: e.tensor_tensor(
                    out=Mh[:, :, :], in0=L[:, :, :], in1=cbt[:, :].unsqueeze(1).to_broadcast([128, 8, 128]), op=ALU.mult),
                    reads=[L, cbt], writes=[Mh])
                xsD = b512.next()
                P.op("pool", lambda e, xs_tm=xs_tm, xsD=xsD: e.tensor_tensor(
                    out=xsD[:, :].rearrange("p (h d) -> p h d", h=8), in0=xs_tm[:, :].rearrange("p (h d) -> p h d", h=8),
                    in1=dsk[:, :].unsqueeze(2).to_broadcast([128, 8, 64]), op=ALU.mult), reads=[xs_tm, dsk], writes=[xsD])
                yield
                pz = psum.next()
                for kc in range(8):
                    P.op("pe", lambda e, pz=pz, kc=kc, hT=hT, sub=sub: e.matmul(
                        pz[:, :], lhsT=hT[:, kc, sub], rhs=w1a[:, kc, 768:1280], start=(kc == 0), stop=(kc == 7)),
                        reads=[w1a, hT], writes=[pz])
                sz = szr.next()
                P.op("act", lambda e, pz=pz, sz=sz: e.activation(out=sz[:, :], in_=pz[:, :], func=AF.Silu), reads=[pz], writes=[sz])
            ctx[s] = dict(locals())
            yield

        def seq(s, m=m, hT=hT, co=co, own=own, yst=yst):
            L_ = ctx[s]
            sub = L_["sub"]
            wst, cdec, xdt, xdtw, Btm = L_["wst"], L_["cdec"], L_["xdt"], L_["xdtw"], L_["Btm"]
            if own:
                Mh, xsD, sz = L_["Mh"], L_["xsD"], L_["sz"]
                pyo, py = psum.next(), psum.next()
                P.op("pe", lambda e, pyo=pyo, co=co, sub=sub: e.matmul(
                    pyo[:, :], lhsT=co[:, 5, sub], rhs=Sbf[:, :], start=True, stop=True), reads=[co, Sbf], writes=[pyo])
                P.op("pe", lambda e, py=py, xsD=xsD: e.matmul(py[:, :], lhsT=ident_b[:, :], rhs=xsD[:, :], start=True, stop=False),
                     reads=[ident_b, xsD], writes=[py])
                for h in range(8):
                    P.op("pe", lambda e, py=py, Mh=Mh, xdt=xdt, h=h: e.matmul(
                        py[:, h * 64:(h + 1) * 64], lhsT=Mh[:, h, :], rhs=xdt[:, h * 64:(h + 1) * 64], start=False, stop=(h == 7)),
                        reads=[Mh, xdt], writes=[py])
                y1, y2, y3 = f512.next(), f512.next(), f512.next()
                P.op("dve", lambda e, pyo=pyo, y1=y1, wst=wst: e.tensor_tensor(
                    out=y1[:, :].rearrange("p (h d) -> p h d", h=8), in0=pyo[:, :].rearrange("p (h d) -> p h d", h=8),
                    in1=wst[:, :].unsqueeze(2).to_broadcast([128, 8, 64]), op=ALU.mult), reads=[pyo, wst], writes=[y1])
                P.op("dve", lambda e, y1=y1, y2=y2, py=py: e.tensor_tensor(out=y2[:, :], in0=y1[:, :], in1=py[:, :], op=ALU.add),
                     reads=[y1, py], writes=[y2])
                P.op("pool", lambda e, y2=y2, y3=y3, sz=sz: e.tensor_tensor(out=y3[:, :], in0=y2[:, :], in1=sz[:, :], op=ALU.mult),
                     reads=[y2, sz], writes=[y3])
                ss = s2.next()
                P.op("act", lambda e, y3=y3, ss=ss: e.activation(out=junk[:, :], in_=y3[:, :], func=AF.Square, accum_out=ss[:, 0:1]),
                     reads=[y3], writes=[junk, ss])
                P.op("act", lambda e, ss=ss: e.activation(out=ss[:, 1:2], in_=ss[:, 0:1], func=AF.Sqrt, scale=1.0 / 512.0, bias=EPS),
                     reads=[ss], writes=[ss])
                P.op("dve", lambda e, ss=ss: e.reciprocal(out=ss[:, 1:2], in_=ss[:, 1:2]), reads=[ss], writes=[ss])
                yn = b512.next()
                P.op("dve", lambda e, y3=y3, ss=ss, yn=yn: e.scalar_tensor_tensor(
                    out=yn[:, :], in0=y3[:, :], scalar=ss[:, 1:2], in1=sng[:, :], op0=ALU.mult, op1=ALU.mult),
                    reads=[y3, ss, sng], writes=[yn])
                pyt = psum.next()
                pytb = pyt.ap.bitcast(BF16)
                for c in range(4):
                    P.op("pe", lambda e, pytb=pytb, yn=yn, c=c: e.transpose(
                        pytb[:, c * 128:(c + 1) * 128], yn[:, c * 128:(c + 1) * 128], ident_b[:, :]),
                        reads=[yn, ident_b], writes=[pyt])
                P.op("act", lambda e, pytb=pytb, yst=yst, sub=sub: e.copy(
                    out=yst[:, :, sub], in_=pytb[:, 0:512].rearrange("p (c n) -> p c n", c=4)), reads=[pyt, yst], writes=[yst])
            pst = psum.next()
            P.op("pe", lambda e, pst=pst, Btm=Btm, xdtw=xdtw: e.matmul(
                pst[:, :], lhsT=Btm[:, 0:128], rhs=xdtw[:, :], start=True, stop=True), reads=[Btm, xdtw], writes=[pst])
            P.op("pool", lambda e, cdec=cdec: e.tensor_tensor(
                out=S[:, :].rearrange("p (h d) -> p h d", h=8), in0=S[:, :].rearrange("p (h d) -> p h d", h=8),
                in1=cdec[:, :].unsqueeze(2).to_broadcast([128, 8, 64]), op=ALU.mult), reads=[S, cdec], writes=[S])
            P.op("dve", lambda e, pst=pst: e.tensor_tensor(out=S[:, :], in0=S[:, :], in1=pst[:, :], op=ALU.add),
                 reads=[S, pst], writes=[S])
            P.op("act", lambda e: e.copy(out=Sbf[:, :], in_=S[:, :]), reads=[S, Sbf], writes=[Sbf])

        for s in range(4):
            pdt = pss["dt%d" % s]
            for kc in range(8):
                P.op("pe", lambda e, kc=kc, hT=hT, s=s, pdt=pdt: e.matmul(
                    pdt[:, :], lhsT=hT[:, kc, s * 128:(s + 1) * 128], rhs=w1a[:, kc, 1280:1288], start=(kc == 0), stop=(kc == 7)),
                    reads=[w1a, hT], writes=[pdt])
        pd_all = TV(pss["dt0"].parent, pss["dt0"].parent.ap[:, 0:32])
        dtr, ax, ee, dtm, am, ar1, ar2 = (s32.next() for _ in range(7))
        a3m = a3mr.next()
        P.op("dve", lambda e, dtr=dtr: e.tensor_tensor(out=dtr[:, :], in0=pd_all[:, :], in1=dtb4[:, :], op=ALU.add),
             reads=[pd_all, dtb4], writes=[dtr])
        P.op("act", lambda e, dtr=dtr, ax=ax: e.activation(out=ax[:, :], in_=dtr[:, :], func=AF.Abs), reads=[dtr], writes=[ax])
        P.op("act", lambda e, ax=ax, ee=ee: e.activation(out=ee[:, :], in_=ax[:, :], func=AF.Exp, scale=-1.0), reads=[ax], writes=[ee])
        P.op("act", lambda e, ee=ee: e.activation(out=ee[:, :], in_=ee[:, :], func=AF.Ln, bias=1.0, scale=1.0), reads=[ee], writes=[ee])
        P.op("dve", lambda e, dtr=dtr, ee=ee, dtm=dtm: e.scalar_tensor_tensor(
            out=dtm[:, :], in0=dtr[:, :], scalar=0.0, in1=ee[:, :], op0=ALU.max, op1=ALU.add), reads=[dtr, ee], writes=[dtm])
        P.op("dve", lambda e, dtm=dtm, m=m: e.tensor_tensor(
            out=dtm[:, :].rearrange("p (s h) -> p s h", s=4), in0=dtm[:, :].rearrange("p (s h) -> p s h", s=4),
            in1=tokmask[:, 4 * m:4 * m + 4].unsqueeze(2).to_broadcast([128, 4, 8]), op=ALU.mult), reads=[dtm, tokmask], writes=[dtm])
        P.op("dve", lambda e, dtm=dtm, am=am: e.tensor_tensor(out=am[:, :], in0=dtm[:, :], in1=Abc4[:, :], op=ALU.mult),
             reads=[dtm, Abc4], writes=[am])
        P.op("act", lambda e, am=am, a3m=a3m: e.copy(out=a3m[:, 0, :], in_=am[:, :]), reads=[am, a3m], writes=[a3m])
        P.op("dve", lambda e, am=am, a3m=a3m, ar1=ar1: e.tensor_tensor(out=ar1[:, :], in0=am[:, :], in1=a3m[:, 0, :], op=ALU.subtract),
             reads=[am, a3m], writes=[ar1])
        P.op("act", lambda e, ar1=ar1, a3m=a3m: e.copy(out=a3m[:, 1, :], in_=ar1[:, :]), reads=[ar1, a3m], writes=[a3m])
        P.op("dve", lambda e, ar1=ar1, a3m=a3m, ar2=ar2: e.tensor_tensor(out=ar2[:, :], in0=ar1[:, :], in1=a3m[:, 1, :], op=ALU.subtract),
             reads=[ar1, a3m], writes=[ar2])
        P.op("act", lambda e, ar2=ar2, a3m=a3m: e.copy(out=a3m[:, 2, :], in_=ar2[:, :]), reads=[ar2, a3m], writes=[a3m])
        gens = [pre(s) for s in range(4)]
        while gens:
            for g_ in list(gens):
                try:
                    next(g_)
                except StopIteration:
                    gens.remove(g_)
        for s in range(4):
            seq(s)
        if own:
            o0 = tok0 - OWN0
            P.op("pool", lambda e, yst=yst, o0=o0: e.dma_start(
                out=e_yb[g, :, :, o0:o0 + 512].rearrange("c p n -> p c n"), in_=yst[:, :, :]),
                reads=[yst], writes=[e_yb_T], dma=True)


def build_p1_init(P, C, din, cst, NTOKW):
    KT = C.sb([96, 4, NTOKW], BF16, "KT")
    VA = C.sb([128, NTOKW // 128, 2, 3, 64], BF16, "VA")
    kmT = C.sbT([64, 4, 32], BF16, "kmT")
    Mpad = [C.sbT([128, 4, 96], BF16, "Mpad") for _ in range(2)]
    for h in range(4):
        P.op("sp", lambda e, h=h: e.dma_start(out=KT[64:96, h, :], in_=din["kind"]), dma=True)
    P.op("pool", lambda e: e.memset(VA[:, :, :, 1, :], 1.0))
    for mp in Mpad:
        P.op("pool", lambda e, mp=mp: e.memset(mp[:, :, :], 0.0), writes=[mp])
    P.op("pool", lambda e: e.memset(kmT[:, :, :], 0.0), writes=[kmT])
    G = C.sbT([128, 512], F32, "G")
    gq, gk = cst["gq"], cst["gk"]
    for h in range(4):
        P.op("dve", lambda e, h=h: e.tensor_scalar(out=G[:, h * 64:(h + 1) * 64], in0=gq[:, :], scalar1=0.125, scalar2=None,
                                                   op0=ALU.mult), reads=[gq, G], writes=[G])
        P.op("dve", lambda e, h=h: e.tensor_copy(out=G[:, 256 + h * 64:256 + (h + 1) * 64], in_=gk[:, :]), reads=[gk, G], writes=[G])
    bb4 = C.sbT([128, 128], F32, "bb4")
    for h in range(4):
        P.op("dve", lambda e, h=h: e.tensor_copy(out=bb4[:, h * 32:(h + 1) * 32], in_=cst["blkbias"][:, :]),
             reads=[cst["blkbias"], bb4], writes=[bb4])
    P.barrier()
    nm = NTOKW // 512
    return dict(KT=KT, VA=VA, kmT=kmT, Mpad=Ring(Mpad), G=G, bb4=bb4,
                KT_T=[T(None, "KT%d" % i) for i in range(nm)], VA_T=[T(None, "VA%d" % i) for i in range(nm)])


def build_p1b(P, C, din, g, cst, A, hT_scr, hT_T, e_att, e_att_T, NTOKW, OWN0):
    psum = cst["psring"]
    po_ring = cst["po_ring"]
    pss = cst["ps_small"]
    ident_b, negm = cst["ident_b"], cst["negm"]
    KT, VA, kmT, G, bb4 = A["KT"], A["VA"], A["kmT"], A["G"], A["bb4"]
    KT_T, VA_T = A["KT_T"], A["VA_T"]
    stg = Ring([C.sbT([128, 8, WB_COLS], F32, "b_stg") for _ in range(2)])
    w1b = C.sbT([128, 8, W1B], BF16, "w1b")
    load_cast_weight(P, C, din["w1b"][g], w1b, W1B, cst["g1"], stg)
    hring = Ring([C.sbT([128, 8, 512], BF16, "b_hT") for _ in range(2)])
    f512 = Ring([C.sbT([128, 512], F32, "b_f512") for _ in range(4)])
    b512 = Ring([C.sbT([128, 512], BF16, "b_b512") for _ in range(3)])
    ptr = Ring([C.sbT([128, 512], BF16, "b_pt") for _ in range(4)])
    s8 = Ring([C.sbT([128, 8], F32, "b_s8") for _ in range(6)])
    g128 = Ring([C.sbT([128, 128], F32, "b_g128") for _ in range(6)])
    t8r = Ring([C.sbT([128, 32], F32, "b_t8") for _ in range(2)])
    kmf = C.sbT([64, 4, 2], F32, "b_kmf")
    QTr = Ring([C.sbT([96, 4, 512], BF16, "b_QT") for _ in range(2)])
    ast = [Ring([C.sbT([128, 512], BF16, "b_ast") for _ in range(2)]) for _ in range(2)]
    rdr = Ring([C.sbT([128, 512], F32, "b_rd") for _ in range(2)])
    outs = []
    for m in range(NTOKW // 512):
        tok0 = m * 512
        own = tok0 >= OWN0
        c0 = 0 if own else 256
        h0 = 0 if own else 4
        hT = hring.next()
        P.op("sp", lambda e, hT=hT, tok0=tok0: e.dma_start(out=hT[:, :, :], in_=hT_scr[:, :, tok0:tok0 + 512]),
             reads=[hT_T[m]], writes=[hT], dma=True)
        QT = QTr.next() if own else None
        for s in range(4):
            sub = slice(s * 128, (s + 1) * 128)
            kt = m * 4 + s
            pqk, pv = psum.next(), psum.next()
            for kc in range(8):
                P.op("pe", lambda e, pqk=pqk, kc=kc, hT=hT, sub=sub, c0=c0: e.matmul(
                    pqk[:, c0:512], lhsT=hT[:, kc, sub], rhs=w1b[:, kc, c0:512], start=(kc == 0), stop=(kc == 7)),
                    reads=[w1b, hT], writes=[pqk])
            for kc in range(8):
                P.op("pe", lambda e, pv=pv, kc=kc, hT=hT, sub=sub: e.matmul(
                    pv[:, 0:256], lhsT=hT[:, kc, sub], rhs=w1b[:, kc, 512:768], start=(kc == 0), stop=(kc == 7)),
                    reads=[w1b, hT], writes=[pv])
            sq, ssum, tt = f512.next(), s8.next(), f512.next()
            P.op("act", lambda e, pqk=pqk, sq=sq, c0=c0: e.activation(out=sq[:, c0:512], in_=pqk[:, c0:512], func=AF.Square),
                 reads=[pqk], writes=[sq])
            P.op("dve", lambda e, sq=sq, ssum=ssum, c0=c0, h0=h0: e.tensor_reduce(
                out=ssum[:, h0:8], in_=sq[:, c0:512].rearrange("p (h d) -> p h d", d=64), axis=AX.X, op=ALU.add),
                reads=[sq], writes=[ssum])
            P.op("act", lambda e, ssum=ssum, h0=h0: e.activation(
                out=ssum[:, h0:8], in_=ssum[:, h0:8], func=AF.Sqrt, scale=1.0 / 64.0, bias=EPS), reads=[ssum], writes=[ssum])
            P.op("dve", lambda e, ssum=ssum, h0=h0: e.reciprocal(out=ssum[:, h0:8], in_=ssum[:, h0:8]), reads=[ssum], writes=[ssum])
            P.op("dve", lambda e, pqk=pqk, tt=tt, ssum=ssum, c0=c0, h0=h0: e.tensor_tensor(
                out=tt[:, c0:512].rearrange("p (h d) -> p h d", d=64), in0=pqk[:, c0:512].rearrange("p (h d) -> p h d", d=64),
                in1=ssum[:, h0:8].unsqueeze(2).to_broadcast([128, 8 - h0, 64]), op=ALU.mult), reads=[pqk, ssum], writes=[tt])
            qkn = b512.next()
            P.op("pool", lambda e, tt=tt, qkn=qkn, c0=c0: e.tensor_tensor(
                out=qkn[:, c0:512], in0=tt[:, c0:512], in1=G[:, c0:512], op=ALU.mult), reads=[tt, G], writes=[qkn])
            pkt = psum.next()
            pktb = pkt.ap.bitcast(BF16)
            for h in range(4):
                P.op("pe", lambda e, pktb=pktb, qkn=qkn, h=h: e.transpose(
                    pktb[0:64, h * 128:(h + 1) * 128], qkn[:, 256 + h * 64:256 + (h + 1) * 64], ident_b[:, :]),
                    reads=[qkn, ident_b], writes=[pkt])
            P.op("act", lambda e, pktb=pktb, tok0=tok0, s=s: e.copy(
                out=KT[0:64, :, tok0 + s * 128:tok0 + (s + 1) * 128], in_=pktb[0:64, 0:512].rearrange("p (h n) -> p h n", h=4)),
                reads=[pkt, KT_T[m]], writes=[KT_T[m]])
            P.op("act", lambda e, pv=pv, kt=kt: e.copy(
                out=VA[:, kt, :, 0, :], in_=pv[:, 0:256].rearrange("p (a b d) -> p a b d", a=2, b=2)[:, :, 0, :]),
                reads=[pv, VA_T[m]], writes=[VA_T[m]])
            P.op("dve", lambda e, pv=pv, kt=kt: e.tensor_copy(
                out=VA[:, kt, :, 2, :], in_=pv[:, 0:256].rearrange("p (a b d) -> p a b d", a=2, b=2)[:, :, 1, :]),
                reads=[pv, VA_T[m]], writes=[VA_T[m]])
            if own:
                pqt = psum.next()
                pqtb = pqt.ap.bitcast(BF16)
                for h in range(4):
                    P.op("pe", lambda e, pqtb=pqtb, qkn=qkn, h=h: e.transpose(
                        pqtb[0:64, h * 128:(h + 1) * 128], qkn[:, h * 64:(h + 1) * 64], ident_b[:, :]),
                        reads=[qkn, ident_b], writes=[pqt])
                P.op("dve", lambda e, pqtb=pqtb, QT=QT, sub=sub: e.tensor_copy(
                    out=QT[0:64, :, sub], in_=pqtb[0:64, 0:512].rearrange("p (h n) -> p h n", h=4)),
                    reads=[pqt, QT], writes=[QT])
        P.op("dve", lambda e, tok0=tok0: e.tensor_reduce(
            out=kmf[:, :, :], in_=KT[0:64, :, tok0:tok0 + 512].rearrange("p h (b k) -> p h b k", b=2), axis=AX.X, op=ALU.add),
            reads=[KT_T[m]], writes=[kmf])
        P.op("dve", lambda e, m=m: e.tensor_scalar(out=kmT[:, :, 2 * m:2 * m + 2], in0=kmf[:, :, :], scalar1=1.0 / 256.0,
                                                   scalar2=None, op0=ALU.mult), reads=[kmf, kmT], writes=[kmT])
        if not own:
            continue
        for s in range(4):
            sub = slice(s * 128, (s + 1) * 128)
            ownblk = 2 * m + s // 2
            pg = pss["gate"]
            for h in range(4):
                P.op("pe", lambda e, h=h, QT=QT, sub=sub: e.matmul(
                    pg[:, h * 32:(h + 1) * 32], lhsT=QT[0:64, h, sub], rhs=kmT[0:64, h, :], start=True, stop=True),
                    reads=[QT, kmT], writes=[pg])
            gm, m1, m2, t8 = g128.next(), g128.next(), g128.next(), t8r.next()
            P.op("dve", lambda e, gm=gm: e.tensor_tensor(out=gm[:, :], in0=pg[:, :], in1=bb4[:, :], op=ALU.add),
                 reads=[pg, bb4], writes=[gm])
            P.op("pool", lambda e, gm=gm, ownblk=ownblk: e.memset(
                gm[:, :].rearrange("p (h b) -> p h b", h=4)[:, :, ownblk:32], NEG), reads=[gm], writes=[gm])
            for h in range(4):
                P.op("dve", lambda e, gm=gm, t8=t8, h=h: e.max(out=t8[:, h * 8:(h + 1) * 8], in_=gm[:, h * 32:(h + 1) * 32]),
                     reads=[gm, t8], writes=[t8])
            P.op("dve", lambda e, gm=gm, m1=m1, t8=t8: e.tensor_tensor(
                out=m1[:, :].rearrange("p (h b) -> p h b", h=4), in0=gm[:, :].rearrange("p (h b) -> p h b", h=4),
                in1=t8[:, :].rearrange("p (h k) -> p h k", h=4)[:, :, 2:3].to_broadcast([128, 4, 32]), op=ALU.is_lt),
                reads=[gm, t8], writes=[m1])
            P.op("dve", lambda e, gm=gm, m2=m2: e.tensor_scalar(
                out=m2[:, :], in0=gm[:, :], scalar1=NEG / 2, scalar2=NEG, op0=ALU.is_lt, op1=ALU.mult), reads=[gm], writes=[m2])
            Mp = A["Mpad"].next()
            P.op("dve", lambda e, Mp=Mp, m1=m1, m2=m2: e.scalar_tensor_tensor(
                out=Mp[:, :, 64:96], in0=m1[:, :].rearrange("p (h b) -> p h b", h=4), scalar=NEG,
                in1=m2[:, :].rearrange("p (h b) -> p h b", h=4), op0=ALU.mult, op1=ALU.min), reads=[m1, m2, Mp], writes=[Mp])
            P.op("pool", lambda e, Mp=Mp, ownblk=ownblk: e.memset(Mp[:, :, 64 + ownblk:65 + ownblk], 0.0), reads=[Mp], writes=[Mp])
            pmt = psum.next()
            pmtb = pmt.ap.bitcast(BF16)
            for h in range(4):
                P.op("pe", lambda e, pmtb=pmtb, Mp=Mp, h=h: e.transpose(
                    pmtb[0:96, h * 128:(h + 1) * 128], Mp[:, h, :], ident_b[:, :]), reads=[Mp, ident_b], writes=[pmt])
            P.op("act", lambda e, pmtb=pmtb, QT=QT, sub=sub: e.copy(
                out=QT[64:96, :, sub], in_=pmtb[64:96, 0:512].rearrange("p (h n) -> p h n", h=4)), reads=[pmt, QT], writes=[QT])
        nkt = (2 * m + 2) * 2
        o0 = tok0 - OWN0
        for h in range(4):
            pair, hb = h // 2, h % 2
            po = po_ring.next()
            def tile_cols(kt):
                blk = kt // 2
                if blk < 2 * m:
                    return 0, 512, None
                if blk == 2 * m:
                    return 0, 512, 0
                return 256, 512, 256

            def emit_s(kt):
                a0, a1, cz = tile_cols(kt)
                mk = kt // 4
                ps = psum.next()
                P.op("pe", lambda e, ps=ps, h=h, kt=kt, QT=QT, a0=a0, a1=a1, cz=cz: e.matmul(
                    ps[:, a0:a1], lhsT=KT[0:96, h, kt * 128:(kt + 1) * 128], rhs=QT[0:96, h, a0:a1],
                    start=True, stop=(cz is None)), reads=[KT_T[mk], QT], writes=[ps])
                if cz is not None:
                    P.op("pe", lambda e, ps=ps, kt=kt, cz=cz: e.matmul(
                        ps[:, cz:cz + 256], lhsT=ident_b[:, :], rhs=negm[:, kt % 2, :], start=False, stop=True),
                        reads=[ident_b, negm], writes=[ps])
                return ps

            def emit_pv(kt, ps):
                a0, a1, cz = tile_cols(kt)
                mk = kt // 4
                pt = ptr.next()
                P.op("act", lambda e, ps=ps, pt=pt, a0=a0, a1=a1: e.activation(out=pt[:, a0:a1], in_=ps[:, a0:a1], func=AF.Exp),
                     reads=[ps], writes=[pt])
                P.op("pe", lambda e, po=po, pt=pt, kt=kt, pair=pair, hb=hb, a0=a0, a1=a1, nkt=nkt: e.matmul(
                    po[:, a0:a1], lhsT=VA[:, kt, pair, hb:hb + 2, :].rearrange("p a d -> p (a d)"), rhs=pt[:, a0:a1],
                    start=(kt == 0), stop=(kt == nkt - 1), skip_group_check=True), reads=[VA_T[mk], pt], writes=[po])

            LOOK = 2
            pend = []
            for kt in range(nkt):
                pend.append((kt, emit_s(kt)))
                if len(pend) > LOOK:
                    emit_pv(*pend.pop(0))
            while pend:
                emit_pv(*pend.pop(0))
            nr = slice(0, 64) if hb == 0 else slice(64, 128)
            dr = slice(64, 128) if hb == 0 else slice(0, 64)
            rd = rdr.next()
            if hb == 0:
                at_ = ast[pair].next()
                ast_cur = at_
            else:
                at_ = ast_cur
            P.op("dve", lambda e, po=po, rd=rd, nr=nr, dr=dr: e.reciprocal(out=rd[nr, :], in_=po[dr, :]), reads=[po], writes=[rd])
            P.op("dve", lambda e, po=po, rd=rd, nr=nr, at_=at_: e.tensor_tensor(
                out=at_[nr, :], in0=po[nr, :], in1=rd[nr, :], op=ALU.mult), reads=[po, rd, at_], writes=[at_])
            if hb == 1:
                outs.append(P.op("pool", lambda e, at_=at_, pair=pair, o0=o0: e.dma_start(
                    out=e_att[g, pair, :, o0:o0 + 512], in_=at_[:, :]), reads=[at_], writes=[e_att_T], dma=True))
    return outs


def load_consts(P, C, din, names_shapes):
    out = {}
    for name, shape, dt in names_shapes:
        t = C.sbT(shape, dt, name)
        P.op("sp", lambda e, t=t, name=name: e.dma_start(out=t.ap, in_=din[name]), writes=[t], dma=True)
        out[name] = t
    return out


def build_program(mode, NTOKW=8192, OWN0=0, NG=1):
    nc = bass.Bass("TRN2", target_bir_lowering=False)
    P = Prog(nc)
    with ExitStack() as es:
        C = Ctx(nc, es, P)
        din = {}

        def inp(name, shape, dt=F32):
            din[name] = C.dram(name, shape, dt, "ExternalInput")

        psum = [T(es.enter_context(nc.psum_tensor("ps%d" % i, [128, 512], F32))[:, :], "ps%d" % i, excl=True) for i in range(8)]
        final = []
        NOWN = NTOKW - OWN0
        if mode in ("p1", "fused"):
            inp("xw", [NTOKW, 1024])
            inp("w1a", [NG, 1024, W1A])
            inp("w1b", [NG, 1024, W1B])
            inp("convw", [NG, 128, 6, 4])
            inp("convb", [NG, 128, 6])
            for nm in ("dtb", "alog", "dsk"):
                inp(nm, [NG, 128, 8])
            inp("sng", [NG, 128, 512])
            inp("kind", [32, NTOKW], BF16)
            shapes1 = [("gq", [128, 64], F32), ("gk", [128, 64], F32), ("g1", [128, 8], F32),
                       ("tokmask", [128, NTOKW // 128], F32), ("blkbias", [128, 32], F32),
                       ("ident_b", [128, 128], BF16), ("ones_b", [128, 128], BF16), ("U", [128, 128], BF16),
                       ("T1", [128, 128], BF16), ("negm", [128, 2, 256], BF16)]
            for nm, shp, dt in shapes1:
                if nm not in din:
                    inp(nm, shp, dt)
            cst = load_consts(P, C, din, shapes1)
            cst["psring"] = Ring(psum[0:5])
            cst["po_ring"] = Ring(psum[5:7])
            cst["ps_small"] = {"gate": TV(psum[7], psum[7].ap[:, 0:128])}
            for s_ in range(4):
                cst["ps_small"]["dt%d" % s_] = TV(psum[5], psum[5].ap[:, 8 * s_:8 * s_ + 8])
                cst["ps_small"]["acs%d" % s_] = TV(psum[5], psum[5].ap[:, 64 + 16 * s_:64 + 16 * s_ + 16])
                cst["ps_small"]["cb%d" % s_] = TV(psum[6], psum[6].ap[:, 128 * s_:128 * s_ + 128])
            kind_e = "ExternalOutput" if mode == "p1" else "Internal"
            e_att = C.dram("e_att", [NG, 2, 128, NOWN], BF16, kind_e)
            e_yb = C.dram("e_yb", [NG, 4, 128, NOWN], BF16, kind_e)
            e_att_T, e_yb_T = T(None, "e_att"), T(None, "e_yb")
            hT_scr = C.dram("hT_scr", [128, 8, NTOKW], BF16, "Internal")
            hT_T = []
            with ExitStack() as es1:
                C1 = Ctx(nc, es1, P)
                if STAGES.get("pro", True):
                    build_prologue(P, C1, din, cst, hT_scr, hT_T, NTOKW)
            P.barrier()
            with ExitStack() as es1:
                C1 = Ctx(nc, es1, P)
                for g in range(NG):
                    if STAGES.get("a", True):
                        with ExitStack() as es2:
                            build_p1a(P, Ctx(nc, es2, P), din, g, cst, hT_scr, hT_T, e_yb, e_yb_T, NTOKW, OWN0)
                        P.barrier()
                    if STAGES.get("b", True):
                        with ExitStack() as es2:
                            C2b = Ctx(nc, es2, P)
                            A = build_p1_init(P, C2b, din, cst, NTOKW)
                            build_p1b(P, C2b, din, g, cst, A, hT_scr, hT_T, e_att, e_att_T, NTOKW, OWN0)
                        P.barrier()
            if mode == "p1":
                final = [o for o in P.ops["pool"] if o.dma][-8:]
        if mode in ("p2", "fused"):
            inp("x2", [2048, 1024])
            inp("p2", [2048, 256])
            for name, (K, N) in P2W.items():
                inp(name, [K, N])
            shapes2 = [("g1", [128, 8], F32), ("g2", [128, 8], F32), ("g3", [128, 8], F32),
                       ("ident_b", [128, 128], BF16), ("ones_b", [128, 128], BF16)]
            for nm, shp, dt in shapes2:
                if nm not in din:
                    inp(nm, shp, dt)
            consts = load_consts(P, C, din, shapes2)
            consts["psum"] = psum
            wscr = {name: C.dram("scr_" + name, [K, N], BF16, "Internal") for name, (K, N) in P2W.items()}
            wT = {}
            with ExitStack() as es2:
                C2 = Ctx(nc, es2, P)
                if STAGES.get("precast", True):
                    build_precast(P, C2, din, wscr, wT, consts)
            P.barrier()
            if mode == "p2":
                inp("e_att", [4, 2, 128, 2048], BF16)
                inp("e_yb", [4, 4, 128, 2048], BF16)
                exch = {"att": din["e_att"], "yb": din["e_yb"], "att_T": T(None), "yb_T": T(None)}
            else:
                exch = {"att": e_att, "yb": e_yb, "att_T": e_att_T, "yb_T": e_yb_T}
            dout = C.dram("out", [2048, 1024], F32, "ExternalOutput")
            if STAGES.get("p2", True):
                with ExitStack() as es3:
                    C3 = Ctx(nc, es3, P)
                    final = build_phase2(P, C3, din, wscr, wT, consts, exch, dout, NTOK=STAGES.get('ntok', 2048))
            else:
                final = [o for o in P.ops["pool"] if o.dma][-8:]
        P.emit(final)
    return nc


BF = ml_dtypes.bfloat16


def host_consts():
    return {
        "ident_b": np.eye(128, dtype=np.float32).astype(BF),
        "ones_b": np.ones((128, 128), dtype=np.float32).astype(BF),
    }


def host_consts1(NTOKW):
    i = np.arange(128)
    U = (i[:, None] <= i[None, :]).astype(np.float32)
    T1 = (i[:, None] > i[None, :]).astype(np.float32)
    q = np.arange(256)
    negm = np.stack([np.where((kt * 128 + i[:, None]) <= q[None, :], 0.0, NEG) for kt in range(2)], 1).astype(np.float32)
    kind = (np.arange(NTOKW)[None, :] // 256 == np.arange(32)[:, None]).astype(np.float32)
    return {"ident_b": np.eye(128, dtype=np.float32).astype(BF), "ones_b": np.ones((128, 128), np.float32).astype(BF),
            "U": U.astype(BF), "T1": T1.astype(BF), "negm": negm.astype(BF), "kind": kind.astype(BF)}


def gain_layout(g):
    return np.ascontiguousarray(g.reshape(8, 128).T)


def bc(v):
    return np.ascontiguousarray(np.broadcast_to(v[None, :], (128, v.shape[0]))).astype(np.float32)


def p1_inputs(inputs, b, groups, xw, tokmask, blkbias, NTOKW):
    w_in = inputs["w_in"][0]
    cw, cbias = inputs["conv_w"][0], inputs["conv_b"][0]
    w1a, w1b, convw, convb, dtb, alog, dsk, sng = [], [], [], [], [], [], [], []
    for g in groups:
        cols_a = np.concatenate([np.arange(5120 + 512 * g, 5120 + 512 * (g + 1)), np.arange(7168 + 128 * g, 7168 + 128 * (g + 1)),
                                 np.arange(7680 + 128 * g, 7680 + 128 * (g + 1)), np.arange(3072 + 512 * g, 3072 + 512 * (g + 1)),
                                 np.arange(8192 + 8 * g, 8192 + 8 * (g + 1))])
        cols_b = np.concatenate([np.arange(256 * g, 256 * (g + 1)), np.arange(1024 + 256 * g, 1024 + 256 * (g + 1)),
                                 np.arange(2048 + 256 * g, 2048 + 256 * (g + 1))])
        w1a.append(w_in[:, cols_a])
        w1b.append(w_in[:, cols_b])
        ch = np.concatenate([np.arange(512 * g, 512 * (g + 1)), np.arange(2048 + 128 * g, 2048 + 128 * (g + 1)),
                             np.arange(2560 + 128 * g, 2560 + 128 * (g + 1))])
        convw.append(cw[:, ch].T.reshape(6, 128, 4).transpose(1, 0, 2))
        convb.append(cbias[ch].reshape(6, 128).T)
        dtb.append(bc(inputs["dt_bias"][0][8 * g:8 * g + 8]))
        alog.append(bc(inputs["a_log"][0][8 * g:8 * g + 8]))
        dsk.append(bc(inputs["d_skip"][0][8 * g:8 * g + 8]))
        sng.append(bc(inputs["ssm_norm_g"][0][512 * g:512 * g + 512]))
    m = {"xw": np.ascontiguousarray(xw), "w1a": np.ascontiguousarray(np.stack(w1a)), "w1b": np.ascontiguousarray(np.stack(w1b)),
         "convw": np.ascontiguousarray(np.stack(convw)), "convb": np.ascontiguousarray(np.stack(convb)),
         "dtb": np.stack(dtb), "alog": np.stack(alog), "dsk": np.stack(dsk), "sng": np.stack(sng),
         "gq": bc(inputs["q_norm_g"][0]), "gk": bc(inputs["k_norm_g"][0]), "g1": gain_layout(inputs["ln1_g"][0]),
         "tokmask": np.ascontiguousarray(tokmask.reshape(-1, 128).T).astype(np.float32), "blkbias": bc(blkbias)}
    m.update(host_consts1(NTOKW))
    return m


def p2_inputs(inputs, core, e_att=None, e_yb=None):
    b, t = core // 4, core % 4
    sl = slice(t * 2048, (t + 1) * 2048)
    m = {
        "x2": np.ascontiguousarray(inputs["x"][b, sl]),
        "p2": np.ascontiguousarray(inputs["p"][0, b, sl]),
        "wg": np.ascontiguousarray(inputs["w_in"][0][:, 8224:10272]),
        "woa": inputs["w_o_attn"][0], "wos": inputs["w_o_ssm"][0], "wout": inputs["w_out"][0],
        "wgu": inputs["w_gate_up"][0], "wd": inputs["w_down"][0], "wpg": inputs["w_ple_gate"][0],
        "wpp": inputs["w_ple_proj"][0],
        "g1": gain_layout(inputs["ln1_g"][0]), "g2": gain_layout(inputs["ln2_g"][0]),
        "g3": gain_layout(inputs["ln3_g"][0]),
    }
    m.update(host_consts())
    if e_att is not None:
        m["e_att"] = e_att
        m["e_yb"] = e_yb
    return m


MODE = "fused"


def kernel(**inputs):
    inputs = {k: np.asarray(v) for k, v in inputs.items()}
    x = inputs["x"]
    out = np.zeros(x.shape, np.float32)
    if MODE == "two":
        nc1 = build_program("p1", NTOKW=8192, OWN0=0, NG=1)
        maps1 = []
        for core in range(8):
            b, g = core // 4, core % 4
            maps1.append(p1_inputs(inputs, b, [g], x[b], np.ones(8192, np.float32), np.zeros(32, np.float32), 8192))
        r1 = run_bass_kernel_spmd(nc1, maps1, core_ids=list(range(8))).results
        nc2 = build_program("p2")
        maps2 = []
        for core in range(8):
            b, t = core // 4, core % 4
            sl = slice(t * 2048, (t + 1) * 2048)
            ea = np.stack([np.asarray(r1[b * 4 + g]["e_att"])[0][:, :, sl] for g in range(4)])
            ey = np.stack([np.asarray(r1[b * 4 + g]["e_yb"])[0][:, :, sl] for g in range(4)])
            maps2.append(p2_inputs(inputs, core, np.ascontiguousarray(ea), np.ascontiguousarray(ey)))
        r2 = run_bass_kernel_spmd(nc2, maps2, core_ids=list(range(8))).results
        for core in range(8):
            b, t = core // 4, core % 4
            out[b, t * 2048:(t + 1) * 2048] = np.asarray(r2[core]["out"])
        return out
    nc = build_program("fused", NTOKW=8192, OWN0=6144, NG=4)
    maps = []
    for core in range(8):
        b, t = core // 4, core % 4
        npad = (3 - t) * 2048
        xw = np.concatenate([np.zeros((npad, 1024), np.float32), x[b, :(t + 1) * 2048]], 0)
        tokmask = np.concatenate([np.zeros(npad, np.float32), np.ones(8192 - npad, np.float32)])
        blkbias = np.where(np.arange(32) < npad // 256, NEG, 0.0).astype(np.float32)
        m = p1_inputs(inputs, b, [0, 1, 2, 3], xw, tokmask, blkbias, 8192)
        m.update(p2_inputs(inputs, core))
        maps.append(m)
    r = run_bass_kernel_spmd(nc, maps, core_ids=list(range(8))).results
    for core in range(8):
        b, t = core // 4, core % 4
        out[b, t * 2048:(t + 1) * 2048] = np.asarray(r[core]["out"])
    return out
```

```python
import numpy as np
from contextlib import ExitStack
import ml_dtypes
import concourse.bass as bass
import concourse.mybir as mybir
from concourse.bass_utils import run_bass_kernel_spmd

F32 = mybir.dt.float32
BF16 = mybir.dt.bfloat16
AF = mybir.ActivationFunctionType
ALU = mybir.AluOpType
AX = mybir.AxisListType

EPS = 1e-6
NEG = -30000.0
SAME_ENGINE_SYNC = True
STAGES = {}


class T:
    __slots__ = ("ap", "w", "r", "name", "excl")

    def __init__(self, ap=None, name="", excl=False):
        self.ap = ap
        self.w = None
        self.r = []
        self.name = name
        self.excl = excl

    def __getitem__(self, k):
        return self.ap[k]


class TV(T):
    __slots__ = ("parent",)

    def __init__(self, parent, ap):
        self.parent = parent
        self.ap = ap
        self.name = parent.name
        self.excl = parent.excl

    @property
    def w(self):
        return self.parent.w

    @w.setter
    def w(self, v):
        self.parent.w = v

    @property
    def r(self):
        return self.parent.r

    @r.setter
    def r(self, v):
        self.parent.r = v


class Op:
    __slots__ = ("eng", "fn", "deps", "dma", "inc", "sem", "ticket", "idx", "prev_same_sem", "vc")

    def __init__(self, eng, fn, dma):
        self.eng = eng
        self.fn = fn
        self.dma = dma
        self.deps = []
        self.inc = False
        self.sem = None
        self.ticket = 0
        self.prev_same_sem = None


class Prog:
    ENGS = ("pe", "act", "dve", "pool", "sp")
    NDS = 8

    def __init__(self, nc):
        self.nc = nc
        self.ops = {e: [] for e in self.ENGS}
        self.all = []
        self.bar = {}

    def op(self, eng, fn, reads=(), writes=(), dma=False):
        o = Op(eng, fn, dma)
        deps = []
        for t in reads:
            if t.w is not None:
                deps.append(t.w)
            if t.excl:
                deps.extend(x for x in t.r if x.eng != eng)
        for t in writes:
            if t.w is not None:
                deps.append(t.w)
            deps.extend(t.r)
        b = self.bar.pop(eng, None)
        if b:
            deps.extend(b)
        seen = set()
        for d in deps:
            if d is o or id(d) in seen:
                continue
            seen.add(id(d))
            if (not d.dma) and d.eng == eng and (eng == "pe" or not SAME_ENGINE_SYNC):
                continue
            o.deps.append(d)
            d.inc = True
        for t in reads:
            if dma:
                t.r.append(o)
            else:
                t.r = [x for x in t.r if x.dma or x.eng != eng] + [o]
        for t in writes:
            t.w = o
            t.r = []
        if dma:
            o.inc = True
        self.ops[eng].append(o)
        self.all.append(o)
        return o

    def barrier(self):
        last = []
        for e in self.ENGS:
            ops = self.ops[e]
            if ops:
                last.append(ops[-1])
            last.extend([o for o in ops if o.dma][-self.NDS:])
        for e in self.ENGS:
            self.bar[e] = list(last)

    def emit(self, final_ops):
        nc = self.nc
        with ExitStack() as es:
            SEM_CAP = 1000
            nsem = {e: sum(1 for o in self.ops[e] if o.inc and not o.dma) // SEM_CAP + 1 for e in ("pe", "act", "dve", "pool")}
            csem = {e: [es.enter_context(nc.semaphore("cs_%s%d" % (e, i))) for i in range(nsem[e])]
                    for e in ("pe", "act", "dve", "pool")}
            dsem = {e: [es.enter_context(nc.semaphore("ds_%s%d" % (e, i))) for i in range(self.NDS)]
                    for e in self.ENGS}
            ccount = {e: 0 for e in self.ENGS}
            dcount = {e: [0] * self.NDS for e in self.ENGS}
            drr = {e: 0 for e in self.ENGS}
            dlast = {e: [None] * self.NDS for e in self.ENGS}
            for e in self.ENGS:
                for o in self.ops[e]:
                    if o.dma:
                        k = drr[e] % self.NDS
                        drr[e] += 1
                        dcount[e][k] += 16
                        o.sem = dsem[e][k]
                        o.ticket = dcount[e][k]
                        o.prev_same_sem = dlast[e][k]
                        dlast[e][k] = o
                    elif o.inc:
                        o.sem = csem[e][ccount[e] // SEM_CAP]
                        o.ticket = ccount[e] % SEM_CAP + 1
                        ccount[e] += 1

            know = {e: {} for e in self.ENGS}
            plan = {}
            for o in self.all:
                kn = know[o.eng]
                waits = []
                dl = list(o.deps)
                if o.dma and o.prev_same_sem is not None:
                    dl.append(o.prev_same_sem)
                for d in dl:
                    key = id(d.sem)
                    if kn.get(key, 0) < d.ticket:
                        waits.append(d)
                        for kk, vv in d.vc.items():
                            if kn.get(kk, 0) < vv:
                                kn[kk] = vv
                plan[id(o)] = waits
                o.vc = dict(kn)
                if o.sem is not None:
                    o.vc[id(o.sem)] = o.ticket
                    if not o.dma:
                        kn[id(o.sem)] = max(kn.get(id(o.sem), 0), 0)

            def run(ename, eng):
                for o in self.ops[ename]:
                    for d in plan[id(o)]:
                        eng.wait_ge(d.sem, d.ticket)
                    ins = o.fn(eng)
                    if o.sem is not None:
                        ins.then_inc(o.sem, 16 if o.dma else 1)
                if ename == "sp":
                    kn = know["sp"]
                    for d in final_ops:
                        if kn.get(id(d.sem), 0) < d.ticket:
                            eng.wait_ge(d.sem, d.ticket)
                            kn[id(d.sem)] = d.ticket

            with nc.Block() as block:
                @block.tensor
                def _(eng):
                    run("pe", eng)

                @block.scalar
                def _(eng):
                    run("act", eng)

                @block.vector
                def _(eng):
                    run("dve", eng)

                @block.gpsimd
                def _(eng):
                    run("pool", eng)

                @block.sync
                def _(eng):
                    run("sp", eng)


class Ctx:
    N = 0

    def __init__(self, nc, es, P):
        self.nc, self.es, self.P = nc, es, P
        self.n = 0

    def sb(self, shape, dt, name=None):
        Ctx.N += 1
        h = self.es.enter_context(self.nc.sbuf_tensor("%s_%d" % (name or "sb", Ctx.N), list(shape), dt))
        return h

    def sbT(self, shape, dt, name=None):
        h = self.sb(shape, dt, name)
        return T(h[tuple(slice(None) for _ in shape)], name or "")

    def dram(self, name, shape, dt, kind):
        return self.nc.dram_tensor(name, list(shape), dt, kind=kind).ap()


class Ring:
    def __init__(self, tiles):
        self.tiles = tiles
        self.i = 0

    def next(self):
        t = self.tiles[self.i % len(self.tiles)]
        self.i += 1
        return t


P2W = {
    "wg": (1024, 2048), "woa": (1024, 1024), "wos": (2048, 1024), "wout": (1024, 1024),
    "wgu": (1024, 5632), "wd": (2816, 1024), "wpg": (1024, 1024), "wpp": (256, 1024),
}
P2W_GAIN = {"wg": "g1", "wgu": "g2", "wpg": "g3"}
WB_COLS = 256


def wblocks(name):
    K, N = P2W[name]
    nkc = K // 128
    kbs = []
    k0 = 0
    while k0 < nkc:
        nk = min(8, nkc - k0)
        kbs.append((k0, nk))
        k0 += nk
    return kbs, N // WB_COLS


def build_precast(P, C, din, wscr, wT, consts):
    nc = P.nc
    stg = Ring([C.sbT([128, 8, WB_COLS], F32, "pc_stg") for _ in range(2)])
    wbf = Ring([C.sbT([128, 8, WB_COLS], BF16, "pc_bf") for _ in range(2)])
    for name in P2W:
        kbs, ncb = wblocks(name)
        src = din[name]
        dst = wscr[name]
        gain = consts.get(P2W_GAIN.get(name))
        for cb in range(ncb):
            for (k0, nk) in kbs:
                s, w = stg.next(), wbf.next()
                sv = src[k0 * 128:(k0 + nk) * 128, cb * WB_COLS:(cb + 1) * WB_COLS].rearrange("(k p) n -> p k n", p=128)
                dv = dst[k0 * 128:(k0 + nk) * 128, cb * WB_COLS:(cb + 1) * WB_COLS].rearrange("(k p) n -> p k n", p=128)
                P.op("sp", lambda e, s=s, sv=sv, nk=nk: e.dma_start(out=s[:, 0:nk, :], in_=sv), writes=[s], dma=True)
                if gain is not None:
                    gv = gain[:, k0:k0 + nk].unsqueeze(2).to_broadcast([128, nk, WB_COLS])
                    P.op("pool", lambda e, s=s, w=w, gv=gv, nk=nk: e.tensor_tensor(
                        out=w[:, 0:nk, :], in0=s[:, 0:nk, :], in1=gv, op=ALU.mult), reads=[s, gain], writes=[w])
                else:
                    P.op("pool", lambda e, s=s, w=w, nk=nk: e.tensor_copy(out=w[:, 0:nk, :], in_=s[:, 0:nk, :]),
                         reads=[s], writes=[w])
                t = T(None, "wscr")
                wT[(name, cb, k0)] = t
                P.op("pool", lambda e, w=w, dv=dv, nk=nk: e.dma_start(out=dv, in_=w[:, 0:nk, :]),
                     reads=[w], writes=[t], dma=True)


def build_phase2(P, C, din, wscr, wT, consts, exch, dout, NTOK=2048, TT=1024):
    nc = P.nc
    NTG = TT // 512
    NSUB = TT // 128
    ident_b = consts["ident_b"]
    ones_b = consts["ones_b"]
    xT = [C.sbT([128, TT], F32, "xT") for _ in range(8)]
    hT = [C.sbT([128, TT], BF16, "hT") for _ in range(8)]
    big = C.sb([128, 24, TT], BF16, "big")
    attT = [T(big[:, c, :], "att") for c in range(8)]
    ybT = [T(big[:, 8 + c, :], "yb") for c in range(16)]
    mg = [C.sbT([128, TT], BF16, "mg") for _ in range(8)]
    pT = [C.sbT([128, TT], BF16, "pT") for _ in range(2)]
    xin = Ring([C.sbT([128, 1024], F32, "xin") for _ in range(2)])
    xhi = Ring([C.sbT([128, 1024], BF16, "xhi") for _ in range(2)])
    xlo = Ring([C.sbT([128, 1024], BF16, "xlo") for _ in range(2)])
    xres = Ring([C.sbT([128, 1024], F32, "xres") for _ in range(1)])
    pin = Ring([C.sbT([128, 256], F32, "pin") for _ in range(2)])
    pinb = Ring([C.sbT([128, 256], BF16, "pinb") for _ in range(2)])
    wring = Ring([C.sbT([128, 8, WB_COLS], BF16, "wr") for _ in range(8)])
    tmpf = Ring([C.sbT([128, 512], F32, "tmpf") for _ in range(8)])
    sqb = Ring([C.sbT([128, 512], BF16, "sqb") for _ in range(3)])
    rstd = [C.sbT([128, 512], F32, "rstd") for _ in range(NTG)]
    psum = Ring(consts["psum"])

    def load_w(name, cb, k0, nk):
        w = wring.next()
        sv = wscr[name][k0 * 128:(k0 + nk) * 128, cb * WB_COLS:(cb + 1) * WB_COLS].rearrange("(k p) n -> p k n", p=128)
        P.op("sp", lambda e, w=w, sv=sv, nk=nk: e.dma_start(out=w[:, 0:nk, :], in_=sv),
             reads=[wT[(name, cb, k0)]], writes=[w], dma=True)
        return w

    def mm_group(ps, wlist, X, tg, oc_in_blk):
        n = sum(nk for _, _, nk in wlist)
        i = 0
        for (w, k0, nk) in wlist:
            for k in range(nk):
                st, sp_ = (i == 0), (i == n - 1)
                xk = X[k0 + k]
                P.op("pe", lambda e, ps=ps, w=w, k=k, xk=xk, st=st, sp_=sp_: e.matmul(
                    ps[:, :], lhsT=w[:, k, oc_in_blk * 128:(oc_in_blk + 1) * 128],
                    rhs=xk[:, tg * 512:(tg + 1) * 512], start=st, stop=sp_),
                    reads=[w, xk], writes=[ps])
                i += 1

    def rmsnorm():
        for tg in range(NTG):
            ps = psum.next()
            for kc in range(8):
                sq = sqb.next()
                P.op("act", lambda e, sq=sq, kc=kc, tg=tg: e.activation(
                    out=sq[:, :], in_=xT[kc][:, tg * 512:(tg + 1) * 512], func=AF.Square), reads=[xT[kc]], writes=[sq])
                P.op("pe", lambda e, ps=ps, sq=sq, kc=kc: e.matmul(
                    ps[:, :], lhsT=ones_b[:, :], rhs=sq[:, :], start=(kc == 0), stop=(kc == 7)),
                    reads=[sq, ones_b], writes=[ps])
            r = rstd[tg]
            P.op("act", lambda e, r=r, ps=ps: e.activation(
                out=r[:, :], in_=ps[:, :], func=AF.Sqrt, scale=1.0 / 1024.0, bias=EPS), reads=[ps], writes=[r])
            P.op("dve", lambda e, r=r: e.reciprocal(out=r[:, :], in_=r[:, :]), reads=[r], writes=[r])
        for kc in range(8):
            for tg in range(NTG):
                P.op("dve", lambda e, kc=kc, tg=tg: e.tensor_tensor(
                    out=hT[kc][:, tg * 512:(tg + 1) * 512], in0=xT[kc][:, tg * 512:(tg + 1) * 512],
                    in1=rstd[tg][:, :], op=ALU.mult), reads=[xT[kc], rstd[tg]], writes=[hT[kc]])

    out_ops = []
    for tt in range(NTOK // TT):
        t0 = tt * TT
        for s in range(NSUB):
            xi = xin.next()
            P.op("sp", lambda e, xi=xi, s=s, t0=t0: e.dma_start(out=xi[:, :], in_=din["x2"][t0 + s * 128:t0 + (s + 1) * 128, :]),
                 writes=[xi], dma=True)
            xh, xl, xr = xhi.next(), xlo.next(), xres.next()
            P.op("act", lambda e, xi=xi, xh=xh: e.copy(out=xh[:, :], in_=xi[:, :]), reads=[xi], writes=[xh])
            P.op("dve", lambda e, xi=xi, xh=xh, xr=xr: e.tensor_tensor(out=xr[:, :], in0=xi[:, :], in1=xh[:, :], op=ALU.subtract),
                 reads=[xi, xh], writes=[xr])
            P.op("pool", lambda e, xr=xr, xl=xl: e.tensor_copy(out=xl[:, :], in_=xr[:, :]), reads=[xr], writes=[xl])
            for half in range(2):
                ps = psum.next()
                for j in range(4):
                    kc = half * 4 + j
                    P.op("pe", lambda e, ps=ps, xh=xh, kc=kc, j=j: e.matmul(
                        ps[:, j * 128:(j + 1) * 128], lhsT=xh[:, kc * 128:(kc + 1) * 128], rhs=ident_b[:, :], start=True, stop=False),
                        reads=[xh, ident_b], writes=[ps])
                    P.op("pe", lambda e, ps=ps, xl=xl, kc=kc, j=j: e.matmul(
                        ps[:, j * 128:(j + 1) * 128], lhsT=xl[:, kc * 128:(kc + 1) * 128], rhs=ident_b[:, :], start=False, stop=True),
                        reads=[xl, ident_b], writes=[ps])
                for j in range(4):
                    kc = half * 4 + j
                    eng = "act" if j % 2 == 0 else "dve"
                    if eng == "act":
                        P.op("act", lambda e, ps=ps, kc=kc, j=j, s=s: e.copy(
                            out=xT[kc][:, s * 128:(s + 1) * 128], in_=ps[:, j * 128:(j + 1) * 128]),
                            reads=[ps], writes=[xT[kc]])
                    else:
                        P.op("dve", lambda e, ps=ps, kc=kc, j=j, s=s: e.tensor_copy(
                            out=xT[kc][:, s * 128:(s + 1) * 128], in_=ps[:, j * 128:(j + 1) * 128]),
                            reads=[ps], writes=[xT[kc]])
            pi, pb = pin.next(), pinb.next()
            P.op("sp", lambda e, pi=pi, s=s, t0=t0: e.dma_start(out=pi[:, :], in_=din["p2"][t0 + s * 128:t0 + (s + 1) * 128, :]),
                 writes=[pi], dma=True)
            P.op("pool", lambda e, pi=pi, pb=pb: e.tensor_copy(out=pb[:, :], in_=pi[:, :]), reads=[pi], writes=[pb])
            ps = psum.next()
            psb = ps.ap.bitcast(BF16)
            for j in range(2):
                P.op("pe", lambda e, psb=psb, pb=pb, j=j: e.transpose(
                    psb[:, j * 128:(j + 1) * 128], pb[:, j * 128:(j + 1) * 128], ident_b[:, :]),
                    reads=[pb, ident_b], writes=[ps])
            for j in range(2):
                P.op("act", lambda e, psb=psb, j=j, s=s: e.copy(
                    out=pT[j][:, s * 128:(s + 1) * 128], in_=psb[:, j * 128:(j + 1) * 128]), reads=[ps], writes=[pT[j]])
        for g in range(4):
            for c in range(2):
                tl = attT[2 * g + c]
                P.op("sp", lambda e, tl=tl, g=g, c=c, t0=t0: e.dma_start(out=tl[:, :], in_=exch["att"][g, c, :, t0:t0 + TT]),
                     reads=[exch["att_T"]], writes=[tl], dma=True)
            for c in range(4):
                tl = ybT[4 * g + c]
                P.op("sp", lambda e, tl=tl, g=g, c=c, t0=t0: e.dma_start(out=tl[:, :], in_=exch["yb"][g, c, :, t0:t0 + TT]),
                     reads=[exch["yb_T"]], writes=[tl], dma=True)
        p2s = STAGES.get('p2s', 'BXCD')
        if 'B' in p2s:
            rmsnorm()
        for j in (range(4) if 'B' in p2s else []):
            wga = load_w("wg", j, 0, 8)
            wgb = load_w("wg", 4 + j, 0, 8)
            wa = load_w("woa", j, 0, 8)
            ws0 = load_w("wos", j, 0, 8)
            ws1 = load_w("wos", j, 8, 8)
            for o2 in range(2):
                oc = 2 * j + o2
                for tg in range(NTG):
                    pga, pgb, pya, pyb = psum.next(), psum.next(), psum.next(), psum.next()
                    mm_group(pga, [(wga, 0, 8)], hT, tg, o2)
                    mm_group(pgb, [(wgb, 0, 8)], hT, tg, o2)
                    mm_group(pya, [(wa, 0, 8)], attT, tg, o2)
                    mm_group(pyb, [(ws0, 0, 8), (ws1, 8, 8)], ybT, tg, o2)
                    sa, sb_, m1, m2 = tmpf.next(), tmpf.next(), tmpf.next(), tmpf.next()
                    P.op("act", lambda e, sa=sa, pga=pga: e.activation(out=sa[:, :], in_=pga[:, :], func=AF.Sigmoid),
                         reads=[pga], writes=[sa])
                    P.op("act", lambda e, sb_=sb_, pgb=pgb: e.activation(out=sb_[:, :], in_=pgb[:, :], func=AF.Sigmoid),
                         reads=[pgb], writes=[sb_])
                    P.op("dve", lambda e, m1=m1, sa=sa, pya=pya: e.tensor_tensor(
                        out=m1[:, :], in0=sa[:, :], in1=pya[:, :], op=ALU.mult), reads=[sa, pya], writes=[m1])
                    P.op("dve", lambda e, m2=m2, sb_=sb_, pyb=pyb: e.tensor_tensor(
                        out=m2[:, :], in0=sb_[:, :], in1=pyb[:, :], op=ALU.mult), reads=[sb_, pyb], writes=[m2])
                    P.op("pool", lambda e, m1=m1, m2=m2, oc=oc, tg=tg: e.tensor_tensor(
                        out=mg[oc][:, tg * 512:(tg + 1) * 512], in0=m1[:, :], in1=m2[:, :], op=ALU.add),
                        reads=[m1, m2], writes=[mg[oc]])
        for j in (range(4) if 'X' in p2s else []):
            w = load_w("wout", j, 0, 8)
            for o2 in range(2):
                oc = 2 * j + o2
                for tg in range(NTG):
                    ps = psum.next()
                    mm_group(ps, [(w, 0, 8)], mg, tg, o2)
                    P.op("dve", lambda e, ps=ps, oc=oc, tg=tg: e.tensor_tensor(
                        out=xT[oc][:, tg * 512:(tg + 1) * 512], in0=ps[:, :], in1=xT[oc][:, tg * 512:(tg + 1) * 512],
                        op=ALU.add), reads=[ps, xT[oc]], writes=[xT[oc]])
        if 'C' in p2s:
            rmsnorm()
        actT = [T(big[:, f, :], "act") for f in range(22)]
        for f in range(22):
            old = attT[f] if f < 8 else ybT[f - 8]
            actT[f].w, actT[f].r = old.w, old.r
        for j in (range(11) if 'C' in p2s else []):
            wgt = load_w("wgu", j, 0, 8)
            wup = load_w("wgu", 11 + j, 0, 8)
            for o2 in range(2):
                f = 2 * j + o2
                for tg in range(NTG):
                    pg, pu = psum.next(), psum.next()
                    mm_group(pg, [(wgt, 0, 8)], hT, tg, o2)
                    mm_group(pu, [(wup, 0, 8)], hT, tg, o2)
                    sg = tmpf.next()
                    P.op("act", lambda e, sg=sg, pg=pg: e.activation(out=sg[:, :], in_=pg[:, :], func=AF.Silu),
                         reads=[pg], writes=[sg])
                    P.op("dve", lambda e, sg=sg, pu=pu, f=f, tg=tg: e.tensor_tensor(
                        out=actT[f][:, tg * 512:(tg + 1) * 512], in0=sg[:, :], in1=pu[:, :], op=ALU.mult),
                        reads=[sg, pu], writes=[actT[f]])
        for j in (range(4) if 'C' in p2s else []):
            w0 = load_w("wd", j, 0, 8)
            w1 = load_w("wd", j, 8, 8)
            w2 = load_w("wd", j, 16, 6)
            for o2 in range(2):
                oc = 2 * j + o2
                for tg in range(NTG):
                    ps = psum.next()
                    mm_group(ps, [(w0, 0, 8), (w1, 8, 8), (w2, 16, 6)], actT, tg, o2)
                    P.op("dve", lambda e, ps=ps, oc=oc, tg=tg: e.tensor_tensor(
                        out=xT[oc][:, tg * 512:(tg + 1) * 512], in0=ps[:, :], in1=xT[oc][:, tg * 512:(tg + 1) * 512],
                        op=ALU.add), reads=[ps, xT[oc]], writes=[xT[oc]])
        for f in range(22):
            old = attT[f] if f < 8 else ybT[f - 8]
            old.w, old.r = actT[f].w, actT[f].r
        if 'D' in p2s:
            rmsnorm()
        for j in (range(4) if 'D' in p2s else []):
            wpg = load_w("wpg", j, 0, 8)
            wpp = load_w("wpp", j, 0, 2)
            for o2 in range(2):
                oc = 2 * j + o2
                for tg in range(NTG):
                    pg, pp = psum.next(), psum.next()
                    mm_group(pg, [(wpg, 0, 8)], hT, tg, o2)
                    mm_group(pp, [(wpp, 0, 2)], pT, tg, o2)
                    sg, m = tmpf.next(), tmpf.next()
                    P.op("act", lambda e, sg=sg, pg=pg: e.activation(out=sg[:, :], in_=pg[:, :], func=AF.Sigmoid),
                         reads=[pg], writes=[sg])
                    P.op("dve", lambda e, m=m, sg=sg, pp=pp: e.tensor_tensor(
                        out=m[:, :], in0=sg[:, :], in1=pp[:, :], op=ALU.mult), reads=[sg, pp], writes=[m])
                    P.op("dve", lambda e, m=m, oc=oc, tg=tg: e.tensor_tensor(
                        out=xT[oc][:, tg * 512:(tg + 1) * 512], in0=m[:, :], in1=xT[oc][:, tg * 512:(tg + 1) * 512],
                        op=ALU.add), reads=[m, xT[oc]], writes=[xT[oc]])
        for kc in range(8):
            P.op("act", lambda e, kc=kc: e.copy(out=hT[kc][:, :], in_=xT[kc][:, :]), reads=[xT[kc], hT[kc]], writes=[hT[kc]])
            P.op("dve", lambda e, kc=kc: e.tensor_tensor(out=xT[kc][:, :], in0=xT[kc][:, :], in1=hT[kc][:, :], op=ALU.subtract),
                 reads=[xT[kc], hT[kc]], writes=[xT[kc]])
            P.op("pool", lambda e, kc=kc: e.tensor_copy(out=mg[kc][:, :], in_=xT[kc][:, :]), reads=[xT[kc], mg[kc]], writes=[mg[kc]])
        for s in range(NSUB):
            xo = xin.next()
            for half in range(2):
                ps = psum.next()
                for j in range(4):
                    kc = half * 4 + j
                    P.op("pe", lambda e, ps=ps, kc=kc, j=j, s=s: e.matmul(
                        ps[:, j * 128:(j + 1) * 128], lhsT=hT[kc][:, s * 128:(s + 1) * 128], rhs=ident_b[:, :], start=True, stop=False),
                        reads=[hT[kc], ident_b], writes=[ps])
                    P.op("pe", lambda e, ps=ps, kc=kc, j=j, s=s: e.matmul(
                        ps[:, j * 128:(j + 1) * 128], lhsT=mg[kc][:, s * 128:(s + 1) * 128], rhs=ident_b[:, :], start=False, stop=True),
                        reads=[mg[kc], ident_b], writes=[ps])
                if half == 0:
                    P.op("act", lambda e, ps=ps, xo=xo: e.copy(out=xo[:, 0:512], in_=ps[:, :]), reads=[ps], writes=[xo])
                else:
                    P.op("dve", lambda e, ps=ps, xo=xo: e.tensor_copy(out=xo[:, 512:1024], in_=ps[:, :]),
                         reads=[ps, xo], writes=[xo])
            out_ops.append(P.op("pool", lambda e, xo=xo, s=s, t0=t0: e.dma_start(
                out=dout[t0 + s * 128:t0 + (s + 1) * 128, :], in_=xo[:, :]), reads=[xo], dma=True))
    return out_ops


W1A = 1288
W1B = 768


def load_cast_weight(P, C, src, w, ncols, gain, stg):
    c0 = 0
    while c0 < ncols:
        n = min(WB_COLS, ncols - c0)
        s = stg.next()
        sv = src[:, c0:c0 + n].rearrange("(k p) n -> p k n", p=128)
        P.op("sp", lambda e, s=s, sv=sv, n=n: e.dma_start(out=s[:, :, 0:n], in_=sv), writes=[s], dma=True)
        gv = gain[:, 0:8].unsqueeze(2).to_broadcast([128, 8, n])
        P.op("pool", lambda e, s=s, gv=gv, n=n, c0=c0: e.tensor_tensor(
            out=w[:, :, c0:c0 + n], in0=s[:, :, 0:n], in1=gv, op=ALU.mult), reads=[s, gain], writes=[w])
        c0 += n


def build_prologue(P, C, din, cst, hT_scr, hT_T, NTOKW):
    xin = Ring([C.sbT([128, 1024], F32, "pxin") for _ in range(3)])
    junk = C.sbT([128, 1024], BF16, "pjunk")
    hb = Ring([C.sbT([128, 1024], BF16, "phb") for _ in range(2)])
    ssr = Ring([C.sbT([128, 2], F32, "pss") for _ in range(4)])
    hst = Ring([C.sbT([128, 8, 512], BF16, "phst") for _ in range(2)])
    psum = cst["psring"]
    ident_b = cst["ident_b"]
    for m in range(NTOKW // 512):
        ht = hst.next()
        for s in range(4):
            t0 = m * 512 + s * 128
            xi, ss, h = xin.next(), ssr.next(), hb.next()
            P.op("sp", lambda e, xi=xi, t0=t0: e.dma_start(out=xi[:, :], in_=din["xw"][t0:t0 + 128, :]), writes=[xi], dma=True)
            P.op("act", lambda e, xi=xi, ss=ss: e.activation(out=junk[:, :], in_=xi[:, :], func=AF.Square, accum_out=ss[:, 0:1]),
                 reads=[xi], writes=[junk, ss])
            P.op("act", lambda e, ss=ss: e.activation(out=ss[:, 1:2], in_=ss[:, 0:1], func=AF.Sqrt, scale=1.0 / 1024.0, bias=EPS),
                 reads=[ss], writes=[ss])
            P.op("dve", lambda e, ss=ss: e.reciprocal(out=ss[:, 1:2], in_=ss[:, 1:2]), reads=[ss], writes=[ss])
            P.op("dve", lambda e, xi=xi, ss=ss, h=h: e.tensor_scalar(
                out=h[:, :], in0=xi[:, :], scalar1=ss[:, 1:2], scalar2=None, op0=ALU.mult), reads=[xi, ss], writes=[h])
            ps = psum.next()
            psb = ps.ap.bitcast(BF16)
            for kc in range(8):
                P.op("pe", lambda e, psb=psb, h=h, kc=kc: e.transpose(
                    psb[:, kc * 128:(kc + 1) * 128], h[:, kc * 128:(kc + 1) * 128], ident_b[:, :]),
                    reads=[h, ident_b], writes=[ps])
            P.op("act", lambda e, psb=psb, ht=ht, s=s: e.copy(
                out=ht[:, 0:4, s * 128:(s + 1) * 128], in_=psb[:, 0:512].rearrange("p (k n) -> p k n", k=4)),
                reads=[ps], writes=[ht])
            P.op("dve", lambda e, psb=psb, ht=ht, s=s: e.tensor_copy(
                out=ht[:, 4:8, s * 128:(s + 1) * 128], in_=psb[:, 512:1024].rearrange("p (k n) -> p k n", k=4)),
                reads=[ps, ht], writes=[ht])
        t = T(None, "hTscr")
        hT_T.append(t)
        P.op("pool", lambda e, ht=ht, m=m: e.dma_start(out=hT_scr[:, :, m * 512:(m + 1) * 512], in_=ht[:, :, :]),
             reads=[ht], writes=[t], dma=True)


def build_p1a(P, C, din, g, cst, hT_scr, hT_T, e_yb, e_yb_T, NTOKW, OWN0):
    psum = cst["psring"]
    ident_b, ones_b, U, T1 = cst["ident_b"], cst["ones_b"], cst["U"], cst["T1"]
    stg = Ring([C.sbT([128, 8, WB_COLS], F32, "a_stg") for _ in range(2)])
    w1a = C.sbT([128, 8, W1A], BF16, "w1a")
    load_cast_weight(P, C, din["w1a"][g], w1a, W1A, cst["g1"], stg)
    small = {}
    for nm, shp in (("convw", [128, 6, 4]), ("convb", [128, 6]), ("dtb", [128, 8]), ("alog", [128, 8]),
                    ("dsk", [128, 8]), ("sng", [128, 512])):
        t = C.sbT(shp, F32, "a_" + nm)
        P.op("sp", lambda e, t=t, nm=nm: e.dma_start(out=t.ap, in_=din[nm][g]), writes=[t], dma=True)
        small[nm] = t
    cw, cb, dtb, alog, dsk, sng = (small[k] for k in ("convw", "convb", "dtb", "alog", "dsk", "sng"))
    tokmask = cst["tokmask"]
    Abc = C.sbT([128, 8], F32, "Abc")
    P.op("act", lambda e: e.activation(out=Abc[:, :], in_=alog[:, :], func=AF.Exp), reads=[alog], writes=[Abc])
    P.op("dve", lambda e: e.tensor_scalar(out=Abc[:, :], in0=Abc[:, :], scalar1=-1.0, scalar2=None, op0=ALU.mult),
         reads=[Abc], writes=[Abc])
    dtb4 = C.sbT([128, 32], F32, "dtb4")
    Abc4 = C.sbT([128, 32], F32, "Abc4")
    for s_ in range(4):
        P.op("dve", lambda e, s_=s_: e.tensor_copy(out=dtb4[:, 8 * s_:8 * s_ + 8], in_=dtb[:, :]), reads=[dtb, dtb4], writes=[dtb4])
        P.op("dve", lambda e, s_=s_: e.tensor_copy(out=Abc4[:, 8 * s_:8 * s_ + 8], in_=Abc[:, :]), reads=[Abc, Abc4], writes=[Abc4])
    s32 = Ring([C.sbT([128, 32], F32, "a_s32") for _ in range(21)])
    a3mr = Ring([C.sbT([128, 3, 32], BF16, "a_a3m") for _ in range(3)])
    s48 = Ring([C.sbT([128, 4, 8], F32, "a_s48") for _ in range(12)])
    S = C.sbT([128, 512], F32, "S")
    Sbf = C.sbT([128, 512], BF16, "Sbf")
    xbc = C.sbT([128, 6, 515], F32, "xbc")
    P.op("pool", lambda e: e.memset(S[:, :], 0.0), writes=[S])
    P.op("pool", lambda e: e.memset(Sbf[:, :], 0.0), writes=[Sbf])
    P.op("pool", lambda e: e.memset(xbc[:, :, 0:3], 0.0), writes=[xbc])
    hring = Ring([C.sbT([128, 8, 512], BF16, "a_hT") for _ in range(2)])
    cring = Ring([C.sbT([128, 6, 512], BF16, "a_co") for _ in range(2)])
    accr = Ring([C.sbT([128, 512], F32, "a_acc") for _ in range(5)])
    f512 = Ring([C.sbT([128, 512], F32, "a_f512") for _ in range(6)])
    szr = Ring([C.sbT([128, 512], F32, "a_sz") for _ in range(5)])
    b512 = Ring([C.sbT([128, 512], BF16, "a_b512") for _ in range(28)])
    s8 = Ring([C.sbT([128, 8], F32, "a_s8") for _ in range(64)])
    s2 = Ring([C.sbT([128, 2], F32, "a_s2") for _ in range(6)])
    Rr = Ring([C.sbT([128, 3, 8, 128], BF16, "a_R") for _ in range(4)])
    a3r = Ring([C.sbT([128, 3, 8], BF16, "a_a3") for _ in range(6)])
    Lr = Ring([C.sbT([128, 8, 128], BF16, "a_L") for _ in range(5)])
    Mr = Ring([C.sbT([128, 8, 128], BF16, "a_M") for _ in range(5)])
    cbm = Ring([C.sbT([128, 128], BF16, "a_cbm") for _ in range(5)])
    ybst = Ring([C.sbT([128, 4, 512], BF16, "a_ybst") for _ in range(2)])
    junk = C.sbT([128, 512], BF16, "a_junk")
    pss = cst["ps_small"]
    i_eng = 0
    for m in range(NTOKW // 512):
        tok0 = m * 512
        own = tok0 >= OWN0
        hT = hring.next()
        P.op("sp", lambda e, hT=hT, tok0=tok0: e.dma_start(out=hT[:, :, :], in_=hT_scr[:, :, tok0:tok0 + 512]),
             reads=[hT_T[m]], writes=[hT], dma=True)
        for c in range(6):
            ps = psum.next()
            for kc in range(8):
                P.op("pe", lambda e, ps=ps, kc=kc, c=c, hT=hT: e.matmul(
                    ps[:, :], lhsT=w1a[:, kc, c * 128:(c + 1) * 128], rhs=hT[:, kc, :], start=(kc == 0), stop=(kc == 7)),
                    reads=[w1a, hT], writes=[ps])
            if c % 2 == 0:
                P.op("act", lambda e, ps=ps, c=c: e.copy(out=xbc[:, c, 3:515], in_=ps[:, :]), reads=[ps, xbc], writes=[xbc])
            else:
                P.op("dve", lambda e, ps=ps, c=c: e.tensor_copy(out=xbc[:, c, 3:515], in_=ps[:, :]), reads=[ps, xbc], writes=[xbc])
        co = cring.next()
        for c in range(6):
            eng = "pool" if c in (1, 4) else "dve"
            acc = accr.next()
            P.op(eng, lambda e, acc=acc, c=c: e.tensor_scalar(
                out=acc[:, :], in0=xbc[:, c, 0:512], scalar1=cw[:, c, 0:1], scalar2=None, op0=ALU.mult),
                reads=[xbc, cw], writes=[acc])
            for k in range(1, 4):
                if eng == "dve":
                    P.op(eng, lambda e, acc=acc, c=c, k=k: e.scalar_tensor_tensor(
                        out=acc[:, :], in0=xbc[:, c, k:k + 512], scalar=cw[:, c, k:k + 1], in1=acc[:, :],
                        op0=ALU.mult, op1=ALU.add), reads=[xbc, cw, acc], writes=[acc])
                else:
                    tmpc = accr.next()
                    P.op(eng, lambda e, tmpc=tmpc, c=c, k=k: e.tensor_scalar(
                        out=tmpc[:, :], in0=xbc[:, c, k:k + 512], scalar1=cw[:, c, k:k + 1], scalar2=None, op0=ALU.mult),
                        reads=[xbc, cw], writes=[tmpc])
                    P.op(eng, lambda e, tmpc=tmpc, acc=acc: e.tensor_tensor(out=acc[:, :], in0=acc[:, :], in1=tmpc[:, :], op=ALU.add),
                         reads=[acc, tmpc], writes=[acc])
            P.op("act", lambda e, acc=acc, c=c, co=co: e.activation(
                out=co[:, c, :], in_=acc[:, :], func=AF.Silu, bias=cb[:, c:c + 1], scale=1.0), reads=[acc, cb, co], writes=[co])
        P.op("pool", lambda e: e.tensor_copy(out=xbc[:, :, 0:3], in_=xbc[:, :, 512:515]), reads=[xbc], writes=[xbc])
        yst = ybst.next() if own else None
        ctx = {}

        def pre(s, m=m, hT=hT, co=co, own=own):
            sub = slice(s * 128, (s + 1) * 128)
            tile_idx = m * 4 + s
            dt_ = TV(dtm, dtm.ap[:, 8 * s:8 * s + 8])
            a3 = TV(a3m, a3m.ap[:, :, 8 * s:8 * s + 8])
            wst = TV(wstm, wstm.ap[:, s, :])
            cdec = TV(cdecm, cdecm.ap[:, s, :])
            wend = TV(wendm, wendm.ap[:, s, :])
            yield
            pxs = psum.next()
            pxb = pxs.ap.bitcast(BF16)
            for c in range(5):
                P.op("pe", lambda e, pxb=pxb, c=c, co=co, sub=sub: e.transpose(
                    pxb[:, c * 128:(c + 1) * 128], co[:, c, sub], ident_b[:, :]), reads=[co, ident_b], writes=[pxs])
            xs_tm, Btm, xdt, xdtw = b512.next(), b512.next(), b512.next(), b512.next()
            P.op("act", lambda e, pxb=pxb, xs_tm=xs_tm: e.copy(out=xs_tm[:, :], in_=pxb[:, 0:512]), reads=[pxs], writes=[xs_tm])
            P.op("act", lambda e, pxb=pxb, Btm=Btm: e.copy(out=Btm[:, 0:128], in_=pxb[:, 512:640]), reads=[pxs], writes=[Btm])
            P.op("pool", lambda e, xs_tm=xs_tm, xdt=xdt, dt_=dt_: e.tensor_tensor(
                out=xdt[:, :].rearrange("p (h d) -> p h d", h=8), in0=xs_tm[:, :].rearrange("p (h d) -> p h d", h=8),
                in1=dt_[:, :].unsqueeze(2).to_broadcast([128, 8, 64]), op=ALU.mult), reads=[xs_tm, dt_], writes=[xdt])
            P.op("pool", lambda e, xdt=xdt, xdtw=xdtw, wend=wend: e.tensor_tensor(
                out=xdtw[:, :].rearrange("p (h d) -> p h d", h=8), in0=xdt[:, :].rearrange("p (h d) -> p h d", h=8),
                in1=wend[:, :].unsqueeze(2).to_broadcast([128, 8, 64]), op=ALU.mult), reads=[xdt, wend], writes=[xdtw])
            yield
            if own:
                R, L, Mh, cbt = Rr.next(), Lr.next(), Mr.next(), cbm.next()
                for i3 in range(3):
                    P.op("dve" if i3 != 1 else "pool", lambda e, R=R, a3=a3, i3=i3: e.tensor_tensor(
                        out=R[:, i3, :, :], in0=U[:, :].unsqueeze(1).to_broadcast([128, 8, 128]),
                        in1=a3[:, i3, :].unsqueeze(2).to_broadcast([128, 8, 128]), op=ALU.mult), reads=[U, a3, R], writes=[R])
                for hh in range(2):
                    pD = psum.next()
                    for i3 in range(3):
                        P.op("pe", lambda e, pD=pD, R=R, hh=hh, i3=i3: e.matmul(
                            pD[:, :], lhsT=T1[:, :], rhs=R[:, i3, hh * 4:(hh + 1) * 4, :].rearrange("p h l -> p (h l)"),
                            start=(i3 == 0), stop=(i3 == 2)), reads=[T1, R], writes=[pD])
                    P.op("act", lambda e, pD=pD, L=L, hh=hh: e.activation(
                        out=L[:, hh * 4:(hh + 1) * 4, :].rearrange("p h l -> p (h l)"), in_=pD[:, :], func=AF.Exp),
                        reads=[pD, L], writes=[L])
                yield
                pcb = pss["cb%d" % s]
                P.op("pe", lambda e, co=co, sub=sub: e.matmul(
                    pcb[:, :], lhsT=co[:, 4, sub], rhs=co[:, 5, sub], start=True, stop=True), reads=[co], writes=[pcb])
                P.op("dve", lambda e, cbt=cbt: e.tensor_tensor(out=cbt[:, :], in0=pcb[:, :], in1=U[:, :], op=ALU.mult),
                     reads=[pcb, U], writes=[cbt])
                P.op("pool", lambda e, Mh=Mh, L=L, cbt=cbt: e.tensor_tensor(
                    out=Mh[:, :, :], in0=L[:, :, :], in1=cbt[:, :].unsqueeze(1).to_broadcast([128, 8, 128]), op=ALU.mult),
                    reads=[L, cbt], writes=[Mh])
                xsD = b512.next()
                P.op("pool", lambda e, xs_tm=xs_tm, xsD=xsD: e.tensor_tensor(
                    out=xsD[:, :].rearrange("p (h d) -> p h d", h=8), in0=xs_tm[:, :].rearrange("p (h d) -> p h d", h=8),
                    in1=dsk[:, :].unsqueeze(2).to_broadcast([128, 8, 64]), op=ALU.mult), reads=[xs_tm, dsk], writes=[xsD])
                yield
                pz = psum.next()
                for kc in range(8):
                    P.op("pe", lambda e, pz=pz, kc=kc, hT=hT, sub=sub: e.matmul(
                        pz[:, :], lhsT=hT[:, kc, sub], rhs=w1a[:, kc, 768:1280], start=(kc == 0), stop=(kc == 7)),
                        reads=[w1a, hT], writes=[pz])
                sz = szr.next()
                P.op("act", lambda e, pz=pz, sz=sz: e.activation(out=sz[:, :], in_=pz[:, :], func=AF.Silu), reads=[pz], writes=[sz])
            ctx[s] = dict(locals())
            yield

        def seq(s, m=m, hT=hT, co=co, own=own, yst=yst):
            L_ = ctx[s]
            sub = L_["sub"]
            wst, cdec, xdt, xdtw, Btm = L_["wst"], L_["cdec"], L_["xdt"], L_["xdtw"], L_["Btm"]
            if own:
                Mh, xsD, sz = L_["Mh"], L_["xsD"], L_["sz"]
                pyo, py = psum.next(), psum.next()
                P.op("pe", lambda e, pyo=pyo, co=co, sub=sub: e.matmul(
                    pyo[:, :], lhsT=co[:, 5, sub], rhs=Sbf[:, :], start=True, stop=True), reads=[co, Sbf], writes=[pyo])
                P.op("pe", lambda e, py=py, xsD=xsD: e.matmul(py[:, :], lhsT=ident_b[:, :], rhs=xsD[:, :], start=True, stop=False),
                     reads=[ident_b, xsD], writes=[py])
                for h in range(8):
                    P.op("pe", lambda e, py=py, Mh=Mh, xdt=xdt, h=h: e.matmul(
                        py[:, h * 64:(h + 1) * 64], lhsT=Mh[:, h, :], rhs=xdt[:, h * 64:(h + 1) * 64], start=False, stop=(h == 7)),
                        reads=[Mh, xdt], writes=[py])
                y1, y2, y3 = f512.next(), f512.next(), f512.next()
                P.op("dve", lambda e, pyo=pyo, y1=y1, wst=wst: e.tensor_tensor(
                    out=y1[:, :].rearrange("p (h d) -> p h d", h=8), in0=pyo[:, :].rearrange("p (h d) -> p h d", h=8),
                    in1=wst[:, :].unsqueeze(2).to_broadcast([128, 8, 64]), op=ALU.mult), reads=[pyo, wst], writes=[y1])
                P.op("dve", lambda e, y1=y1, y2=y2, py=py: e.tensor_tensor(out=y2[:, :], in0=y1[:, :], in1=py[:, :], op=ALU.add),
                     reads=[y1, py], writes=[y2])
                P.op("pool", lambda e, y2=y2, y3=y3, sz=sz: e.tensor_tensor(out=y3[:, :], in0=y2[:, :], in1=sz[:, :], op=ALU.mult),
                     reads=[y2, sz], writes=[y3])
                ss = s2.next()
                P.op("act", lambda e, y3=y3, ss=ss: e.activation(out=junk[:, :], in_=y3[:, :], func=AF.Square, accum_out=ss[:, 0:1]),
                     reads=[y3], writes=[junk, ss])
                P.op("act", lambda e, ss=ss: e.activation(out=ss[:, 1:2], in_=ss[:, 0:1], func=AF.Sqrt, scale=1.0 / 512.0, bias=EPS),
                     reads=[ss], writes=[ss])
                P.op("dve", lambda e, ss=ss: e.reciprocal(out=ss[:, 1:2], in_=ss[:, 1:2]), reads=[ss], writes=[ss])
                yn = b512.next()
                P.op("dve", lambda e, y3=y3, ss=ss, yn=yn: e.scalar_tensor_tensor(
                    out=yn[:, :], in0=y3[:, :], scalar=ss[:, 1:2], in1=sng[:, :], op0=ALU.mult, op1=ALU.mult),
                    reads=[y3, ss, sng], writes=[yn])
                pyt = psum.next()
                pytb = pyt.ap.bitcast(BF16)
                for c in range(4):
                    P.op("pe", lambda e, pytb=pytb, yn=yn, c=c: e.transpose(
                        pytb[:, c * 128:(c + 1) * 128], yn[:, c * 128:(c + 1) * 128], ident_b[:, :]),
                        reads=[yn, ident_b], writes=[pyt])
                P.op("act", lambda e, pytb=pytb, yst=yst, sub=sub: e.copy(
                    out=yst[:, :, sub], in_=pytb[:, 0:512].rearrange("p (c n) -> p c n", c=4)), reads=[pyt, yst], writes=[yst])
            pst = psum.next()
            P.op("pe", lambda e, pst=pst, Btm=Btm, xdtw=xdtw: e.matmul(
                pst[:, :], lhsT=Btm[:, 0:128], rhs=xdtw[:, :], start=True, stop=True), reads=[Btm, xdtw], writes=[pst])
            P.op("pool", lambda e, cdec=cdec: e.tensor_tensor(
                out=S[:, :].rearrange("p (h d) -> p h d", h=8), in0=S[:, :].rearrange("p (h d) -> p h d", h=8),
                in1=cdec[:, :].unsqueeze(2).to_broadcast([128, 8, 64]), op=ALU.mult), reads=[S, cdec], writes=[S])
            P.op("dve", lambda e, pst=pst: e.tensor_tensor(out=S[:, :], in0=S[:, :], in1=pst[:, :], op=ALU.add),
                 reads=[S, pst], writes=[S])
            P.op("act", lambda e: e.copy(out=Sbf[:, :], in_=S[:, :]), reads=[S, Sbf], writes=[Sbf])

        for s in range(4):
            pdt = pss["dt%d" % s]
            for kc in range(8):
                P.op("pe", lambda e, kc=kc, hT=hT, s=s, pdt=pdt: e.matmul(
                    pdt[:, :], lhsT=hT[:, kc, s * 128:(s + 1) * 128], rhs=w1a[:, kc, 1280:1288], start=(kc == 0), stop=(kc == 7)),
                    reads=[w1a, hT], writes=[pdt])
        pd_all = TV(pss["dt0"].parent, pss["dt0"].parent.ap[:, 0:32])
        dtr, ax, ee, dtm, am, ar1, ar2 = (s32.next() for _ in range(7))
        a3m = a3mr.next()
        P.op("dve", lambda e, dtr=dtr: e.tensor_tensor(out=dtr[:, :], in0=pd_all[:, :], in1=dtb4[:, :], op=ALU.add),
             reads=[pd_all, dtb4], writes=[dtr])
        P.op("act", lambda e, dtr=dtr, ax=ax: e.activation(out=ax[:, :], in_=dtr[:, :], func=AF.Abs), reads=[dtr], writes=[ax])
        P.op("act", lambda e, ax=ax, ee=ee: e.activation(out=ee[:, :], in_=ax[:, :], func=AF.Exp, scale=-1.0), reads=[ax], writes=[ee])
        P.op("act", lambda e, ee=ee: e.activation(out=ee[:, :], in_=ee[:, :], func=AF.Ln, bias=1.0, scale=1.0), reads=[ee], writes=[ee])
        P.op("dve", lambda e, dtr=dtr, ee=ee, dtm=dtm: e.scalar_tensor_tensor(
            out=dtm[:, :], in0=dtr[:, :], scalar=0.0, in1=ee[:, :], op0=ALU.max, op1=ALU.add), reads=[dtr, ee], writes=[dtm])
        P.op("dve", lambda e, dtm=dtm, m=m: e.tensor_tensor(
            out=dtm[:, :].rearrange("p (s h) -> p s h", s=4), in0=dtm[:, :].rearrange("p (s h) -> p s h", s=4),
            in1=tokmask[:, 4 * m:4 * m + 4].unsqueeze(2).to_broadcast([128, 4, 8]), op=ALU.mult), reads=[dtm, tokmask], writes=[dtm])
        P.op("dve", lambda e, dtm=dtm, am=am: e.tensor_tensor(out=am[:, :], in0=dtm[:, :], in1=Abc4[:, :], op=ALU.mult),
             reads=[dtm, Abc4], writes=[am])
        P.op("act", lambda e, am=am, a3m=a3m: e.copy(out=a3m[:, 0, :], in_=am[:, :]), reads=[am, a3m], writes=[a3m])
        P.op("dve", lambda e, am=am, a3m=a3m, ar1=ar1: e.tensor_tensor(out=ar1[:, :], in0=am[:, :], in1=a3m[:, 0, :], op=ALU.subtract),
             reads=[am, a3m], writes=[ar1])
        P.op("act", lambda e, ar1=ar1, a3m=a3m: e.copy(out=a3m[:, 1, :], in_=ar1[:, :]), reads=[ar1, a3m], writes=[a3m])
        P.op("dve", lambda e, ar1=ar1, a3m=a3m, ar2=ar2: e.tensor_tensor(out=ar2[:, :], in0=ar1[:, :], in1=a3m[:, 1, :], op=ALU.subtract),
             reads=[ar1, a3m], writes=[ar2])
        P.op("act", lambda e, ar2=ar2, a3m=a3m: e.copy(out=a3m[:, 2, :], in_=ar2[:, :]), reads=[ar2, a3m], writes=[a3m])
        for s in range(4):
            pac = pss["acs%d" % s]
            for i3 in range(3):
                P.op("pe", lambda e, a3m=a3m, i3=i3, s=s, pac=pac: e.matmul(
                    pac[:, 0:8], lhsT=U[:, :], rhs=a3m[:, i3, 8 * s:8 * s + 8], start=(i3 == 0), stop=(i3 == 2)),
                    reads=[U, a3m], writes=[pac])
            for i3 in range(3):
                P.op("pe", lambda e, a3m=a3m, i3=i3, s=s, pac=pac: e.matmul(
                    pac[:, 8:16], lhsT=ones_b[:, :], rhs=a3m[:, i3, 8 * s:8 * s + 8], start=(i3 == 0), stop=(i3 == 2)),
                    reads=[ones_b, a3m], writes=[pac])
        par = pss["acs0"].parent
        pacs = TV(par, par.ap[:, 64:128].rearrange("p (s t h) -> p s t h", s=4, t=2)[:, :, 0, :])
        ptot = TV(par, par.ap[:, 64:128].rearrange("p (s t h) -> p s t h", s=4, t=2)[:, :, 1, :])
        acsm, wstm, cdecm, wendm = (s48.next() for _ in range(4))
        P.op("act", lambda e, acsm=acsm: e.copy(out=acsm[:, :, :], in_=pacs[:, :, :]), reads=[pacs], writes=[acsm])
        P.op("act", lambda e, wstm=wstm: e.activation(out=wstm[:, :, :], in_=pacs[:, :, :], func=AF.Exp), reads=[pacs], writes=[wstm])
        P.op("act", lambda e, cdecm=cdecm: e.activation(out=cdecm[:, :, :], in_=ptot[:, :, :], func=AF.Exp), reads=[ptot], writes=[cdecm])
        P.op("dve", lambda e, wendm=wendm, acsm=acsm: e.tensor_tensor(out=wendm[:, :, :], in0=ptot[:, :, :], in1=acsm[:, :, :], op=ALU.subtract),
             reads=[ptot, acsm], writes=[wendm])
        P.op("act", lambda e, wendm=wendm: e.activation(out=wendm[:, :, :], in_=wendm[:, :, :], func=AF.Exp), reads=[wendm], writes=[wendm])
        gens = [pre(s) for s in range(4)]
        while gens:
            for g_ in list(gens):
                try:
                    next(g_)
                except StopIteration:
                    gens.remove(g_)
        for s in range(4):
            seq(s)
        if own:
            o0 = tok0 - OWN0
            P.op("pool", lambda e, yst=yst, o0=o0: e.dma_start(
                out=e_yb[g, :, :, o0:o0 + 512].rearrange("c p n -> p c n"), in_=yst[:, :, :]),
                reads=[yst], writes=[e_yb_T], dma=True)


def build_p1_init(P, C, din, cst, NTOKW):
    KT = C.sb([96, 4, NTOKW], BF16, "KT")
    VA = C.sb([128, NTOKW // 128, 2, 3, 64], BF16, "VA")
    kmT = C.sbT([64, 4, 32], BF16, "kmT")
    Mpad = [C.sbT([128, 4, 96], BF16, "Mpad") for _ in range(2)]
    for h in range(4):
        P.op("sp", lambda e, h=h: e.dma_start(out=KT[64:96, h, :], in_=din["kind"]), dma=True)
    P.op("pool", lambda e: e.memset(VA[:, :, :, 1, :], 1.0))
    for mp in Mpad:
        P.op("pool", lambda e, mp=mp: e.memset(mp[:, :, :], 0.0), writes=[mp])
    P.op("pool", lambda e: e.memset(kmT[:, :, :], 0.0), writes=[kmT])
    G = C.sbT([128, 512], F32, "G")
    gq, gk = cst["gq"], cst["gk"]
    for h in range(4):
        P.op("dve", lambda e, h=h: e.tensor_scalar(out=G[:, h * 64:(h + 1) * 64], in0=gq[:, :], scalar1=0.125, scalar2=None,
                                                   op0=ALU.mult), reads=[gq, G], writes=[G])
        P.op("dve", lambda e, h=h: e.tensor_copy(out=G[:, 256 + h * 64:256 + (h + 1) * 64], in_=gk[:, :]), reads=[gk, G], writes=[G])
    bb4 = C.sbT([128, 128], F32, "bb4")
    for h in range(4):
        P.op("dve", lambda e, h=h: e.tensor_copy(out=bb4[:, h * 32:(h + 1) * 32], in_=cst["blkbias"][:, :]),
             reads=[cst["blkbias"], bb4], writes=[bb4])
    P.barrier()
    nm = NTOKW // 512
    return dict(KT=KT, VA=VA, kmT=kmT, Mpad=Ring(Mpad), G=G, bb4=bb4,
                KT_T=[T(None, "KT%d" % i) for i in range(nm)], VA_T=[T(None, "VA%d" % i) for i in range(nm)])


def build_p1b(P, C, din, g, cst, A, hT_scr, hT_T, e_att, e_att_T, NTOKW, OWN0):
    psum = cst["psring"]
    po_ring = cst["po_ring"]
    pss = cst["ps_small"]
    ident_b, negm = cst["ident_b"], cst["negm"]
    KT, VA, kmT, G, bb4 = A["KT"], A["VA"], A["kmT"], A["G"], A["bb4"]
    KT_T, VA_T = A["KT_T"], A["VA_T"]
    stg = Ring([C.sbT([128, 8, WB_COLS], F32, "b_stg") for _ in range(2)])
    w1b = C.sbT([128, 8, W1B], BF16, "w1b")
    load_cast_weight(P, C, din["w1b"][g], w1b, W1B, cst["g1"], stg)
    hring = Ring([C.sbT([128, 8, 512], BF16, "b_hT") for _ in range(2)])
    f512 = Ring([C.sbT([128, 512], F32, "b_f512") for _ in range(4)])
    b512 = Ring([C.sbT([128, 512], BF16, "b_b512") for _ in range(3)])
    ptr = Ring([C.sbT([128, 512], BF16, "b_pt") for _ in range(4)])
    s8 = Ring([C.sbT([128, 8], F32, "b_s8") for _ in range(6)])
    g128 = Ring([C.sbT([128, 128], F32, "b_g128") for _ in range(6)])
    t8r = Ring([C.sbT([128, 32], F32, "b_t8") for _ in range(2)])
    kmf = C.sbT([64, 4, 2], F32, "b_kmf")
    QTr = Ring([C.sbT([96, 4, 512], BF16, "b_QT") for _ in range(2)])
    ast = [Ring([C.sbT([128, 512], BF16, "b_ast") for _ in range(2)]) for _ in range(2)]
    rdr = Ring([C.sbT([128, 512], F32, "b_rd") for _ in range(2)])
    outs = []
    for m in range(NTOKW // 512):
        tok0 = m * 512
        own = tok0 >= OWN0
        c0 = 0 if own else 256
        h0 = 0 if own else 4
        hT = hring.next()
        P.op("sp", lambda e, hT=hT, tok0=tok0: e.dma_start(out=hT[:, :, :], in_=hT_scr[:, :, tok0:tok0 + 512]),
             reads=[hT_T[m]], writes=[hT], dma=True)
        QT = QTr.next() if own else None
        for s in range(4):
            sub = slice(s * 128, (s + 1) * 128)
            kt = m * 4 + s
            pqk, pv = psum.next(), psum.next()
            for kc in range(8):
                P.op("pe", lambda e, pqk=pqk, kc=kc, hT=hT, sub=sub, c0=c0: e.matmul(
                    pqk[:, c0:512], lhsT=hT[:, kc, sub], rhs=w1b[:, kc, c0:512], start=(kc == 0), stop=(kc == 7)),
                    reads=[w1b, hT], writes=[pqk])
            for kc in range(8):
                P.op("pe", lambda e, pv=pv, kc=kc, hT=hT, sub=sub: e.matmul(
                    pv[:, 0:256], lhsT=hT[:, kc, sub], rhs=w1b[:, kc, 512:768], start=(kc == 0), stop=(kc == 7)),
                    reads=[w1b, hT], writes=[pv])
            sq, ssum, tt = f512.next(), s8.next(), f512.next()
            P.op("act", lambda e, pqk=pqk, sq=sq, c0=c0: e.activation(out=sq[:, c0:512], in_=pqk[:, c0:512], func=AF.Square),
                 reads=[pqk], writes=[sq])
            P.op("dve", lambda e, sq=sq, ssum=ssum, c0=c0, h0=h0: e.tensor_reduce(
                out=ssum[:, h0:8], in_=sq[:, c0:512].rearrange("p (h d) -> p h d", d=64), axis=AX.X, op=ALU.add),
                reads=[sq], writes=[ssum])
            P.op("act", lambda e, ssum=ssum, h0=h0: e.activation(
                out=ssum[:, h0:8], in_=ssum[:, h0:8], func=AF.Sqrt, scale=1.0 / 64.0, bias=EPS), reads=[ssum], writes=[ssum])
            P.op("dve", lambda e, ssum=ssum, h0=h0: e.reciprocal(out=ssum[:, h0:8], in_=ssum[:, h0:8]), reads=[ssum], writes=[ssum])
            P.op("dve", lambda e, pqk=pqk, tt=tt, ssum=ssum, c0=c0, h0=h0: e.tensor_tensor(
                out=tt[:, c0:512].rearrange("p (h d) -> p h d", d=64), in0=pqk[:, c0:512].rearrange("p (h d) -> p h d", d=64),
                in1=ssum[:, h0:8].unsqueeze(2).to_broadcast([128, 8 - h0, 64]), op=ALU.mult), reads=[pqk, ssum], writes=[tt])
            qkn = b512.next()
            P.op("pool", lambda e, tt=tt, qkn=qkn, c0=c0: e.tensor_tensor(
                out=qkn[:, c0:512], in0=tt[:, c0:512], in1=G[:, c0:512], op=ALU.mult), reads=[tt, G], writes=[qkn])
            pkt = psum.next()
            pktb = pkt.ap.bitcast(BF16)
            for h in range(4):
                P.op("pe", lambda e, pktb=pktb, qkn=qkn, h=h: e.transpose(
                    pktb[0:64, h * 128:(h + 1) * 128], qkn[:, 256 + h * 64:256 + (h + 1) * 64], ident_b[:, :]),
                    reads=[qkn, ident_b], writes=[pkt])
            P.op("act", lambda e, pktb=pktb, tok0=tok0, s=s: e.copy(
                out=KT[0:64, :, tok0 + s * 128:tok0 + (s + 1) * 128], in_=pktb[0:64, 0:512].rearrange("p (h n) -> p h n", h=4)),
                reads=[pkt, KT_T[m]], writes=[KT_T[m]])
            P.op("act", lambda e, pv=pv, kt=kt: e.copy(
                out=VA[:, kt, :, 0, :], in_=pv[:, 0:256].rearrange("p (a b d) -> p a b d", a=2, b=2)[:, :, 0, :]),
                reads=[pv, VA_T[m]], writes=[VA_T[m]])
            P.op("dve", lambda e, pv=pv, kt=kt: e.tensor_copy(
                out=VA[:, kt, :, 2, :], in_=pv[:, 0:256].rearrange("p (a b d) -> p a b d", a=2, b=2)[:, :, 1, :]),
                reads=[pv, VA_T[m]], writes=[VA_T[m]])
            if own:
                pqt = psum.next()
                pqtb = pqt.ap.bitcast(BF16)
                for h in range(4):
                    P.op("pe", lambda e, pqtb=pqtb, qkn=qkn, h=h: e.transpose(
                        pqtb[0:64, h * 128:(h + 1) * 128], qkn[:, h * 64:(h + 1) * 64], ident_b[:, :]),
                        reads=[qkn, ident_b], writes=[pqt])
                P.op("dve", lambda e, pqtb=pqtb, QT=QT, sub=sub: e.tensor_copy(
                    out=QT[0:64, :, sub], in_=pqtb[0:64, 0:512].rearrange("p (h n) -> p h n", h=4)),
                    reads=[pqt, QT], writes=[QT])
        P.op("dve", lambda e, tok0=tok0: e.tensor_reduce(
            out=kmf[:, :, :], in_=KT[0:64, :, tok0:tok0 + 512].rearrange("p h (b k) -> p h b k", b=2), axis=AX.X, op=ALU.add),
            reads=[KT_T[m]], writes=[kmf])
        P.op("dve", lambda e, m=m: e.tensor_scalar(out=kmT[:, :, 2 * m:2 * m + 2], in0=kmf[:, :, :], scalar1=1.0 / 256.0,
                                                   scalar2=None, op0=ALU.mult), reads=[kmf, kmT], writes=[kmT])
        if not own:
            continue
        for s in range(4):
            sub = slice(s * 128, (s + 1) * 128)
            ownblk = 2 * m + s // 2
            pg = pss["gate"]
            for h in range(4):
                P.op("pe", lambda e, h=h, QT=QT, sub=sub: e.matmul(
                    pg[:, h * 32:(h + 1) * 32], lhsT=QT[0:64, h, sub], rhs=kmT[0:64, h, :], start=True, stop=True),
                    reads=[QT, kmT], writes=[pg])
            gm, m1, m2, t8 = g128.next(), g128.next(), g128.next(), t8r.next()
            P.op("dve", lambda e, gm=gm: e.tensor_tensor(out=gm[:, :], in0=pg[:, :], in1=bb4[:, :], op=ALU.add),
                 reads=[pg, bb4], writes=[gm])
            P.op("pool", lambda e, gm=gm, ownblk=ownblk: e.memset(
                gm[:, :].rearrange("p (h b) -> p h b", h=4)[:, :, ownblk:32], NEG), reads=[gm], writes=[gm])
            for h in range(4):
                P.op("dve", lambda e, gm=gm, t8=t8, h=h: e.max(out=t8[:, h * 8:(h + 1) * 8], in_=gm[:, h * 32:(h + 1) * 32]),
                     reads=[gm, t8], writes=[t8])
            P.op("dve", lambda e, gm=gm, m1=m1, t8=t8: e.tensor_tensor(
                out=m1[:, :].rearrange("p (h b) -> p h b", h=4), in0=gm[:, :].rearrange("p (h b) -> p h b", h=4),
                in1=t8[:, :].rearrange("p (h k) -> p h k", h=4)[:, :, 2:3].to_broadcast([128, 4, 32]), op=ALU.is_lt),
                reads=[gm, t8], writes=[m1])
            P.op("dve", lambda e, gm=gm, m2=m2: e.tensor_scalar(
                out=m2[:, :], in0=gm[:, :], scalar1=NEG / 2, scalar2=NEG, op0=ALU.is_lt, op1=ALU.mult), reads=[gm], writes=[m2])
            Mp = A["Mpad"].next()
            P.op("dve", lambda e, Mp=Mp, m1=m1, m2=m2: e.scalar_tensor_tensor(
                out=Mp[:, :, 64:96], in0=m1[:, :].rearrange("p (h b) -> p h b", h=4), scalar=NEG,
                in1=m2[:, :].rearrange("p (h b) -> p h b", h=4), op0=ALU.mult, op1=ALU.min), reads=[m1, m2, Mp], writes=[Mp])
            P.op("pool", lambda e, Mp=Mp, ownblk=ownblk: e.memset(Mp[:, :, 64 + ownblk:65 + ownblk], 0.0), reads=[Mp], writes=[Mp])
            pmt = psum.next()
            pmtb = pmt.ap.bitcast(BF16)
            for h in range(4):
                P.op("pe", lambda e, pmtb=pmtb, Mp=Mp, h=h: e.transpose(
                    pmtb[0:96, h * 128:(h + 1) * 128], Mp[:, h, :], ident_b[:, :]), reads=[Mp, ident_b], writes=[pmt])
            P.op("act", lambda e, pmtb=pmtb, QT=QT, sub=sub: e.copy(
                out=QT[64:96, :, sub], in_=pmtb[64:96, 0:512].rearrange("p (h n) -> p h n", h=4)), reads=[pmt, QT], writes=[QT])
        nkt = (2 * m + 2) * 2
        o0 = tok0 - OWN0
        for h in range(4):
            pair, hb = h // 2, h % 2
            po = po_ring.next()
            def tile_cols(kt):
                blk = kt // 2
                if blk < 2 * m:
                    return 0, 512, None
                if blk == 2 * m:
                    return 0, 512, 0
                return 256, 512, 256

            def emit_s(kt):
                a0, a1, cz = tile_cols(kt)
                mk = kt // 4
                ps = psum.next()
                P.op("pe", lambda e, ps=ps, h=h, kt=kt, QT=QT, a0=a0, a1=a1, cz=cz: e.matmul(
                    ps[:, a0:a1], lhsT=KT[0:96, h, kt * 128:(kt + 1) * 128], rhs=QT[0:96, h, a0:a1],
                    start=True, stop=(cz is None)), reads=[KT_T[mk], QT], writes=[ps])
                if cz is not None:
                    P.op("pe", lambda e, ps=ps, kt=kt, cz=cz: e.matmul(
                        ps[:, cz:cz + 256], lhsT=ident_b[:, :], rhs=negm[:, kt % 2, :], start=False, stop=True),
                        reads=[ident_b, negm], writes=[ps])
                return ps

            def emit_pv(kt, ps):
                a0, a1, cz = tile_cols(kt)
                mk = kt // 4
                pt = ptr.next()
                P.op("act", lambda e, ps=ps, pt=pt, a0=a0, a1=a1: e.activation(out=pt[:, a0:a1], in_=ps[:, a0:a1], func=AF.Exp),
                     reads=[ps], writes=[pt])
                P.op("pe", lambda e, po=po, pt=pt, kt=kt, pair=pair, hb=hb, a0=a0, a1=a1, nkt=nkt: e.matmul(
                    po[:, a0:a1], lhsT=VA[:, kt, pair, hb:hb + 2, :].rearrange("p a d -> p (a d)"), rhs=pt[:, a0:a1],
                    start=(kt == 0), stop=(kt == nkt - 1), skip_group_check=True), reads=[VA_T[mk], pt], writes=[po])

            LOOK = 2
            pend = []
            for kt in range(nkt):
                pend.append((kt, emit_s(kt)))
                if len(pend) > LOOK:
                    emit_pv(*pend.pop(0))
            while pend:
                emit_pv(*pend.pop(0))
            nr = slice(0, 64) if hb == 0 else slice(64, 128)
            dr = slice(64, 128) if hb == 0 else slice(0, 64)
            rd = rdr.next()
            if hb == 0:
                at_ = ast[pair].next()
                ast_cur = at_
            else:
                at_ = ast_cur
            P.op("dve", lambda e, po=po, rd=rd, nr=nr, dr=dr: e.reciprocal(out=rd[nr, :], in_=po[dr, :]), reads=[po], writes=[rd])
            P.op("dve", lambda e, po=po, rd=rd, nr=nr, at_=at_: e.tensor_tensor(
                out=at_[nr, :], in0=po[nr, :], in1=rd[nr, :], op=ALU.mult), reads=[po, rd, at_], writes=[at_])
            if hb == 1:
                outs.append(P.op("pool", lambda e, at_=at_, pair=pair, o0=o0: e.dma_start(
                    out=e_att[g, pair, :, o0:o0 + 512], in_=at_[:, :]), reads=[at_], writes=[e_att_T], dma=True))
    return outs


def load_consts(P, C, din, names_shapes):
    out = {}
    for name, shape, dt in names_shapes:
        t = C.sbT(shape, dt, name)
        P.op("sp", lambda e, t=t, name=name: e.dma_start(out=t.ap, in_=din[name]), writes=[t], dma=True)
        out[name] = t
    return out


def build_program(mode, NTOKW=8192, OWN0=0, NG=1):
    nc = bass.Bass("TRN2", target_bir_lowering=False)
    P = Prog(nc)
    with ExitStack() as es:
        C = Ctx(nc, es, P)
        din = {}

        def inp(name, shape, dt=F32):
            din[name] = C.dram(name, shape, dt, "ExternalInput")

        psum = [T(es.enter_context(nc.psum_tensor("ps%d" % i, [128, 512], F32))[:, :], "ps%d" % i, excl=True) for i in range(8)]
        final = []
        NOWN = NTOKW - OWN0
        if mode in ("p1", "fused"):
            inp("xw", [NTOKW, 1024])
            inp("w1a", [NG, 1024, W1A])
            inp("w1b", [NG, 1024, W1B])
            inp("convw", [NG, 128, 6, 4])
            inp("convb", [NG, 128, 6])
            for nm in ("dtb", "alog", "dsk"):
                inp(nm, [NG, 128, 8])
            inp("sng", [NG, 128, 512])
            inp("kind", [32, NTOKW], BF16)
            shapes1 = [("gq", [128, 64], F32), ("gk", [128, 64], F32), ("g1", [128, 8], F32),
                       ("tokmask", [128, NTOKW // 128], F32), ("blkbias", [128, 32], F32),
                       ("ident_b", [128, 128], BF16), ("ones_b", [128, 128], BF16), ("U", [128, 128], BF16),
                       ("T1", [128, 128], BF16), ("negm", [128, 2, 256], BF16)]
            for nm, shp, dt in shapes1:
                if nm not in din:
                    inp(nm, shp, dt)
            cst = load_consts(P, C, din, shapes1)
            cst["psring"] = Ring(psum[0:5])
            cst["po_ring"] = Ring(psum[5:7])
            cst["ps_small"] = {"gate": TV(psum[7], psum[7].ap[:, 0:128])}
            for s_ in range(4):
                cst["ps_small"]["dt%d" % s_] = TV(psum[5], psum[5].ap[:, 8 * s_:8 * s_ + 8])
                cst["ps_small"]["acs%d" % s_] = TV(psum[5], psum[5].ap[:, 64 + 16 * s_:64 + 16 * s_ + 16])
                cst["ps_small"]["cb%d" % s_] = TV(psum[6], psum[6].ap[:, 128 * s_:128 * s_ + 128])
            kind_e = "ExternalOutput" if mode == "p1" else "Internal"
            e_att = C.dram("e_att", [NG, 2, 128, NOWN], BF16, kind_e)
            e_yb = C.dram("e_yb", [NG, 4, 128, NOWN], BF16, kind_e)
            e_att_T, e_yb_T = T(None, "e_att"), T(None, "e_yb")
            hT_scr = C.dram("hT_scr", [128, 8, NTOKW], BF16, "Internal")
            hT_T = []
            with ExitStack() as es1:
                C1 = Ctx(nc, es1, P)
                if STAGES.get("pro", True):
                    build_prologue(P, C1, din, cst, hT_scr, hT_T, NTOKW)
            P.barrier()
            with ExitStack() as es1:
                C1 = Ctx(nc, es1, P)
                for g in range(NG):
                    if STAGES.get("a", True):
                        with ExitStack() as es2:
                            build_p1a(P, Ctx(nc, es2, P), din, g, cst, hT_scr, hT_T, e_yb, e_yb_T, NTOKW, OWN0)
                        P.barrier()
                    if STAGES.get("b", True):
                        with ExitStack() as es2:
                            C2b = Ctx(nc, es2, P)
                            A = build_p1_init(P, C2b, din, cst, NTOKW)
                            build_p1b(P, C2b, din, g, cst, A, hT_scr, hT_T, e_att, e_att_T, NTOKW, OWN0)
                        P.barrier()
            if mode == "p1":
                final = [o for o in P.ops["pool"] if o.dma][-8:]
        if mode in ("p2", "fused"):
            inp("x2", [2048, 1024])
            inp("p2", [2048, 256])
            for name, (K, N) in P2W.items():
                inp(name, [K, N])
            shapes2 = [("g1", [128, 8], F32), ("g2", [128, 8], F32), ("g3", [128, 8], F32),
                       ("ident_b", [128, 128], BF16), ("ones_b", [128, 128], BF16)]
            for nm, shp, dt in shapes2:
                if nm not in din:
                    inp(nm, shp, dt)
            consts = load_consts(P, C, din, shapes2)
            consts["psum"] = psum
            wscr = {name: C.dram("scr_" + name, [K, N], BF16, "Internal") for name, (K, N) in P2W.items()}
            wT = {}
            with ExitStack() as es2:
                C2 = Ctx(nc, es2, P)
                if STAGES.get("precast", True):
                    build_precast(P, C2, din, wscr, wT, consts)
            P.barrier()
            if mode == "p2":
                inp("e_att", [4, 2, 128, 2048], BF16)
                inp("e_yb", [4, 4, 128, 2048], BF16)
                exch = {"att": din["e_att"], "yb": din["e_yb"], "att_T": T(None), "yb_T": T(None)}
            else:
                exch = {"att": e_att, "yb": e_yb, "att_T": e_att_T, "yb_T": e_yb_T}
            dout = C.dram("out", [2048, 1024], F32, "ExternalOutput")
            if STAGES.get("p2", True):
                with ExitStack() as es3:
                    C3 = Ctx(nc, es3, P)
                    final = build_phase2(P, C3, din, wscr, wT, consts, exch, dout, NTOK=STAGES.get('ntok', 2048))
            else:
                final = [o for o in P.ops["pool"] if o.dma][-8:]
        P.emit(final)
    return nc


BF = ml_dtypes.bfloat16


def host_consts():
    return {
        "ident_b": np.eye(128, dtype=np.float32).astype(BF),
        "ones_b": np.ones((128, 128), dtype=np.float32).astype(BF),
    }


def host_consts1(NTOKW):
    i = np.arange(128)
    U = (i[:, None] <= i[None, :]).astype(np.float32)
    T1 = (i[:, None] > i[None, :]).astype(np.float32)
    q = np.arange(256)
    negm = np.stack([np.where((kt * 128 + i[:, None]) <= q[None, :], 0.0, NEG) for kt in range(2)], 1).astype(np.float32)
    kind = (np.arange(NTOKW)[None, :] // 256 == np.arange(32)[:, None]).astype(np.float32)
    return {"ident_b": np.eye(128, dtype=np.float32).astype(BF), "ones_b": np.ones((128, 128), np.float32).astype(BF),
            "U": U.astype(BF), "T1": T1.astype(BF), "negm": negm.astype(BF), "kind": kind.astype(BF)}


def gain_layout(g):
    return np.ascontiguousarray(g.reshape(8, 128).T)


def bc(v):
    return np.ascontiguousarray(np.broadcast_to(v[None, :], (128, v.shape[0]))).astype(np.float32)


def p1_inputs(inputs, b, groups, xw, tokmask, blkbias, NTOKW):
    w_in = inputs["w_in"][0]
    cw, cbias = inputs["conv_w"][0], inputs["conv_b"][0]
    w1a, w1b, convw, convb, dtb, alog, dsk, sng = [], [], [], [], [], [], [], []
    for g in groups:
        cols_a = np.concatenate([np.arange(5120 + 512 * g, 5120 + 512 * (g + 1)), np.arange(7168 + 128 * g, 7168 + 128 * (g + 1)),
                                 np.arange(7680 + 128 * g, 7680 + 128 * (g + 1)), np.arange(3072 + 512 * g, 3072 + 512 * (g + 1)),
                                 np.arange(8192 + 8 * g, 8192 + 8 * (g + 1))])
        cols_b = np.concatenate([np.arange(256 * g, 256 * (g + 1)), np.arange(1024 + 256 * g, 1024 + 256 * (g + 1)),
                                 np.arange(2048 + 256 * g, 2048 + 256 * (g + 1))])
        w1a.append(w_in[:, cols_a])
        w1b.append(w_in[:, cols_b])
        ch = np.concatenate([np.arange(512 * g, 512 * (g + 1)), np.arange(2048 + 128 * g, 2048 + 128 * (g + 1)),
                             np.arange(2560 + 128 * g, 2560 + 128 * (g + 1))])
        convw.append(cw[:, ch].T.reshape(6, 128, 4).transpose(1, 0, 2))
        convb.append(cbias[ch].reshape(6, 128).T)
        dtb.append(bc(inputs["dt_bias"][0][8 * g:8 * g + 8]))
        alog.append(bc(inputs["a_log"][0][8 * g:8 * g + 8]))
        dsk.append(bc(inputs["d_skip"][0][8 * g:8 * g + 8]))
        sng.append(bc(inputs["ssm_norm_g"][0][512 * g:512 * g + 512]))
    m = {"xw": np.ascontiguousarray(xw), "w1a": np.ascontiguousarray(np.stack(w1a)), "w1b": np.ascontiguousarray(np.stack(w1b)),
         "convw": np.ascontiguousarray(np.stack(convw)), "convb": np.ascontiguousarray(np.stack(convb)),
         "dtb": np.stack(dtb), "alog": np.stack(alog), "dsk": np.stack(dsk), "sng": np.stack(sng),
         "gq": bc(inputs["q_norm_g"][0]), "gk": bc(inputs["k_norm_g"][0]), "g1": gain_layout(inputs["ln1_g"][0]),
         "tokmask": np.ascontiguousarray(tokmask.reshape(-1, 128).T).astype(np.float32), "blkbias": bc(blkbias)}
    m.update(host_consts1(NTOKW))
    return m


def p2_inputs(inputs, core, e_att=None, e_yb=None):
    b, t = core // 4, core % 4
    sl = slice(t * 2048, (t + 1) * 2048)
    m = {
        "x2": np.ascontiguousarray(inputs["x"][b, sl]),
        "p2": np.ascontiguousarray(inputs["p"][0, b, sl]),
        "wg": np.ascontiguousarray(inputs["w_in"][0][:, 8224:10272]),
        "woa": inputs["w_o_attn"][0], "wos": inputs["w_o_ssm"][0], "wout": inputs["w_out"][0],
        "wgu": inputs["w_gate_up"][0], "wd": inputs["w_down"][0], "wpg": inputs["w_ple_gate"][0],
        "wpp": inputs["w_ple_proj"][0],
        "g1": gain_layout(inputs["ln1_g"][0]), "g2": gain_layout(inputs["ln2_g"][0]),
        "g3": gain_layout(inputs["ln3_g"][0]),
    }
    m.update(host_consts())
    if e_att is not None:
        m["e_att"] = e_att
        m["e_yb"] = e_yb
    return m


MODE = "fused"


def kernel(**inputs):
    inputs = {k: np.asarray(v) for k, v in inputs.items()}
    x = inputs["x"]
    out = np.zeros(x.shape, np.float32)
    if MODE == "two":
        nc1 = build_program("p1", NTOKW=8192, OWN0=0, NG=1)
        maps1 = []
        for core in range(8):
            b, g = core // 4, core % 4
            maps1.append(p1_inputs(inputs, b, [g], x[b], np.ones(8192, np.float32), np.zeros(32, np.float32), 8192))
        r1 = run_bass_kernel_spmd(nc1, maps1, core_ids=list(range(8))).results
        nc2 = build_program("p2")
        maps2 = []
        for core in range(8):
            b, t = core // 4, core % 4
            sl = slice(t * 2048, (t + 1) * 2048)
            ea = np.stack([np.asarray(r1[b * 4 + g]["e_att"])[0][:, :, sl] for g in range(4)])
            ey = np.stack([np.asarray(r1[b * 4 + g]["e_yb"])[0][:, :, sl] for g in range(4)])
            maps2.append(p2_inputs(inputs, core, np.ascontiguousarray(ea), np.ascontiguousarray(ey)))
        r2 = run_bass_kernel_spmd(nc2, maps2, core_ids=list(range(8))).results
        for core in range(8):
            b, t = core // 4, core % 4
            out[b, t * 2048:(t + 1) * 2048] = np.asarray(r2[core]["out"])
        return out
    nc = build_program("fused", NTOKW=8192, OWN0=6144, NG=4)
    maps = []
    for core in range(8):
        b, t = core // 4, core % 4
        npad = (3 - t) * 2048
        xw = np.concatenate([np.zeros((npad, 1024), np.float32), x[b, :(t + 1) * 2048]], 0)
        tokmask = np.concatenate([np.zeros(npad, np.float32), np.ones(8192 - npad, np.float32)])
        blkbias = np.where(np.arange(32) < npad // 256, NEG, 0.0).astype(np.float32)
        m = p1_inputs(inputs, b, [0, 1, 2, 3], xw, tokmask, blkbias, 8192)
        m.update(p2_inputs(inputs, core))
        maps.append(m)
    r = run_bass_kernel_spmd(nc, maps, core_ids=list(range(8))).results
    for core in range(8):
        b, t = core // 4, core % 4
        out[b, t * 2048:(t + 1) * 2048] = np.asarray(r[core]["out"])
    return out
```

```python
import numpy as np
from contextlib import ExitStack
import ml_dtypes
import concourse.bass as bass
import concourse.mybir as mybir
from concourse.bass_utils import run_bass_kernel_spmd

F32 = mybir.dt.float32
BF16 = mybir.dt.bfloat16
AF = mybir.ActivationFunctionType
ALU = mybir.AluOpType
AX = mybir.AxisListType

EPS = 1e-6
NEG = -30000.0
SAME_ENGINE_SYNC = True
STAGES = {}


class T:
    __slots__ = ("ap", "w", "r", "name", "excl")

    def __init__(self, ap=None, name="", excl=False):
        self.ap = ap
        self.w = None
        self.r = []
        self.name = name
        self.excl = excl

    def __getitem__(self, k):
        return self.ap[k]


class TV(T):
    __slots__ = ("parent",)

    def __init__(self, parent, ap):
        self.parent = parent
        self.ap = ap
        self.name = parent.name
        self.excl = parent.excl

    @property
    def w(self):
        return self.parent.w

    @w.setter
    def w(self, v):
        self.parent.w = v

    @property
    def r(self):
        return self.parent.r

    @r.setter
    def r(self, v):
        self.parent.r = v


class Op:
    __slots__ = ("eng", "fn", "deps", "dma", "inc", "sem", "ticket", "idx", "prev_same_sem", "vc")

    def __init__(self, eng, fn, dma):
        self.eng = eng
        self.fn = fn
        self.dma = dma
        self.deps = []
        self.inc = False
        self.sem = None
        self.ticket = 0
        self.prev_same_sem = None


class Prog:
    ENGS = ("pe", "act", "dve", "pool", "sp")
    NDS = 8

    def __init__(self, nc):
        self.nc = nc
        self.ops = {e: [] for e in self.ENGS}
        self.all = []
        self.bar = {}

    def op(self, eng, fn, reads=(), writes=(), dma=False):
        o = Op(eng, fn, dma)
        deps = []
        for t in reads:
            if t.w is not None:
                deps.append(t.w)
            if t.excl:
                deps.extend(x for x in t.r if x.eng != eng)
        for t in writes:
            if t.w is not None:
                deps.append(t.w)
            deps.extend(t.r)
        b = self.bar.pop(eng, None)
        if b:
            deps.extend(b)
        seen = set()
        for d in deps:
            if d is o or id(d) in seen:
                continue
            seen.add(id(d))
            if (not d.dma) and d.eng == eng and (eng == "pe" or not SAME_ENGINE_SYNC):
                continue
            o.deps.append(d)
            d.inc = True
        for t in reads:
            if dma:
                t.r.append(o)
            else:
                t.r = [x for x in t.r if x.dma or x.eng != eng] + [o]
        for t in writes:
            t.w = o
            t.r = []
        if dma:
            o.inc = True
        self.ops[eng].append(o)
        self.all.append(o)
        return o

    def barrier(self):
        last = []
        for e in self.ENGS:
            ops = self.ops[e]
            if ops:
                last.append(ops[-1])
            last.extend([o for o in ops if o.dma][-self.NDS:])
        for e in self.ENGS:
            self.bar[e] = list(last)

    def emit(self, final_ops):
        nc = self.nc
        with ExitStack() as es:
            SEM_CAP = 1000
            nsem = {e: sum(1 for o in self.ops[e] if o.inc and not o.dma) // SEM_CAP + 1 for e in ("pe", "act", "dve", "pool")}
            csem = {e: [es.enter_context(nc.semaphore("cs_%s%d" % (e, i))) for i in range(nsem[e])]
                    for e in ("pe", "act", "dve", "pool")}
            dsem = {e: [es.enter_context(nc.semaphore("ds_%s%d" % (e, i))) for i in range(self.NDS)]
                    for e in self.ENGS}
            ccount = {e: 0 for e in self.ENGS}
            dcount = {e: [0] * self.NDS for e in self.ENGS}
            drr = {e: 0 for e in self.ENGS}
            dlast = {e: [None] * self.NDS for e in self.ENGS}
            for e in self.ENGS:
                for o in self.ops[e]:
                    if o.dma:
                        k = drr[e] % self.NDS
                        drr[e] += 1
                        dcount[e][k] += 16
                        o.sem = dsem[e][k]
                        o.ticket = dcount[e][k]
                        o.prev_same_sem = dlast[e][k]
                        dlast[e][k] = o
                    elif o.inc:
                        o.sem = csem[e][ccount[e] // SEM_CAP]
                        o.ticket = ccount[e] % SEM_CAP + 1
                        ccount[e] += 1

            know = {e: {} for e in self.ENGS}
            plan = {}
            for o in self.all:
                kn = know[o.eng]
                waits = []
                dl = list(o.deps)
                if o.dma and o.prev_same_sem is not None:
                    dl.append(o.prev_same_sem)
                for d in dl:
                    key = id(d.sem)
                    if kn.get(key, 0) < d.ticket:
                        waits.append(d)
                        for kk, vv in d.vc.items():
                            if kn.get(kk, 0) < vv:
                                kn[kk] = vv
                plan[id(o)] = waits
                o.vc = dict(kn)
                if o.sem is not None:
                    o.vc[id(o.sem)] = o.ticket
                    if not o.dma:
                        kn[id(o.sem)] = max(kn.get(id(o.sem), 0), 0)

            def run(ename, eng):
                for o in self.ops[ename]:
                    for d in plan[id(o)]:
                        eng.wait_ge(d.sem, d.ticket)
                    ins = o.fn(eng)
                    if o.sem is not None:
                        ins.then_inc(o.sem, 16 if o.dma else 1)
                if ename == "sp":
                    kn = know["sp"]
                    for d in final_ops:
                        if kn.get(id(d.sem), 0) < d.ticket:
                            eng.wait_ge(d.sem, d.ticket)
                            kn[id(d.sem)] = d.ticket

            with nc.Block() as block:
                @block.tensor
                def _(eng):
                    run("pe", eng)

                @block.scalar
                def _(eng):
                    run("act", eng)

                @block.vector
                def _(eng):
                    run("dve", eng)

                @block.gpsimd
                def _(eng):
                    run("pool", eng)

                @block.sync
                def _(eng):
                    run("sp", eng)


class Ctx:
    N = 0

    def __init__(self, nc, es, P):
        self.nc, self.es, self.P = nc, es, P
        self.n = 0

    def sb(self, shape, dt, name=None):
        Ctx.N += 1
        h = self.es.enter_context(self.nc.sbuf_tensor("%s_%d" % (name or "sb", Ctx.N), list(shape), dt))
        return h

    def sbT(self, shape, dt, name=None):
        h = self.sb(shape, dt, name)
        return T(h[tuple(slice(None) for _ in shape)], name or "")

    def dram(self, name, shape, dt, kind):
        return self.nc.dram_tensor(name, list(shape), dt, kind=kind).ap()


class Ring:
    def __init__(self, tiles):
        self.tiles = tiles
        self.i = 0

    def next(self):
        t = self.tiles[self.i % len(self.tiles)]
        self.i += 1
        return t


P2W = {
    "wg": (1024, 2048), "woa": (1024, 1024), "wos": (2048, 1024), "wout": (1024, 1024),
    "wgu": (1024, 5632), "wd": (2816, 1024), "wpg": (1024, 1024), "wpp": (256, 1024),
}
P2W_GAIN = {"wg": "g1", "wgu": "g2", "wpg": "g3"}
WB_COLS = 256


def wblocks(name):
    K, N = P2W[name]
    nkc = K // 128
    kbs = []
    k0 = 0
    while k0 < nkc:
        nk = min(8, nkc - k0)
        kbs.append((k0, nk))
        k0 += nk
    return kbs, N // WB_COLS


def build_precast(P, C, din, wscr, wT, consts):
    nc = P.nc
    stg = Ring([C.sbT([128, 8, WB_COLS], F32, "pc_stg") for _ in range(2)])
    wbf = Ring([C.sbT([128, 8, WB_COLS], BF16, "pc_bf") for _ in range(2)])
    for name in P2W:
        kbs, ncb = wblocks(name)
        src = din[name]
        dst = wscr[name]
        gain = consts.get(P2W_GAIN.get(name))
        for cb in range(ncb):
            for (k0, nk) in kbs:
                s, w = stg.next(), wbf.next()
                sv = src[k0 * 128:(k0 + nk) * 128, cb * WB_COLS:(cb + 1) * WB_COLS].rearrange("(k p) n -> p k n", p=128)
                dv = dst[k0 * 128:(k0 + nk) * 128, cb * WB_COLS:(cb + 1) * WB_COLS].rearrange("(k p) n -> p k n", p=128)
                P.op("sp", lambda e, s=s, sv=sv, nk=nk: e.dma_start(out=s[:, 0:nk, :], in_=sv), writes=[s], dma=True)
                if gain is not None:
                    gv = gain[:, k0:k0 + nk].unsqueeze(2).to_broadcast([128, nk, WB_COLS])
                    P.op("pool", lambda e, s=s, w=w, gv=gv, nk=nk: e.tensor_tensor(
                        out=w[:, 0:nk, :], in0=s[:, 0:nk, :], in1=gv, op=ALU.mult), reads=[s, gain], writes=[w])
                else:
                    P.op("pool", lambda e, s=s, w=w, nk=nk: e.tensor_copy(out=w[:, 0:nk, :], in_=s[:, 0:nk, :]),
                         reads=[s], writes=[w])
                t = T(None, "wscr")
                wT[(name, cb, k0)] = t
                P.op("pool", lambda e, w=w, dv=dv, nk=nk: e.dma_start(out=dv, in_=w[:, 0:nk, :]),
                     reads=[w], writes=[t], dma=True)


def build_phase2(P, C, din, wscr, wT, consts, exch, dout, NTOK=2048, TT=1024):
    nc = P.nc
    NTG = TT // 512
    NSUB = TT // 128
    ident_b = consts["ident_b"]
    ones_b = consts["ones_b"]
    xT = [C.sbT([128, TT], F32, "xT") for _ in range(8)]
    hT = [C.sbT([128, TT], BF16, "hT") for _ in range(8)]
    big = C.sb([128, 24, TT], BF16, "big")
    attT = [T(big[:, c, :], "att") for c in range(8)]
    ybT = [T(big[:, 8 + c, :], "yb") for c in range(16)]
    mg = [C.sbT([128, TT], BF16, "mg") for _ in range(8)]
    pT = [C.sbT([128, TT], BF16, "pT") for _ in range(2)]
    xin = Ring([C.sbT([128, 1024], F32, "xin") for _ in range(2)])
    xhi = Ring([C.sbT([128, 1024], BF16, "xhi") for _ in range(2)])
    xlo = Ring([C.sbT([128, 1024], BF16, "xlo") for _ in range(2)])
    xres = Ring([C.sbT([128, 1024], F32, "xres") for _ in range(1)])
    pin = Ring([C.sbT([128, 256], F32, "pin") for _ in range(2)])
    pinb = Ring([C.sbT([128, 256], BF16, "pinb") for _ in range(2)])
    wring = Ring([C.sbT([128, 8, WB_COLS], BF16, "wr") for _ in range(8)])
    tmpf = Ring([C.sbT([128, 512], F32, "tmpf") for _ in range(8)])
    sqb = Ring([C.sbT([128, 512], BF16, "sqb") for _ in range(3)])
    rstd = [C.sbT([128, 512], F32, "rstd") for _ in range(NTG)]
    psum = Ring(consts["psum"])

    def load_w(name, cb, k0, nk):
        w = wring.next()
        sv = wscr[name][k0 * 128:(k0 + nk) * 128, cb * WB_COLS:(cb + 1) * WB_COLS].rearrange("(k p) n -> p k n", p=128)
        P.op("sp", lambda e, w=w, sv=sv, nk=nk: e.dma_start(out=w[:, 0:nk, :], in_=sv),
             reads=[wT[(name, cb, k0)]], writes=[w], dma=True)
        return w

    def mm_group(ps, wlist, X, tg, oc_in_blk):
        n = sum(nk for _, _, nk in wlist)
        i = 0
        for (w, k0, nk) in wlist:
            for k in range(nk):
                st, sp_ = (i == 0), (i == n - 1)
                xk = X[k0 + k]
                P.op("pe", lambda e, ps=ps, w=w, k=k, xk=xk, st=st, sp_=sp_: e.matmul(
                    ps[:, :], lhsT=w[:, k, oc_in_blk * 128:(oc_in_blk + 1) * 128],
                    rhs=xk[:, tg * 512:(tg + 1) * 512], start=st, stop=sp_),
                    reads=[w, xk], writes=[ps])
                i += 1

    def rmsnorm():
        for tg in range(NTG):
            ps = psum.next()
            for kc in range(8):
                sq = sqb.next()
                P.op("act", lambda e, sq=sq, kc=kc, tg=tg: e.activation(
                    out=sq[:, :], in_=xT[kc][:, tg * 512:(tg + 1) * 512], func=AF.Square), reads=[xT[kc]], writes=[sq])
                P.op("pe", lambda e, ps=ps, sq=sq, kc=kc: e.matmul(
                    ps[:, :], lhsT=ones_b[:, :], rhs=sq[:, :], start=(kc == 0), stop=(kc == 7)),
                    reads=[sq, ones_b], writes=[ps])
            r = rstd[tg]
            P.op("act", lambda e, r=r, ps=ps: e.activation(
                out=r[:, :], in_=ps[:, :], func=AF.Sqrt, scale=1.0 / 1024.0, bias=EPS), reads=[ps], writes=[r])
            P.op("dve", lambda e, r=r: e.reciprocal(out=r[:, :], in_=r[:, :]), reads=[r], writes=[r])
        for kc in range(8):
            for tg in range(NTG):
                P.op("dve", lambda e, kc=kc, tg=tg: e.tensor_tensor(
                    out=hT[kc][:, tg * 512:(tg + 1) * 512], in0=xT[kc][:, tg * 512:(tg + 1) * 512],
                    in1=rstd[tg][:, :], op=ALU.mult), reads=[xT[kc], rstd[tg]], writes=[hT[kc]])

    out_ops = []
    for tt in range(NTOK // TT):
        t0 = tt * TT
        for s in range(NSUB):
            xi = xin.next()
            P.op("sp", lambda e, xi=xi, s=s, t0=t0: e.dma_start(out=xi[:, :], in_=din["x2"][t0 + s * 128:t0 + (s + 1) * 128, :]),
                 writes=[xi], dma=True)
            xh, xl, xr = xhi.next(), xlo.next(), xres.next()
            P.op("act", lambda e, xi=xi, xh=xh: e.copy(out=xh[:, :], in_=xi[:, :]), reads=[xi], writes=[xh])
            P.op("dve", lambda e, xi=xi, xh=xh, xr=xr: e.tensor_tensor(out=xr[:, :], in0=xi[:, :], in1=xh[:, :], op=ALU.subtract),
                 reads=[xi, xh], writes=[xr])
            P.op("pool", lambda e, xr=xr, xl=xl: e.tensor_copy(out=xl[:, :], in_=xr[:, :]), reads=[xr], writes=[xl])
            for half in range(2):
                ps = psum.next()
                for j in range(4):
                    kc = half * 4 + j
                    P.op("pe", lambda e, ps=ps, xh=xh, kc=kc, j=j: e.matmul(
                        ps[:, j * 128:(j + 1) * 128], lhsT=xh[:, kc * 128:(kc + 1) * 128], rhs=ident_b[:, :], start=True, stop=False),
                        reads=[xh, ident_b], writes=[ps])
                    P.op("pe", lambda e, ps=ps, xl=xl, kc=kc, j=j: e.matmul(
                        ps[:, j * 128:(j + 1) * 128], lhsT=xl[:, kc * 128:(kc + 1) * 128], rhs=ident_b[:, :], start=False, stop=True),
                        reads=[xl, ident_b], writes=[ps])
                for j in range(4):
                    kc = half * 4 + j
                    eng = "act" if j % 2 == 0 else "dve"
                    if eng == "act":
                        P.op("act", lambda e, ps=ps, kc=kc, j=j, s=s: e.copy(
                            out=xT[kc][:, s * 128:(s + 1) * 128], in_=ps[:, j * 128:(j + 1) * 128]),
                            reads=[ps], writes=[xT[kc]])
                    else:
                        P.op("dve", lambda e, ps=ps, kc=kc, j=j, s=s: e.tensor_copy(
                            out=xT[kc][:, s * 128:(s + 1) * 128], in_=ps[:, j * 128:(j + 1) * 128]),
                            reads=[ps], writes=[xT[kc]])
            pi, pb = pin.next(), pinb.next()
            P.op("sp", lambda e, pi=pi, s=s, t0=t0: e.dma_start(out=pi[:, :], in_=din["p2"][t0 + s * 128:t0 + (s + 1) * 128, :]),
                 writes=[pi], dma=True)
            P.op("pool", lambda e, pi=pi, pb=pb: e.tensor_copy(out=pb[:, :], in_=pi[:, :]), reads=[pi], writes=[pb])
            ps = psum.next()
            psb = ps.ap.bitcast(BF16)
            for j in range(2):
                P.op("pe", lambda e, psb=psb, pb=pb, j=j: e.transpose(
                    psb[:, j * 128:(j + 1) * 128], pb[:, j * 128:(j + 1) * 128], ident_b[:, :]),
                    reads=[pb, ident_b], writes=[ps])
            for j in range(2):
                P.op("act", lambda e, psb=psb, j=j, s=s: e.copy(
                    out=pT[j][:, s * 128:(s + 1) * 128], in_=psb[:, j * 128:(j + 1) * 128]), reads=[ps], writes=[pT[j]])
        for g in range(4):
            for c in range(2):
                tl = attT[2 * g + c]
                P.op("sp", lambda e, tl=tl, g=g, c=c, t0=t0: e.dma_start(out=tl[:, :], in_=exch["att"][g, c, :, t0:t0 + TT]),
                     reads=[exch["att_T"]], writes=[tl], dma=True)
            for c in range(4):
                tl = ybT[4 * g + c]
                P.op("sp", lambda e, tl=tl, g=g, c=c, t0=t0: e.dma_start(out=tl[:, :], in_=exch["yb"][g, c, :, t0:t0 + TT]),
                     reads=[exch["yb_T"]], writes=[tl], dma=True)
        p2s = STAGES.get('p2s', 'BXCD')
        if 'B' in p2s:
            rmsnorm()
        for j in (range(4) if 'B' in p2s else []):
            wga = load_w("wg", j, 0, 8)
            wgb = load_w("wg", 4 + j, 0, 8)
            wa = load_w("woa", j, 0, 8)
            ws0 = load_w("wos", j, 0, 8)
            ws1 = load_w("wos", j, 8, 8)
            for o2 in range(2):
                oc = 2 * j + o2
                for tg in range(NTG):
                    pga, pgb, pya, pyb = psum.next(), psum.next(), psum.next(), psum.next()
                    mm_group(pga, [(wga, 0, 8)], hT, tg, o2)
                    mm_group(pgb, [(wgb, 0, 8)], hT, tg, o2)
                    mm_group(pya, [(wa, 0, 8)], attT, tg, o2)
                    mm_group(pyb, [(ws0, 0, 8), (ws1, 8, 8)], ybT, tg, o2)
                    sa, sb_, m1, m2 = tmpf.next(), tmpf.next(), tmpf.next(), tmpf.next()
                    P.op("act", lambda e, sa=sa, pga=pga: e.activation(out=sa[:, :], in_=pga[:, :], func=AF.Sigmoid),
                         reads=[pga], writes=[sa])
                    P.op("act", lambda e, sb_=sb_, pgb=pgb: e.activation(out=sb_[:, :], in_=pgb[:, :], func=AF.Sigmoid),
                         reads=[pgb], writes=[sb_])
                    P.op("dve", lambda e, m1=m1, sa=sa, pya=pya: e.tensor_tensor(
                        out=m1[:, :], in0=sa[:, :], in1=pya[:, :], op=ALU.mult), reads=[sa, pya], writes=[m1])
                    P.op("dve", lambda e, m2=m2, sb_=sb_, pyb=pyb: e.tensor_tensor(
                        out=m2[:, :], in0=sb_[:, :], in1=pyb[:, :], op=ALU.mult), reads=[sb_, pyb], writes=[m2])
                    P.op("pool", lambda e, m1=m1, m2=m2, oc=oc, tg=tg: e.tensor_tensor(
                        out=mg[oc][:, tg * 512:(tg + 1) * 512], in0=m1[:, :], in1=m2[:, :], op=ALU.add),
                        reads=[m1, m2], writes=[mg[oc]])
        for j in (range(4) if 'X' in p2s else []):
            w = load_w("wout", j, 0, 8)
            for o2 in range(2):
                oc = 2 * j + o2
                for tg in range(NTG):
                    ps = psum.next()
                    mm_group(ps, [(w, 0, 8)], mg, tg, o2)
                    P.op("dve", lambda e, ps=ps, oc=oc, tg=tg: e.tensor_tensor(
                        out=xT[oc][:, tg * 512:(tg + 1) * 512], in0=ps[:, :], in1=xT[oc][:, tg * 512:(tg + 1) * 512],
                        op=ALU.add), reads=[ps, xT[oc]], writes=[xT[oc]])
        if 'C' in p2s:
            rmsnorm()
        actT = [T(big[:, f, :], "act") for f in range(22)]
        for f in range(22):
            old = attT[f] if f < 8 else ybT[f - 8]
            actT[f].w, actT[f].r = old.w, old.r
        for j in (range(11) if 'C' in p2s else []):
            wgt = load_w("wgu", j, 0, 8)
            wup = load_w("wgu", 11 + j, 0, 8)
            for o2 in range(2):
                f = 2 * j + o2
                for tg in range(NTG):
                    pg, pu = psum.next(), psum.next()
                    mm_group(pg, [(wgt, 0, 8)], hT, tg, o2)
                    mm_group(pu, [(wup, 0, 8)], hT, tg, o2)
                    sg = tmpf.next()
                    P.op("act", lambda e, sg=sg, pg=pg: e.activation(out=sg[:, :], in_=pg[:, :], func=AF.Silu),
                         reads=[pg], writes=[sg])
                    P.op("dve", lambda e, sg=sg, pu=pu, f=f, tg=tg: e.tensor_tensor(
                        out=actT[f][:, tg * 512:(tg + 1) * 512], in0=sg[:, :], in1=pu[:, :], op=ALU.mult),
                        reads=[sg, pu], writes=[actT[f]])
        for j in (range(4) if 'C' in p2s else []):
            w0 = load_w("wd", j, 0, 8)
            w1 = load_w("wd", j, 8, 8)
            w2 = load_w("wd", j, 16, 6)
            for o2 in range(2):
                oc = 2 * j + o2
                for tg in range(NTG):
                    ps = psum.next()
                    mm_group(ps, [(w0, 0, 8), (w1, 8, 8), (w2, 16, 6)], actT, tg, o2)
                    P.op("dve", lambda e, ps=ps, oc=oc, tg=tg: e.tensor_tensor(
                        out=xT[oc][:, tg * 512:(tg + 1) * 512], in0=ps[:, :], in1=xT[oc][:, tg * 512:(tg + 1) * 512],
                        op=ALU.add), reads=[ps, xT[oc]], writes=[xT[oc]])
        for f in range(22):
            old = attT[f] if f < 8 else ybT[f - 8]
            old.w, old.r = actT[f].w, actT[f].r
        if 'D' in p2s:
            rmsnorm()
        for j in (range(4) if 'D' in p2s else []):
            wpg = load_w("wpg", j, 0, 8)
            wpp = load_w("wpp", j, 0, 2)
            for o2 in range(2):
                oc = 2 * j + o2
                for tg in range(NTG):
                    pg, pp = psum.next(), psum.next()
                    mm_group(pg, [(wpg, 0, 8)], hT, tg, o2)
                    mm_group(pp, [(wpp, 0, 2)], pT, tg, o2)
                    sg, m = tmpf.next(), tmpf.next()
                    P.op("act", lambda e, sg=sg, pg=pg: e.activation(out=sg[:, :], in_=pg[:, :], func=AF.Sigmoid),
                         reads=[pg], writes=[sg])
                    P.op("dve", lambda e, m=m, sg=sg, pp=pp: e.tensor_tensor(
                        out=m[:, :], in0=sg[:, :], in1=pp[:, :], op=ALU.mult), reads=[sg, pp], writes=[m])
                    P.op("dve", lambda e, m=m, oc=oc, tg=tg: e.tensor_tensor(
                        out=xT[oc][:, tg * 512:(tg + 1) * 512], in0=m[:, :], in1=xT[oc][:, tg * 512:(tg + 1) * 512],
                        op=ALU.add), reads=[m, xT[oc]], writes=[xT[oc]])
        for kc in range(8):
            P.op("act", lambda e, kc=kc: e.copy(out=hT[kc][:, :], in_=xT[kc][:, :]), reads=[xT[kc], hT[kc]], writes=[hT[kc]])
            P.op("dve", lambda e, kc=kc: e.tensor_tensor(out=xT[kc][:, :], in0=xT[kc][:, :], in1=hT[kc][:, :], op=ALU.subtract),
                 reads=[xT[kc], hT[kc]], writes=[xT[kc]])
            P.op("pool", lambda e, kc=kc: e.tensor_copy(out=mg[kc][:, :], in_=xT[kc][:, :]), reads=[xT[kc], mg[kc]], writes=[mg[kc]])
        for s in range(NSUB):
            xo = xin.next()
            for half in range(2):
                ps = psum.next()
                for j in range(4):
                    kc = half * 4 + j
                    P.op("pe", lambda e, ps=ps, kc=kc, j=j, s=s: e.matmul(
                        ps[:, j * 128:(j + 1) * 128], lhsT=hT[kc][:, s * 128:(s + 1) * 128], rhs=ident_b[:, :], start=True, stop=False),
                        reads=[hT[kc], ident_b], writes=[ps])
                    P.op("pe", lambda e, ps=ps, kc=kc, j=j, s=s: e.matmul(
                        ps[:, j * 128:(j + 1) * 128], lhsT=mg[kc][:, s * 128:(s + 1) * 128], rhs=ident_b[:, :], start=False, stop=True),
                        reads=[mg[kc], ident_b], writes=[ps])
                if half == 0:
                    P.op("act", lambda e, ps=ps, xo=xo: e.copy(out=xo[:, 0:512], in_=ps[:, :]), reads=[ps], writes=[xo])
                else:
                    P.op("dve", lambda e, ps=ps, xo=xo: e.tensor_copy(out=xo[:, 512:1024], in_=ps[:, :]),
                         reads=[ps, xo], writes=[xo])
            out_ops.append(P.op("pool", lambda e, xo=xo, s=s, t0=t0: e.dma_start(
                out=dout[t0 + s * 128:t0 + (s + 1) * 128, :], in_=xo[:, :]), reads=[xo], dma=True))
    return out_ops


W1A = 1288
W1B = 768


def load_cast_weight(P, C, src, w, ncols, gain, stg):
    c0 = 0
    while c0 < ncols:
        n = min(WB_COLS, ncols - c0)
        s = stg.next()
        sv = src[:, c0:c0 + n].rearrange("(k p) n -> p k n", p=128)
        P.op("sp", lambda e, s=s, sv=sv, n=n: e.dma_start(out=s[:, :, 0:n], in_=sv), writes=[s], dma=True)
        gv = gain[:, 0:8].unsqueeze(2).to_broadcast([128, 8, n])
        P.op("pool", lambda e, s=s, gv=gv, n=n, c0=c0: e.tensor_tensor(
            out=w[:, :, c0:c0 + n], in0=s[:, :, 0:n], in1=gv, op=ALU.mult), reads=[s, gain], writes=[w])
        c0 += n


def build_prologue(P, C, din, cst, hT_scr, hT_T, NTOKW):
    xin = Ring([C.sbT([128, 1024], F32, "pxin") for _ in range(3)])
    junk = C.sbT([128, 1024], BF16, "pjunk")
    hb = Ring([C.sbT([128, 1024], BF16, "phb") for _ in range(2)])
    ssr = Ring([C.sbT([128, 2], F32, "pss") for _ in range(4)])
    hst = Ring([C.sbT([128, 8, 512], BF16, "phst") for _ in range(2)])
    psum = cst["psring"]
    ident_b = cst["ident_b"]
    for m in range(NTOKW // 512):
        ht = hst.next()
        for s in range(4):
            t0 = m * 512 + s * 128
            xi, ss, h = xin.next(), ssr.next(), hb.next()
            P.op("sp", lambda e, xi=xi, t0=t0: e.dma_start(out=xi[:, :], in_=din["xw"][t0:t0 + 128, :]), writes=[xi], dma=True)
            P.op("act", lambda e, xi=xi, ss=ss: e.activation(out=junk[:, :], in_=xi[:, :], func=AF.Square, accum_out=ss[:, 0:1]),
                 reads=[xi], writes=[junk, ss])
            P.op("act", lambda e, ss=ss: e.activation(out=ss[:, 1:2], in_=ss[:, 0:1], func=AF.Sqrt, scale=1.0 / 1024.0, bias=EPS),
                 reads=[ss], writes=[ss])
            P.op("dve", lambda e, ss=ss: e.reciprocal(out=ss[:, 1:2], in_=ss[:, 1:2]), reads=[ss], writes=[ss])
            P.op("dve", lambda e, xi=xi, ss=ss, h=h: e.tensor_scalar(
                out=h[:, :], in0=xi[:, :], scalar1=ss[:, 1:2], scalar2=None, op0=ALU.mult), reads=[xi, ss], writes=[h])
            ps = psum.next()
            psb = ps.ap.bitcast(BF16)
            for kc in range(8):
                P.op("pe", lambda e, psb=psb, h=h, kc=kc: e.transpose(
                    psb[:, kc * 128:(kc + 1) * 128], h[:, kc * 128:(kc + 1) * 128], ident_b[:, :]),
                    reads=[h, ident_b], writes=[ps])
            P.op("act", lambda e, psb=psb, ht=ht, s=s: e.copy(
                out=ht[:, 0:4, s * 128:(s + 1) * 128], in_=psb[:, 0:512].rearrange("p (k n) -> p k n", k=4)),
                reads=[ps], writes=[ht])
            P.op("dve", lambda e, psb=psb, ht=ht, s=s: e.tensor_copy(
                out=ht[:, 4:8, s * 128:(s + 1) * 128], in_=psb[:, 512:1024].rearrange("p (k n) -> p k n", k=4)),
                reads=[ps, ht], writes=[ht])
        t = T(None, "hTscr")
        hT_T.append(t)
        P.op("pool", lambda e, ht=ht, m=m: e.dma_start(out=hT_scr[:, :, m * 512:(m + 1) * 512], in_=ht[:, :, :]),
             reads=[ht], writes=[t], dma=True)


def build_p1a(P, C, din, g, cst, hT_scr, hT_T, e_yb, e_yb_T, NTOKW, OWN0):
    psum = cst["psring"]
    ident_b, ones_b, U, T1 = cst["ident_b"], cst["ones_b"], cst["U"], cst["T1"]
    stg = Ring([C.sbT([128, 8, WB_COLS], F32, "a_stg") for _ in range(2)])
    w1a = C.sbT([128, 8, W1A], BF16, "w1a")
    load_cast_weight(P, C, din["w1a"][g], w1a, W1A, cst["g1"], stg)
    small = {}
    for nm, shp in (("convw", [128, 6, 4]), ("convb", [128, 6]), ("dtb", [128, 8]), ("alog", [128, 8]),
                    ("dsk", [128, 8]), ("sng", [128, 512])):
        t = C.sbT(shp, F32, "a_" + nm)
        P.op("sp", lambda e, t=t, nm=nm: e.dma_start(out=t.ap, in_=din[nm][g]), writes=[t], dma=True)
        small[nm] = t
    cw, cb, dtb, alog, dsk, sng = (small[k] for k in ("convw", "convb", "dtb", "alog", "dsk", "sng"))
    tokmask = cst["tokmask"]
    Abc = C.sbT([128, 8], F32, "Abc")
    P.op("act", lambda e: e.activation(out=Abc[:, :], in_=alog[:, :], func=AF.Exp), reads=[alog], writes=[Abc])
    P.op("dve", lambda e: e.tensor_scalar(out=Abc[:, :], in0=Abc[:, :], scalar1=-1.0, scalar2=None, op0=ALU.mult),
         reads=[Abc], writes=[Abc])
    dtb4 = C.sbT([128, 32], F32, "dtb4")
    Abc4 = C.sbT([128, 32], F32, "Abc4")
    for s_ in range(4):
        P.op("dve", lambda e, s_=s_: e.tensor_copy(out=dtb4[:, 8 * s_:8 * s_ + 8], in_=dtb[:, :]), reads=[dtb, dtb4], writes=[dtb4])
        P.op("dve", lambda e, s_=s_: e.tensor_copy(out=Abc4[:, 8 * s_:8 * s_ + 8], in_=Abc[:, :]), reads=[Abc, Abc4], writes=[Abc4])
    s32 = Ring([C.sbT([128, 32], F32, "a_s32") for _ in range(21)])
    a3mr = Ring([C.sbT([128, 3, 32], BF16, "a_a3m") for _ in range(3)])
    s48 = Ring([C.sbT([128, 4, 8], F32, "a_s48") for _ in range(12)])
    S = C.sbT([128, 512], F32, "S")
    Sbf = C.sbT([128, 512], BF16, "Sbf")
    xbc = C.sbT([128, 6, 515], F32, "xbc")
    P.op("pool", lambda e: e.memset(S[:, :], 0.0), writes=[S])
    P.op("pool", lambda e: e.memset(Sbf[:, :], 0.0), writes=[Sbf])
    P.op("pool", lambda e: e.memset(xbc[:, :, 0:3], 0.0), writes=[xbc])
    hring = Ring([C.sbT([128, 8, 512], BF16, "a_hT") for _ in range(2)])
    cring = Ring([C.sbT([128, 6, 512], BF16, "a_co") for _ in range(2)])
    accr = Ring([C.sbT([128, 512], F32, "a_acc") for _ in range(5)])
    f512 = Ring([C.sbT([128, 512], F32, "a_f512") for _ in range(6)])
    szr = Ring([C.sbT([128, 512], F32, "a_sz") for _ in range(5)])
    b512 = Ring([C.sbT([128, 512], BF16, "a_b512") for _ in range(28)])
    s8 = Ring([C.sbT([128, 8], F32, "a_s8") for _ in range(64)])
    s2 = Ring([C.sbT([128, 2], F32, "a_s2") for _ in range(6)])
    Rr = Ring([C.sbT([128, 3, 8, 128], BF16, "a_R") for _ in range(4)])
    a3r = Ring([C.sbT([128, 3, 8], BF16, "a_a3") for _ in range(6)])
    Lr = Ring([C.sbT([128, 8, 128], BF16, "a_L") for _ in range(5)])
    Mr = Ring([C.sbT([128, 8, 128], BF16, "a_M") for _ in range(5)])
    cbm = Ring([C.sbT([128, 128], BF16, "a_cbm") for _ in range(5)])
    ybst = Ring([C.sbT([128, 4, 512], BF16, "a_ybst") for _ in range(2)])
    junk = C.sbT([128, 512], BF16, "a_junk")
    pss = cst["ps_small"]
    i_eng = 0
    for m in range(NTOKW // 512):
        tok0 = m * 512
        own = tok0 >= OWN0
        hT = hring.next()
        P.op("sp", lambda e, hT=hT, tok0=tok0: e.dma_start(out=hT[:, :, :], in_=hT_scr[:, :, tok0:tok0 + 512]),
             reads=[hT_T[m]], writes=[hT], dma=True)
        for c in range(6):
            ps = psum.next()
            for kc in range(8):
                P.op("pe", lambda e, ps=ps, kc=kc, c=c, hT=hT: e.matmul(
                    ps[:, :], lhsT=w1a[:, kc, c * 128:(c + 1) * 128], rhs=hT[:, kc, :], start=(kc == 0), stop=(kc == 7)),
                    reads=[w1a, hT], writes=[ps])
            if c % 2 == 0:
                P.op("act", lambda e, ps=ps, c=c: e.copy(out=xbc[:, c, 3:515], in_=ps[:, :]), reads=[ps, xbc], writes=[xbc])
            else:
                P.op("dve", lambda e, ps=ps, c=c: e.tensor_copy(out=xbc[:, c, 3:515], in_=ps[:, :]), reads=[ps, xbc], writes=[xbc])
        co = cring.next()
        for c in range(6):
            eng = "pool" if c in (1, 4) else "dve"
            acc = accr.next()
            P.op(eng, lambda e, acc=acc, c=c: e.tensor_scalar(
                out=acc[:, :], in0=xbc[:, c, 0:512], scalar1=cw[:, c, 0:1], scalar2=None, op0=ALU.mult),
                reads=[xbc, cw], writes=[acc])
            for k in range(1, 4):
                if eng == "dve":
                    P.op(eng, lambda e, acc=acc, c=c, k=k: e.scalar_tensor_tensor(
                        out=acc[:, :], in0=xbc[:, c, k:k + 512], scalar=cw[:, c, k:k + 1], in1=acc[:, :],
                        op0=ALU.mult, op1=ALU.add), reads=[xbc, cw, acc], writes=[acc])
                else:
                    tmpc = accr.next()
                    P.op(eng, lambda e, tmpc=tmpc, c=c, k=k: e.tensor_scalar(
                        out=tmpc[:, :], in0=xbc[:, c, k:k + 512], scalar1=cw[:, c, k:k + 1], scalar2=None, op0=ALU.mult),
                        reads=[xbc, cw], writes=[tmpc])
                    P.op(eng, lambda e, tmpc=tmpc, acc=acc: e.tensor_tensor(out=acc[:, :], in0=acc[:, :], in1=tmpc[:, :], op=ALU.add),
                         reads=[acc, tmpc], writes=[acc])
            P.op("act", lambda e, acc=acc, c=c, co=co: e.activation(
                out=co[:, c, :], in_=acc[:, :], func=AF.Silu, bias=cb[:, c:c + 1], scale=1.0), reads=[acc, cb, co], writes=[co])
        P.op("pool", lambda e: e.tensor_copy(out=xbc[:, :, 0:3], in_=xbc[:, :, 512:515]), reads=[xbc], writes=[xbc])
        yst = ybst.next() if own else None
        ctx = {}

        def pre(s, m=m, hT=hT, co=co, own=own):
            sub = slice(s * 128, (s + 1) * 128)
            tile_idx = m * 4 + s
            dt_ = TV(dtm, dtm.ap[:, 8 * s:8 * s + 8])
            a3 = TV(a3m, a3m.ap[:, :, 8 * s:8 * s + 8])
            wst = TV(wstm, wstm.ap[:, s, :])
            cdec = TV(cdecm, cdecm.ap[:, s, :])
            wend = TV(wendm, wendm.ap[:, s, :])
            yield
            pxs = psum.next()
            pxb = pxs.ap.bitcast(BF16)
            for c in range(5):
                P.op("pe", lambda e, pxb=pxb, c=c, co=co, sub=sub: e.transpose(
                    pxb[:, c * 128:(c + 1) * 128], co[:, c, sub], ident_b[:, :]), reads=[co, ident_b], writes=[pxs])
            xs_tm, Btm, xdt, xdtw = b512.next(), b512.next(), b512.next(), b512.next()
            P.op("act", lambda e, pxb=pxb, xs_tm=xs_tm: e.copy(out=xs_tm[:, :], in_=pxb[:, 0:512]), reads=[pxs], writes=[xs_tm])
            P.op("act", lambda e, pxb=pxb, Btm=Btm: e.copy(out=Btm[:, 0:128], in_=pxb[:, 512:640]), reads=[pxs], writes=[Btm])
            P.op("pool", lambda e, xs_tm=xs_tm, xdt=xdt, dt_=dt_: e.tensor_tensor(
                out=xdt[:, :].rearrange("p (h d) -> p h d", h=8), in0=xs_tm[:, :].rearrange("p (h d) -> p h d", h=8),
                in1=dt_[:, :].unsqueeze(2).to_broadcast([128, 8, 64]), op=ALU.mult), reads=[xs_tm, dt_], writes=[xdt])
            P.op("pool", lambda e, xdt=xdt, xdtw=xdtw, wend=wend: e.tensor_tensor(
                out=xdtw[:, :].rearrange("p (h d) -> p h d", h=8), in0=xdt[:, :].rearrange("p (h d) -> p h d", h=8),
                in1=wend[:, :].unsqueeze(2).to_broadcast([128, 8, 64]), op=ALU.mult), reads=[xdt, wend], writes=[xdtw])
            yield
            if own:
                R, L, Mh, cbt = Rr.next(), Lr.next(), Mr.next(), cbm.next()
                for i3 in range(3):
                    P.op("dve" if i3 != 1 else "pool", lambda e, R=R, a3=a3, i3=i3: e.tensor_tensor(
                        out=R[:, i3, :, :], in0=U[:, :].unsqueeze(1).to_broadcast([128, 8, 128]),
                        in1=a3[:, i3, :].unsqueeze(2).to_broadcast([128, 8, 128]), op=ALU.mult), reads=[U, a3, R], writes=[R])
                for hh in range(2):
                    pD = psum.next()
                    for i3 in range(3):
                        P.op("pe", lambda e, pD=pD, R=R, hh=hh, i3=i3: e.matmul(
                            pD[:, :], lhsT=T1[:, :], rhs=R[:, i3, hh * 4:(hh + 1) * 4, :].rearrange("p h l -> p (h l)"),
                            start=(i3 == 0), stop=(i3 == 2)), reads=[T1, R], writes=[pD])
                    P.op("act", lambda e, pD=pD, L=L, hh=hh: e.activation(
                        out=L[:, hh * 4:(hh + 1) * 4, :].rearrange("p h l -> p (h l)"), in_=pD[:, :], func=AF.Exp),
                        reads=[pD, L], writes=[L])
                yield
                pcb = pss["cb%d" % s]
                P.op("pe", lambda e, co=co, sub=sub: e.matmul(
                    pcb[:, :], lhsT=co[:, 4, sub], rhs=co[:, 5, sub], start=True, stop=True), reads=[co], writes=[pcb])
                P.op("dve", lambda e, cbt=cbt: e.tensor_tensor(out=cbt[:, :], in0=pcb[:, :], in1=U[:, :], op=ALU.mult),
                     reads=[pcb, U], writes=[cbt])
                P.op("pool", lambda e, Mh=Mh, L=L, cbt=cbt: e.tensor_tensor(
                    out=Mh[:, :, :], in0=L[:, :, :], in1=cbt[:, :].unsqueeze(1).to_broadcast([128, 8, 128]), op=ALU.mult),
                    reads=[L, cbt], writes=[Mh])
                xsD = b512.next()
                P.op("pool", lambda e, xs_tm=xs_tm, xsD=xsD: e.tensor_tensor(
                    out=xsD[:, :].rearrange("p (h d) -> p h d", h=8), in0=xs_tm[:, :].rearrange("p (h d) -> p h d", h=8),
                    in1=dsk[:, :].unsqueeze(2).to_broadcast([128, 8, 64]), op=ALU.mult), reads=[xs_tm, dsk], writes=[xsD])
                yield
                pz = psum.next()
                for kc in range(8):
                    P.op("pe", lambda e, pz=pz, kc=kc, hT=hT, sub=sub: e.matmul(
                        pz[:, :], lhsT=hT[:, kc, sub], rhs=w1a[:, kc, 768:1280], start=(kc == 0), stop=(kc == 7)),
                        reads=[w1a, hT], writes=[pz])
                sz = szr.next()
                P.op("act", lambda e, pz=pz, sz=sz: e.activation(out=sz[:, :], in_=pz[:, :], func=AF.Silu), reads=[pz], writes=[sz])
            ctx[s] = dict(locals())
            yield

        def seq(s, m=m, hT=hT, co=co, own=own, yst=yst):
            L_ = ctx[s]
            sub = L_["sub"]
            wst, cdec, xdt, xdtw, Btm = L_["wst"], L_["cdec"], L_["xdt"], L_["xdtw"], L_["Btm"]
            if own:
                Mh, xsD, sz = L_["Mh"], L_["xsD"], L_["sz"]
                pyo, py = psum.next(), psum.next()
                P.op("pe", lambda e, pyo=pyo, co=co, sub=sub: e.matmul(
                    pyo[:, :], lhsT=co[:, 5, sub], rhs=Sbf[:, :], start=True, stop=True), reads=[co, Sbf], writes=[pyo])
                P.op("pe", lambda e, py=py, xsD=xsD: e.matmul(py[:, :], lhsT=ident_b[:, :], rhs=xsD[:, :], start=True, stop=False),
                     reads=[ident_b, xsD], writes=[py])
                for h in range(8):
                    P.op("pe", lambda e, py=py, Mh=Mh, xdt=xdt, h=h: e.matmul(
                        py[:, h * 64:(h + 1) * 64], lhsT=Mh[:, h, :], rhs=xdt[:, h * 64:(h + 1) * 64], start=False, stop=(h == 7)),
                        reads=[Mh, xdt], writes=[py])
                y1, y2, y3 = f512.next(), f512.next(), f512.next()
                P.op("dve", lambda e, pyo=pyo, y1=y1, wst=wst: e.tensor_tensor(
                    out=y1[:, :].rearrange("p (h d) -> p h d", h=8), in0=pyo[:, :].rearrange("p (h d) -> p h d", h=8),
                    in1=wst[:, :].unsqueeze(2).to_broadcast([128, 8, 64]), op=ALU.mult), reads=[pyo, wst], writes=[y1])
                P.op("dve", lambda e, y1=y1, y2=y2, py=py: e.tensor_tensor(out=y2[:, :], in0=y1[:, :], in1=py[:, :], op=ALU.add),
                     reads=[y1, py], writes=[y2])
                P.op("pool", lambda e, y2=y2, y3=y3, sz=sz: e.tensor_tensor(out=y3[:, :], in0=y2[:, :], in1=sz[:, :], op=ALU.mult),
                     reads=[y2, sz], writes=[y3])
                ss = s2.next()
                P.op("act", lambda e, y3=y3, ss=ss: e.activation(out=junk[:, :], in_=y3[:, :], func=AF.Square, accum_out=ss[:, 0:1]),
                     reads=[y3], writes=[junk, ss])
                P.op("act", lambda e, ss=ss: e.activation(out=ss[:, 1:2], in_=ss[:, 0:1], func=AF.Sqrt, scale=1.0 / 512.0, bias=EPS),
                     reads=[ss], writes=[ss])
                P.op("dve", lambda e, ss=ss: e.reciprocal(out=ss[:, 1:2], in_=ss[:, 1:2]), reads=[ss], writes=[ss])
                yn = b512.next()
                P.op("dve", lambda e, y3=y3, ss=ss, yn=yn: e.scalar_tensor_tensor(
                    out=yn[:, :], in0=y3[:, :], scalar=ss[:, 1:2], in1=sng[:, :], op0=ALU.mult, op1=ALU.mult),
                    reads=[y3, ss, sng], writes=[yn])
                pyt = psum.next()
                pytb = pyt.ap.bitcast(BF16)
                for c in range(4):
                    P.op("pe", lambda e, pytb=pytb, yn=yn, c=c: e.transpose(
                        pytb[:, c * 128:(c + 1) * 128], yn[:, c * 128:(c + 1) * 128], ident_b[:, :]),
                        reads=[yn, ident_b], writes=[pyt])
                P.op("act", lambda e, pytb=pytb, yst=yst, sub=sub: e.copy(
                    out=yst[:, :, sub], in_=pytb[:, 0:512].rearrange("p (c n) -> p c n", c=4)), reads=[pyt, yst], writes=[yst])
            pst = psum.next()
            P.op("pe", lambda e, pst=pst, Btm=Btm, xdtw=xdtw: e.matmul(
                pst[:, :], lhsT=Btm[:, 0:128], rhs=xdtw[:, :], start=True, stop=True), reads=[Btm, xdtw], writes=[pst])
            P.op("pool", lambda e, cdec=cdec: e.tensor_tensor(
                out=S[:, :].rearrange("p (h d) -> p h d", h=8), in0=S[:, :].rearrange("p (h d) -> p h d", h=8),
                in1=cdec[:, :].unsqueeze(2).to_broadcast([128, 8, 64]), op=ALU.mult), reads=[S, cdec], writes=[S])
            P.op("dve", lambda e, pst=pst: e.tensor_tensor(out=S[:, :], in0=S[:, :], in1=pst[:, :], op=ALU.add),
                 reads=[S, pst], writes=[S])
            if own or (m * 4 + s + 1) * 128 >= OWN0:
                P.op("act", lambda e: e.copy(out=Sbf[:, :], in_=S[:, :]), reads=[S, Sbf], writes=[Sbf])

        for s in range(4):
            pdt = pss["dt%d" % s]
            for kc in range(8):
                P.op("pe", lambda e, kc=kc, hT=hT, s=s, pdt=pdt: e.matmul(
                    pdt[:, :], lhsT=hT[:, kc, s * 128:(s + 1) * 128], rhs=w1a[:, kc, 1280:1288], start=(kc == 0), stop=(kc == 7)),
                    reads=[w1a, hT], writes=[pdt])
        pd_all = TV(pss["dt0"].parent, pss["dt0"].parent.ap[:, 0:32])
        dtr, ax, ee, dtm, am, ar1, ar2 = (s32.next() for _ in range(7))
        a3m = a3mr.next()
        P.op("dve", lambda e, dtr=dtr: e.tensor_tensor(out=dtr[:, :], in0=pd_all[:, :], in1=dtb4[:, :], op=ALU.add),
             reads=[pd_all, dtb4], writes=[dtr])
        P.op("act", lambda e, dtr=dtr, ax=ax: e.activation(out=ax[:, :], in_=dtr[:, :], func=AF.Abs), reads=[dtr], writes=[ax])
        P.op("act", lambda e, ax=ax, ee=ee: e.activation(out=ee[:, :], in_=ax[:, :], func=AF.Exp, scale=-1.0), reads=[ax], writes=[ee])
        P.op("act", lambda e, ee=ee: e.activation(out=ee[:, :], in_=ee[:, :], func=AF.Ln, bias=1.0, scale=1.0), reads=[ee], writes=[ee])
        P.op("dve", lambda e, dtr=dtr, ee=ee, dtm=dtm: e.scalar_tensor_tensor(
            out=dtm[:, :], in0=dtr[:, :], scalar=0.0, in1=ee[:, :], op0=ALU.max, op1=ALU.add), reads=[dtr, ee], writes=[dtm])
        P.op("dve", lambda e, dtm=dtm, m=m: e.tensor_tensor(
            out=dtm[:, :].rearrange("p (s h) -> p s h", s=4), in0=dtm[:, :].rearrange("p (s h) -> p s h", s=4),
            in1=tokmask[:, 4 * m:4 * m + 4].unsqueeze(2).to_broadcast([128, 4, 8]), op=ALU.mult), reads=[dtm, tokmask], writes=[dtm])
        P.op("dve", lambda e, dtm=dtm, am=am: e.tensor_tensor(out=am[:, :], in0=dtm[:, :], in1=Abc4[:, :], op=ALU.mult),
             reads=[dtm, Abc4], writes=[am])
        P.op("act", lambda e, am=am, a3m=a3m: e.copy(out=a3m[:, 0, :], in_=am[:, :]), reads=[am, a3m], writes=[a3m])
        P.op("dve", lambda e, am=am, a3m=a3m, ar1=ar1: e.tensor_tensor(out=ar1[:, :], in0=am[:, :], in1=a3m[:, 0, :], op=ALU.subtract),
             reads=[am, a3m], writes=[ar1])
        P.op("act", lambda e, ar1=ar1, a3m=a3m: e.copy(out=a3m[:, 1, :], in_=ar1[:, :]), reads=[ar1, a3m], writes=[a3m])
        P.op("dve", lambda e, ar1=ar1, a3m=a3m, ar2=ar2: e.tensor_tensor(out=ar2[:, :], in0=ar1[:, :], in1=a3m[:, 1, :], op=ALU.subtract),
             reads=[ar1, a3m], writes=[ar2])
        P.op("act", lambda e, ar2=ar2, a3m=a3m: e.copy(out=a3m[:, 2, :], in_=ar2[:, :]), reads=[ar2, a3m], writes=[a3m])
        for s in range(4):
            pac = pss["acs%d" % s]
            for i3 in range(3):
                P.op("pe", lambda e, a3m=a3m, i3=i3, s=s, pac=pac: e.matmul(
                    pac[:, 0:8], lhsT=U[:, :], rhs=a3m[:, i3, 8 * s:8 * s + 8], start=(i3 == 0), stop=(i3 == 2)),
                    reads=[U, a3m], writes=[pac])
            for i3 in range(3):
                P.op("pe", lambda e, a3m=a3m, i3=i3, s=s, pac=pac: e.matmul(
                    pac[:, 8:16], lhsT=ones_b[:, :], rhs=a3m[:, i3, 8 * s:8 * s + 8], start=(i3 == 0), stop=(i3 == 2)),
                    reads=[ones_b, a3m], writes=[pac])
        par = pss["acs0"].parent
        pacs = TV(par, par.ap[:, 64:128].rearrange("p (s t h) -> p s t h", s=4, t=2)[:, :, 0, :])
        ptot = TV(par, par.ap[:, 64:128].rearrange("p (s t h) -> p s t h", s=4, t=2)[:, :, 1, :])
        acsm, wstm, cdecm, wendm = (s48.next() for _ in range(4))
        P.op("act", lambda e, acsm=acsm: e.copy(out=acsm[:, :, :], in_=pacs[:, :, :]), reads=[pacs], writes=[acsm])
        P.op("act", lambda e, wstm=wstm: e.activation(out=wstm[:, :, :], in_=pacs[:, :, :], func=AF.Exp), reads=[pacs], writes=[wstm])
        P.op("act", lambda e, cdecm=cdecm: e.activation(out=cdecm[:, :, :], in_=ptot[:, :, :], func=AF.Exp), reads=[ptot], writes=[cdecm])
        P.op("dve", lambda e, wendm=wendm, acsm=acsm: e.tensor_tensor(out=wendm[:, :, :], in0=ptot[:, :, :], in1=acsm[:, :, :], op=ALU.subtract),
             reads=[ptot, acsm], writes=[wendm])
        P.op("act", lambda e, wendm=wendm: e.activation(out=wendm[:, :, :], in_=wendm[:, :, :], func=AF.Exp), reads=[wendm], writes=[wendm])
        gens = [pre(s) for s in range(4)]
        while gens:
            for g_ in list(gens):
                try:
                    next(g_)
                except StopIteration:
                    gens.remove(g_)
        for s in range(4):
            seq(s)
        if own:
            o0 = tok0 - OWN0
            P.op("pool", lambda e, yst=yst, o0=o0: e.dma_start(
                out=e_yb[g, :, :, o0:o0 + 512].rearrange("c p n -> p c n"), in_=yst[:, :, :]),
                reads=[yst], writes=[e_yb_T], dma=True)


def build_p1_init(P, C, din, cst, NTOKW):
    KT = C.sb([96, 4, NTOKW], BF16, "KT")
    VA = C.sb([128, NTOKW // 128, 2, 3, 64], BF16, "VA")
    kmT = C.sbT([64, 4, 32], BF16, "kmT")
    Mpad = [C.sbT([128, 4, 96], BF16, "Mpad") for _ in range(2)]
    for h in range(4):
        P.op("sp", lambda e, h=h: e.dma_start(out=KT[64:96, h, :], in_=din["kind"]), dma=True)
    P.op("pool", lambda e: e.memset(VA[:, :, :, 1, :], 1.0))
    for mp in Mpad:
        P.op("pool", lambda e, mp=mp: e.memset(mp[:, :, :], 0.0), writes=[mp])
    P.op("pool", lambda e: e.memset(kmT[:, :, :], 0.0), writes=[kmT])
    G = C.sbT([128, 512], F32, "G")
    gq, gk = cst["gq"], cst["gk"]
    for h in range(4):
        P.op("dve", lambda e, h=h: e.tensor_scalar(out=G[:, h * 64:(h + 1) * 64], in0=gq[:, :], scalar1=0.125, scalar2=None,
                                                   op0=ALU.mult), reads=[gq, G], writes=[G])
        P.op("dve", lambda e, h=h: e.tensor_copy(out=G[:, 256 + h * 64:256 + (h + 1) * 64], in_=gk[:, :]), reads=[gk, G], writes=[G])
    bb4 = C.sbT([128, 128], F32, "bb4")
    for h in range(4):
        P.op("dve", lambda e, h=h: e.tensor_copy(out=bb4[:, h * 32:(h + 1) * 32], in_=cst["blkbias"][:, :]),
             reads=[cst["blkbias"], bb4], writes=[bb4])
    P.barrier()
    nm = NTOKW // 512
    return dict(KT=KT, VA=VA, kmT=kmT, Mpad=Ring(Mpad), G=G, bb4=bb4,
                KT_T=[T(None, "KT%d" % i) for i in range(nm)], VA_T=[T(None, "VA%d" % i) for i in range(nm)])


def build_p1b(P, C, din, g, cst, A, hT_scr, hT_T, e_att, e_att_T, NTOKW, OWN0):
    psum = cst["psring"]
    po_ring = cst["po_ring"]
    pss = cst["ps_small"]
    ident_b, negm = cst["ident_b"], cst["negm"]
    KT, VA, kmT, G, bb4 = A["KT"], A["VA"], A["kmT"], A["G"], A["bb4"]
    KT_T, VA_T = A["KT_T"], A["VA_T"]
    stg = Ring([C.sbT([128, 8, WB_COLS], F32, "b_stg") for _ in range(2)])
    w1b = C.sbT([128, 8, W1B], BF16, "w1b")
    load_cast_weight(P, C, din["w1b"][g], w1b, W1B, cst["g1"], stg)
    hring = Ring([C.sbT([128, 8, 512], BF16, "b_hT") for _ in range(2)])
    f512 = Ring([C.sbT([128, 512], F32, "b_f512") for _ in range(4)])
    b512 = Ring([C.sbT([128, 512], BF16, "b_b512") for _ in range(3)])
    ptr = Ring([C.sbT([128, 512], BF16, "b_pt") for _ in range(6)])
    s8 = Ring([C.sbT([128, 8], F32, "b_s8") for _ in range(6)])
    g128 = Ring([C.sbT([128, 128], F32, "b_g128") for _ in range(6)])
    t8r = Ring([C.sbT([128, 32], F32, "b_t8") for _ in range(2)])
    kmf = C.sbT([64, 4, 2], F32, "b_kmf")
    QTr = Ring([C.sbT([96, 4, 512], BF16, "b_QT") for _ in range(2)])
    ast = [Ring([C.sbT([128, 512], BF16, "b_ast") for _ in range(2)]) for _ in range(2)]
    rdr = Ring([C.sbT([128, 512], F32, "b_rd") for _ in range(2)])
    outs = []
    for m in range(NTOKW // 512):
        tok0 = m * 512
        own = tok0 >= OWN0
        c0 = 0 if own else 256
        h0 = 0 if own else 4
        hT = hring.next()
        P.op("sp", lambda e, hT=hT, tok0=tok0: e.dma_start(out=hT[:, :, :], in_=hT_scr[:, :, tok0:tok0 + 512]),
             reads=[hT_T[m]], writes=[hT], dma=True)
        QT = QTr.next() if own else None
        for s in range(4):
            sub = slice(s * 128, (s + 1) * 128)
            kt = m * 4 + s
            pqk, pv = psum.next(), psum.next()
            for kc in range(8):
                P.op("pe", lambda e, pqk=pqk, kc=kc, hT=hT, sub=sub, c0=c0: e.matmul(
                    pqk[:, c0:512], lhsT=hT[:, kc, sub], rhs=w1b[:, kc, c0:512], start=(kc == 0), stop=(kc == 7)),
                    reads=[w1b, hT], writes=[pqk])
            for kc in range(8):
                P.op("pe", lambda e, pv=pv, kc=kc, hT=hT, sub=sub: e.matmul(
                    pv[:, 0:256], lhsT=hT[:, kc, sub], rhs=w1b[:, kc, 512:768], start=(kc == 0), stop=(kc == 7)),
                    reads=[w1b, hT], writes=[pv])
            sq, ssum, tt = f512.next(), s8.next(), f512.next()
            P.op("act", lambda e, pqk=pqk, sq=sq, c0=c0: e.activation(out=sq[:, c0:512], in_=pqk[:, c0:512], func=AF.Square),
                 reads=[pqk], writes=[sq])
            P.op("dve", lambda e, sq=sq, ssum=ssum, c0=c0, h0=h0: e.tensor_reduce(
                out=ssum[:, h0:8], in_=sq[:, c0:512].rearrange("p (h d) -> p h d", d=64), axis=AX.X, op=ALU.add),
                reads=[sq], writes=[ssum])
            P.op("act", lambda e, ssum=ssum, h0=h0: e.activation(
                out=ssum[:, h0:8], in_=ssum[:, h0:8], func=AF.Sqrt, scale=1.0 / 64.0, bias=EPS), reads=[ssum], writes=[ssum])
            P.op("dve", lambda e, ssum=ssum, h0=h0: e.reciprocal(out=ssum[:, h0:8], in_=ssum[:, h0:8]), reads=[ssum], writes=[ssum])
            P.op("dve", lambda e, pqk=pqk, tt=tt, ssum=ssum, c0=c0, h0=h0: e.tensor_tensor(
                out=tt[:, c0:512].rearrange("p (h d) -> p h d", d=64), in0=pqk[:, c0:512].rearrange("p (h d) -> p h d", d=64),
                in1=ssum[:, h0:8].unsqueeze(2).to_broadcast([128, 8 - h0, 64]), op=ALU.mult), reads=[pqk, ssum], writes=[tt])
            qkn = b512.next()
            P.op("pool", lambda e, tt=tt, qkn=qkn, c0=c0: e.tensor_tensor(
                out=qkn[:, c0:512], in0=tt[:, c0:512], in1=G[:, c0:512], op=ALU.mult), reads=[tt, G], writes=[qkn])
            pkt = psum.next()
            pktb = pkt.ap.bitcast(BF16)
            for h in range(4):
                P.op("pe", lambda e, pktb=pktb, qkn=qkn, h=h: e.transpose(
                    pktb[0:64, h * 128:(h + 1) * 128], qkn[:, 256 + h * 64:256 + (h + 1) * 64], ident_b[:, :]),
                    reads=[qkn, ident_b], writes=[pkt])
            P.op("act", lambda e, pktb=pktb, tok0=tok0, s=s: e.copy(
                out=KT[0:64, :, tok0 + s * 128:tok0 + (s + 1) * 128], in_=pktb[0:64, 0:512].rearrange("p (h n) -> p h n", h=4)),
                reads=[pkt, KT_T[m]], writes=[KT_T[m]])
            P.op("act", lambda e, pv=pv, kt=kt: e.copy(
                out=VA[:, kt, :, 0, :], in_=pv[:, 0:256].rearrange("p (a b d) -> p a b d", a=2, b=2)[:, :, 0, :]),
                reads=[pv, VA_T[m]], writes=[VA_T[m]])
            P.op("dve", lambda e, pv=pv, kt=kt: e.tensor_copy(
                out=VA[:, kt, :, 2, :], in_=pv[:, 0:256].rearrange("p (a b d) -> p a b d", a=2, b=2)[:, :, 1, :]),
                reads=[pv, VA_T[m]], writes=[VA_T[m]])
            if own:
                pqt = psum.next()
                pqtb = pqt.ap.bitcast(BF16)
                for h in range(4):
                    P.op("pe", lambda e, pqtb=pqtb, qkn=qkn, h=h: e.transpose(
                        pqtb[0:64, h * 128:(h + 1) * 128], qkn[:, h * 64:(h + 1) * 64], ident_b[:, :]),
                        reads=[qkn, ident_b], writes=[pqt])
                P.op("dve", lambda e, pqtb=pqtb, QT=QT, sub=sub: e.tensor_copy(
                    out=QT[0:64, :, sub], in_=pqtb[0:64, 0:512].rearrange("p (h n) -> p h n", h=4)),
                    reads=[pqt, QT], writes=[QT])
        P.op("dve", lambda e, tok0=tok0: e.tensor_reduce(
            out=kmf[:, :, :], in_=KT[0:64, :, tok0:tok0 + 512].rearrange("p h (b k) -> p h b k", b=2), axis=AX.X, op=ALU.add),
            reads=[KT_T[m]], writes=[kmf])
        P.op("dve", lambda e, m=m: e.tensor_scalar(out=kmT[:, :, 2 * m:2 * m + 2], in0=kmf[:, :, :], scalar1=1.0 / 256.0,
                                                   scalar2=None, op0=ALU.mult), reads=[kmf, kmT], writes=[kmT])
        if not own:
            continue
        for s in range(4):
            sub = slice(s * 128, (s + 1) * 128)
            ownblk = 2 * m + s // 2
            pg = pss["gate"]
            for h in range(4):
                P.op("pe", lambda e, h=h, QT=QT, sub=sub: e.matmul(
                    pg[:, h * 32:(h + 1) * 32], lhsT=QT[0:64, h, sub], rhs=kmT[0:64, h, :], start=True, stop=True),
                    reads=[QT, kmT], writes=[pg])
            gm, m1, m2, t8 = g128.next(), g128.next(), g128.next(), t8r.next()
            P.op("dve", lambda e, gm=gm: e.tensor_tensor(out=gm[:, :], in0=pg[:, :], in1=bb4[:, :], op=ALU.add),
                 reads=[pg, bb4], writes=[gm])
            P.op("pool", lambda e, gm=gm, ownblk=ownblk: e.memset(
                gm[:, :].rearrange("p (h b) -> p h b", h=4)[:, :, ownblk:32], NEG), reads=[gm], writes=[gm])
            for h in range(4):
                P.op("dve", lambda e, gm=gm, t8=t8, h=h: e.max(out=t8[:, h * 8:(h + 1) * 8], in_=gm[:, h * 32:(h + 1) * 32]),
                     reads=[gm, t8], writes=[t8])
            P.op("dve", lambda e, gm=gm, m1=m1, t8=t8: e.tensor_tensor(
                out=m1[:, :].rearrange("p (h b) -> p h b", h=4), in0=gm[:, :].rearrange("p (h b) -> p h b", h=4),
                in1=t8[:, :].rearrange("p (h k) -> p h k", h=4)[:, :, 2:3].to_broadcast([128, 4, 32]), op=ALU.is_lt),
                reads=[gm, t8], writes=[m1])
            P.op("dve", lambda e, gm=gm, m2=m2: e.tensor_scalar(
                out=m2[:, :], in0=gm[:, :], scalar1=NEG / 2, scalar2=NEG, op0=ALU.is_lt, op1=ALU.mult), reads=[gm], writes=[m2])
            Mp = A["Mpad"].next()
            P.op("dve", lambda e, Mp=Mp, m1=m1, m2=m2: e.scalar_tensor_tensor(
                out=Mp[:, :, 64:96], in0=m1[:, :].rearrange("p (h b) -> p h b", h=4), scalar=NEG,
                in1=m2[:, :].rearrange("p (h b) -> p h b", h=4), op0=ALU.mult, op1=ALU.min), reads=[m1, m2, Mp], writes=[Mp])
            P.op("pool", lambda e, Mp=Mp, ownblk=ownblk: e.memset(Mp[:, :, 64 + ownblk:65 + ownblk], 0.0), reads=[Mp], writes=[Mp])
            pmt = psum.next()
            pmtb = pmt.ap.bitcast(BF16)
            for h in range(4):
                P.op("pe", lambda e, pmtb=pmtb, Mp=Mp, h=h: e.transpose(
                    pmtb[0:96, h * 128:(h + 1) * 128], Mp[:, h, :], ident_b[:, :]), reads=[Mp, ident_b], writes=[pmt])
            P.op("act", lambda e, pmtb=pmtb, QT=QT, sub=sub: e.copy(
                out=QT[64:96, :, sub], in_=pmtb[64:96, 0:512].rearrange("p (h n) -> p h n", h=4)), reads=[pmt, QT], writes=[QT])
        nkt = (2 * m + 2) * 2
        o0 = tok0 - OWN0
        for h in range(4):
            pair, hb = h // 2, h % 2
            po = po_ring.next()
            def tile_cols(kt):
                blk = kt // 2
                if blk < 2 * m:
                    return 0, 512, None
                if blk == 2 * m:
                    return 0, 512, 0
                return 256, 512, 256

            def emit_s(kt):
                a0, a1, cz = tile_cols(kt)
                mk = kt // 4
                ps = psum.next()
                P.op("pe", lambda e, ps=ps, h=h, kt=kt, QT=QT, a0=a0, a1=a1, cz=cz: e.matmul(
                    ps[:, a0:a1], lhsT=KT[0:96, h, kt * 128:(kt + 1) * 128], rhs=QT[0:96, h, a0:a1],
                    start=True, stop=(cz is None)), reads=[KT_T[mk], QT], writes=[ps])
                if cz is not None:
                    P.op("pe", lambda e, ps=ps, kt=kt, cz=cz: e.matmul(
                        ps[:, cz:cz + 256], lhsT=ident_b[:, :], rhs=negm[:, kt % 2, :], start=False, stop=True),
                        reads=[ident_b, negm], writes=[ps])
                return ps

            def emit_pv(kt, ps):
                a0, a1, cz = tile_cols(kt)
                mk = kt // 4
                pt = ptr.next()
                P.op("act", lambda e, ps=ps, pt=pt, a0=a0, a1=a1: e.activation(out=pt[:, a0:a1], in_=ps[:, a0:a1], func=AF.Exp),
                     reads=[ps], writes=[pt])
                P.op("pe", lambda e, po=po, pt=pt, kt=kt, pair=pair, hb=hb, a0=a0, a1=a1, nkt=nkt: e.matmul(
                    po[:, a0:a1], lhsT=VA[:, kt, pair, hb:hb + 2, :].rearrange("p a d -> p (a d)"), rhs=pt[:, a0:a1],
                    start=(kt == 0), stop=(kt == nkt - 1), skip_group_check=True), reads=[VA_T[mk], pt], writes=[po])

            LOOK = 3
            pend = []
            for kt in range(nkt):
                pend.append((kt, emit_s(kt)))
                if len(pend) > LOOK:
                    emit_pv(*pend.pop(0))
            while pend:
                emit_pv(*pend.pop(0))
            nr = slice(0, 64) if hb == 0 else slice(64, 128)
            dr = slice(64, 128) if hb == 0 else slice(0, 64)
            rd = rdr.next()
            if hb == 0:
                at_ = ast[pair].next()
                ast_cur = at_
            else:
                at_ = ast_cur
            P.op("dve", lambda e, po=po, rd=rd, nr=nr, dr=dr: e.reciprocal(out=rd[nr, :], in_=po[dr, :]), reads=[po], writes=[rd])
            P.op("dve", lambda e, po=po, rd=rd, nr=nr, at_=at_: e.tensor_tensor(
                out=at_[nr, :], in0=po[nr, :], in1=rd[nr, :], op=ALU.mult), reads=[po, rd, at_], writes=[at_])
            if hb == 1:
                outs.append(P.op("pool", lambda e, at_=at_, pair=pair, o0=o0: e.dma_start(
                    out=e_att[g, pair, :, o0:o0 + 512], in_=at_[:, :]), reads=[at_], writes=[e_att_T], dma=True))
    return outs


def load_consts(P, C, din, names_shapes):
    out = {}
    for name, shape, dt in names_shapes:
        t = C.sbT(shape, dt, name)
        P.op("sp", lambda e, t=t, name=name: e.dma_start(out=t.ap, in_=din[name]), writes=[t], dma=True)
        out[name] = t
    return out


def build_program(mode, NTOKW=8192, OWN0=0, NG=1):
    nc = bass.Bass("TRN2", target_bir_lowering=False)
    P = Prog(nc)
    with ExitStack() as es:
        C = Ctx(nc, es, P)
        din = {}

        def inp(name, shape, dt=F32):
            din[name] = C.dram(name, shape, dt, "ExternalInput")

        psum = [T(es.enter_context(nc.psum_tensor("ps%d" % i, [128, 512], F32))[:, :], "ps%d" % i, excl=True) for i in range(8)]
        final = []
        NOWN = NTOKW - OWN0
        if mode in ("p1", "fused"):
            inp("xw", [NTOKW, 1024])
            inp("w1a", [NG, 1024, W1A])
            inp("w1b", [NG, 1024, W1B])
            inp("convw", [NG, 128, 6, 4])
            inp("convb", [NG, 128, 6])
            for nm in ("dtb", "alog", "dsk"):
                inp(nm, [NG, 128, 8])
            inp("sng", [NG, 128, 512])
            inp("kind", [32, NTOKW], BF16)
            shapes1 = [("gq", [128, 64], F32), ("gk", [128, 64], F32), ("g1", [128, 8], F32),
                       ("tokmask", [128, NTOKW // 128], F32), ("blkbias", [128, 32], F32),
                       ("ident_b", [128, 128], BF16), ("ones_b", [128, 128], BF16), ("U", [128, 128], BF16),
                       ("T1", [128, 128], BF16), ("negm", [128, 2, 256], BF16)]
            for nm, shp, dt in shapes1:
                if nm not in din:
                    inp(nm, shp, dt)
            cst = load_consts(P, C, din, shapes1)
            cst["psring"] = Ring(psum[0:5])
            cst["po_ring"] = Ring(psum[5:7])
            cst["ps_small"] = {"gate": TV(psum[7], psum[7].ap[:, 0:128])}
            for s_ in range(4):
                cst["ps_small"]["dt%d" % s_] = TV(psum[5], psum[5].ap[:, 8 * s_:8 * s_ + 8])
                cst["ps_small"]["acs%d" % s_] = TV(psum[5], psum[5].ap[:, 64 + 16 * s_:64 + 16 * s_ + 16])
                cst["ps_small"]["cb%d" % s_] = TV(psum[6], psum[6].ap[:, 128 * s_:128 * s_ + 128])
            kind_e = "ExternalOutput" if mode == "p1" else "Internal"
            e_att = C.dram("e_att", [NG, 2, 128, NOWN], BF16, kind_e)
            e_yb = C.dram("e_yb", [NG, 4, 128, NOWN], BF16, kind_e)
            e_att_T, e_yb_T = T(None, "e_att"), T(None, "e_yb")
            hT_scr = C.dram("hT_scr", [128, 8, NTOKW], BF16, "Internal")
            hT_T = []
            with ExitStack() as es1:
                C1 = Ctx(nc, es1, P)
                if STAGES.get("pro", True):
                    build_prologue(P, C1, din, cst, hT_scr, hT_T, NTOKW)
            P.barrier()
            with ExitStack() as es1:
                C1 = Ctx(nc, es1, P)
                for g in range(NG):
                    if STAGES.get("a", True):
                        with ExitStack() as es2:
                            build_p1a(P, Ctx(nc, es2, P), din, g, cst, hT_scr, hT_T, e_yb, e_yb_T, NTOKW, OWN0)
                        P.barrier()
                    if STAGES.get("b", True):
                        with ExitStack() as es2:
                            C2b = Ctx(nc, es2, P)
                            A = build_p1_init(P, C2b, din, cst, NTOKW)
                            build_p1b(P, C2b, din, g, cst, A, hT_scr, hT_T, e_att, e_att_T, NTOKW, OWN0)
                        P.barrier()
            if mode == "p1":
                final = [o for o in P.ops["pool"] if o.dma][-8:]
        if mode in ("p2", "fused"):
            inp("x2", [2048, 1024])
            inp("p2", [2048, 256])
            for name, (K, N) in P2W.items():
                inp(name, [K, N])
            shapes2 = [("g1", [128, 8], F32), ("g2", [128, 8], F32), ("g3", [128, 8], F32),
                       ("ident_b", [128, 128], BF16), ("ones_b", [128, 128], BF16)]
            for nm, shp, dt in shapes2:
                if nm not in din:
                    inp(nm, shp, dt)
            consts = load_consts(P, C, din, shapes2)
            consts["psum"] = psum
            wscr = {name: C.dram("scr_" + name, [K, N], BF16, "Internal") for name, (K, N) in P2W.items()}
            wT = {}
            with ExitStack() as es2:
                C2 = Ctx(nc, es2, P)
                if STAGES.get("precast", True):
                    build_precast(P, C2, din, wscr, wT, consts)
            P.barrier()
            if mode == "p2":
                inp("e_att", [4, 2, 128, 2048], BF16)
                inp("e_yb", [4, 4, 128, 2048], BF16)
                exch = {"att": din["e_att"], "yb": din["e_yb"], "att_T": T(None), "yb_T": T(None)}
            else:
                exch = {"att": e_att, "yb": e_yb, "att_T": e_att_T, "yb_T": e_yb_T}
            dout = C.dram("out", [2048, 1024], F32, "ExternalOutput")
            if STAGES.get("p2", True):
                with ExitStack() as es3:
                    C3 = Ctx(nc, es3, P)
                    final = build_phase2(P, C3, din, wscr, wT, consts, exch, dout, NTOK=STAGES.get('ntok', 2048))
            else:
                final = [o for o in P.ops["pool"] if o.dma][-8:]
        P.emit(final)
    return nc


BF = ml_dtypes.bfloat16


def host_consts():
    return {
        "ident_b": np.eye(128, dtype=np.float32).astype(BF),
        "ones_b": np.ones((128, 128), dtype=np.float32).astype(BF),
    }


def host_consts1(NTOKW):
    i = np.arange(128)
    U = (i[:, None] <= i[None, :]).astype(np.float32)
    T1 = (i[:, None] > i[None, :]).astype(np.float32)
    q = np.arange(256)
    negm = np.stack([np.where((kt * 128 + i[:, None]) <= q[None, :], 0.0, NEG) for kt in range(2)], 1).astype(np.float32)
    kind = (np.arange(NTOKW)[None, :] // 256 == np.arange(32)[:, None]).astype(np.float32)
    return {"ident_b": np.eye(128, dtype=np.float32).astype(BF), "ones_b": np.ones((128, 128), np.float32).astype(BF),
            "U": U.astype(BF), "T1": T1.astype(BF), "negm": negm.astype(BF), "kind": kind.astype(BF)}


def gain_layout(g):
    return np.ascontiguousarray(g.reshape(8, 128).T)


def bc(v):
    return np.ascontiguousarray(np.broadcast_to(v[None, :], (128, v.shape[0]))).astype(np.float32)


def p1_inputs(inputs, b, groups, xw, tokmask, blkbias, NTOKW):
    w_in = inputs["w_in"][0]
    cw, cbias = inputs["conv_w"][0], inputs["conv_b"][0]
    w1a, w1b, convw, convb, dtb, alog, dsk, sng = [], [], [], [], [], [], [], []
    for g in groups:
        cols_a = np.concatenate([np.arange(5120 + 512 * g, 5120 + 512 * (g + 1)), np.arange(7168 + 128 * g, 7168 + 128 * (g + 1)),
                                 np.arange(7680 + 128 * g, 7680 + 128 * (g + 1)), np.arange(3072 + 512 * g, 3072 + 512 * (g + 1)),
                                 np.arange(8192 + 8 * g, 8192 + 8 * (g + 1))])
        cols_b = np.concatenate([np.arange(256 * g, 256 * (g + 1)), np.arange(1024 + 256 * g, 1024 + 256 * (g + 1)),
                                 np.arange(2048 + 256 * g, 2048 + 256 * (g + 1))])
        w1a.append(w_in[:, cols_a])
        w1b.append(w_in[:, cols_b])
        ch = np.concatenate([np.arange(512 * g, 512 * (g + 1)), np.arange(2048 + 128 * g, 2048 + 128 * (g + 1)),
                             np.arange(2560 + 128 * g, 2560 + 128 * (g + 1))])
        convw.append(cw[:, ch].T.reshape(6, 128, 4).transpose(1, 0, 2))
        convb.append(cbias[ch].reshape(6, 128).T)
        dtb.append(bc(inputs["dt_bias"][0][8 * g:8 * g + 8]))
        alog.append(bc(inputs["a_log"][0][8 * g:8 * g + 8]))
        dsk.append(bc(inputs["d_skip"][0][8 * g:8 * g + 8]))
        sng.append(bc(inputs["ssm_norm_g"][0][512 * g:512 * g + 512]))
    m = {"xw": np.ascontiguousarray(xw), "w1a": np.ascontiguousarray(np.stack(w1a)), "w1b": np.ascontiguousarray(np.stack(w1b)),
         "convw": np.ascontiguousarray(np.stack(convw)), "convb": np.ascontiguousarray(np.stack(convb)),
         "dtb": np.stack(dtb), "alog": np.stack(alog), "dsk": np.stack(dsk), "sng": np.stack(sng),
         "gq": bc(inputs["q_norm_g"][0]), "gk": bc(inputs["k_norm_g"][0]), "g1": gain_layout(inputs["ln1_g"][0]),
         "tokmask": np.ascontiguousarray(tokmask.reshape(-1, 128).T).astype(np.float32), "blkbias": bc(blkbias)}
    m.update(host_consts1(NTOKW))
    return m


def p2_inputs(inputs, core, e_att=None, e_yb=None):
    b, t = core // 4, core % 4
    sl = slice(t * 2048, (t + 1) * 2048)
    m = {
        "x2": np.ascontiguousarray(inputs["x"][b, sl]),
        "p2": np.ascontiguousarray(inputs["p"][0, b, sl]),
        "wg": np.ascontiguousarray(inputs["w_in"][0][:, 8224:10272]),
        "woa": inputs["w_o_attn"][0], "wos": inputs["w_o_ssm"][0], "wout": inputs["w_out"][0],
        "wgu": inputs["w_gate_up"][0], "wd": inputs["w_down"][0], "wpg": inputs["w_ple_gate"][0],
        "wpp": inputs["w_ple_proj"][0],
        "g1": gain_layout(inputs["ln1_g"][0]), "g2": gain_layout(inputs["ln2_g"][0]),
        "g3": gain_layout(inputs["ln3_g"][0]),
    }
    m.update(host_consts())
    if e_att is not None:
        m["e_att"] = e_att
        m["e_yb"] = e_yb
    return m


MODE = "fused"


def kernel(**inputs):
    inputs = {k: np.asarray(v) for k, v in inputs.items()}
    x = inputs["x"]
    out = np.zeros(x.shape, np.float32)
    if MODE == "two":
        nc1 = build_program("p1", NTOKW=8192, OWN0=0, NG=1)
        maps1 = []
        for core in range(8):
            b, g = core // 4, core % 4
            maps1.append(p1_inputs(inputs, b, [g], x[b], np.ones(8192, np.float32), np.zeros(32, np.float32), 8192))
        r1 = run_bass_kernel_spmd(nc1, maps1, core_ids=list(range(8))).results
        nc2 = build_program("p2")
        maps2 = []
        for core in range(8):
            b, t = core // 4, core % 4
            sl = slice(t * 2048, (t + 1) * 2048)
            ea = np.stack([np.asarray(r1[b * 4 + g]["e_att"])[0][:, :, sl] for g in range(4)])
            ey = np.stack([np.asarray(r1[b * 4 + g]["e_yb"])[0][:, :, sl] for g in range(4)])
            maps2.append(p2_inputs(inputs, core, np.ascontiguousarray(ea), np.ascontiguousarray(ey)))
        r2 = run_bass_kernel_spmd(nc2, maps2, core_ids=list(range(8))).results
        for core in range(8):
            b, t = core // 4, core % 4
            out[b, t * 2048:(t + 1) * 2048] = np.asarray(r2[core]["out"])
        return out
    nc = build_program("fused", NTOKW=8192, OWN0=6144, NG=4)
    maps = []
    for core in range(8):
        b, t = core // 4, core % 4
        npad = (3 - t) * 2048
        xw = np.concatenate([np.zeros((npad, 1024), np.float32), x[b, :(t + 1) * 2048]], 0)
        tokmask = np.concatenate([np.zeros(npad, np.float32), np.ones(8192 - npad, np.float32)])
        blkbias = np.where(np.arange(32) < npad // 256, NEG, 0.0).astype(np.float32)
        m = p1_inputs(inputs, b, [0, 1, 2, 3], xw, tokmask, blkbias, 8192)
        m.update(p2_inputs(inputs, core))
        maps.append(m)
    r = run_bass_kernel_spmd(nc, maps, core_ids=list(range(8))).results
    for core in range(8):
        b, t = core // 4, core % 4
        out[b, t * 2048:(t + 1) * 2048] = np.asarray(r[core]["out"])
    return out
```

```python
import numpy as np
from contextlib import ExitStack
import ml_dtypes
import concourse.bass as bass
import concourse.mybir as mybir
from concourse.bass_utils import run_bass_kernel_spmd

F32 = mybir.dt.float32
BF16 = mybir.dt.bfloat16
AF = mybir.ActivationFunctionType
ALU = mybir.AluOpType
AX = mybir.AxisListType

EPS = 1e-6
NEG = -30000.0
SAME_ENGINE_SYNC = True
STAGES = {}


class T:
    __slots__ = ("ap", "w", "r", "name", "excl")

    def __init__(self, ap=None, name="", excl=False):
        self.ap = ap
        self.w = None
        self.r = []
        self.name = name
        self.excl = excl

    def __getitem__(self, k):
        return self.ap[k]


class TV(T):
    __slots__ = ("parent",)

    def __init__(self, parent, ap):
        self.parent = parent
        self.ap = ap
        self.name = parent.name
        self.excl = parent.excl

    @property
    def w(self):
        return self.parent.w

    @w.setter
    def w(self, v):
        self.parent.w = v

    @property
    def r(self):
        return self.parent.r

    @r.setter
    def r(self, v):
        self.parent.r = v


class Op:
    __slots__ = ("eng", "fn", "deps", "dma", "inc", "sem", "ticket", "idx", "prev_same_sem", "vc")

    def __init__(self, eng, fn, dma):
        self.eng = eng
        self.fn = fn
        self.dma = dma
        self.deps = []
        self.inc = False
        self.sem = None
        self.ticket = 0
        self.prev_same_sem = None


class Prog:
    ENGS = ("pe", "act", "dve", "pool", "sp")
    NDS = 8

    def __init__(self, nc):
        self.nc = nc
        self.ops = {e: [] for e in self.ENGS}
        self.all = []
        self.bar = {}

    def op(self, eng, fn, reads=(), writes=(), dma=False):
        o = Op(eng, fn, dma)
        deps = []
        for t in reads:
            if t.w is not None:
                deps.append(t.w)
            if t.excl:
                deps.extend(x for x in t.r if x.eng != eng)
        for t in writes:
            if t.w is not None:
                deps.append(t.w)
            deps.extend(t.r)
        b = self.bar.pop(eng, None)
        if b:
            deps.extend(b)
        seen = set()
        for d in deps:
            if d is o or id(d) in seen:
                continue
            seen.add(id(d))
            if (not d.dma) and d.eng == eng and (eng == "pe" or not SAME_ENGINE_SYNC):
                continue
            o.deps.append(d)
            d.inc = True
        for t in reads:
            if dma:
                t.r.append(o)
            else:
                t.r = [x for x in t.r if x.dma or x.eng != eng] + [o]
        for t in writes:
            t.w = o
            t.r = []
        if dma:
            o.inc = True
        self.ops[eng].append(o)
        self.all.append(o)
        return o

    def barrier(self):
        last = []
        for e in self.ENGS:
            ops = self.ops[e]
            if ops:
                last.append(ops[-1])
            last.extend([o for o in ops if o.dma][-self.NDS:])
        for e in self.ENGS:
            self.bar[e] = list(last)

    def emit(self, final_ops):
        nc = self.nc
        with ExitStack() as es:
            SEM_CAP = 1000
            nsem = {e: sum(1 for o in self.ops[e] if o.inc and not o.dma) // SEM_CAP + 1 for e in ("pe", "act", "dve", "pool")}
            csem = {e: [es.enter_context(nc.semaphore("cs_%s%d" % (e, i))) for i in range(nsem[e])]
                    for e in ("pe", "act", "dve", "pool")}
            dsem = {e: [es.enter_context(nc.semaphore("ds_%s%d" % (e, i))) for i in range(self.NDS)]
                    for e in self.ENGS}
            ccount = {e: 0 for e in self.ENGS}
            dcount = {e: [0] * self.NDS for e in self.ENGS}
            drr = {e: 0 for e in self.ENGS}
            dlast = {e: [None] * self.NDS for e in self.ENGS}
            for e in self.ENGS:
                for o in self.ops[e]:
                    if o.dma:
                        k = drr[e] % self.NDS
                        drr[e] += 1
                        dcount[e][k] += 16
                        o.sem = dsem[e][k]
                        o.ticket = dcount[e][k]
                        o.prev_same_sem = dlast[e][k]
                        dlast[e][k] = o
                    elif o.inc:
                        o.sem = csem[e][ccount[e] // SEM_CAP]
                        o.ticket = ccount[e] % SEM_CAP + 1
                        ccount[e] += 1

            know = {e: {} for e in self.ENGS}
            plan = {}
            for o in self.all:
                kn = know[o.eng]
                waits = []
                dl = list(o.deps)
                if o.dma and o.prev_same_sem is not None:
                    dl.append(o.prev_same_sem)
                for d in dl:
                    key = id(d.sem)
                    if kn.get(key, 0) < d.ticket:
                        waits.append(d)
                        for kk, vv in d.vc.items():
                            if kn.get(kk, 0) < vv:
                                kn[kk] = vv
                plan[id(o)] = waits
                o.vc = dict(kn)
                if o.sem is not None:
                    o.vc[id(o.sem)] = o.ticket
                    if not o.dma:
                        kn[id(o.sem)] = max(kn.get(id(o.sem), 0), 0)

            def run(ename, eng):
                for o in self.ops[ename]:
                    for d in plan[id(o)]:
                        eng.wait_ge(d.sem, d.ticket)
                    ins = o.fn(eng)
                    if o.sem is not None:
                        ins.then_inc(o.sem, 16 if o.dma else 1)
                if ename == "sp":
                    kn = know["sp"]
                    for d in final_ops:
                        if kn.get(id(d.sem), 0) < d.ticket:
                            eng.wait_ge(d.sem, d.ticket)
                            kn[id(d.sem)] = d.ticket

            with nc.Block() as block:
                @block.tensor
                def _(eng):
                    run("pe", eng)

                @block.scalar
                def _(eng):
                    run("act", eng)

                @block.vector
                def _(eng):
                    run("dve", eng)

                @block.gpsimd
                def _(eng):
                    run("pool", eng)

                @block.sync
                def _(eng):
                    run("sp", eng)


class Ctx:
    N = 0

    def __init__(self, nc, es, P):
        self.nc, self.es, self.P = nc, es, P
        self.n = 0

    def sb(self, shape, dt, name=None):
        Ctx.N += 1
        h = self.es.enter_context(self.nc.sbuf_tensor("%s_%d" % (name or "sb", Ctx.N), list(shape), dt))
        return h

    def sbT(self, shape, dt, name=None):
        h = self.sb(shape, dt, name)
        return T(h[tuple(slice(None) for _ in shape)], name or "")

    def dram(self, name, shape, dt, kind):
        return self.nc.dram_tensor(name, list(shape), dt, kind=kind).ap()


class Ring:
    def __init__(self, tiles):
        self.tiles = tiles
        self.i = 0

    def next(self):
        t = self.tiles[self.i % len(self.tiles)]
        self.i += 1
        return t


P2W = {
    "wg": (1024, 2048), "woa": (1024, 1024), "wos": (2048, 1024), "wout": (1024, 1024),
    "wgu": (1024, 5632), "wd": (2816, 1024), "wpg": (1024, 1024), "wpp": (256, 1024),
}
P2W_GAIN = {"wg": "g1", "wgu": "g2", "wpg": "g3"}
WB_COLS = 256


def wblocks(name):
    K, N = P2W[name]
    nkc = K // 128
    kbs = []
    k0 = 0
    while k0 < nkc:
        nk = min(8, nkc - k0)
        kbs.append((k0, nk))
        k0 += nk
    return kbs, N // WB_COLS


def build_precast(P, C, din, wscr, wT, consts):
    nc = P.nc
    stg = Ring([C.sbT([128, 8, WB_COLS], F32, "pc_stg") for _ in range(2)])
    wbf = Ring([C.sbT([128, 8, WB_COLS], BF16, "pc_bf") for _ in range(2)])
    for name in P2W:
        kbs, ncb = wblocks(name)
        src = din[name]
        dst = wscr[name]
        gain = consts.get(P2W_GAIN.get(name))
        for cb in range(ncb):
            for (k0, nk) in kbs:
                s, w = stg.next(), wbf.next()
                sv = src[k0 * 128:(k0 + nk) * 128, cb * WB_COLS:(cb + 1) * WB_COLS].rearrange("(k p) n -> p k n", p=128)
                dv = dst[k0 * 128:(k0 + nk) * 128, cb * WB_COLS:(cb + 1) * WB_COLS].rearrange("(k p) n -> p k n", p=128)
                P.op("sp", lambda e, s=s, sv=sv, nk=nk: e.dma_start(out=s[:, 0:nk, :], in_=sv), writes=[s], dma=True)
                if gain is not None:
                    gv = gain[:, k0:k0 + nk].unsqueeze(2).to_broadcast([128, nk, WB_COLS])
                    P.op("pool", lambda e, s=s, w=w, gv=gv, nk=nk: e.tensor_tensor(
                        out=w[:, 0:nk, :], in0=s[:, 0:nk, :], in1=gv, op=ALU.mult), reads=[s, gain], writes=[w])
                else:
                    P.op("pool", lambda e, s=s, w=w, nk=nk: e.tensor_copy(out=w[:, 0:nk, :], in_=s[:, 0:nk, :]),
                         reads=[s], writes=[w])
                t = T(None, "wscr")
                wT[(name, cb, k0)] = t
                P.op("pool", lambda e, w=w, dv=dv, nk=nk: e.dma_start(out=dv, in_=w[:, 0:nk, :]),
                     reads=[w], writes=[t], dma=True)


def build_phase2(P, C, din, wscr, wT, consts, exch, dout, NTOK=2048, TT=1024):
    nc = P.nc
    NTG = TT // 512
    NSUB = TT // 128
    ident_b = consts["ident_b"]
    ones_b = consts["ones_b"]
    xT = [C.sbT([128, TT], F32, "xT") for _ in range(8)]
    hT = [C.sbT([128, TT], BF16, "hT") for _ in range(8)]
    big = C.sb([128, 24, TT], BF16, "big")
    attT = [T(big[:, c, :], "att") for c in range(8)]
    ybT = [T(big[:, 8 + c, :], "yb") for c in range(16)]
    mg = [C.sbT([128, TT], BF16, "mg") for _ in range(8)]
    pT = [C.sbT([128, TT], BF16, "pT") for _ in range(2)]
    xin = Ring([C.sbT([128, 1024], F32, "xin") for _ in range(2)])
    xhi = Ring([C.sbT([128, 1024], BF16, "xhi") for _ in range(2)])
    xlo = Ring([C.sbT([128, 1024], BF16, "xlo") for _ in range(2)])
    xres = Ring([C.sbT([128, 1024], F32, "xres") for _ in range(1)])
    pin = Ring([C.sbT([128, 256], F32, "pin") for _ in range(2)])
    pinb = Ring([C.sbT([128, 256], BF16, "pinb") for _ in range(2)])
    wring = Ring([C.sbT([128, 8, WB_COLS], BF16, "wr") for _ in range(8)])
    tmpf = Ring([C.sbT([128, 512], F32, "tmpf") for _ in range(8)])
    sqb = Ring([C.sbT([128, 512], BF16, "sqb") for _ in range(3)])
    rstd = [C.sbT([128, 512], F32, "rstd") for _ in range(NTG)]
    psum = Ring(consts["psum"])

    def load_w(name, cb, k0, nk):
        w = wring.next()
        sv = wscr[name][k0 * 128:(k0 + nk) * 128, cb * WB_COLS:(cb + 1) * WB_COLS].rearrange("(k p) n -> p k n", p=128)
        P.op("sp", lambda e, w=w, sv=sv, nk=nk: e.dma_start(out=w[:, 0:nk, :], in_=sv),
             reads=[wT[(name, cb, k0)]], writes=[w], dma=True)
        return w

    def mm_group(ps, wlist, X, tg, oc_in_blk):
        n = sum(nk for _, _, nk in wlist)
        i = 0
        for (w, k0, nk) in wlist:
            for k in range(nk):
                st, sp_ = (i == 0), (i == n - 1)
                xk = X[k0 + k]
                P.op("pe", lambda e, ps=ps, w=w, k=k, xk=xk, st=st, sp_=sp_: e.matmul(
                    ps[:, :], lhsT=w[:, k, oc_in_blk * 128:(oc_in_blk + 1) * 128],
                    rhs=xk[:, tg * 512:(tg + 1) * 512], start=st, stop=sp_),
                    reads=[w, xk], writes=[ps])
                i += 1

    def rmsnorm():
        for tg in range(NTG):
            ps = psum.next()
            for kc in range(8):
                sq = sqb.next()
                P.op("act", lambda e, sq=sq, kc=kc, tg=tg: e.activation(
                    out=sq[:, :], in_=xT[kc][:, tg * 512:(tg + 1) * 512], func=AF.Square), reads=[xT[kc]], writes=[sq])
                P.op("pe", lambda e, ps=ps, sq=sq, kc=kc: e.matmul(
                    ps[:, :], lhsT=ones_b[:, :], rhs=sq[:, :], start=(kc == 0), stop=(kc == 7)),
                    reads=[sq, ones_b], writes=[ps])
            r = rstd[tg]
            P.op("act", lambda e, r=r, ps=ps: e.activation(
                out=r[:, :], in_=ps[:, :], func=AF.Sqrt, scale=1.0 / 1024.0, bias=EPS), reads=[ps], writes=[r])
            P.op("dve", lambda e, r=r: e.reciprocal(out=r[:, :], in_=r[:, :]), reads=[r], writes=[r])
        for kc in range(8):
            for tg in range(NTG):
                P.op("dve", lambda e, kc=kc, tg=tg: e.tensor_tensor(
                    out=hT[kc][:, tg * 512:(tg + 1) * 512], in0=xT[kc][:, tg * 512:(tg + 1) * 512],
                    in1=rstd[tg][:, :], op=ALU.mult), reads=[xT[kc], rstd[tg]], writes=[hT[kc]])

    out_ops = []
    for tt in range(NTOK // TT):
        t0 = tt * TT
        for s in range(NSUB):
            xi = xin.next()
            P.op("sp", lambda e, xi=xi, s=s, t0=t0: e.dma_start(out=xi[:, :], in_=din["x2"][t0 + s * 128:t0 + (s + 1) * 128, :]),
                 writes=[xi], dma=True)
            xh, xl, xr = xhi.next(), xlo.next(), xres.next()
            P.op("act", lambda e, xi=xi, xh=xh: e.copy(out=xh[:, :], in_=xi[:, :]), reads=[xi], writes=[xh])
            P.op("dve", lambda e, xi=xi, xh=xh, xr=xr: e.tensor_tensor(out=xr[:, :], in0=xi[:, :], in1=xh[:, :], op=ALU.subtract),
                 reads=[xi, xh], writes=[xr])
            P.op("pool", lambda e, xr=xr, xl=xl: e.tensor_copy(out=xl[:, :], in_=xr[:, :]), reads=[xr], writes=[xl])
            for half in range(2):
                ps = psum.next()
                for j in range(4):
                    kc = half * 4 + j
                    P.op("pe", lambda e, ps=ps, xh=xh, kc=kc, j=j: e.matmul(
                        ps[:, j * 128:(j + 1) * 128], lhsT=xh[:, kc * 128:(kc + 1) * 128], rhs=ident_b[:, :], start=True, stop=False),
                        reads=[xh, ident_b], writes=[ps])
                    P.op("pe", lambda e, ps=ps, xl=xl, kc=kc, j=j: e.matmul(
                        ps[:, j * 128:(j + 1) * 128], lhsT=xl[:, kc * 128:(kc + 1) * 128], rhs=ident_b[:, :], start=False, stop=True),
                        reads=[xl, ident_b], writes=[ps])
                for j in range(4):
                    kc = half * 4 + j
                    eng = "act" if j % 2 == 0 else "dve"
                    if eng == "act":
                        P.op("act", lambda e, ps=ps, kc=kc, j=j, s=s: e.copy(
                            out=xT[kc][:, s * 128:(s + 1) * 128], in_=ps[:, j * 128:(j + 1) * 128]),
                            reads=[ps], writes=[xT[kc]])
                    else:
                        P.op("dve", lambda e, ps=ps, kc=kc, j=j, s=s: e.tensor_copy(
                            out=xT[kc][:, s * 128:(s + 1) * 128], in_=ps[:, j * 128:(j + 1) * 128]),
                            reads=[ps], writes=[xT[kc]])
            pi, pb = pin.next(), pinb.next()
            P.op("sp", lambda e, pi=pi, s=s, t0=t0: e.dma_start(out=pi[:, :], in_=din["p2"][t0 + s * 128:t0 + (s + 1) * 128, :]),
                 writes=[pi], dma=True)
            P.op("pool", lambda e, pi=pi, pb=pb: e.tensor_copy(out=pb[:, :], in_=pi[:, :]), reads=[pi], writes=[pb])
            ps = psum.next()
            psb = ps.ap.bitcast(BF16)
            for j in range(2):
                P.op("pe", lambda e, psb=psb, pb=pb, j=j: e.transpose(
                    psb[:, j * 128:(j + 1) * 128], pb[:, j * 128:(j + 1) * 128], ident_b[:, :]),
                    reads=[pb, ident_b], writes=[ps])
            for j in range(2):
                P.op("act", lambda e, psb=psb, j=j, s=s: e.copy(
                    out=pT[j][:, s * 128:(s + 1) * 128], in_=psb[:, j * 128:(j + 1) * 128]), reads=[ps], writes=[pT[j]])
        for g in range(4):
            for c in range(2):
                tl = attT[2 * g + c]
                P.op("sp", lambda e, tl=tl, g=g, c=c, t0=t0: e.dma_start(out=tl[:, :], in_=exch["att"][g, c, :, t0:t0 + TT]),
                     reads=[exch["att_T"]], writes=[tl], dma=True)
            for c in range(4):
                tl = ybT[4 * g + c]
                P.op("sp", lambda e, tl=tl, g=g, c=c, t0=t0: e.dma_start(out=tl[:, :], in_=exch["yb"][g, c, :, t0:t0 + TT]),
                     reads=[exch["yb_T"]], writes=[tl], dma=True)
        p2s = STAGES.get('p2s', 'BXCD')
        if 'B' in p2s:
            rmsnorm()
        for j in (range(4) if 'B' in p2s else []):
            wga = load_w("wg", j, 0, 8)
            wgb = load_w("wg", 4 + j, 0, 8)
            wa = load_w("woa", j, 0, 8)
            ws0 = load_w("wos", j, 0, 8)
            ws1 = load_w("wos", j, 8, 8)
            for o2 in range(2):
                oc = 2 * j + o2
                for tg in range(NTG):
                    pga, pgb, pya, pyb = psum.next(), psum.next(), psum.next(), psum.next()
                    mm_group(pga, [(wga, 0, 8)], hT, tg, o2)
                    mm_group(pgb, [(wgb, 0, 8)], hT, tg, o2)
                    mm_group(pya, [(wa, 0, 8)], attT, tg, o2)
                    mm_group(pyb, [(ws0, 0, 8), (ws1, 8, 8)], ybT, tg, o2)
                    sa, sb_, m1, m2 = tmpf.next(), tmpf.next(), tmpf.next(), tmpf.next()
                    P.op("act", lambda e, sa=sa, pga=pga: e.activation(out=sa[:, :], in_=pga[:, :], func=AF.Sigmoid),
                         reads=[pga], writes=[sa])
                    P.op("act", lambda e, sb_=sb_, pgb=pgb: e.activation(out=sb_[:, :], in_=pgb[:, :], func=AF.Sigmoid),
                         reads=[pgb], writes=[sb_])
                    P.op("dve", lambda e, m1=m1, sa=sa, pya=pya: e.tensor_tensor(
                        out=m1[:, :], in0=sa[:, :], in1=pya[:, :], op=ALU.mult), reads=[sa, pya], writes=[m1])
                    P.op("dve", lambda e, m2=m2, sb_=sb_, pyb=pyb: e.tensor_tensor(
                        out=m2[:, :], in0=sb_[:, :], in1=pyb[:, :], op=ALU.mult), reads=[sb_, pyb], writes=[m2])
                    P.op("pool", lambda e, m1=m1, m2=m2, oc=oc, tg=tg: e.tensor_tensor(
                        out=mg[oc][:, tg * 512:(tg + 1) * 512], in0=m1[:, :], in1=m2[:, :], op=ALU.add),
                        reads=[m1, m2], writes=[mg[oc]])
        for j in (range(4) if 'X' in p2s else []):
            w = load_w("wout", j, 0, 8)
            for o2 in range(2):
                oc = 2 * j + o2
                for tg in range(NTG):
                    ps = psum.next()
                    mm_group(ps, [(w, 0, 8)], mg, tg, o2)
                    P.op("dve", lambda e, ps=ps, oc=oc, tg=tg: e.tensor_tensor(
                        out=xT[oc][:, tg * 512:(tg + 1) * 512], in0=ps[:, :], in1=xT[oc][:, tg * 512:(tg + 1) * 512],
                        op=ALU.add), reads=[ps, xT[oc]], writes=[xT[oc]])
        if 'C' in p2s:
            rmsnorm()
        actT = [T(big[:, f, :], "act") for f in range(22)]
        for f in range(22):
            old = attT[f] if f < 8 else ybT[f - 8]
            actT[f].w, actT[f].r = old.w, old.r
        for j in (range(11) if 'C' in p2s else []):
            wgt = load_w("wgu", j, 0, 8)
            wup = load_w("wgu", 11 + j, 0, 8)
            for o2 in range(2):
                f = 2 * j + o2
                for tg in range(NTG):
                    pg, pu = psum.next(), psum.next()
                    mm_group(pg, [(wgt, 0, 8)], hT, tg, o2)
                    mm_group(pu, [(wup, 0, 8)], hT, tg, o2)
                    sg = tmpf.next()
                    P.op("act", lambda e, sg=sg, pg=pg: e.activation(out=sg[:, :], in_=pg[:, :], func=AF.Silu),
                         reads=[pg], writes=[sg])
                    P.op("dve", lambda e, sg=sg, pu=pu, f=f, tg=tg: e.tensor_tensor(
                        out=actT[f][:, tg * 512:(tg + 1) * 512], in0=sg[:, :], in1=pu[:, :], op=ALU.mult),
                        reads=[sg, pu], writes=[actT[f]])
        for j in (range(4) if 'C' in p2s else []):
            w0 = load_w("wd", j, 0, 8)
            w1 = load_w("wd", j, 8, 8)
            w2 = load_w("wd", j, 16, 6)
            for o2 in range(2):
                oc = 2 * j + o2
                for tg in range(NTG):
                    ps = psum.next()
                    mm_group(ps, [(w0, 0, 8), (w1, 8, 8), (w2, 16, 6)], actT, tg, o2)
                    P.op("dve", lambda e, ps=ps, oc=oc, tg=tg: e.tensor_tensor(
                        out=xT[oc][:, tg * 512:(tg + 1) * 512], in0=ps[:, :], in1=xT[oc][:, tg * 512:(tg + 1) * 512],
                        op=ALU.add), reads=[ps, xT[oc]], writes=[xT[oc]])
        for f in range(22):
            old = attT[f] if f < 8 else ybT[f - 8]
            old.w, old.r = actT[f].w, actT[f].r
        if 'D' in p2s:
            rmsnorm()
        for j in (range(4) if 'D' in p2s else []):
            wpg = load_w("wpg", j, 0, 8)
            wpp = load_w("wpp", j, 0, 2)
            for o2 in range(2):
                oc = 2 * j + o2
                for tg in range(NTG):
                    pg, pp = psum.next(), psum.next()
                    mm_group(pg, [(wpg, 0, 8)], hT, tg, o2)
                    mm_group(pp, [(wpp, 0, 2)], pT, tg, o2)
                    sg, m = tmpf.next(), tmpf.next()
                    P.op("act", lambda e, sg=sg, pg=pg: e.activation(out=sg[:, :], in_=pg[:, :], func=AF.Sigmoid),
                         reads=[pg], writes=[sg])
                    P.op("dve", lambda e, m=m, sg=sg, pp=pp: e.tensor_tensor(
                        out=m[:, :], in0=sg[:, :], in1=pp[:, :], op=ALU.mult), reads=[sg, pp], writes=[m])
                    P.op("dve", lambda e, m=m, oc=oc, tg=tg: e.tensor_tensor(
                        out=xT[oc][:, tg * 512:(tg + 1) * 512], in0=m[:, :], in1=xT[oc][:, tg * 512:(tg + 1) * 512],
                        op=ALU.add), reads=[m, xT[oc]], writes=[xT[oc]])
        for kc in range(8):
            P.op("act", lambda e, kc=kc: e.copy(out=hT[kc][:, :], in_=xT[kc][:, :]), reads=[xT[kc], hT[kc]], writes=[hT[kc]])
            P.op("dve", lambda e, kc=kc: e.tensor_tensor(out=xT[kc][:, :], in0=xT[kc][:, :], in1=hT[kc][:, :], op=ALU.subtract),
                 reads=[xT[kc], hT[kc]], writes=[xT[kc]])
            P.op("pool", lambda e, kc=kc: e.tensor_copy(out=mg[kc][:, :], in_=xT[kc][:, :]), reads=[xT[kc], mg[kc]], writes=[mg[kc]])
        for s in range(NSUB):
            xo = xin.next()
            for half in range(2):
                ps = psum.next()
                for j in range(4):
                    kc = half * 4 + j
                    P.op("pe", lambda e, ps=ps, kc=kc, j=j, s=s: e.matmul(
                        ps[:, j * 128:(j + 1) * 128], lhsT=hT[kc][:, s * 128:(s + 1) * 128], rhs=ident_b[:, :], start=True, stop=False),
                        reads=[hT[kc], ident_b], writes=[ps])
                    P.op("pe", lambda e, ps=ps, kc=kc, j=j, s=s: e.matmul(
                        ps[:, j * 128:(j + 1) * 128], lhsT=mg[kc][:, s * 128:(s + 1) * 128], rhs=ident_b[:, :], start=False, stop=True),
                        reads=[mg[kc], ident_b], writes=[ps])
                if half == 0:
                    P.op("act", lambda e, ps=ps, xo=xo: e.copy(out=xo[:, 0:512], in_=ps[:, :]), reads=[ps], writes=[xo])
                else:
                    P.op("dve", lambda e, ps=ps, xo=xo: e.tensor_copy(out=xo[:, 512:1024], in_=ps[:, :]),
                         reads=[ps, xo], writes=[xo])
            out_ops.append(P.op("pool", lambda e, xo=xo, s=s, t0=t0: e.dma_start(
                out=dout[t0 + s * 128:t0 + (s + 1) * 128, :], in_=xo[:, :]), reads=[xo], dma=True))
    return out_ops


W1A = 1288
W1B = 768


def load_cast_weight(P, C, src, w, ncols, gain, stg):
    c0 = 0
    while c0 < ncols:
        n = min(WB_COLS, ncols - c0)
        s = stg.next()
        sv = src[:, c0:c0 + n].rearrange("(k p) n -> p k n", p=128)
        P.op("sp", lambda e, s=s, sv=sv, n=n: e.dma_start(out=s[:, :, 0:n], in_=sv), writes=[s], dma=True)
        gv = gain[:, 0:8].unsqueeze(2).to_broadcast([128, 8, n])
        P.op("pool", lambda e, s=s, gv=gv, n=n, c0=c0: e.tensor_tensor(
            out=w[:, :, c0:c0 + n], in0=s[:, :, 0:n], in1=gv, op=ALU.mult), reads=[s, gain], writes=[w])
        c0 += n


def build_prologue(P, C, din, cst, hT_scr, hT_T, NTOKW):
    xin = Ring([C.sbT([128, 1024], F32, "pxin") for _ in range(3)])
    junk = C.sbT([128, 1024], BF16, "pjunk")
    hb = Ring([C.sbT([128, 1024], BF16, "phb") for _ in range(2)])
    ssr = Ring([C.sbT([128, 2], F32, "pss") for _ in range(4)])
    hst = Ring([C.sbT([128, 8, 512], BF16, "phst") for _ in range(2)])
    psum = cst["psring"]
    ident_b = cst["ident_b"]
    for m in range(NTOKW // 512):
        ht = hst.next()
        for s in range(4):
            t0 = m * 512 + s * 128
            xi, ss, h = xin.next(), ssr.next(), hb.next()
            P.op("sp", lambda e, xi=xi, t0=t0: e.dma_start(out=xi[:, :], in_=din["xw"][t0:t0 + 128, :]), writes=[xi], dma=True)
            P.op("act", lambda e, xi=xi, ss=ss: e.activation(out=junk[:, :], in_=xi[:, :], func=AF.Square, accum_out=ss[:, 0:1]),
                 reads=[xi], writes=[junk, ss])
            P.op("act", lambda e, ss=ss: e.activation(out=ss[:, 1:2], in_=ss[:, 0:1], func=AF.Sqrt, scale=1.0 / 1024.0, bias=EPS),
                 reads=[ss], writes=[ss])
            P.op("dve", lambda e, ss=ss: e.reciprocal(out=ss[:, 1:2], in_=ss[:, 1:2]), reads=[ss], writes=[ss])
            P.op("dve", lambda e, xi=xi, ss=ss, h=h: e.tensor_scalar(
                out=h[:, :], in0=xi[:, :], scalar1=ss[:, 1:2], scalar2=None, op0=ALU.mult), reads=[xi, ss], writes=[h])
            ps = psum.next()
            psb = ps.ap.bitcast(BF16)
            for kc in range(8):
                P.op("pe", lambda e, psb=psb, h=h, kc=kc: e.transpose(
                    psb[:, kc * 128:(kc + 1) * 128], h[:, kc * 128:(kc + 1) * 128], ident_b[:, :]),
                    reads=[h, ident_b], writes=[ps])
            P.op("act", lambda e, psb=psb, ht=ht, s=s: e.copy(
                out=ht[:, 0:4, s * 128:(s + 1) * 128], in_=psb[:, 0:512].rearrange("p (k n) -> p k n", k=4)),
                reads=[ps], writes=[ht])
            P.op("dve", lambda e, psb=psb, ht=ht, s=s: e.tensor_copy(
                out=ht[:, 4:8, s * 128:(s + 1) * 128], in_=psb[:, 512:1024].rearrange("p (k n) -> p k n", k=4)),
                reads=[ps, ht], writes=[ht])
        t = T(None, "hTscr")
        hT_T.append(t)
        P.op("pool", lambda e, ht=ht, m=m: e.dma_start(out=hT_scr[:, :, m * 512:(m + 1) * 512], in_=ht[:, :, :]),
             reads=[ht], writes=[t], dma=True)


def build_p1a(P, C, din, g, cst, hT_scr, hT_T, e_yb, e_yb_T, NTOKW, OWN0):
    psum = cst["psring"]
    ident_b, ones_b, U, T1 = cst["ident_b"], cst["ones_b"], cst["U"], cst["T1"]
    stg = Ring([C.sbT([128, 8, WB_COLS], F32, "a_stg") for _ in range(2)])
    w1a = C.sbT([128, 8, W1A], BF16, "w1a")
    load_cast_weight(P, C, din["w1a"][g], w1a, W1A, cst["g1"], stg)
    small = {}
    for nm, shp in (("convw", [128, 6, 4]), ("convb", [128, 6]), ("dtb", [128, 8]), ("alog", [128, 8]),
                    ("dsk", [128, 8]), ("sng", [128, 512])):
        t = C.sbT(shp, F32, "a_" + nm)
        P.op("sp", lambda e, t=t, nm=nm: e.dma_start(out=t.ap, in_=din[nm][g]), writes=[t], dma=True)
        small[nm] = t
    cw, cb, dtb, alog, dsk, sng = (small[k] for k in ("convw", "convb", "dtb", "alog", "dsk", "sng"))
    tokmask = cst["tokmask"]
    Abc = C.sbT([128, 8], F32, "Abc")
    P.op("act", lambda e: e.activation(out=Abc[:, :], in_=alog[:, :], func=AF.Exp), reads=[alog], writes=[Abc])
    P.op("dve", lambda e: e.tensor_scalar(out=Abc[:, :], in0=Abc[:, :], scalar1=-1.0, scalar2=None, op0=ALU.mult),
         reads=[Abc], writes=[Abc])
    dtb4 = C.sbT([128, 32], F32, "dtb4")
    Abc4 = C.sbT([128, 32], F32, "Abc4")
    for s_ in range(4):
        P.op("dve", lambda e, s_=s_: e.tensor_copy(out=dtb4[:, 8 * s_:8 * s_ + 8], in_=dtb[:, :]), reads=[dtb, dtb4], writes=[dtb4])
        P.op("dve", lambda e, s_=s_: e.tensor_copy(out=Abc4[:, 8 * s_:8 * s_ + 8], in_=Abc[:, :]), reads=[Abc, Abc4], writes=[Abc4])
    s32 = Ring([C.sbT([128, 32], F32, "a_s32") for _ in range(21)])
    a3mr = Ring([C.sbT([128, 3, 32], BF16, "a_a3m") for _ in range(3)])
    s48 = Ring([C.sbT([128, 4, 8], F32, "a_s48") for _ in range(12)])
    S = C.sbT([128, 512], F32, "S")
    Sbf = C.sbT([128, 512], BF16, "Sbf")
    xbc = C.sbT([128, 6, 515], F32, "xbc")
    P.op("pool", lambda e: e.memset(S[:, :], 0.0), writes=[S])
    P.op("pool", lambda e: e.memset(Sbf[:, :], 0.0), writes=[Sbf])
    P.op("pool", lambda e: e.memset(xbc[:, :, 0:3], 0.0), writes=[xbc])
    hring = Ring([C.sbT([128, 8, 512], BF16, "a_hT") for _ in range(2)])
    cring = Ring([C.sbT([128, 6, 512], BF16, "a_co") for _ in range(2)])
    accr = Ring([C.sbT([128, 512], F32, "a_acc") for _ in range(5)])
    f512 = Ring([C.sbT([128, 512], F32, "a_f512") for _ in range(6)])
    szr = Ring([C.sbT([128, 512], F32, "a_sz") for _ in range(5)])
    b512 = Ring([C.sbT([128, 512], BF16, "a_b512") for _ in range(28)])
    s8 = Ring([C.sbT([128, 8], F32, "a_s8") for _ in range(64)])
    s2 = Ring([C.sbT([128, 2], F32, "a_s2") for _ in range(6)])
    Rr = Ring([C.sbT([128, 3, 8, 128], BF16, "a_R") for _ in range(4)])
    a3r = Ring([C.sbT([128, 3, 8], BF16, "a_a3") for _ in range(6)])
    Lr = Ring([C.sbT([128, 8, 128], BF16, "a_L") for _ in range(5)])
    Mr = Ring([C.sbT([128, 8, 128], BF16, "a_M") for _ in range(5)])
    cbm = Ring([C.sbT([128, 128], BF16, "a_cbm") for _ in range(5)])
    ybst = Ring([C.sbT([128, 4, 512], BF16, "a_ybst") for _ in range(2)])
    junk = C.sbT([128, 512], BF16, "a_junk")
    pss = cst["ps_small"]
    i_eng = 0
    for m in range(NTOKW // 512):
        tok0 = m * 512
        own = tok0 >= OWN0
        hT = hring.next()
        P.op("sp", lambda e, hT=hT, tok0=tok0: e.dma_start(out=hT[:, :, :], in_=hT_scr[:, :, tok0:tok0 + 512]),
             reads=[hT_T[m]], writes=[hT], dma=True)
        for c in range(6):
            ps = psum.next()
            for kc in range(8):
                P.op("pe", lambda e, ps=ps, kc=kc, c=c, hT=hT: e.matmul(
                    ps[:, :], lhsT=w1a[:, kc, c * 128:(c + 1) * 128], rhs=hT[:, kc, :], start=(kc == 0), stop=(kc == 7)),
                    reads=[w1a, hT], writes=[ps])
            if c % 2 == 0:
                P.op("act", lambda e, ps=ps, c=c: e.copy(out=xbc[:, c, 3:515], in_=ps[:, :]), reads=[ps, xbc], writes=[xbc])
            else:
                P.op("dve", lambda e, ps=ps, c=c: e.tensor_copy(out=xbc[:, c, 3:515], in_=ps[:, :]), reads=[ps, xbc], writes=[xbc])
        co = cring.next()
        for c in range(6):
            eng = "pool" if c in (1, 4) else "dve"
            acc = accr.next()
            P.op(eng, lambda e, acc=acc, c=c: e.tensor_scalar(
                out=acc[:, :], in0=xbc[:, c, 0:512], scalar1=cw[:, c, 0:1], scalar2=None, op0=ALU.mult),
                reads=[xbc, cw], writes=[acc])
            for k in range(1, 4):
                if eng == "dve":
                    P.op(eng, lambda e, acc=acc, c=c, k=k: e.scalar_tensor_tensor(
                        out=acc[:, :], in0=xbc[:, c, k:k + 512], scalar=cw[:, c, k:k + 1], in1=acc[:, :],
                        op0=ALU.mult, op1=ALU.add), reads=[xbc, cw, acc], writes=[acc])
                else:
                    tmpc = accr.next()
                    P.op(eng, lambda e, tmpc=tmpc, c=c, k=k: e.tensor_scalar(
                        out=tmpc[:, :], in0=xbc[:, c, k:k + 512], scalar1=cw[:, c, k:k + 1], scalar2=None, op0=ALU.mult),
                        reads=[xbc, cw], writes=[tmpc])
                    P.op(eng, lambda e, tmpc=tmpc, acc=acc: e.tensor_tensor(out=acc[:, :], in0=acc[:, :], in1=tmpc[:, :], op=ALU.add),
                         reads=[acc, tmpc], writes=[acc])
            P.op("act", lambda e, acc=acc, c=c, co=co: e.activation(
                out=co[:, c, :], in_=acc[:, :], func=AF.Silu, bias=cb[:, c:c + 1], scale=1.0), reads=[acc, cb, co], writes=[co])
        P.op("pool", lambda e: e.tensor_copy(out=xbc[:, :, 0:3], in_=xbc[:, :, 512:515]), reads=[xbc], writes=[xbc])
        yst = ybst.next() if own else None
        ctx = {}

        def pre(s, m=m, hT=hT, co=co, own=own):
            sub = slice(s * 128, (s + 1) * 128)
            tile_idx = m * 4 + s
            dt_ = TV(dtm, dtm.ap[:, 8 * s:8 * s + 8])
            a3 = TV(a3m, a3m.ap[:, :, 8 * s:8 * s + 8])
            wst = TV(wstm, wstm.ap[:, s, :])
            cdec = TV(cdecm, cdecm.ap[:, s, :])
            wend = TV(wendm, wendm.ap[:, s, :])
            yield
            pxs = psum.next()
            pxb = pxs.ap.bitcast(BF16)
            for c in range(5):
                P.op("pe", lambda e, pxb=pxb, c=c, co=co, sub=sub: e.transpose(
                    pxb[:, c * 128:(c + 1) * 128], co[:, c, sub], ident_b[:, :]), reads=[co, ident_b], writes=[pxs])
            xs_tm, Btm, xdt, xdtw = b512.next(), b512.next(), b512.next(), b512.next()
            P.op("act", lambda e, pxb=pxb, xs_tm=xs_tm: e.copy(out=xs_tm[:, :], in_=pxb[:, 0:512]), reads=[pxs], writes=[xs_tm])
            P.op("act", lambda e, pxb=pxb, Btm=Btm: e.copy(out=Btm[:, 0:128], in_=pxb[:, 512:640]), reads=[pxs], writes=[Btm])
            P.op("pool", lambda e, xs_tm=xs_tm, xdt=xdt, dt_=dt_: e.tensor_tensor(
                out=xdt[:, :].rearrange("p (h d) -> p h d", h=8), in0=xs_tm[:, :].rearrange("p (h d) -> p h d", h=8),
                in1=dt_[:, :].unsqueeze(2).to_broadcast([128, 8, 64]), op=ALU.mult), reads=[xs_tm, dt_], writes=[xdt])
            P.op("pool", lambda e, xdt=xdt, xdtw=xdtw, wend=wend: e.tensor_tensor(
                out=xdtw[:, :].rearrange("p (h d) -> p h d", h=8), in0=xdt[:, :].rearrange("p (h d) -> p h d", h=8),
                in1=wend[:, :].unsqueeze(2).to_broadcast([128, 8, 64]), op=ALU.mult), reads=[xdt, wend], writes=[xdtw])
            yield
            if own:
                R, L, Mh, cbt = Rr.next(), Lr.next(), Mr.next(), cbm.next()
                for i3 in range(3):
                    P.op("dve" if i3 != 1 else "pool", lambda e, R=R, a3=a3, i3=i3: e.tensor_tensor(
                        out=R[:, i3, :, :], in0=U[:, :].unsqueeze(1).to_broadcast([128, 8, 128]),
                        in1=a3[:, i3, :].unsqueeze(2).to_broadcast([128, 8, 128]), op=ALU.mult), reads=[U, a3, R], writes=[R])
                for hh in range(2):
                    pD = psum.next()
                    for i3 in range(3):
                        P.op("pe", lambda e, pD=pD, R=R, hh=hh, i3=i3: e.matmul(
                            pD[:, :], lhsT=T1[:, :], rhs=R[:, i3, hh * 4:(hh + 1) * 4, :].rearrange("p h l -> p (h l)"),
                            start=(i3 == 0), stop=(i3 == 2)), reads=[T1, R], writes=[pD])
                    P.op("act", lambda e, pD=pD, L=L, hh=hh: e.activation(
                        out=L[:, hh * 4:(hh + 1) * 4, :].rearrange("p h l -> p (h l)"), in_=pD[:, :], func=AF.Exp),
                        reads=[pD, L], writes=[L])
                yield
                pcb = pss["cb%d" % s]
                P.op("pe", lambda e, co=co, sub=sub: e.matmul(
                    pcb[:, :], lhsT=co[:, 4, sub], rhs=co[:, 5, sub], start=True, stop=True), reads=[co], writes=[pcb])
                P.op("dve", lambda e, cbt=cbt: e.tensor_tensor(out=cbt[:, :], in0=pcb[:, :], in1=U[:, :], op=ALU.mult),
                     reads=[pcb, U], writes=[cbt])
                P.op("pool", lambda e, Mh=Mh, L=L, cbt=cbt: e.tensor_tensor(
                    out=Mh[:, :, :], in0=L[:, :, :], in1=cbt[:, :].unsqueeze(1).to_broadcast([128, 8, 128]), op=ALU.mult),
                    reads=[L, cbt], writes=[Mh])
                xsD = b512.next()
                P.op("pool", lambda e, xs_tm=xs_tm, xsD=xsD: e.tensor_tensor(
                    out=xsD[:, :].rearrange("p (h d) -> p h d", h=8), in0=xs_tm[:, :].rearrange("p (h d) -> p h d", h=8),
                    in1=dsk[:, :].unsqueeze(2).to_broadcast([128, 8, 64]), op=ALU.mult), reads=[xs_tm, dsk], writes=[xsD])
                yield
                pz = psum.next()
                for kc in range(8):
                    P.op("pe", lambda e, pz=pz, kc=kc, hT=hT, sub=sub: e.matmul(
                        pz[:, :], lhsT=hT[:, kc, sub], rhs=w1a[:, kc, 768:1280], start=(kc == 0), stop=(kc == 7)),
                        reads=[w1a, hT], writes=[pz])
                sz = szr.next()
                P.op("act", lambda e, pz=pz, sz=sz: e.activation(out=sz[:, :], in_=pz[:, :], func=AF.Silu), reads=[pz], writes=[sz])
            ctx[s] = dict(locals())
            yield

        def seq(s, m=m, hT=hT, co=co, own=own, yst=yst):
            L_ = ctx[s]
            sub = L_["sub"]
            wst, cdec, xdt, xdtw, Btm = L_["wst"], L_["cdec"], L_["xdt"], L_["xdtw"], L_["Btm"]
            if own:
                Mh, xsD, sz = L_["Mh"], L_["xsD"], L_["sz"]
                pyo, py = psum.next(), psum.next()
                P.op("pe", lambda e, pyo=pyo, co=co, sub=sub: e.matmul(
                    pyo[:, :], lhsT=co[:, 5, sub], rhs=Sbf[:, :], start=True, stop=True), reads=[co, Sbf], writes=[pyo])
                P.op("pe", lambda e, py=py, xsD=xsD: e.matmul(py[:, :], lhsT=ident_b[:, :], rhs=xsD[:, :], start=True, stop=False),
                     reads=[ident_b, xsD], writes=[py])
                for h in range(8):
                    P.op("pe", lambda e, py=py, Mh=Mh, xdt=xdt, h=h: e.matmul(
                        py[:, h * 64:(h + 1) * 64], lhsT=Mh[:, h, :], rhs=xdt[:, h * 64:(h + 1) * 64], start=False, stop=(h == 7)),
                        reads=[Mh, xdt], writes=[py])
                y1, y2, y3 = f512.next(), f512.next(), f512.next()
                P.op("dve", lambda e, pyo=pyo, y1=y1, wst=wst: e.tensor_tensor(
                    out=y1[:, :].rearrange("p (h d) -> p h d", h=8), in0=pyo[:, :].rearrange("p (h d) -> p h d", h=8),
                    in1=wst[:, :].unsqueeze(2).to_broadcast([128, 8, 64]), op=ALU.mult), reads=[pyo, wst], writes=[y1])
                P.op("dve", lambda e, y1=y1, y2=y2, py=py: e.tensor_tensor(out=y2[:, :], in0=y1[:, :], in1=py[:, :], op=ALU.add),
                     reads=[y1, py], writes=[y2])
                P.op("pool", lambda e, y2=y2, y3=y3, sz=sz: e.tensor_tensor(out=y3[:, :], in0=y2[:, :], in1=sz[:, :], op=ALU.mult),
                     reads=[y2, sz], writes=[y3])
                ss = s2.next()
                P.op("act", lambda e, y3=y3, ss=ss: e.activation(out=junk[:, :], in_=y3[:, :], func=AF.Square, accum_out=ss[:, 0:1]),
                     reads=[y3], writes=[junk, ss])
                P.op("act", lambda e, ss=ss: e.activation(out=ss[:, 1:2], in_=ss[:, 0:1], func=AF.Sqrt, scale=1.0 / 512.0, bias=EPS),
                     reads=[ss], writes=[ss])
                P.op("dve", lambda e, ss=ss: e.reciprocal(out=ss[:, 1:2], in_=ss[:, 1:2]), reads=[ss], writes=[ss])
                yn = b512.next()
                P.op("dve", lambda e, y3=y3, ss=ss, yn=yn: e.scalar_tensor_tensor(
                    out=yn[:, :], in0=y3[:, :], scalar=ss[:, 1:2], in1=sng[:, :], op0=ALU.mult, op1=ALU.mult),
                    reads=[y3, ss, sng], writes=[yn])
                pyt = psum.next()
                pytb = pyt.ap.bitcast(BF16)
                for c in range(4):
                    P.op("pe", lambda e, pytb=pytb, yn=yn, c=c: e.transpose(
                        pytb[:, c * 128:(c + 1) * 128], yn[:, c * 128:(c + 1) * 128], ident_b[:, :]),
                        reads=[yn, ident_b], writes=[pyt])
                P.op("act", lambda e, pytb=pytb, yst=yst, sub=sub: e.copy(
                    out=yst[:, :, sub], in_=pytb[:, 0:512].rearrange("p (c n) -> p c n", c=4)), reads=[pyt, yst], writes=[yst])
            pst = psum.next()
            P.op("pe", lambda e, pst=pst, Btm=Btm, xdtw=xdtw: e.matmul(
                pst[:, :], lhsT=Btm[:, 0:128], rhs=xdtw[:, :], start=True, stop=True), reads=[Btm, xdtw], writes=[pst])
            P.op("pool", lambda e, cdec=cdec: e.tensor_tensor(
                out=S[:, :].rearrange("p (h d) -> p h d", h=8), in0=S[:, :].rearrange("p (h d) -> p h d", h=8),
                in1=cdec[:, :].unsqueeze(2).to_broadcast([128, 8, 64]), op=ALU.mult), reads=[S, cdec], writes=[S])
            P.op("dve", lambda e, pst=pst: e.tensor_tensor(out=S[:, :], in0=S[:, :], in1=pst[:, :], op=ALU.add),
                 reads=[S, pst], writes=[S])
            if own or (m * 4 + s + 1) * 128 >= OWN0:
                P.op("act", lambda e: e.copy(out=Sbf[:, :], in_=S[:, :]), reads=[S, Sbf], writes=[Sbf])

        for s in range(4):
            pdt = pss["dt%d" % s]
            for kc in range(8):
                P.op("pe", lambda e, kc=kc, hT=hT, s=s, pdt=pdt: e.matmul(
                    pdt[:, :], lhsT=hT[:, kc, s * 128:(s + 1) * 128], rhs=w1a[:, kc, 1280:1288], start=(kc == 0), stop=(kc == 7)),
                    reads=[w1a, hT], writes=[pdt])
        pd_all = TV(pss["dt0"].parent, pss["dt0"].parent.ap[:, 0:32])
        dtr, ax, ee, dtm, am, ar1, ar2 = (s32.next() for _ in range(7))
        a3m = a3mr.next()
        P.op("dve", lambda e, dtr=dtr: e.tensor_tensor(out=dtr[:, :], in0=pd_all[:, :], in1=dtb4[:, :], op=ALU.add),
             reads=[pd_all, dtb4], writes=[dtr])
        P.op("act", lambda e, dtr=dtr, ax=ax: e.activation(out=ax[:, :], in_=dtr[:, :], func=AF.Abs), reads=[dtr], writes=[ax])
        P.op("act", lambda e, ax=ax, ee=ee: e.activation(out=ee[:, :], in_=ax[:, :], func=AF.Exp, scale=-1.0), reads=[ax], writes=[ee])
        P.op("act", lambda e, ee=ee: e.activation(out=ee[:, :], in_=ee[:, :], func=AF.Ln, bias=1.0, scale=1.0), reads=[ee], writes=[ee])
        P.op("dve", lambda e, dtr=dtr, ee=ee, dtm=dtm: e.scalar_tensor_tensor(
            out=dtm[:, :], in0=dtr[:, :], scalar=0.0, in1=ee[:, :], op0=ALU.max, op1=ALU.add), reads=[dtr, ee], writes=[dtm])
        P.op("dve", lambda e, dtm=dtm, m=m: e.tensor_tensor(
            out=dtm[:, :].rearrange("p (s h) -> p s h", s=4), in0=dtm[:, :].rearrange("p (s h) -> p s h", s=4),
            in1=tokmask[:, 4 * m:4 * m + 4].unsqueeze(2).to_broadcast([128, 4, 8]), op=ALU.mult), reads=[dtm, tokmask], writes=[dtm])
        P.op("dve", lambda e, dtm=dtm, am=am: e.tensor_tensor(out=am[:, :], in0=dtm[:, :], in1=Abc4[:, :], op=ALU.mult),
             reads=[dtm, Abc4], writes=[am])
        P.op("act", lambda e, am=am, a3m=a3m: e.copy(out=a3m[:, 0, :], in_=am[:, :]), reads=[am, a3m], writes=[a3m])
        P.op("dve", lambda e, am=am, a3m=a3m, ar1=ar1: e.tensor_tensor(out=ar1[:, :], in0=am[:, :], in1=a3m[:, 0, :], op=ALU.subtract),
             reads=[am, a3m], writes=[ar1])
        P.op("act", lambda e, ar1=ar1, a3m=a3m: e.copy(out=a3m[:, 1, :], in_=ar1[:, :]), reads=[ar1, a3m], writes=[a3m])
        P.op("dve", lambda e, ar1=ar1, a3m=a3m, ar2=ar2: e.tensor_tensor(out=ar2[:, :], in0=ar1[:, :], in1=a3m[:, 1, :], op=ALU.subtract),
             reads=[ar1, a3m], writes=[ar2])
        P.op("act", lambda e, ar2=ar2, a3m=a3m: e.copy(out=a3m[:, 2, :], in_=ar2[:, :]), reads=[ar2, a3m], writes=[a3m])
        for s in range(4):
            pac = pss["acs%d" % s]
            for i3 in range(3):
                P.op("pe", lambda e, a3m=a3m, i3=i3, s=s, pac=pac: e.matmul(
                    pac[:, 0:8], lhsT=U[:, :], rhs=a3m[:, i3, 8 * s:8 * s + 8], start=(i3 == 0), stop=(i3 == 2)),
                    reads=[U, a3m], writes=[pac])
            for i3 in range(3):
                P.op("pe", lambda e, a3m=a3m, i3=i3, s=s, pac=pac: e.matmul(
                    pac[:, 8:16], lhsT=ones_b[:, :], rhs=a3m[:, i3, 8 * s:8 * s + 8], start=(i3 == 0), stop=(i3 == 2)),
                    reads=[ones_b, a3m], writes=[pac])
        par = pss["acs0"].parent
        pacs = TV(par, par.ap[:, 64:128].rearrange("p (s t h) -> p s t h", s=4, t=2)[:, :, 0, :])
        ptot = TV(par, par.ap[:, 64:128].rearrange("p (s t h) -> p s t h", s=4, t=2)[:, :, 1, :])
        acsm, wstm, cdecm, wendm = (s48.next() for _ in range(4))
        P.op("act", lambda e, acsm=acsm: e.copy(out=acsm[:, :, :], in_=pacs[:, :, :]), reads=[pacs], writes=[acsm])
        P.op("act", lambda e, wstm=wstm: e.activation(out=wstm[:, :, :], in_=pacs[:, :, :], func=AF.Exp), reads=[pacs], writes=[wstm])
        P.op("act", lambda e, cdecm=cdecm: e.activation(out=cdecm[:, :, :], in_=ptot[:, :, :], func=AF.Exp), reads=[ptot], writes=[cdecm])
        P.op("dve", lambda e, wendm=wendm, acsm=acsm: e.tensor_tensor(out=wendm[:, :, :], in0=ptot[:, :, :], in1=acsm[:, :, :], op=ALU.subtract),
             reads=[ptot, acsm], writes=[wendm])
        P.op("act", lambda e, wendm=wendm: e.activation(out=wendm[:, :, :], in_=wendm[:, :, :], func=AF.Exp), reads=[wendm], writes=[wendm])
        gens = [pre(s) for s in range(4)]
        while gens:
            for g_ in list(gens):
                try:
                    next(g_)
                except StopIteration:
                    gens.remove(g_)
        for s in range(4):
            seq(s)
        if own:
            o0 = tok0 - OWN0
            P.op("pool", lambda e, yst=yst, o0=o0: e.dma_start(
                out=e_yb[g, :, :, o0:o0 + 512].rearrange("c p n -> p c n"), in_=yst[:, :, :]),
                reads=[yst], writes=[e_yb_T], dma=True)


def build_p1_init(P, C, din, cst, NTOKW):
    KT = C.sb([96, 4, NTOKW], BF16, "KT")
    VA = C.sb([128, NTOKW // 128, 2, 3, 64], BF16, "VA")
    kmT = C.sbT([64, 4, 32], BF16, "kmT")
    Mpad = [C.sbT([128, 4, 96], BF16, "Mpad") for _ in range(2)]
    for h in range(4):
        P.op("sp", lambda e, h=h: e.dma_start(out=KT[64:96, h, :], in_=din["kind"]), dma=True)
    P.op("pool", lambda e: e.memset(VA[:, :, :, 1, :], 1.0))
    for mp in Mpad:
        P.op("pool", lambda e, mp=mp: e.memset(mp[:, :, :], 0.0), writes=[mp])
    P.op("pool", lambda e: e.memset(kmT[:, :, :], 0.0), writes=[kmT])
    G = C.sbT([128, 512], F32, "G")
    gq, gk = cst["gq"], cst["gk"]
    for h in range(4):
        P.op("dve", lambda e, h=h: e.tensor_scalar(out=G[:, h * 64:(h + 1) * 64], in0=gq[:, :], scalar1=0.125, scalar2=None,
                                                   op0=ALU.mult), reads=[gq, G], writes=[G])
        P.op("dve", lambda e, h=h: e.tensor_copy(out=G[:, 256 + h * 64:256 + (h + 1) * 64], in_=gk[:, :]), reads=[gk, G], writes=[G])
    bb4 = C.sbT([128, 128], F32, "bb4")
    for h in range(4):
        P.op("dve", lambda e, h=h: e.tensor_copy(out=bb4[:, h * 32:(h + 1) * 32], in_=cst["blkbias"][:, :]),
             reads=[cst["blkbias"], bb4], writes=[bb4])
    P.barrier()
    nm = NTOKW // 512
    return dict(KT=KT, VA=VA, kmT=kmT, Mpad=Ring(Mpad), G=G, bb4=bb4,
                KT_T=[T(None, "KT%d" % i) for i in range(nm)], VA_T=[T(None, "VA%d" % i) for i in range(nm)])


def build_p1b(P, C, din, g, cst, A, hT_scr, hT_T, e_att, e_att_T, NTOKW, OWN0):
    psum = cst["psring"]
    po_ring = cst["po_ring"]
    pss = cst["ps_small"]
    ident_b, negm = cst["ident_b"], cst["negm"]
    KT, VA, kmT, G, bb4 = A["KT"], A["VA"], A["kmT"], A["G"], A["bb4"]
    KT_T, VA_T = A["KT_T"], A["VA_T"]
    stg = Ring([C.sbT([128, 8, WB_COLS], F32, "b_stg") for _ in range(2)])
    w1b = C.sbT([128, 8, W1B], BF16, "w1b")
    load_cast_weight(P, C, din["w1b"][g], w1b, W1B, cst["g1"], stg)
    hring = Ring([C.sbT([128, 8, 512], BF16, "b_hT") for _ in range(2)])
    f512 = Ring([C.sbT([128, 512], F32, "b_f512") for _ in range(4)])
    b512 = Ring([C.sbT([128, 512], BF16, "b_b512") for _ in range(3)])
    ptr = Ring([C.sbT([128, 512], BF16, "b_pt") for _ in range(6)])
    s8 = Ring([C.sbT([128, 8], F32, "b_s8") for _ in range(6)])
    g128 = Ring([C.sbT([128, 128], F32, "b_g128") for _ in range(6)])
    t8r = Ring([C.sbT([128, 32], F32, "b_t8") for _ in range(2)])
    kmf = C.sbT([64, 4, 2], F32, "b_kmf")
    QTr = Ring([C.sbT([96, 4, 512], BF16, "b_QT") for _ in range(2)])
    ast = [Ring([C.sbT([128, 512], BF16, "b_ast") for _ in range(2)]) for _ in range(2)]
    rdr = Ring([C.sbT([128, 512], F32, "b_rd") for _ in range(2)])
    outs = []
    for m in range(NTOKW // 512):
        tok0 = m * 512
        own = tok0 >= OWN0
        c0 = 0 if own else 256
        h0 = 0 if own else 4
        hT = hring.next()
        P.op("sp", lambda e, hT=hT, tok0=tok0: e.dma_start(out=hT[:, :, :], in_=hT_scr[:, :, tok0:tok0 + 512]),
             reads=[hT_T[m]], writes=[hT], dma=True)
        QT = QTr.next() if own else None
        def kv(s, m=m, hT=hT, QT=QT, own=own, c0=c0, h0=h0, tok0=tok0):
            sub = slice(s * 128, (s + 1) * 128)
            kt = m * 4 + s
            pqk, pv = psum.next(), psum.next()
            for kc in range(8):
                P.op("pe", lambda e, pqk=pqk, kc=kc, hT=hT, sub=sub, c0=c0: e.matmul(
                    pqk[:, c0:512], lhsT=hT[:, kc, sub], rhs=w1b[:, kc, c0:512], start=(kc == 0), stop=(kc == 7)),
                    reads=[w1b, hT], writes=[pqk])
            for kc in range(8):
                P.op("pe", lambda e, pv=pv, kc=kc, hT=hT, sub=sub: e.matmul(
                    pv[:, 0:256], lhsT=hT[:, kc, sub], rhs=w1b[:, kc, 512:768], start=(kc == 0), stop=(kc == 7)),
                    reads=[w1b, hT], writes=[pv])
            yield
            P.op("act", lambda e, pv=pv, kt=kt: e.copy(
                out=VA[:, kt, :, 0, :], in_=pv[:, 0:256].rearrange("p (a b d) -> p a b d", a=2, b=2)[:, :, 0, :]),
                reads=[pv, VA_T[m]], writes=[VA_T[m]])
            P.op("dve", lambda e, pv=pv, kt=kt: e.tensor_copy(
                out=VA[:, kt, :, 2, :], in_=pv[:, 0:256].rearrange("p (a b d) -> p a b d", a=2, b=2)[:, :, 1, :]),
                reads=[pv, VA_T[m]], writes=[VA_T[m]])
            sq, ssum, tt = f512.next(), s8.next(), f512.next()
            P.op("act", lambda e, pqk=pqk, sq=sq, c0=c0: e.activation(out=sq[:, c0:512], in_=pqk[:, c0:512], func=AF.Square),
                 reads=[pqk], writes=[sq])
            yield
            P.op("dve", lambda e, sq=sq, ssum=ssum, c0=c0, h0=h0: e.tensor_reduce(
                out=ssum[:, h0:8], in_=sq[:, c0:512].rearrange("p (h d) -> p h d", d=64), axis=AX.X, op=ALU.add),
                reads=[sq], writes=[ssum])
            P.op("act", lambda e, ssum=ssum, h0=h0: e.activation(
                out=ssum[:, h0:8], in_=ssum[:, h0:8], func=AF.Sqrt, scale=1.0 / 64.0, bias=EPS), reads=[ssum], writes=[ssum])
            P.op("dve", lambda e, ssum=ssum, h0=h0: e.reciprocal(out=ssum[:, h0:8], in_=ssum[:, h0:8]), reads=[ssum], writes=[ssum])
            yield
            P.op("dve", lambda e, pqk=pqk, tt=tt, ssum=ssum, c0=c0, h0=h0: e.tensor_tensor(
                out=tt[:, c0:512].rearrange("p (h d) -> p h d", d=64), in0=pqk[:, c0:512].rearrange("p (h d) -> p h d", d=64),
                in1=ssum[:, h0:8].unsqueeze(2).to_broadcast([128, 8 - h0, 64]), op=ALU.mult), reads=[pqk, ssum], writes=[tt])
            yield
            qkn = b512.next()
            P.op("pool", lambda e, tt=tt, qkn=qkn, c0=c0: e.tensor_tensor(
                out=qkn[:, c0:512], in0=tt[:, c0:512], in1=G[:, c0:512], op=ALU.mult), reads=[tt, G], writes=[qkn])
            yield
            pkt = psum.next()
            pktb = pkt.ap.bitcast(BF16)
            for h in range(4):
                P.op("pe", lambda e, pktb=pktb, qkn=qkn, h=h: e.transpose(
                    pktb[0:64, h * 128:(h + 1) * 128], qkn[:, 256 + h * 64:256 + (h + 1) * 64], ident_b[:, :]),
                    reads=[qkn, ident_b], writes=[pkt])
            P.op("act", lambda e, pktb=pktb, tok0=tok0, s=s: e.copy(
                out=KT[0:64, :, tok0 + s * 128:tok0 + (s + 1) * 128], in_=pktb[0:64, 0:512].rearrange("p (h n) -> p h n", h=4)),
                reads=[pkt, KT_T[m]], writes=[KT_T[m]])
            yield
            if own:
                pqt = psum.next()
                pqtb = pqt.ap.bitcast(BF16)
                for h in range(4):
                    P.op("pe", lambda e, pqtb=pqtb, qkn=qkn, h=h: e.transpose(
                        pqtb[0:64, h * 128:(h + 1) * 128], qkn[:, h * 64:(h + 1) * 64], ident_b[:, :]),
                        reads=[qkn, ident_b], writes=[pqt])
                P.op("dve", lambda e, pqtb=pqtb, QT=QT, sub=sub: e.tensor_copy(
                    out=QT[0:64, :, sub], in_=pqtb[0:64, 0:512].rearrange("p (h n) -> p h n", h=4)),
                    reads=[pqt, QT], writes=[QT])
        for pair_ in ((0, 1), (2, 3)):
            gens = [kv(s_) for s_ in pair_]
            while gens:
                for g_ in list(gens):
                    try:
                        next(g_)
                    except StopIteration:
                        gens.remove(g_)
        P.op("dve", lambda e, tok0=tok0: e.tensor_reduce(
            out=kmf[:, :, :], in_=KT[0:64, :, tok0:tok0 + 512].rearrange("p h (b k) -> p h b k", b=2), axis=AX.X, op=ALU.add),
            reads=[KT_T[m]], writes=[kmf])
        P.op("dve", lambda e, m=m: e.tensor_scalar(out=kmT[:, :, 2 * m:2 * m + 2], in0=kmf[:, :, :], scalar1=1.0 / 256.0,
                                                   scalar2=None, op0=ALU.mult), reads=[kmf, kmT], writes=[kmT])
        if not own:
            continue
        for s in range(4):
            sub = slice(s * 128, (s + 1) * 128)
            ownblk = 2 * m + s // 2
            pg = pss["gate"]
            for h in range(4):
                P.op("pe", lambda e, h=h, QT=QT, sub=sub: e.matmul(
                    pg[:, h * 32:(h + 1) * 32], lhsT=QT[0:64, h, sub], rhs=kmT[0:64, h, :], start=True, stop=True),
                    reads=[QT, kmT], writes=[pg])
            gm, m1, m2, t8 = g128.next(), g128.next(), g128.next(), t8r.next()
            P.op("dve", lambda e, gm=gm: e.tensor_tensor(out=gm[:, :], in0=pg[:, :], in1=bb4[:, :], op=ALU.add),
                 reads=[pg, bb4], writes=[gm])
            P.op("pool", lambda e, gm=gm, ownblk=ownblk: e.memset(
                gm[:, :].rearrange("p (h b) -> p h b", h=4)[:, :, ownblk:32], NEG), reads=[gm], writes=[gm])
            for h in range(4):
                P.op("dve", lambda e, gm=gm, t8=t8, h=h: e.max(out=t8[:, h * 8:(h + 1) * 8], in_=gm[:, h * 32:(h + 1) * 32]),
                     reads=[gm, t8], writes=[t8])
            P.op("dve", lambda e, gm=gm, m1=m1, t8=t8: e.tensor_tensor(
                out=m1[:, :].rearrange("p (h b) -> p h b", h=4), in0=gm[:, :].rearrange("p (h b) -> p h b", h=4),
                in1=t8[:, :].rearrange("p (h k) -> p h k", h=4)[:, :, 2:3].to_broadcast([128, 4, 32]), op=ALU.is_lt),
                reads=[gm, t8], writes=[m1])
            P.op("dve", lambda e, gm=gm, m2=m2: e.tensor_scalar(
                out=m2[:, :], in0=gm[:, :], scalar1=NEG / 2, scalar2=NEG, op0=ALU.is_lt, op1=ALU.mult), reads=[gm], writes=[m2])
            Mp = A["Mpad"].next()
            P.op("dve", lambda e, Mp=Mp, m1=m1, m2=m2: e.scalar_tensor_tensor(
                out=Mp[:, :, 64:96], in0=m1[:, :].rearrange("p (h b) -> p h b", h=4), scalar=NEG,
                in1=m2[:, :].rearrange("p (h b) -> p h b", h=4), op0=ALU.mult, op1=ALU.min), reads=[m1, m2, Mp], writes=[Mp])
            P.op("pool", lambda e, Mp=Mp, ownblk=ownblk: e.memset(Mp[:, :, 64 + ownblk:65 + ownblk], 0.0), reads=[Mp], writes=[Mp])
            pmt = psum.next()
            pmtb = pmt.ap.bitcast(BF16)
            for h in range(4):
                P.op("pe", lambda e, pmtb=pmtb, Mp=Mp, h=h: e.transpose(
                    pmtb[0:96, h * 128:(h + 1) * 128], Mp[:, h, :], ident_b[:, :]), reads=[Mp, ident_b], writes=[pmt])
            P.op("act", lambda e, pmtb=pmtb, QT=QT, sub=sub: e.copy(
                out=QT[64:96, :, sub], in_=pmtb[64:96, 0:512].rearrange("p (h n) -> p h n", h=4)), reads=[pmt, QT], writes=[QT])
        nkt = (2 * m + 2) * 2
        o0 = tok0 - OWN0
        for h in range(4):
            pair, hb = h // 2, h % 2
            po = po_ring.next()
            def tile_cols(kt):
                blk = kt // 2
                if blk < 2 * m:
                    return 0, 512, None
                if blk == 2 * m:
                    return 0, 512, 0
                return 256, 512, 256

            def emit_s(kt):
                a0, a1, cz = tile_cols(kt)
                mk = kt // 4
                ps = psum.next()
                P.op("pe", lambda e, ps=ps, h=h, kt=kt, QT=QT, a0=a0, a1=a1, cz=cz: e.matmul(
                    ps[:, a0:a1], lhsT=KT[0:96, h, kt * 128:(kt + 1) * 128], rhs=QT[0:96, h, a0:a1],
                    start=True, stop=(cz is None)), reads=[KT_T[mk], QT], writes=[ps])
                if cz is not None:
                    P.op("pe", lambda e, ps=ps, kt=kt, cz=cz: e.matmul(
                        ps[:, cz:cz + 256], lhsT=ident_b[:, :], rhs=negm[:, kt % 2, :], start=False, stop=True),
                        reads=[ident_b, negm], writes=[ps])
                return ps

            def emit_pv(kt, ps):
                a0, a1, cz = tile_cols(kt)
                mk = kt // 4
                pt = ptr.next()
                P.op("act", lambda e, ps=ps, pt=pt, a0=a0, a1=a1: e.activation(out=pt[:, a0:a1], in_=ps[:, a0:a1], func=AF.Exp),
                     reads=[ps], writes=[pt])
                P.op("pe", lambda e, po=po, pt=pt, kt=kt, pair=pair, hb=hb, a0=a0, a1=a1, nkt=nkt: e.matmul(
                    po[:, a0:a1], lhsT=VA[:, kt, pair, hb:hb + 2, :].rearrange("p a d -> p (a d)"), rhs=pt[:, a0:a1],
                    start=(kt == 0), stop=(kt == nkt - 1), skip_group_check=True), reads=[VA_T[mk], pt], writes=[po])

            LOOK = 3
            pend = []
            for kt in range(nkt):
                pend.append((kt, emit_s(kt)))
                if len(pend) > LOOK:
                    emit_pv(*pend.pop(0))
            while pend:
                emit_pv(*pend.pop(0))
            nr = slice(0, 64) if hb == 0 else slice(64, 128)
            dr = slice(64, 128) if hb == 0 else slice(0, 64)
            rd = rdr.next()
            if hb == 0:
                at_ = ast[pair].next()
                ast_cur = at_
            else:
                at_ = ast_cur
            P.op("dve", lambda e, po=po, rd=rd, nr=nr, dr=dr: e.reciprocal(out=rd[nr, :], in_=po[dr, :]), reads=[po], writes=[rd])
            P.op("dve", lambda e, po=po, rd=rd, nr=nr, at_=at_: e.tensor_tensor(
                out=at_[nr, :], in0=po[nr, :], in1=rd[nr, :], op=ALU.mult), reads=[po, rd, at_], writes=[at_])
            if hb == 1:
                outs.append(P.op("pool", lambda e, at_=at_, pair=pair, o0=o0: e.dma_start(
                    out=e_att[g, pair, :, o0:o0 + 512], in_=at_[:, :]), reads=[at_], writes=[e_att_T], dma=True))
    return outs


def load_consts(P, C, din, names_shapes):
    out = {}
    for name, shape, dt in names_shapes:
        t = C.sbT(shape, dt, name)
        P.op("sp", lambda e, t=t, name=name: e.dma_start(out=t.ap, in_=din[name]), writes=[t], dma=True)
        out[name] = t
    return out


def build_program(mode, NTOKW=8192, OWN0=0, NG=1):
    nc = bass.Bass("TRN2", target_bir_lowering=False)
    P = Prog(nc)
    with ExitStack() as es:
        C = Ctx(nc, es, P)
        din = {}

        def inp(name, shape, dt=F32):
            din[name] = C.dram(name, shape, dt, "ExternalInput")

        psum = [T(es.enter_context(nc.psum_tensor("ps%d" % i, [128, 512], F32))[:, :], "ps%d" % i, excl=True) for i in range(8)]
        final = []
        NOWN = NTOKW - OWN0
        if mode in ("p1", "fused"):
            inp("xw", [NTOKW, 1024])
            inp("w1a", [NG, 1024, W1A])
            inp("w1b", [NG, 1024, W1B])
            inp("convw", [NG, 128, 6, 4])
            inp("convb", [NG, 128, 6])
            for nm in ("dtb", "alog", "dsk"):
                inp(nm, [NG, 128, 8])
            inp("sng", [NG, 128, 512])
            inp("kind", [32, NTOKW], BF16)
            shapes1 = [("gq", [128, 64], F32), ("gk", [128, 64], F32), ("g1", [128, 8], F32),
                       ("tokmask", [128, NTOKW // 128], F32), ("blkbias", [128, 32], F32),
                       ("ident_b", [128, 128], BF16), ("ones_b", [128, 128], BF16), ("U", [128, 128], BF16),
                       ("T1", [128, 128], BF16), ("negm", [128, 2, 256], BF16)]
            for nm, shp, dt in shapes1:
                if nm not in din:
                    inp(nm, shp, dt)
            cst = load_consts(P, C, din, shapes1)
            cst["psring"] = Ring(psum[0:5])
            cst["po_ring"] = Ring(psum[5:7])
            cst["ps_small"] = {"gate": TV(psum[7], psum[7].ap[:, 0:128])}
            for s_ in range(4):
                cst["ps_small"]["dt%d" % s_] = TV(psum[5], psum[5].ap[:, 8 * s_:8 * s_ + 8])
                cst["ps_small"]["acs%d" % s_] = TV(psum[5], psum[5].ap[:, 64 + 16 * s_:64 + 16 * s_ + 16])
                cst["ps_small"]["cb%d" % s_] = TV(psum[6], psum[6].ap[:, 128 * s_:128 * s_ + 128])
            kind_e = "ExternalOutput" if mode == "p1" else "Internal"
            e_att = C.dram("e_att", [NG, 2, 128, NOWN], BF16, kind_e)
            e_yb = C.dram("e_yb", [NG, 4, 128, NOWN], BF16, kind_e)
            e_att_T, e_yb_T = T(None, "e_att"), T(None, "e_yb")
            hT_scr = C.dram("hT_scr", [128, 8, NTOKW], BF16, "Internal")
            hT_T = []
            with ExitStack() as es1:
                C1 = Ctx(nc, es1, P)
                if STAGES.get("pro", True):
                    build_prologue(P, C1, din, cst, hT_scr, hT_T, NTOKW)
            P.barrier()
            with ExitStack() as es1:
                C1 = Ctx(nc, es1, P)
                for g in range(NG):
                    if STAGES.get("a", True):
                        with ExitStack() as es2:
                            build_p1a(P, Ctx(nc, es2, P), din, g, cst, hT_scr, hT_T, e_yb, e_yb_T, NTOKW, OWN0)
                        P.barrier()
                    if STAGES.get("b", True):
                        with ExitStack() as es2:
                            C2b = Ctx(nc, es2, P)
                            A = build_p1_init(P, C2b, din, cst, NTOKW)
                            build_p1b(P, C2b, din, g, cst, A, hT_scr, hT_T, e_att, e_att_T, NTOKW, OWN0)
                        P.barrier()
            if mode == "p1":
                final = [o for o in P.ops["pool"] if o.dma][-8:]
        if mode in ("p2", "fused"):
            inp("x2", [2048, 1024])
            inp("p2", [2048, 256])
            for name, (K, N) in P2W.items():
                inp(name, [K, N])
            shapes2 = [("g1", [128, 8], F32), ("g2", [128, 8], F32), ("g3", [128, 8], F32),
                       ("ident_b", [128, 128], BF16), ("ones_b", [128, 128], BF16)]
            for nm, shp, dt in shapes2:
                if nm not in din:
                    inp(nm, shp, dt)
            consts = load_consts(P, C, din, shapes2)
            consts["psum"] = psum
            wscr = {name: C.dram("scr_" + name, [K, N], BF16, "Internal") for name, (K, N) in P2W.items()}
            wT = {}
            with ExitStack() as es2:
                C2 = Ctx(nc, es2, P)
                if STAGES.get("precast", True):
                    build_precast(P, C2, din, wscr, wT, consts)
            P.barrier()
            if mode == "p2":
                inp("e_att", [4, 2, 128, 2048], BF16)
                inp("e_yb", [4, 4, 128, 2048], BF16)
                exch = {"att": din["e_att"], "yb": din["e_yb"], "att_T": T(None), "yb_T": T(None)}
            else:
                exch = {"att": e_att, "yb": e_yb, "att_T": e_att_T, "yb_T": e_yb_T}
            dout = C.dram("out", [2048, 1024], F32, "ExternalOutput")
            if STAGES.get("p2", True):
                with ExitStack() as es3:
                    C3 = Ctx(nc, es3, P)
                    final = build_phase2(P, C3, din, wscr, wT, consts, exch, dout, NTOK=STAGES.get('ntok', 2048))
            else:
                final = [o for o in P.ops["pool"] if o.dma][-8:]
        P.emit(final)
    return nc


BF = ml_dtypes.bfloat16


def host_consts():
    return {
        "ident_b": np.eye(128, dtype=np.float32).astype(BF),
        "ones_b": np.ones((128, 128), dtype=np.float32).astype(BF),
    }


def host_consts1(NTOKW):
    i = np.arange(128)
    U = (i[:, None] <= i[None, :]).astype(np.float32)
    T1 = (i[:, None] > i[None, :]).astype(np.float32)
    q = np.arange(256)
    negm = np.stack([np.where((kt * 128 + i[:, None]) <= q[None, :], 0.0, NEG) for kt in range(2)], 1).astype(np.float32)
    kind = (np.arange(NTOKW)[None, :] // 256 == np.arange(32)[:, None]).astype(np.float32)
    return {"ident_b": np.eye(128, dtype=np.float32).astype(BF), "ones_b": np.ones((128, 128), np.float32).astype(BF),
            "U": U.astype(BF), "T1": T1.astype(BF), "negm": negm.astype(BF), "kind": kind.astype(BF)}


def gain_layout(g):
    return np.ascontiguousarray(g.reshape(8, 128).T)


def bc(v):
    return np.ascontiguousarray(np.broadcast_to(v[None, :], (128, v.shape[0]))).astype(np.float32)


def p1_inputs(inputs, b, groups, xw, tokmask, blkbias, NTOKW):
    w_in = inputs["w_in"][0]
    cw, cbias = inputs["conv_w"][0], inputs["conv_b"][0]
    w1a, w1b, convw, convb, dtb, alog, dsk, sng = [], [], [], [], [], [], [], []
    for g in groups:
        cols_a = np.concatenate([np.arange(5120 + 512 * g, 5120 + 512 * (g + 1)), np.arange(7168 + 128 * g, 7168 + 128 * (g + 1)),
                                 np.arange(7680 + 128 * g, 7680 + 128 * (g + 1)), np.arange(3072 + 512 * g, 3072 + 512 * (g + 1)),
                                 np.arange(8192 + 8 * g, 8192 + 8 * (g + 1))])
        cols_b = np.concatenate([np.arange(256 * g, 256 * (g + 1)), np.arange(1024 + 256 * g, 1024 + 256 * (g + 1)),
                                 np.arange(2048 + 256 * g, 2048 + 256 * (g + 1))])
        w1a.append(w_in[:, cols_a])
        w1b.append(w_in[:, cols_b])
        ch = np.concatenate([np.arange(512 * g, 512 * (g + 1)), np.arange(2048 + 128 * g, 2048 + 128 * (g + 1)),
                             np.arange(2560 + 128 * g, 2560 + 128 * (g + 1))])
        convw.append(cw[:, ch].T.reshape(6, 128, 4).transpose(1, 0, 2))
        convb.append(cbias[ch].reshape(6, 128).T)
        dtb.append(bc(inputs["dt_bias"][0][8 * g:8 * g + 8]))
        alog.append(bc(inputs["a_log"][0][8 * g:8 * g + 8]))
        dsk.append(bc(inputs["d_skip"][0][8 * g:8 * g + 8]))
        sng.append(bc(inputs["ssm_norm_g"][0][512 * g:512 * g + 512]))
    m = {"xw": np.ascontiguousarray(xw), "w1a": np.ascontiguousarray(np.stack(w1a)), "w1b": np.ascontiguousarray(np.stack(w1b)),
         "convw": np.ascontiguousarray(np.stack(convw)), "convb": np.ascontiguousarray(np.stack(convb)),
         "dtb": np.stack(dtb), "alog": np.stack(alog), "dsk": np.stack(dsk), "sng": np.stack(sng),
         "gq": bc(inputs["q_norm_g"][0]), "gk": bc(inputs["k_norm_g"][0]), "g1": gain_layout(inputs["ln1_g"][0]),
         "tokmask": np.ascontiguousarray(tokmask.reshape(-1, 128).T).astype(np.float32), "blkbias": bc(blkbias)}
    m.update(host_consts1(NTOKW))
    return m


def p2_inputs(inputs, core, e_att=None, e_yb=None):
    b, t = core // 4, core % 4
    sl = slice(t * 2048, (t + 1) * 2048)
    m = {
        "x2": np.ascontiguousarray(inputs["x"][b, sl]),
        "p2": np.ascontiguousarray(inputs["p"][0, b, sl]),
        "wg": np.ascontiguousarray(inputs["w_in"][0][:, 8224:10272]),
        "woa": inputs["w_o_attn"][0], "wos": inputs["w_o_ssm"][0], "wout": inputs["w_out"][0],
        "wgu": inputs["w_gate_up"][0], "wd": inputs["w_down"][0], "wpg": inputs["w_ple_gate"][0],
        "wpp": inputs["w_ple_proj"][0],
        "g1": gain_layout(inputs["ln1_g"][0]), "g2": gain_layout(inputs["ln2_g"][0]),
        "g3": gain_layout(inputs["ln3_g"][0]),
    }
    m.update(host_consts())
    if e_att is not None:
        m["e_att"] = e_att
        m["e_yb"] = e_yb
    return m


MODE = "fused"


def kernel(**inputs):
    inputs = {k: np.asarray(v) for k, v in inputs.items()}
    x = inputs["x"]
    out = np.zeros(x.shape, np.float32)
    if MODE == "two":
        nc1 = build_program("p1", NTOKW=8192, OWN0=0, NG=1)
        maps1 = []
        for core in range(8):
            b, g = core // 4, core % 4
            maps1.append(p1_inputs(inputs, b, [g], x[b], np.ones(8192, np.float32), np.zeros(32, np.float32), 8192))
        r1 = run_bass_kernel_spmd(nc1, maps1, core_ids=list(range(8))).results
        nc2 = build_program("p2")
        maps2 = []
        for core in range(8):
            b, t = core // 4, core % 4
            sl = slice(t * 2048, (t + 1) * 2048)
            ea = np.stack([np.asarray(r1[b * 4 + g]["e_att"])[0][:, :, sl] for g in range(4)])
            ey = np.stack([np.asarray(r1[b * 4 + g]["e_yb"])[0][:, :, sl] for g in range(4)])
            maps2.append(p2_inputs(inputs, core, np.ascontiguousarray(ea), np.ascontiguousarray(ey)))
        r2 = run_bass_kernel_spmd(nc2, maps2, core_ids=list(range(8))).results
        for core in range(8):
            b, t = core // 4, core % 4
            out[b, t * 2048:(t + 1) * 2048] = np.asarray(r2[core]["out"])
        return out
    nc = build_program("fused", NTOKW=8192, OWN0=6144, NG=4)
    maps = []
    for core in range(8):
        b, t = core // 4, core % 4
        npad = (3 - t) * 2048
        xw = np.concatenate([np.zeros((npad, 1024), np.float32), x[b, :(t + 1) * 2048]], 0)
        tokmask = np.concatenate([np.zeros(npad, np.float32), np.ones(8192 - npad, np.float32)])
        blkbias = np.where(np.arange(32) < npad // 256, NEG, 0.0).astype(np.float32)
        m = p1_inputs(inputs, b, [0, 1, 2, 3], xw, tokmask, blkbias, 8192)
        m.update(p2_inputs(inputs, core))
        maps.append(m)
    r = run_bass_kernel_spmd(nc, maps, core_ids=list(range(8))).results
    for core in range(8):
        b, t = core // 4, core % 4
        out[b, t * 2048:(t + 1) * 2048] = np.asarray(r[core]["out"])
    return out
```

```python
import numpy as np
from contextlib import ExitStack
import ml_dtypes
import concourse.bass as bass
import concourse.mybir as mybir
from concourse.bass_utils import run_bass_kernel_spmd

F32 = mybir.dt.float32
BF16 = mybir.dt.bfloat16
AF = mybir.ActivationFunctionType
ALU = mybir.AluOpType
AX = mybir.AxisListType

EPS = 1e-6
NEG = -30000.0
SAME_ENGINE_SYNC = True
STAGES = {}


class T:
    __slots__ = ("ap", "w", "r", "name", "excl")

    def __init__(self, ap=None, name="", excl=False):
        self.ap = ap
        self.w = None
        self.r = []
        self.name = name
        self.excl = excl

    def __getitem__(self, k):
        return self.ap[k]


class TV(T):
    __slots__ = ("parent",)

    def __init__(self, parent, ap):
        self.parent = parent
        self.ap = ap
        self.name = parent.name
        self.excl = parent.excl

    @property
    def w(self):
        return self.parent.w

    @w.setter
    def w(self, v):
        self.parent.w = v

    @property
    def r(self):
        return self.parent.r

    @r.setter
    def r(self, v):
        self.parent.r = v


class Op:
    __slots__ = ("eng", "fn", "deps", "dma", "inc", "sem", "ticket", "idx", "prev_same_sem", "vc")

    def __init__(self, eng, fn, dma):
        self.eng = eng
        self.fn = fn
        self.dma = dma
        self.deps = []
        self.inc = False
        self.sem = None
        self.ticket = 0
        self.prev_same_sem = None


class Prog:
    ENGS = ("pe", "act", "dve", "pool", "sp")
    NDS = 8

    def __init__(self, nc):
        self.nc = nc
        self.ops = {e: [] for e in self.ENGS}
        self.all = []
        self.bar = {}

    def op(self, eng, fn, reads=(), writes=(), dma=False):
        o = Op(eng, fn, dma)
        deps = []
        for t in reads:
            if t.w is not None:
                deps.append(t.w)
            if t.excl:
                deps.extend(x for x in t.r if x.eng != eng)
        for t in writes:
            if t.w is not None:
                deps.append(t.w)
            deps.extend(t.r)
        b = self.bar.pop(eng, None)
        if b:
            deps.extend(b)
        seen = set()
        for d in deps:
            if d is o or id(d) in seen:
                continue
            seen.add(id(d))
            if (not d.dma) and d.eng == eng and (eng == "pe" or not SAME_ENGINE_SYNC):
                continue
            o.deps.append(d)
            d.inc = True
        for t in reads:
            if dma:
                t.r.append(o)
            else:
                t.r = [x for x in t.r if x.dma or x.eng != eng] + [o]
        for t in writes:
            t.w = o
            t.r = []
        if dma:
            o.inc = True
        self.ops[eng].append(o)
        self.all.append(o)
        return o

    def barrier(self):
        last = []
        for e in self.ENGS:
            ops = self.ops[e]
            if ops:
                last.append(ops[-1])
            last.extend([o for o in ops if o.dma][-self.NDS:])
        for e in self.ENGS:
            self.bar[e] = list(last)

    def emit(self, final_ops):
        nc = self.nc
        with ExitStack() as es:
            SEM_CAP = 1000
            nsem = {e: sum(1 for o in self.ops[e] if o.inc and not o.dma) // SEM_CAP + 1 for e in ("pe", "act", "dve", "pool")}
            csem = {e: [es.enter_context(nc.semaphore("cs_%s%d" % (e, i))) for i in range(nsem[e])]
                    for e in ("pe", "act", "dve", "pool")}
            dsem = {e: [es.enter_context(nc.semaphore("ds_%s%d" % (e, i))) for i in range(self.NDS)]
                    for e in self.ENGS}
            ccount = {e: 0 for e in self.ENGS}
            dcount = {e: [0] * self.NDS for e in self.ENGS}
            drr = {e: 0 for e in self.ENGS}
            dlast = {e: [None] * self.NDS for e in self.ENGS}
            for e in self.ENGS:
                for o in self.ops[e]:
                    if o.dma:
                        k = drr[e] % self.NDS
                        drr[e] += 1
                        dcount[e][k] += 16
                        o.sem = dsem[e][k]
                        o.ticket = dcount[e][k]
                        o.prev_same_sem = dlast[e][k]
                        dlast[e][k] = o
                    elif o.inc:
                        o.sem = csem[e][ccount[e] // SEM_CAP]
                        o.ticket = ccount[e] % SEM_CAP + 1
                        ccount[e] += 1

            know = {e: {} for e in self.ENGS}
            plan = {}
            for o in self.all:
                kn = know[o.eng]
                waits = []
                dl = list(o.deps)
                if o.dma and o.prev_same_sem is not None:
                    dl.append(o.prev_same_sem)
                for d in dl:
                    key = id(d.sem)
                    if kn.get(key, 0) < d.ticket:
                        waits.append(d)
                        for kk, vv in d.vc.items():
                            if kn.get(kk, 0) < vv:
                                kn[kk] = vv
                plan[id(o)] = waits
                o.vc = dict(kn)
                if o.sem is not None:
                    o.vc[id(o.sem)] = o.ticket
                    if not o.dma:
                        kn[id(o.sem)] = max(kn.get(id(o.sem), 0), 0)

            def run(ename, eng):
                for o in self.ops[ename]:
                    for d in plan[id(o)]:
                        eng.wait_ge(d.sem, d.ticket)
                    ins = o.fn(eng)
                    if o.sem is not None:
                        ins.then_inc(o.sem, 16 if o.dma else 1)
                if ename == "sp":
                    kn = know["sp"]
                    for d in final_ops:
                        if kn.get(id(d.sem), 0) < d.ticket:
                            eng.wait_ge(d.sem, d.ticket)
                            kn[id(d.sem)] = d.ticket

            with nc.Block() as block:
                @block.tensor
                def _(eng):
                    run("pe", eng)

                @block.scalar
                def _(eng):
                    run("act", eng)

                @block.vector
                def _(eng):
                    run("dve", eng)

                @block.gpsimd
                def _(eng):
                    run("pool", eng)

                @block.sync
                def _(eng):
                    run("sp", eng)


class Ctx:
    N = 0

    def __init__(self, nc, es, P):
        self.nc, self.es, self.P = nc, es, P
        self.n = 0

    def sb(self, shape, dt, name=None):
        Ctx.N += 1
        h = self.es.enter_context(self.nc.sbuf_tensor("%s_%d" % (name or "sb", Ctx.N), list(shape), dt))
        return h

    def sbT(self, shape, dt, name=None):
        h = self.sb(shape, dt, name)
        return T(h[tuple(slice(None) for _ in shape)], name or "")

    def dram(self, name, shape, dt, kind):
        return self.nc.dram_tensor(name, list(shape), dt, kind=kind).ap()


class Ring:
    def __init__(self, tiles):
        self.tiles = tiles
        self.i = 0

    def next(self):
        t = self.tiles[self.i % len(self.tiles)]
        self.i += 1
        return t


P2W = {
    "wg": (1024, 2048), "woa": (1024, 1024), "wos": (2048, 1024), "wout": (1024, 1024),
    "wgu": (1024, 5632), "wd": (2816, 1024), "wpg": (1024, 1024), "wpp": (256, 1024),
}
P2W_GAIN = {"wg": "g1", "wgu": "g2", "wpg": "g3"}
WB_COLS = 256


def wblocks(name):
    K, N = P2W[name]
    nkc = K // 128
    kbs = []
    k0 = 0
    while k0 < nkc:
        nk = min(8, nkc - k0)
        kbs.append((k0, nk))
        k0 += nk
    return kbs, N // WB_COLS


def build_precast(P, C, din, wscr, wT, consts):
    nc = P.nc
    stg = Ring([C.sbT([128, 8, WB_COLS], F32, "pc_stg") for _ in range(2)])
    wbf = Ring([C.sbT([128, 8, WB_COLS], BF16, "pc_bf") for _ in range(2)])
    for name in P2W:
        kbs, ncb = wblocks(name)
        src = din[name]
        dst = wscr[name]
        gain = consts.get(P2W_GAIN.get(name))
        for cb in range(ncb):
            for (k0, nk) in kbs:
                s, w = stg.next(), wbf.next()
                sv = src[k0 * 128:(k0 + nk) * 128, cb * WB_COLS:(cb + 1) * WB_COLS].rearrange("(k p) n -> p k n", p=128)
                dv = dst[k0 * 128:(k0 + nk) * 128, cb * WB_COLS:(cb + 1) * WB_COLS].rearrange("(k p) n -> p k n", p=128)
                P.op("sp", lambda e, s=s, sv=sv, nk=nk: e.dma_start(out=s[:, 0:nk, :], in_=sv), writes=[s], dma=True)
                if gain is not None:
                    gv = gain[:, k0:k0 + nk].unsqueeze(2).to_broadcast([128, nk, WB_COLS])
                    P.op("pool", lambda e, s=s, w=w, gv=gv, nk=nk: e.tensor_tensor(
                        out=w[:, 0:nk, :], in0=s[:, 0:nk, :], in1=gv, op=ALU.mult), reads=[s, gain], writes=[w])
                else:
                    P.op("pool", lambda e, s=s, w=w, nk=nk: e.tensor_copy(out=w[:, 0:nk, :], in_=s[:, 0:nk, :]),
                         reads=[s], writes=[w])
                t = T(None, "wscr")
                wT[(name, cb, k0)] = t
                P.op("pool", lambda e, w=w, dv=dv, nk=nk: e.dma_start(out=dv, in_=w[:, 0:nk, :]),
                     reads=[w], writes=[t], dma=True)


def build_phase2(P, C, din, wscr, wT, consts, exch, dout, NTOK=2048, TT=1024):
    nc = P.nc
    NTG = TT // 512
    NSUB = TT // 128
    ident_b = consts["ident_b"]
    ones_b = consts["ones_b"]
    xT = [C.sbT([128, TT], F32, "xT") for _ in range(8)]
    hT = [C.sbT([128, TT], BF16, "hT") for _ in range(8)]
    big = C.sb([128, 24, TT], BF16, "big")
    attT = [T(big[:, c, :], "att") for c in range(8)]
    ybT = [T(big[:, 8 + c, :], "yb") for c in range(16)]
    mg = [C.sbT([128, TT], BF16, "mg") for _ in range(8)]
    pT = [C.sbT([128, TT], BF16, "pT") for _ in range(2)]
    xin = Ring([C.sbT([128, 1024], F32, "xin") for _ in range(2)])
    xhi = Ring([C.sbT([128, 1024], BF16, "xhi") for _ in range(2)])
    xlo = Ring([C.sbT([128, 1024], BF16, "xlo") for _ in range(2)])
    xres = Ring([C.sbT([128, 1024], F32, "xres") for _ in range(1)])
    pin = Ring([C.sbT([128, 256], F32, "pin") for _ in range(2)])
    pinb = Ring([C.sbT([128, 256], BF16, "pinb") for _ in range(2)])
    wring = Ring([C.sbT([128, 8, WB_COLS], BF16, "wr") for _ in range(8)])
    tmpf = Ring([C.sbT([128, 512], F32, "tmpf") for _ in range(8)])
    sqb = Ring([C.sbT([128, 512], BF16, "sqb") for _ in range(3)])
    rstd = [C.sbT([128, 512], F32, "rstd") for _ in range(NTG)]
    psum = Ring(consts["psum"])

    def load_w(name, cb, k0, nk):
        w = wring.next()
        sv = wscr[name][k0 * 128:(k0 + nk) * 128, cb * WB_COLS:(cb + 1) * WB_COLS].rearrange("(k p) n -> p k n", p=128)
        P.op("sp", lambda e, w=w, sv=sv, nk=nk: e.dma_start(out=w[:, 0:nk, :], in_=sv),
             reads=[wT[(name, cb, k0)]], writes=[w], dma=True)
        return w

    def mm_group(ps, wlist, X, tg, oc_in_blk):
        n = sum(nk for _, _, nk in wlist)
        i = 0
        for (w, k0, nk) in wlist:
            for k in range(nk):
                st, sp_ = (i == 0), (i == n - 1)
                xk = X[k0 + k]
                P.op("pe", lambda e, ps=ps, w=w, k=k, xk=xk, st=st, sp_=sp_: e.matmul(
                    ps[:, :], lhsT=w[:, k, oc_in_blk * 128:(oc_in_blk + 1) * 128],
                    rhs=xk[:, tg * 512:(tg + 1) * 512], start=st, stop=sp_),
                    reads=[w, xk], writes=[ps])
                i += 1

    def rmsnorm():
        for tg in range(NTG):
            ps = psum.next()
            for kc in range(8):
                sq = sqb.next()
                P.op("act", lambda e, sq=sq, kc=kc, tg=tg: e.activation(
                    out=sq[:, :], in_=xT[kc][:, tg * 512:(tg + 1) * 512], func=AF.Square), reads=[xT[kc]], writes=[sq])
                P.op("pe", lambda e, ps=ps, sq=sq, kc=kc: e.matmul(
                    ps[:, :], lhsT=ones_b[:, :], rhs=sq[:, :], start=(kc == 0), stop=(kc == 7)),
                    reads=[sq, ones_b], writes=[ps])
            r = rstd[tg]
            P.op("act", lambda e, r=r, ps=ps: e.activation(
                out=r[:, :], in_=ps[:, :], func=AF.Sqrt, scale=1.0 / 1024.0, bias=EPS), reads=[ps], writes=[r])
            P.op("dve", lambda e, r=r: e.reciprocal(out=r[:, :], in_=r[:, :]), reads=[r], writes=[r])
        for kc in range(8):
            for tg in range(NTG):
                P.op("dve" if kc % 2 == 0 else "pool", lambda e, kc=kc, tg=tg: e.tensor_tensor(
                    out=hT[kc][:, tg * 512:(tg + 1) * 512], in0=xT[kc][:, tg * 512:(tg + 1) * 512],
                    in1=rstd[tg][:, :], op=ALU.mult), reads=[xT[kc], rstd[tg]], writes=[hT[kc]])

    out_ops = []
    for tt in range(NTOK // TT):
        t0 = tt * TT
        for s in range(NSUB):
            xi = xin.next()
            P.op("sp", lambda e, xi=xi, s=s, t0=t0: e.dma_start(out=xi[:, :], in_=din["x2"][t0 + s * 128:t0 + (s + 1) * 128, :]),
                 writes=[xi], dma=True)
            xh, xl, xr = xhi.next(), xlo.next(), xres.next()
            P.op("act", lambda e, xi=xi, xh=xh: e.copy(out=xh[:, :], in_=xi[:, :]), reads=[xi], writes=[xh])
            P.op("dve", lambda e, xi=xi, xh=xh, xr=xr: e.tensor_tensor(out=xr[:, :], in0=xi[:, :], in1=xh[:, :], op=ALU.subtract),
                 reads=[xi, xh], writes=[xr])
            P.op("pool", lambda e, xr=xr, xl=xl: e.tensor_copy(out=xl[:, :], in_=xr[:, :]), reads=[xr], writes=[xl])
            for half in range(2):
                ps = psum.next()
                for j in range(4):
                    kc = half * 4 + j
                    P.op("pe", lambda e, ps=ps, xh=xh, kc=kc, j=j: e.matmul(
                        ps[:, j * 128:(j + 1) * 128], lhsT=xh[:, kc * 128:(kc + 1) * 128], rhs=ident_b[:, :], start=True, stop=False),
                        reads=[xh, ident_b], writes=[ps])
                    P.op("pe", lambda e, ps=ps, xl=xl, kc=kc, j=j: e.matmul(
                        ps[:, j * 128:(j + 1) * 128], lhsT=xl[:, kc * 128:(kc + 1) * 128], rhs=ident_b[:, :], start=False, stop=True),
                        reads=[xl, ident_b], writes=[ps])
                for j in range(4):
                    kc = half * 4 + j
                    eng = "act" if j % 2 == 0 else "dve"
                    if eng == "act":
                        P.op("act", lambda e, ps=ps, kc=kc, j=j, s=s: e.copy(
                            out=xT[kc][:, s * 128:(s + 1) * 128], in_=ps[:, j * 128:(j + 1) * 128]),
                            reads=[ps], writes=[xT[kc]])
                    else:
                        P.op("dve", lambda e, ps=ps, kc=kc, j=j, s=s: e.tensor_copy(
                            out=xT[kc][:, s * 128:(s + 1) * 128], in_=ps[:, j * 128:(j + 1) * 128]),
                            reads=[ps], writes=[xT[kc]])
            pi, pb = pin.next(), pinb.next()
            P.op("sp", lambda e, pi=pi, s=s, t0=t0: e.dma_start(out=pi[:, :], in_=din["p2"][t0 + s * 128:t0 + (s + 1) * 128, :]),
                 writes=[pi], dma=True)
            P.op("pool", lambda e, pi=pi, pb=pb: e.tensor_copy(out=pb[:, :], in_=pi[:, :]), reads=[pi], writes=[pb])
            ps = psum.next()
            psb = ps.ap.bitcast(BF16)
            for j in range(2):
                P.op("pe", lambda e, psb=psb, pb=pb, j=j: e.transpose(
                    psb[:, j * 128:(j + 1) * 128], pb[:, j * 128:(j + 1) * 128], ident_b[:, :]),
                    reads=[pb, ident_b], writes=[ps])
            for j in range(2):
                P.op("act", lambda e, psb=psb, j=j, s=s: e.copy(
                    out=pT[j][:, s * 128:(s + 1) * 128], in_=psb[:, j * 128:(j + 1) * 128]), reads=[ps], writes=[pT[j]])
        for g in range(4):
            for c in range(2):
                tl = attT[2 * g + c]
                P.op("sp", lambda e, tl=tl, g=g, c=c, t0=t0: e.dma_start(out=tl[:, :], in_=exch["att"][g, c, :, t0:t0 + TT]),
                     reads=[exch["att_T"]], writes=[tl], dma=True)
            for c in range(4):
                tl = ybT[4 * g + c]
                P.op("sp", lambda e, tl=tl, g=g, c=c, t0=t0: e.dma_start(out=tl[:, :], in_=exch["yb"][g, c, :, t0:t0 + TT]),
                     reads=[exch["yb_T"]], writes=[tl], dma=True)
        p2s = STAGES.get('p2s', 'BXCD')
        if 'B' in p2s:
            rmsnorm()
        for j in (range(4) if 'B' in p2s else []):
            wga = load_w("wg", j, 0, 8)
            wgb = load_w("wg", 4 + j, 0, 8)
            wa = load_w("woa", j, 0, 8)
            ws0 = load_w("wos", j, 0, 8)
            ws1 = load_w("wos", j, 8, 8)
            for o2 in range(2):
                oc = 2 * j + o2
                for tg in range(NTG):
                    pga, pgb, pya, pyb = psum.next(), psum.next(), psum.next(), psum.next()
                    mm_group(pga, [(wga, 0, 8)], hT, tg, o2)
                    mm_group(pgb, [(wgb, 0, 8)], hT, tg, o2)
                    mm_group(pya, [(wa, 0, 8)], attT, tg, o2)
                    mm_group(pyb, [(ws0, 0, 8), (ws1, 8, 8)], ybT, tg, o2)
                    sa, sb_, m1, m2 = tmpf.next(), tmpf.next(), tmpf.next(), tmpf.next()
                    P.op("act", lambda e, sa=sa, pga=pga: e.activation(out=sa[:, :], in_=pga[:, :], func=AF.Sigmoid),
                         reads=[pga], writes=[sa])
                    P.op("act", lambda e, sb_=sb_, pgb=pgb: e.activation(out=sb_[:, :], in_=pgb[:, :], func=AF.Sigmoid),
                         reads=[pgb], writes=[sb_])
                    P.op("dve", lambda e, m1=m1, sa=sa, pya=pya: e.tensor_tensor(
                        out=m1[:, :], in0=sa[:, :], in1=pya[:, :], op=ALU.mult), reads=[sa, pya], writes=[m1])
                    P.op("dve", lambda e, m2=m2, sb_=sb_, pyb=pyb: e.tensor_tensor(
                        out=m2[:, :], in0=sb_[:, :], in1=pyb[:, :], op=ALU.mult), reads=[sb_, pyb], writes=[m2])
                    P.op("pool", lambda e, m1=m1, m2=m2, oc=oc, tg=tg: e.tensor_tensor(
                        out=mg[oc][:, tg * 512:(tg + 1) * 512], in0=m1[:, :], in1=m2[:, :], op=ALU.add),
                        reads=[m1, m2], writes=[mg[oc]])
        for j in (range(4) if 'X' in p2s else []):
            w = load_w("wout", j, 0, 8)
            for o2 in range(2):
                oc = 2 * j + o2
                for tg in range(NTG):
                    ps = psum.next()
                    mm_group(ps, [(w, 0, 8)], mg, tg, o2)
                    P.op("dve", lambda e, ps=ps, oc=oc, tg=tg: e.tensor_tensor(
                        out=xT[oc][:, tg * 512:(tg + 1) * 512], in0=ps[:, :], in1=xT[oc][:, tg * 512:(tg + 1) * 512],
                        op=ALU.add), reads=[ps, xT[oc]], writes=[xT[oc]])
        if 'C' in p2s:
            rmsnorm()
        actT = [T(big[:, f, :], "act") for f in range(22)]
        for f in range(22):
            old = attT[f] if f < 8 else ybT[f - 8]
            actT[f].w, actT[f].r = old.w, old.r
        for j in (range(11) if 'C' in p2s else []):
            wgt = load_w("wgu", j, 0, 8)
            wup = load_w("wgu", 11 + j, 0, 8)
            for o2 in range(2):
                f = 2 * j + o2
                for tg in range(NTG):
                    pg, pu = psum.next(), psum.next()
                    mm_group(pg, [(wgt, 0, 8)], hT, tg, o2)
                    mm_group(pu, [(wup, 0, 8)], hT, tg, o2)
                    sg = tmpf.next()
                    P.op("act", lambda e, sg=sg, pg=pg: e.activation(out=sg[:, :], in_=pg[:, :], func=AF.Silu),
                         reads=[pg], writes=[sg])
                    P.op("dve", lambda e, sg=sg, pu=pu, f=f, tg=tg: e.tensor_tensor(
                        out=actT[f][:, tg * 512:(tg + 1) * 512], in0=sg[:, :], in1=pu[:, :], op=ALU.mult),
                        reads=[sg, pu], writes=[actT[f]])
        for j in (range(4) if 'C' in p2s else []):
            w0 = load_w("wd", j, 0, 8)
            w1 = load_w("wd", j, 8, 8)
            w2 = load_w("wd", j, 16, 6)
            for o2 in range(2):
                oc = 2 * j + o2
                for tg in range(NTG):
                    ps = psum.next()
                    mm_group(ps, [(w0, 0, 8), (w1, 8, 8), (w2, 16, 6)], actT, tg, o2)
                    P.op("dve", lambda e, ps=ps, oc=oc, tg=tg: e.tensor_tensor(
                        out=xT[oc][:, tg * 512:(tg + 1) * 512], in0=ps[:, :], in1=xT[oc][:, tg * 512:(tg + 1) * 512],
                        op=ALU.add), reads=[ps, xT[oc]], writes=[xT[oc]])
        for f in range(22):
            old = attT[f] if f < 8 else ybT[f - 8]
            old.w, old.r = actT[f].w, actT[f].r
        if 'D' in p2s:
            rmsnorm()
        for j in (range(4) if 'D' in p2s else []):
            wpg = load_w("wpg", j, 0, 8)
            wpp = load_w("wpp", j, 0, 2)
            for o2 in range(2):
                oc = 2 * j + o2
                for tg in range(NTG):
                    pg, pp = psum.next(), psum.next()
                    mm_group(pg, [(wpg, 0, 8)], hT, tg, o2)
                    mm_group(pp, [(wpp, 0, 2)], pT, tg, o2)
                    sg, m = tmpf.next(), tmpf.next()
                    P.op("act", lambda e, sg=sg, pg=pg: e.activation(out=sg[:, :], in_=pg[:, :], func=AF.Sigmoid),
                         reads=[pg], writes=[sg])
                    P.op("dve", lambda e, m=m, sg=sg, pp=pp: e.tensor_tensor(
                        out=m[:, :], in0=sg[:, :], in1=pp[:, :], op=ALU.mult), reads=[sg, pp], writes=[m])
                    P.op("dve", lambda e, m=m, oc=oc, tg=tg: e.tensor_tensor(
                        out=xT[oc][:, tg * 512:(tg + 1) * 512], in0=m[:, :], in1=xT[oc][:, tg * 512:(tg + 1) * 512],
                        op=ALU.add), reads=[m, xT[oc]], writes=[xT[oc]])
        for kc in range(8):
            P.op("act", lambda e, kc=kc: e.copy(out=hT[kc][:, :], in_=xT[kc][:, :]), reads=[xT[kc], hT[kc]], writes=[hT[kc]])
            P.op("dve", lambda e, kc=kc: e.tensor_tensor(out=xT[kc][:, :], in0=xT[kc][:, :], in1=hT[kc][:, :], op=ALU.subtract),
                 reads=[xT[kc], hT[kc]], writes=[xT[kc]])
            P.op("pool", lambda e, kc=kc: e.tensor_copy(out=mg[kc][:, :], in_=xT[kc][:, :]), reads=[xT[kc], mg[kc]], writes=[mg[kc]])
        for s in range(NSUB):
            xo = xin.next()
            for half in range(2):
                ps = psum.next()
                for j in range(4):
                    kc = half * 4 + j
                    P.op("pe", lambda e, ps=ps, kc=kc, j=j, s=s: e.matmul(
                        ps[:, j * 128:(j + 1) * 128], lhsT=hT[kc][:, s * 128:(s + 1) * 128], rhs=ident_b[:, :], start=True, stop=False),
                        reads=[hT[kc], ident_b], writes=[ps])
                    P.op("pe", lambda e, ps=ps, kc=kc, j=j, s=s: e.matmul(
                        ps[:, j * 128:(j + 1) * 128], lhsT=mg[kc][:, s * 128:(s + 1) * 128], rhs=ident_b[:, :], start=False, stop=True),
                        reads=[mg[kc], ident_b], writes=[ps])
                if half == 0:
                    P.op("act", lambda e, ps=ps, xo=xo: e.copy(out=xo[:, 0:512], in_=ps[:, :]), reads=[ps], writes=[xo])
                else:
                    P.op("dve", lambda e, ps=ps, xo=xo: e.tensor_copy(out=xo[:, 512:1024], in_=ps[:, :]),
                         reads=[ps, xo], writes=[xo])
            out_ops.append(P.op("pool", lambda e, xo=xo, s=s, t0=t0: e.dma_start(
                out=dout[t0 + s * 128:t0 + (s + 1) * 128, :], in_=xo[:, :]), reads=[xo], dma=True))
    return out_ops


W1A = 1288
W1B = 768


def load_cast_weight(P, C, src, w, ncols, gain, stg):
    c0 = 0
    while c0 < ncols:
        n = min(WB_COLS, ncols - c0)
        s = stg.next()
        sv = src[:, c0:c0 + n].rearrange("(k p) n -> p k n", p=128)
        P.op("sp", lambda e, s=s, sv=sv, n=n: e.dma_start(out=s[:, :, 0:n], in_=sv), writes=[s], dma=True)
        gv = gain[:, 0:8].unsqueeze(2).to_broadcast([128, 8, n])
        P.op("pool", lambda e, s=s, gv=gv, n=n, c0=c0: e.tensor_tensor(
            out=w[:, :, c0:c0 + n], in0=s[:, :, 0:n], in1=gv, op=ALU.mult), reads=[s, gain], writes=[w])
        c0 += n


def build_prologue(P, C, din, cst, hT_scr, hT_T, NTOKW):
    xin = Ring([C.sbT([128, 1024], F32, "pxin") for _ in range(3)])
    junk = C.sbT([128, 1024], BF16, "pjunk")
    hb = Ring([C.sbT([128, 1024], BF16, "phb") for _ in range(2)])
    ssr = Ring([C.sbT([128, 2], F32, "pss") for _ in range(4)])
    hst = Ring([C.sbT([128, 8, 512], BF16, "phst") for _ in range(2)])
    psum = cst["psring"]
    ident_b = cst["ident_b"]
    for m in range(NTOKW // 512):
        ht = hst.next()
        for s in range(4):
            t0 = m * 512 + s * 128
            xi, ss, h = xin.next(), ssr.next(), hb.next()
            P.op("sp", lambda e, xi=xi, t0=t0: e.dma_start(out=xi[:, :], in_=din["xw"][t0:t0 + 128, :]), writes=[xi], dma=True)
            P.op("act", lambda e, xi=xi, ss=ss: e.activation(out=junk[:, :], in_=xi[:, :], func=AF.Square, accum_out=ss[:, 0:1]),
                 reads=[xi], writes=[junk, ss])
            P.op("act", lambda e, ss=ss: e.activation(out=ss[:, 1:2], in_=ss[:, 0:1], func=AF.Sqrt, scale=1.0 / 1024.0, bias=EPS),
                 reads=[ss], writes=[ss])
            P.op("dve", lambda e, ss=ss: e.reciprocal(out=ss[:, 1:2], in_=ss[:, 1:2]), reads=[ss], writes=[ss])
            P.op("dve", lambda e, xi=xi, ss=ss, h=h: e.tensor_scalar(
                out=h[:, :], in0=xi[:, :], scalar1=ss[:, 1:2], scalar2=None, op0=ALU.mult), reads=[xi, ss], writes=[h])
            ps = psum.next()
            psb = ps.ap.bitcast(BF16)
            for kc in range(8):
                P.op("pe", lambda e, psb=psb, h=h, kc=kc: e.transpose(
                    psb[:, kc * 128:(kc + 1) * 128], h[:, kc * 128:(kc + 1) * 128], ident_b[:, :]),
                    reads=[h, ident_b], writes=[ps])
            P.op("act", lambda e, psb=psb, ht=ht, s=s: e.copy(
                out=ht[:, 0:4, s * 128:(s + 1) * 128], in_=psb[:, 0:512].rearrange("p (k n) -> p k n", k=4)),
                reads=[ps], writes=[ht])
            P.op("dve", lambda e, psb=psb, ht=ht, s=s: e.tensor_copy(
                out=ht[:, 4:8, s * 128:(s + 1) * 128], in_=psb[:, 512:1024].rearrange("p (k n) -> p k n", k=4)),
                reads=[ps, ht], writes=[ht])
        t = T(None, "hTscr")
        hT_T.append(t)
        P.op("pool", lambda e, ht=ht, m=m: e.dma_start(out=hT_scr[:, :, m * 512:(m + 1) * 512], in_=ht[:, :, :]),
             reads=[ht], writes=[t], dma=True)


def build_p1a(P, C, din, g, cst, hT_scr, hT_T, e_yb, e_yb_T, NTOKW, OWN0):
    psum = cst["psring"]
    ident_b, ones_b, U, T1 = cst["ident_b"], cst["ones_b"], cst["U"], cst["T1"]
    stg = Ring([C.sbT([128, 8, WB_COLS], F32, "a_stg") for _ in range(2)])
    w1a = C.sbT([128, 8, W1A], BF16, "w1a")
    load_cast_weight(P, C, din["w1a"][g], w1a, W1A, cst["g1"], stg)
    small = {}
    for nm, shp in (("convw", [128, 6, 4]), ("convb", [128, 6]), ("dtb", [128, 8]), ("alog", [128, 8]),
                    ("dsk", [128, 8]), ("sng", [128, 512])):
        t = C.sbT(shp, F32, "a_" + nm)
        P.op("sp", lambda e, t=t, nm=nm: e.dma_start(out=t.ap, in_=din[nm][g]), writes=[t], dma=True)
        small[nm] = t
    cw, cb, dtb, alog, dsk, sng = (small[k] for k in ("convw", "convb", "dtb", "alog", "dsk", "sng"))
    tokmask = cst["tokmask"]
    Abc = C.sbT([128, 8], F32, "Abc")
    P.op("act", lambda e: e.activation(out=Abc[:, :], in_=alog[:, :], func=AF.Exp), reads=[alog], writes=[Abc])
    P.op("dve", lambda e: e.tensor_scalar(out=Abc[:, :], in0=Abc[:, :], scalar1=-1.0, scalar2=None, op0=ALU.mult),
         reads=[Abc], writes=[Abc])
    dtb4 = C.sbT([128, 32], F32, "dtb4")
    Abc4 = C.sbT([128, 32], F32, "Abc4")
    for s_ in range(4):
        P.op("dve", lambda e, s_=s_: e.tensor_copy(out=dtb4[:, 8 * s_:8 * s_ + 8], in_=dtb[:, :]), reads=[dtb, dtb4], writes=[dtb4])
        P.op("dve", lambda e, s_=s_: e.tensor_copy(out=Abc4[:, 8 * s_:8 * s_ + 8], in_=Abc[:, :]), reads=[Abc, Abc4], writes=[Abc4])
    s32 = Ring([C.sbT([128, 32], F32, "a_s32") for _ in range(21)])
    a3mr = Ring([C.sbT([128, 3, 32], BF16, "a_a3m") for _ in range(3)])
    s48 = Ring([C.sbT([128, 4, 8], F32, "a_s48") for _ in range(12)])
    S = C.sbT([128, 512], F32, "S")
    Sbf = C.sbT([128, 512], BF16, "Sbf")
    xbc = C.sbT([128, 6, 515], F32, "xbc")
    P.op("pool", lambda e: e.memset(S[:, :], 0.0), writes=[S])
    P.op("pool", lambda e: e.memset(Sbf[:, :], 0.0), writes=[Sbf])
    P.op("pool", lambda e: e.memset(xbc[:, :, 0:3], 0.0), writes=[xbc])
    hring = Ring([C.sbT([128, 8, 512], BF16, "a_hT") for _ in range(2)])
    cring = Ring([C.sbT([128, 6, 512], BF16, "a_co") for _ in range(2)])
    accr = Ring([C.sbT([128, 512], F32, "a_acc") for _ in range(5)])
    f512 = Ring([C.sbT([128, 512], F32, "a_f512") for _ in range(6)])
    szr = Ring([C.sbT([128, 512], F32, "a_sz") for _ in range(5)])
    b512 = Ring([C.sbT([128, 512], BF16, "a_b512") for _ in range(28)])
    s8 = Ring([C.sbT([128, 8], F32, "a_s8") for _ in range(64)])
    s2 = Ring([C.sbT([128, 2], F32, "a_s2") for _ in range(6)])
    Rr = Ring([C.sbT([128, 3, 8, 128], BF16, "a_R") for _ in range(4)])
    a3r = Ring([C.sbT([128, 3, 8], BF16, "a_a3") for _ in range(6)])
    Lr = Ring([C.sbT([128, 8, 128], BF16, "a_L") for _ in range(5)])
    Mr = Ring([C.sbT([128, 8, 128], BF16, "a_M") for _ in range(5)])
    cbm = Ring([C.sbT([128, 128], BF16, "a_cbm") for _ in range(5)])
    ybst = Ring([C.sbT([128, 4, 512], BF16, "a_ybst") for _ in range(2)])
    junk = C.sbT([128, 512], BF16, "a_junk")
    pss = cst["ps_small"]
    i_eng = 0
    for m in range(NTOKW // 512):
        tok0 = m * 512
        own = tok0 >= OWN0
        hT = hring.next()
        P.op("sp", lambda e, hT=hT, tok0=tok0: e.dma_start(out=hT[:, :, :], in_=hT_scr[:, :, tok0:tok0 + 512]),
             reads=[hT_T[m]], writes=[hT], dma=True)
        for c in range(6):
            ps = psum.next()
            for kc in range(8):
                P.op("pe", lambda e, ps=ps, kc=kc, c=c, hT=hT: e.matmul(
                    ps[:, :], lhsT=w1a[:, kc, c * 128:(c + 1) * 128], rhs=hT[:, kc, :], start=(kc == 0), stop=(kc == 7)),
                    reads=[w1a, hT], writes=[ps])
            if c % 2 == 0:
                P.op("act", lambda e, ps=ps, c=c: e.copy(out=xbc[:, c, 3:515], in_=ps[:, :]), reads=[ps, xbc], writes=[xbc])
            else:
                P.op("dve", lambda e, ps=ps, c=c: e.tensor_copy(out=xbc[:, c, 3:515], in_=ps[:, :]), reads=[ps, xbc], writes=[xbc])
        co = cring.next()
        for c in range(6):
            eng = "pool" if c in (1, 4) else "dve"
            acc = accr.next()
            P.op(eng, lambda e, acc=acc, c=c: e.tensor_scalar(
                out=acc[:, :], in0=xbc[:, c, 0:512], scalar1=cw[:, c, 0:1], scalar2=None, op0=ALU.mult),
                reads=[xbc, cw], writes=[acc])
            for k in range(1, 4):
                if eng == "dve":
                    P.op(eng, lambda e, acc=acc, c=c, k=k: e.scalar_tensor_tensor(
                        out=acc[:, :], in0=xbc[:, c, k:k + 512], scalar=cw[:, c, k:k + 1], in1=acc[:, :],
                        op0=ALU.mult, op1=ALU.add), reads=[xbc, cw, acc], writes=[acc])
                else:
                    tmpc = accr.next()
                    P.op(eng, lambda e, tmpc=tmpc, c=c, k=k: e.tensor_scalar(
                        out=tmpc[:, :], in0=xbc[:, c, k:k + 512], scalar1=cw[:, c, k:k + 1], scalar2=None, op0=ALU.mult),
                        reads=[xbc, cw], writes=[tmpc])
                    P.op(eng, lambda e, tmpc=tmpc, acc=acc: e.tensor_tensor(out=acc[:, :], in0=acc[:, :], in1=tmpc[:, :], op=ALU.add),
                         reads=[acc, tmpc], writes=[acc])
            P.op("act", lambda e, acc=acc, c=c, co=co: e.activation(
                out=co[:, c, :], in_=acc[:, :], func=AF.Silu, bias=cb[:, c:c + 1], scale=1.0), reads=[acc, cb, co], writes=[co])
        P.op("pool", lambda e: e.tensor_copy(out=xbc[:, :, 0:3], in_=xbc[:, :, 512:515]), reads=[xbc], writes=[xbc])
        yst = ybst.next() if own else None
        ctx = {}

        def pre(s, m=m, hT=hT, co=co, own=own):
            sub = slice(s * 128, (s + 1) * 128)
            tile_idx = m * 4 + s
            dt_ = TV(dtm, dtm.ap[:, 8 * s:8 * s + 8])
            a3 = TV(a3m, a3m.ap[:, :, 8 * s:8 * s + 8])
            wst = TV(wstm, wstm.ap[:, s, :])
            cdec = TV(cdecm, cdecm.ap[:, s, :])
            wend = TV(wendm, wendm.ap[:, s, :])
            yield
            pxs = psum.next()
            pxb = pxs.ap.bitcast(BF16)
            for c in range(5):
                P.op("pe", lambda e, pxb=pxb, c=c, co=co, sub=sub: e.transpose(
                    pxb[:, c * 128:(c + 1) * 128], co[:, c, sub], ident_b[:, :]), reads=[co, ident_b], writes=[pxs])
            xs_tm, Btm, xdt, xdtw = b512.next(), b512.next(), b512.next(), b512.next()
            P.op("act", lambda e, pxb=pxb, xs_tm=xs_tm: e.copy(out=xs_tm[:, :], in_=pxb[:, 0:512]), reads=[pxs], writes=[xs_tm])
            P.op("act", lambda e, pxb=pxb, Btm=Btm: e.copy(out=Btm[:, 0:128], in_=pxb[:, 512:640]), reads=[pxs], writes=[Btm])
            P.op("pool", lambda e, xs_tm=xs_tm, xdt=xdt, dt_=dt_: e.tensor_tensor(
                out=xdt[:, :].rearrange("p (h d) -> p h d", h=8), in0=xs_tm[:, :].rearrange("p (h d) -> p h d", h=8),
                in1=dt_[:, :].unsqueeze(2).to_broadcast([128, 8, 64]), op=ALU.mult), reads=[xs_tm, dt_], writes=[xdt])
            P.op("pool", lambda e, xdt=xdt, xdtw=xdtw, wend=wend: e.tensor_tensor(
                out=xdtw[:, :].rearrange("p (h d) -> p h d", h=8), in0=xdt[:, :].rearrange("p (h d) -> p h d", h=8),
                in1=wend[:, :].unsqueeze(2).to_broadcast([128, 8, 64]), op=ALU.mult), reads=[xdt, wend], writes=[xdtw])
            yield
            if own:
                R, L, Mh, cbt = Rr.next(), Lr.next(), Mr.next(), cbm.next()
                for i3 in range(3):
                    P.op("dve" if i3 != 1 else "pool", lambda e, R=R, a3=a3, i3=i3: e.tensor_tensor(
                        out=R[:, i3, :, :], in0=U[:, :].unsqueeze(1).to_broadcast([128, 8, 128]),
                        in1=a3[:, i3, :].unsqueeze(2).to_broadcast([128, 8, 128]), op=ALU.mult), reads=[U, a3, R], writes=[R])
                for hh in range(2):
                    pD = psum.next()
                    for i3 in range(3):
                        P.op("pe", lambda e, pD=pD, R=R, hh=hh, i3=i3: e.matmul(
                            pD[:, :], lhsT=T1[:, :], rhs=R[:, i3, hh * 4:(hh + 1) * 4, :].rearrange("p h l -> p (h l)"),
                            start=(i3 == 0), stop=(i3 == 2)), reads=[T1, R], writes=[pD])
                    P.op("act", lambda e, pD=pD, L=L, hh=hh: e.activation(
                        out=L[:, hh * 4:(hh + 1) * 4, :].rearrange("p h l -> p (h l)"), in_=pD[:, :], func=AF.Exp),
                        reads=[pD, L], writes=[L])
                yield
                pcb = pss["cb%d" % s]
                P.op("pe", lambda e, co=co, sub=sub: e.matmul(
                    pcb[:, :], lhsT=co[:, 4, sub], rhs=co[:, 5, sub], start=True, stop=True), reads=[co], writes=[pcb])
                P.op("dve", lambda e, cbt=cbt: e.tensor_tensor(out=cbt[:, :], in0=pcb[:, :], in1=U[:, :], op=ALU.mult),
                     reads=[pcb, U], writes=[cbt])
                P.op("pool", lambda e, Mh=Mh, L=L, cbt=cbt: e.tensor_tensor(
                    out=Mh[:, :, :], in0=L[:, :, :], in1=cbt[:, :].unsqueeze(1).to_broadcast([128, 8, 128]), op=ALU.mult),
                    reads=[L, cbt], writes=[Mh])
                xsD = b512.next()
                P.op("pool", lambda e, xs_tm=xs_tm, xsD=xsD: e.tensor_tensor(
                    out=xsD[:, :].rearrange("p (h d) -> p h d", h=8), in0=xs_tm[:, :].rearrange("p (h d) -> p h d", h=8),
                    in1=dsk[:, :].unsqueeze(2).to_broadcast([128, 8, 64]), op=ALU.mult), reads=[xs_tm, dsk], writes=[xsD])
                yield
                pz = psum.next()
                for kc in range(8):
                    P.op("pe", lambda e, pz=pz, kc=kc, hT=hT, sub=sub: e.matmul(
                        pz[:, :], lhsT=hT[:, kc, sub], rhs=w1a[:, kc, 768:1280], start=(kc == 0), stop=(kc == 7)),
                        reads=[w1a, hT], writes=[pz])
                sz = szr.next()
                P.op("act", lambda e, pz=pz, sz=sz: e.activation(out=sz[:, :], in_=pz[:, :], func=AF.Silu), reads=[pz], writes=[sz])
            ctx[s] = dict(locals())
            yield

        def seq(s, m=m, hT=hT, co=co, own=own, yst=yst):
            L_ = ctx[s]
            sub = L_["sub"]
            wst, cdec, xdt, xdtw, Btm = L_["wst"], L_["cdec"], L_["xdt"], L_["xdtw"], L_["Btm"]
            if own:
                Mh, xsD, sz = L_["Mh"], L_["xsD"], L_["sz"]
                pyo, py = psum.next(), psum.next()
                P.op("pe", lambda e, pyo=pyo, co=co, sub=sub: e.matmul(
                    pyo[:, :], lhsT=co[:, 5, sub], rhs=Sbf[:, :], start=True, stop=True), reads=[co, Sbf], writes=[pyo])
                P.op("pe", lambda e, py=py, xsD=xsD: e.matmul(py[:, :], lhsT=ident_b[:, :], rhs=xsD[:, :], start=True, stop=False),
                     reads=[ident_b, xsD], writes=[py])
                for h in range(8):
                    P.op("pe", lambda e, py=py, Mh=Mh, xdt=xdt, h=h: e.matmul(
                        py[:, h * 64:(h + 1) * 64], lhsT=Mh[:, h, :], rhs=xdt[:, h * 64:(h + 1) * 64], start=False, stop=(h == 7)),
                        reads=[Mh, xdt], writes=[py])
                y1, y2, y3 = f512.next(), f512.next(), f512.next()
                P.op("dve", lambda e, pyo=pyo, y1=y1, wst=wst: e.tensor_tensor(
                    out=y1[:, :].rearrange("p (h d) -> p h d", h=8), in0=pyo[:, :].rearrange("p (h d) -> p h d", h=8),
                    in1=wst[:, :].unsqueeze(2).to_broadcast([128, 8, 64]), op=ALU.mult), reads=[pyo, wst], writes=[y1])
                P.op("dve", lambda e, y1=y1, y2=y2, py=py: e.tensor_tensor(out=y2[:, :], in0=y1[:, :], in1=py[:, :], op=ALU.add),
                     reads=[y1, py], writes=[y2])
                P.op("pool", lambda e, y2=y2, y3=y3, sz=sz: e.tensor_tensor(out=y3[:, :], in0=y2[:, :], in1=sz[:, :], op=ALU.mult),
                     reads=[y2, sz], writes=[y3])
                ss = s2.next()
                P.op("act", lambda e, y3=y3, ss=ss: e.activation(out=junk[:, :], in_=y3[:, :], func=AF.Square, accum_out=ss[:, 0:1]),
                     reads=[y3], writes=[junk, ss])
                P.op("act", lambda e, ss=ss: e.activation(out=ss[:, 1:2], in_=ss[:, 0:1], func=AF.Sqrt, scale=1.0 / 512.0, bias=EPS),
                     reads=[ss], writes=[ss])
                P.op("dve", lambda e, ss=ss: e.reciprocal(out=ss[:, 1:2], in_=ss[:, 1:2]), reads=[ss], writes=[ss])
                yn = b512.next()
                P.op("dve", lambda e, y3=y3, ss=ss, yn=yn: e.scalar_tensor_tensor(
                    out=yn[:, :], in0=y3[:, :], scalar=ss[:, 1:2], in1=sng[:, :], op0=ALU.mult, op1=ALU.mult),
                    reads=[y3, ss, sng], writes=[yn])
                pyt = psum.next()
                pytb = pyt.ap.bitcast(BF16)
                for c in range(4):
                    P.op("pe", lambda e, pytb=pytb, yn=yn, c=c: e.transpose(
                        pytb[:, c * 128:(c + 1) * 128], yn[:, c * 128:(c + 1) * 128], ident_b[:, :]),
                        reads=[yn, ident_b], writes=[pyt])
                P.op("act", lambda e, pytb=pytb, yst=yst, sub=sub: e.copy(
                    out=yst[:, :, sub], in_=pytb[:, 0:512].rearrange("p (c n) -> p c n", c=4)), reads=[pyt, yst], writes=[yst])
            pst = psum.next()
            P.op("pe", lambda e, pst=pst, Btm=Btm, xdtw=xdtw: e.matmul(
                pst[:, :], lhsT=Btm[:, 0:128], rhs=xdtw[:, :], start=True, stop=True), reads=[Btm, xdtw], writes=[pst])
            P.op("pool", lambda e, cdec=cdec: e.tensor_tensor(
                out=S[:, :].rearrange("p (h d) -> p h d", h=8), in0=S[:, :].rearrange("p (h d) -> p h d", h=8),
                in1=cdec[:, :].unsqueeze(2).to_broadcast([128, 8, 64]), op=ALU.mult), reads=[S, cdec], writes=[S])
            P.op("dve", lambda e, pst=pst: e.tensor_tensor(out=S[:, :], in0=S[:, :], in1=pst[:, :], op=ALU.add),
                 reads=[S, pst], writes=[S])
            if own or (m * 4 + s + 1) * 128 >= OWN0:
                P.op("act", lambda e: e.copy(out=Sbf[:, :], in_=S[:, :]), reads=[S, Sbf], writes=[Sbf])

        for s in range(4):
            pdt = pss["dt%d" % s]
            for kc in range(8):
                P.op("pe", lambda e, kc=kc, hT=hT, s=s, pdt=pdt: e.matmul(
                    pdt[:, :], lhsT=hT[:, kc, s * 128:(s + 1) * 128], rhs=w1a[:, kc, 1280:1288], start=(kc == 0), stop=(kc == 7)),
                    reads=[w1a, hT], writes=[pdt])
        pd_all = TV(pss["dt0"].parent, pss["dt0"].parent.ap[:, 0:32])
        dtr, ax, ee, dtm, am, ar1, ar2 = (s32.next() for _ in range(7))
        a3m = a3mr.next()
        P.op("dve", lambda e, dtr=dtr: e.tensor_tensor(out=dtr[:, :], in0=pd_all[:, :], in1=dtb4[:, :], op=ALU.add),
             reads=[pd_all, dtb4], writes=[dtr])
        P.op("act", lambda e, dtr=dtr, ax=ax: e.activation(out=ax[:, :], in_=dtr[:, :], func=AF.Abs), reads=[dtr], writes=[ax])
        P.op("act", lambda e, ax=ax, ee=ee: e.activation(out=ee[:, :], in_=ax[:, :], func=AF.Exp, scale=-1.0), reads=[ax], writes=[ee])
        P.op("act", lambda e, ee=ee: e.activation(out=ee[:, :], in_=ee[:, :], func=AF.Ln, bias=1.0, scale=1.0), reads=[ee], writes=[ee])
        P.op("dve", lambda e, dtr=dtr, ee=ee, dtm=dtm: e.scalar_tensor_tensor(
            out=dtm[:, :], in0=dtr[:, :], scalar=0.0, in1=ee[:, :], op0=ALU.max, op1=ALU.add), reads=[dtr, ee], writes=[dtm])
        P.op("dve", lambda e, dtm=dtm, m=m: e.tensor_tensor(
            out=dtm[:, :].rearrange("p (s h) -> p s h", s=4), in0=dtm[:, :].rearrange("p (s h) -> p s h", s=4),
            in1=tokmask[:, 4 * m:4 * m + 4].unsqueeze(2).to_broadcast([128, 4, 8]), op=ALU.mult), reads=[dtm, tokmask], writes=[dtm])
        P.op("dve", lambda e, dtm=dtm, am=am: e.tensor_tensor(out=am[:, :], in0=dtm[:, :], in1=Abc4[:, :], op=ALU.mult),
             reads=[dtm, Abc4], writes=[am])
        P.op("act", lambda e, am=am, a3m=a3m: e.copy(out=a3m[:, 0, :], in_=am[:, :]), reads=[am, a3m], writes=[a3m])
        P.op("dve", lambda e, am=am, a3m=a3m, ar1=ar1: e.tensor_tensor(out=ar1[:, :], in0=am[:, :], in1=a3m[:, 0, :], op=ALU.subtract),
             reads=[am, a3m], writes=[ar1])
        P.op("act", lambda e, ar1=ar1, a3m=a3m: e.copy(out=a3m[:, 1, :], in_=ar1[:, :]), reads=[ar1, a3m], writes=[a3m])
        P.op("dve", lambda e, ar1=ar1, a3m=a3m, ar2=ar2: e.tensor_tensor(out=ar2[:, :], in0=ar1[:, :], in1=a3m[:, 1, :], op=ALU.subtract),
             reads=[ar1, a3m], writes=[ar2])
        P.op("act", lambda e, ar2=ar2, a3m=a3m: e.copy(out=a3m[:, 2, :], in_=ar2[:, :]), reads=[ar2, a3m], writes=[a3m])
        for s in range(4):
            pac = pss["acs%d" % s]
            for i3 in range(3):
                P.op("pe", lambda e, a3m=a3m, i3=i3, s=s, pac=pac: e.matmul(
                    pac[:, 0:8], lhsT=U[:, :], rhs=a3m[:, i3, 8 * s:8 * s + 8], start=(i3 == 0), stop=(i3 == 2)),
                    reads=[U, a3m], writes=[pac])
            for i3 in range(3):
                P.op("pe", lambda e, a3m=a3m, i3=i3, s=s, pac=pac: e.matmul(
                    pac[:, 8:16], lhsT=ones_b[:, :], rhs=a3m[:, i3, 8 * s:8 * s + 8], start=(i3 == 0), stop=(i3 == 2)),
                    reads=[ones_b, a3m], writes=[pac])
        par = pss["acs0"].parent
        pacs = TV(par, par.ap[:, 64:128].rearrange("p (s t h) -> p s t h", s=4, t=2)[:, :, 0, :])
        ptot = TV(par, par.ap[:, 64:128].rearrange("p (s t h) -> p s t h", s=4, t=2)[:, :, 1, :])
        acsm, wstm, cdecm, wendm = (s48.next() for _ in range(4))
        P.op("act", lambda e, acsm=acsm: e.copy(out=acsm[:, :, :], in_=pacs[:, :, :]), reads=[pacs], writes=[acsm])
        P.op("act", lambda e, wstm=wstm: e.activation(out=wstm[:, :, :], in_=pacs[:, :, :], func=AF.Exp), reads=[pacs], writes=[wstm])
        P.op("act", lambda e, cdecm=cdecm: e.activation(out=cdecm[:, :, :], in_=ptot[:, :, :], func=AF.Exp), reads=[ptot], writes=[cdecm])
        P.op("dve", lambda e, wendm=wendm, acsm=acsm: e.tensor_tensor(out=wendm[:, :, :], in0=ptot[:, :, :], in1=acsm[:, :, :], op=ALU.subtract),
             reads=[ptot, acsm], writes=[wendm])
        P.op("act", lambda e, wendm=wendm: e.activation(out=wendm[:, :, :], in_=wendm[:, :, :], func=AF.Exp), reads=[wendm], writes=[wendm])
        gens = [pre(s) for s in range(4)]
        while gens:
            for g_ in list(gens):
                try:
                    next(g_)
                except StopIteration:
                    gens.remove(g_)
        for s in range(4):
            seq(s)
        if own:
            o0 = tok0 - OWN0
            P.op("pool", lambda e, yst=yst, o0=o0: e.dma_start(
                out=e_yb[g, :, :, o0:o0 + 512].rearrange("c p n -> p c n"), in_=yst[:, :, :]),
                reads=[yst], writes=[e_yb_T], dma=True)


def build_p1_init(P, C, din, cst, NTOKW):
    KT = C.sb([96, 4, NTOKW], BF16, "KT")
    VA = C.sb([128, NTOKW // 128, 2, 3, 64], BF16, "VA")
    kmT = C.sbT([64, 4, 32], BF16, "kmT")
    Mpad = [C.sbT([128, 4, 96], BF16, "Mpad") for _ in range(2)]
    for h in range(4):
        P.op("sp", lambda e, h=h: e.dma_start(out=KT[64:96, h, :], in_=din["kind"]), dma=True)
    P.op("pool", lambda e: e.memset(VA[:, :, :, 1, :], 1.0))
    for mp in Mpad:
        P.op("pool", lambda e, mp=mp: e.memset(mp[:, :, :], 0.0), writes=[mp])
    P.op("pool", lambda e: e.memset(kmT[:, :, :], 0.0), writes=[kmT])
    G = C.sbT([128, 512], F32, "G")
    gq, gk = cst["gq"], cst["gk"]
    for h in range(4):
        P.op("dve", lambda e, h=h: e.tensor_scalar(out=G[:, h * 64:(h + 1) * 64], in0=gq[:, :], scalar1=0.125, scalar2=None,
                                                   op0=ALU.mult), reads=[gq, G], writes=[G])
        P.op("dve", lambda e, h=h: e.tensor_copy(out=G[:, 256 + h * 64:256 + (h + 1) * 64], in_=gk[:, :]), reads=[gk, G], writes=[G])
    bb4 = C.sbT([128, 128], F32, "bb4")
    for h in range(4):
        P.op("dve", lambda e, h=h: e.tensor_copy(out=bb4[:, h * 32:(h + 1) * 32], in_=cst["blkbias"][:, :]),
             reads=[cst["blkbias"], bb4], writes=[bb4])
    P.barrier()
    nm = NTOKW // 512
    return dict(KT=KT, VA=VA, kmT=kmT, Mpad=Ring(Mpad), G=G, bb4=bb4,
                KT_T=[T(None, "KT%d" % i) for i in range(nm)], VA_T=[T(None, "VA%d" % i) for i in range(nm)])


def build_p1b(P, C, din, g, cst, A, hT_scr, hT_T, e_att, e_att_T, NTOKW, OWN0):
    psum = cst["psring"]
    po_ring = cst["po_ring"]
    pss = cst["ps_small"]
    ident_b, negm = cst["ident_b"], cst["negm"]
    KT, VA, kmT, G, bb4 = A["KT"], A["VA"], A["kmT"], A["G"], A["bb4"]
    KT_T, VA_T = A["KT_T"], A["VA_T"]
    stg = Ring([C.sbT([128, 8, WB_COLS], F32, "b_stg") for _ in range(2)])
    w1b = C.sbT([128, 8, W1B], BF16, "w1b")
    load_cast_weight(P, C, din["w1b"][g], w1b, W1B, cst["g1"], stg)
    hring = Ring([C.sbT([128, 8, 512], BF16, "b_hT") for _ in range(2)])
    f512 = Ring([C.sbT([128, 512], F32, "b_f512") for _ in range(4)])
    b512 = Ring([C.sbT([128, 512], BF16, "b_b512") for _ in range(3)])
    ptr = Ring([C.sbT([128, 512], BF16, "b_pt") for _ in range(6)])
    s8 = Ring([C.sbT([128, 8], F32, "b_s8") for _ in range(6)])
    g128 = Ring([C.sbT([128, 128], F32, "b_g128") for _ in range(6)])
    t8r = Ring([C.sbT([128, 32], F32, "b_t8") for _ in range(2)])
    kmf = C.sbT([64, 4, 2], F32, "b_kmf")
    QTr = Ring([C.sbT([96, 4, 512], BF16, "b_QT") for _ in range(2)])
    ast = [Ring([C.sbT([128, 512], BF16, "b_ast") for _ in range(2)]) for _ in range(2)]
    rdr = Ring([C.sbT([128, 512], F32, "b_rd") for _ in range(2)])
    outs = []
    for m in range(NTOKW // 512):
        tok0 = m * 512
        own = tok0 >= OWN0
        c0 = 0 if own else 256
        h0 = 0 if own else 4
        hT = hring.next()
        P.op("sp", lambda e, hT=hT, tok0=tok0: e.dma_start(out=hT[:, :, :], in_=hT_scr[:, :, tok0:tok0 + 512]),
             reads=[hT_T[m]], writes=[hT], dma=True)
        QT = QTr.next() if own else None
        def kv(s, m=m, hT=hT, QT=QT, own=own, c0=c0, h0=h0, tok0=tok0):
            sub = slice(s * 128, (s + 1) * 128)
            kt = m * 4 + s
            pqk, pv = psum.next(), psum.next()
            for kc in range(8):
                P.op("pe", lambda e, pqk=pqk, kc=kc, hT=hT, sub=sub, c0=c0: e.matmul(
                    pqk[:, c0:512], lhsT=hT[:, kc, sub], rhs=w1b[:, kc, c0:512], start=(kc == 0), stop=(kc == 7)),
                    reads=[w1b, hT], writes=[pqk])
            for kc in range(8):
                P.op("pe", lambda e, pv=pv, kc=kc, hT=hT, sub=sub: e.matmul(
                    pv[:, 0:256], lhsT=hT[:, kc, sub], rhs=w1b[:, kc, 512:768], start=(kc == 0), stop=(kc == 7)),
                    reads=[w1b, hT], writes=[pv])
            yield
            P.op("act", lambda e, pv=pv, kt=kt: e.copy(
                out=VA[:, kt, :, 0, :], in_=pv[:, 0:256].rearrange("p (a b d) -> p a b d", a=2, b=2)[:, :, 0, :]),
                reads=[pv, VA_T[m]], writes=[VA_T[m]])
            P.op("dve", lambda e, pv=pv, kt=kt: e.tensor_copy(
                out=VA[:, kt, :, 2, :], in_=pv[:, 0:256].rearrange("p (a b d) -> p a b d", a=2, b=2)[:, :, 1, :]),
                reads=[pv, VA_T[m]], writes=[VA_T[m]])
            sq, ssum, tt = f512.next(), s8.next(), f512.next()
            P.op("act", lambda e, pqk=pqk, sq=sq, c0=c0: e.activation(out=sq[:, c0:512], in_=pqk[:, c0:512], func=AF.Square),
                 reads=[pqk], writes=[sq])
            yield
            P.op("dve", lambda e, sq=sq, ssum=ssum, c0=c0, h0=h0: e.tensor_reduce(
                out=ssum[:, h0:8], in_=sq[:, c0:512].rearrange("p (h d) -> p h d", d=64), axis=AX.X, op=ALU.add),
                reads=[sq], writes=[ssum])
            P.op("act", lambda e, ssum=ssum, h0=h0: e.activation(
                out=ssum[:, h0:8], in_=ssum[:, h0:8], func=AF.Sqrt, scale=1.0 / 64.0, bias=EPS), reads=[ssum], writes=[ssum])
            P.op("dve", lambda e, ssum=ssum, h0=h0: e.reciprocal(out=ssum[:, h0:8], in_=ssum[:, h0:8]), reads=[ssum], writes=[ssum])
            yield
            P.op("dve", lambda e, pqk=pqk, tt=tt, ssum=ssum, c0=c0, h0=h0: e.tensor_tensor(
                out=tt[:, c0:512].rearrange("p (h d) -> p h d", d=64), in0=pqk[:, c0:512].rearrange("p (h d) -> p h d", d=64),
                in1=ssum[:, h0:8].unsqueeze(2).to_broadcast([128, 8 - h0, 64]), op=ALU.mult), reads=[pqk, ssum], writes=[tt])
            yield
            qkn = b512.next()
            P.op("pool", lambda e, tt=tt, qkn=qkn, c0=c0: e.tensor_tensor(
                out=qkn[:, c0:512], in0=tt[:, c0:512], in1=G[:, c0:512], op=ALU.mult), reads=[tt, G], writes=[qkn])
            yield
            pkt = psum.next()
            pktb = pkt.ap.bitcast(BF16)
            for h in range(4):
                P.op("pe", lambda e, pktb=pktb, qkn=qkn, h=h: e.transpose(
                    pktb[0:64, h * 128:(h + 1) * 128], qkn[:, 256 + h * 64:256 + (h + 1) * 64], ident_b[:, :]),
                    reads=[qkn, ident_b], writes=[pkt])
            P.op("act", lambda e, pktb=pktb, tok0=tok0, s=s: e.copy(
                out=KT[0:64, :, tok0 + s * 128:tok0 + (s + 1) * 128], in_=pktb[0:64, 0:512].rearrange("p (h n) -> p h n", h=4)),
                reads=[pkt, KT_T[m]], writes=[KT_T[m]])
            yield
            if own:
                pqt = psum.next()
                pqtb = pqt.ap.bitcast(BF16)
                for h in range(4):
                    P.op("pe", lambda e, pqtb=pqtb, qkn=qkn, h=h: e.transpose(
                        pqtb[0:64, h * 128:(h + 1) * 128], qkn[:, h * 64:(h + 1) * 64], ident_b[:, :]),
                        reads=[qkn, ident_b], writes=[pqt])
                P.op("dve", lambda e, pqtb=pqtb, QT=QT, sub=sub: e.tensor_copy(
                    out=QT[0:64, :, sub], in_=pqtb[0:64, 0:512].rearrange("p (h n) -> p h n", h=4)),
                    reads=[pqt, QT], writes=[QT])
        for pair_ in ((0, 1), (2, 3)):
            gens = [kv(s_) for s_ in pair_]
            while gens:
                for g_ in list(gens):
                    try:
                        next(g_)
                    except StopIteration:
                        gens.remove(g_)
        P.op("dve", lambda e, tok0=tok0: e.tensor_reduce(
            out=kmf[:, :, :], in_=KT[0:64, :, tok0:tok0 + 512].rearrange("p h (b k) -> p h b k", b=2), axis=AX.X, op=ALU.add),
            reads=[KT_T[m]], writes=[kmf])
        P.op("dve", lambda e, m=m: e.tensor_scalar(out=kmT[:, :, 2 * m:2 * m + 2], in0=kmf[:, :, :], scalar1=1.0 / 256.0,
                                                   scalar2=None, op0=ALU.mult), reads=[kmf, kmT], writes=[kmT])
        if not own:
            continue
        for s in range(4):
            sub = slice(s * 128, (s + 1) * 128)
            ownblk = 2 * m + s // 2
            pg = pss["gate"]
            for h in range(4):
                P.op("pe", lambda e, h=h, QT=QT, sub=sub: e.matmul(
                    pg[:, h * 32:(h + 1) * 32], lhsT=QT[0:64, h, sub], rhs=kmT[0:64, h, :], start=True, stop=True),
                    reads=[QT, kmT], writes=[pg])
            gm, m1, m2, t8 = g128.next(), g128.next(), g128.next(), t8r.next()
            P.op("dve", lambda e, gm=gm: e.tensor_tensor(out=gm[:, :], in0=pg[:, :], in1=bb4[:, :], op=ALU.add),
                 reads=[pg, bb4], writes=[gm])
            P.op("pool", lambda e, gm=gm, ownblk=ownblk: e.memset(
                gm[:, :].rearrange("p (h b) -> p h b", h=4)[:, :, ownblk:32], NEG), reads=[gm], writes=[gm])
            for h in range(4):
                P.op("dve", lambda e, gm=gm, t8=t8, h=h: e.max(out=t8[:, h * 8:(h + 1) * 8], in_=gm[:, h * 32:(h + 1) * 32]),
                     reads=[gm, t8], writes=[t8])
            P.op("dve", lambda e, gm=gm, m1=m1, t8=t8: e.tensor_tensor(
                out=m1[:, :].rearrange("p (h b) -> p h b", h=4), in0=gm[:, :].rearrange("p (h b) -> p h b", h=4),
                in1=t8[:, :].rearrange("p (h k) -> p h k", h=4)[:, :, 2:3].to_broadcast([128, 4, 32]), op=ALU.is_lt),
                reads=[gm, t8], writes=[m1])
            P.op("dve", lambda e, gm=gm, m2=m2: e.tensor_scalar(
                out=m2[:, :], in0=gm[:, :], scalar1=NEG / 2, scalar2=NEG, op0=ALU.is_lt, op1=ALU.mult), reads=[gm], writes=[m2])
            Mp = A["Mpad"].next()
            P.op("dve", lambda e, Mp=Mp, m1=m1, m2=m2: e.scalar_tensor_tensor(
                out=Mp[:, :, 64:96], in0=m1[:, :].rearrange("p (h b) -> p h b", h=4), scalar=NEG,
                in1=m2[:, :].rearrange("p (h b) -> p h b", h=4), op0=ALU.mult, op1=ALU.min), reads=[m1, m2, Mp], writes=[Mp])
            P.op("pool", lambda e, Mp=Mp, ownblk=ownblk: e.memset(Mp[:, :, 64 + ownblk:65 + ownblk], 0.0), reads=[Mp], writes=[Mp])
            pmt = psum.next()
            pmtb = pmt.ap.bitcast(BF16)
            for h in range(4):
                P.op("pe", lambda e, pmtb=pmtb, Mp=Mp, h=h: e.transpose(
                    pmtb[0:96, h * 128:(h + 1) * 128], Mp[:, h, :], ident_b[:, :]), reads=[Mp, ident_b], writes=[pmt])
            P.op("act", lambda e, pmtb=pmtb, QT=QT, sub=sub: e.copy(
                out=QT[64:96, :, sub], in_=pmtb[64:96, 0:512].rearrange("p (h n) -> p h n", h=4)), reads=[pmt, QT], writes=[QT])
        nkt = (2 * m + 2) * 2
        o0 = tok0 - OWN0
        for h in range(4):
            pair, hb = h // 2, h % 2
            po = po_ring.next()
            def tile_cols(kt):
                blk = kt // 2
                if blk < 2 * m:
                    return 0, 512, None
                if blk == 2 * m:
                    return 0, 512, 0
                return 256, 512, 256

            def emit_s(kt):
                a0, a1, cz = tile_cols(kt)
                mk = kt // 4
                ps = psum.next()
                P.op("pe", lambda e, ps=ps, h=h, kt=kt, QT=QT, a0=a0, a1=a1, cz=cz: e.matmul(
                    ps[:, a0:a1], lhsT=KT[0:96, h, kt * 128:(kt + 1) * 128], rhs=QT[0:96, h, a0:a1],
                    start=True, stop=(cz is None)), reads=[KT_T[mk], QT], writes=[ps])
                if cz is not None:
                    P.op("pe", lambda e, ps=ps, kt=kt, cz=cz: e.matmul(
                        ps[:, cz:cz + 256], lhsT=ident_b[:, :], rhs=negm[:, kt % 2, :], start=False, stop=True),
                        reads=[ident_b, negm], writes=[ps])
                return ps

            def emit_pv(kt, ps):
                a0, a1, cz = tile_cols(kt)
                mk = kt // 4
                pt = ptr.next()
                P.op("act", lambda e, ps=ps, pt=pt, a0=a0, a1=a1: e.activation(out=pt[:, a0:a1], in_=ps[:, a0:a1], func=AF.Exp),
                     reads=[ps], writes=[pt])
                P.op("pe", lambda e, po=po, pt=pt, kt=kt, pair=pair, hb=hb, a0=a0, a1=a1, nkt=nkt: e.matmul(
                    po[:, a0:a1], lhsT=VA[:, kt, pair, hb:hb + 2, :].rearrange("p a d -> p (a d)"), rhs=pt[:, a0:a1],
                    start=(kt == 0), stop=(kt == nkt - 1), skip_group_check=True), reads=[VA_T[mk], pt], writes=[po])

            LOOK = 3
            pend = []
            for kt in range(nkt):
                pend.append((kt, emit_s(kt)))
                if len(pend) > LOOK:
                    emit_pv(*pend.pop(0))
            while pend:
                emit_pv(*pend.pop(0))
            nr = slice(0, 64) if hb == 0 else slice(64, 128)
            dr = slice(64, 128) if hb == 0 else slice(0, 64)
            rd = rdr.next()
            if hb == 0:
                at_ = ast[pair].next()
                ast_cur = at_
            else:
                at_ = ast_cur
            P.op("dve", lambda e, po=po, rd=rd, nr=nr, dr=dr: e.reciprocal(out=rd[nr, :], in_=po[dr, :]), reads=[po], writes=[rd])
            P.op("dve", lambda e, po=po, rd=rd, nr=nr, at_=at_: e.tensor_tensor(
                out=at_[nr, :], in0=po[nr, :], in1=rd[nr, :], op=ALU.mult), reads=[po, rd, at_], writes=[at_])
            if hb == 1:
                outs.append(P.op("pool", lambda e, at_=at_, pair=pair, o0=o0: e.dma_start(
                    out=e_att[g, pair, :, o0:o0 + 512], in_=at_[:, :]), reads=[at_], writes=[e_att_T], dma=True))
    return outs


def load_consts(P, C, din, names_shapes):
    out = {}
    for name, shape, dt in names_shapes:
        t = C.sbT(shape, dt, name)
        P.op("sp", lambda e, t=t, name=name: e.dma_start(out=t.ap, in_=din[name]), writes=[t], dma=True)
        out[name] = t
    return out


def build_program(mode, NTOKW=8192, OWN0=0, NG=1):
    nc = bass.Bass("TRN2", target_bir_lowering=False)
    P = Prog(nc)
    with ExitStack() as es:
        C = Ctx(nc, es, P)
        din = {}

        def inp(name, shape, dt=F32):
            din[name] = C.dram(name, shape, dt, "ExternalInput")

        psum = [T(es.enter_context(nc.psum_tensor("ps%d" % i, [128, 512], F32))[:, :], "ps%d" % i, excl=True) for i in range(8)]
        final = []
        NOWN = NTOKW - OWN0
        if mode in ("p1", "fused"):
            inp("xw", [NTOKW, 1024])
            inp("w1a", [NG, 1024, W1A])
            inp("w1b", [NG, 1024, W1B])
            inp("convw", [NG, 128, 6, 4])
            inp("convb", [NG, 128, 6])
            for nm in ("dtb", "alog", "dsk"):
                inp(nm, [NG, 128, 8])
            inp("sng", [NG, 128, 512])
            inp("kind", [32, NTOKW], BF16)
            shapes1 = [("gq", [128, 64], F32), ("gk", [128, 64], F32), ("g1", [128, 8], F32),
                       ("tokmask", [128, NTOKW // 128], F32), ("blkbias", [128, 32], F32),
                       ("ident_b", [128, 128], BF16), ("ones_b", [128, 128], BF16), ("U", [128, 128], BF16),
                       ("T1", [128, 128], BF16), ("negm", [128, 2, 256], BF16)]
            for nm, shp, dt in shapes1:
                if nm not in din:
                    inp(nm, shp, dt)
            cst = load_consts(P, C, din, shapes1)
            cst["psring"] = Ring(psum[0:5])
            cst["po_ring"] = Ring(psum[5:7])
            cst["ps_small"] = {"gate": TV(psum[7], psum[7].ap[:, 0:128])}
            for s_ in range(4):
                cst["ps_small"]["dt%d" % s_] = TV(psum[5], psum[5].ap[:, 8 * s_:8 * s_ + 8])
                cst["ps_small"]["acs%d" % s_] = TV(psum[5], psum[5].ap[:, 64 + 16 * s_:64 + 16 * s_ + 16])
                cst["ps_small"]["cb%d" % s_] = TV(psum[6], psum[6].ap[:, 128 * s_:128 * s_ + 128])
            kind_e = "ExternalOutput" if mode == "p1" else "Internal"
            e_att = C.dram("e_att", [NG, 2, 128, NOWN], BF16, kind_e)
            e_yb = C.dram("e_yb", [NG, 4, 128, NOWN], BF16, kind_e)
            e_att_T, e_yb_T = T(None, "e_att"), T(None, "e_yb")
            hT_scr = C.dram("hT_scr", [128, 8, NTOKW], BF16, "Internal")
            hT_T = []
            with ExitStack() as es1:
                C1 = Ctx(nc, es1, P)
                if STAGES.get("pro", True):
                    build_prologue(P, C1, din, cst, hT_scr, hT_T, NTOKW)
            P.barrier()
            with ExitStack() as es1:
                C1 = Ctx(nc, es1, P)
                for g in range(NG):
                    if STAGES.get("a", True):
                        with ExitStack() as es2:
                            build_p1a(P, Ctx(nc, es2, P), din, g, cst, hT_scr, hT_T, e_yb, e_yb_T, NTOKW, OWN0)
                        P.barrier()
                    if STAGES.get("b", True):
                        with ExitStack() as es2:
                            C2b = Ctx(nc, es2, P)
                            A = build_p1_init(P, C2b, din, cst, NTOKW)
                            build_p1b(P, C2b, din, g, cst, A, hT_scr, hT_T, e_att, e_att_T, NTOKW, OWN0)
                        P.barrier()
            if mode == "p1":
                final = [o for o in P.ops["pool"] if o.dma][-8:]
        if mode in ("p2", "fused"):
            inp("x2", [2048, 1024])
            inp("p2", [2048, 256])
            for name, (K, N) in P2W.items():
                inp(name, [K, N])
            shapes2 = [("g1", [128, 8], F32), ("g2", [128, 8], F32), ("g3", [128, 8], F32),
                       ("ident_b", [128, 128], BF16), ("ones_b", [128, 128], BF16)]
            for nm, shp, dt in shapes2:
                if nm not in din:
                    inp(nm, shp, dt)
            consts = load_consts(P, C, din, shapes2)
            consts["psum"] = psum
            wscr = {name: C.dram("scr_" + name, [K, N], BF16, "Internal") for name, (K, N) in P2W.items()}
            wT = {}
            with ExitStack() as es2:
                C2 = Ctx(nc, es2, P)
                if STAGES.get("precast", True):
                    build_precast(P, C2, din, wscr, wT, consts)
            P.barrier()
            if mode == "p2":
                inp("e_att", [4, 2, 128, 2048], BF16)
                inp("e_yb", [4, 4, 128, 2048], BF16)
                exch = {"att": din["e_att"], "yb": din["e_yb"], "att_T": T(None), "yb_T": T(None)}
            else:
                exch = {"att": e_att, "yb": e_yb, "att_T": e_att_T, "yb_T": e_yb_T}
            dout = C.dram("out", [2048, 1024], F32, "ExternalOutput")
            if STAGES.get("p2", True):
                with ExitStack() as es3:
                    C3 = Ctx(nc, es3, P)
                    final = build_phase2(P, C3, din, wscr, wT, consts, exch, dout, NTOK=STAGES.get('ntok', 2048))
            else:
                final = [o for o in P.ops["pool"] if o.dma][-8:]
        P.emit(final)
    return nc


BF = ml_dtypes.bfloat16


def host_consts():
    return {
        "ident_b": np.eye(128, dtype=np.float32).astype(BF),
        "ones_b": np.ones((128, 128), dtype=np.float32).astype(BF),
    }


def host_consts1(NTOKW):
    i = np.arange(128)
    U = (i[:, None] <= i[None, :]).astype(np.float32)
    T1 = (i[:, None] > i[None, :]).astype(np.float32)
    q = np.arange(256)
    negm = np.stack([np.where((kt * 128 + i[:, None]) <= q[None, :], 0.0, NEG) for kt in range(2)], 1).astype(np.float32)
    kind = (np.arange(NTOKW)[None, :] // 256 == np.arange(32)[:, None]).astype(np.float32)
    return {"ident_b": np.eye(128, dtype=np.float32).astype(BF), "ones_b": np.ones((128, 128), np.float32).astype(BF),
            "U": U.astype(BF), "T1": T1.astype(BF), "negm": negm.astype(BF), "kind": kind.astype(BF)}


def gain_layout(g):
    return np.ascontiguousarray(g.reshape(8, 128).T)


def bc(v):
    return np.ascontiguousarray(np.broadcast_to(v[None, :], (128, v.shape[0]))).astype(np.float32)


def p1_inputs(inputs, b, groups, xw, tokmask, blkbias, NTOKW):
    w_in = inputs["w_in"][0]
    cw, cbias = inputs["conv_w"][0], inputs["conv_b"][0]
    w1a, w1b, convw, convb, dtb, alog, dsk, sng = [], [], [], [], [], [], [], []
    for g in groups:
        cols_a = np.concatenate([np.arange(5120 + 512 * g, 5120 + 512 * (g + 1)), np.arange(7168 + 128 * g, 7168 + 128 * (g + 1)),
                                 np.arange(7680 + 128 * g, 7680 + 128 * (g + 1)), np.arange(3072 + 512 * g, 3072 + 512 * (g + 1)),
                                 np.arange(8192 + 8 * g, 8192 + 8 * (g + 1))])
        cols_b = np.concatenate([np.arange(256 * g, 256 * (g + 1)), np.arange(1024 + 256 * g, 1024 + 256 * (g + 1)),
                                 np.arange(2048 + 256 * g, 2048 + 256 * (g + 1))])
        w1a.append(w_in[:, cols_a])
        w1b.append(w_in[:, cols_b])
        ch = np.concatenate([np.arange(512 * g, 512 * (g + 1)), np.arange(2048 + 128 * g, 2048 + 128 * (g + 1)),
                             np.arange(2560 + 128 * g, 2560 + 128 * (g + 1))])
        convw.append(cw[:, ch].T.reshape(6, 128, 4).transpose(1, 0, 2))
        convb.append(cbias[ch].reshape(6, 128).T)
        dtb.append(bc(inputs["dt_bias"][0][8 * g:8 * g + 8]))
        alog.append(bc(inputs["a_log"][0][8 * g:8 * g + 8]))
        dsk.append(bc(inputs["d_skip"][0][8 * g:8 * g + 8]))
        sng.append(bc(inputs["ssm_norm_g"][0][512 * g:512 * g + 512]))
    m = {"xw": np.ascontiguousarray(xw), "w1a": np.ascontiguousarray(np.stack(w1a)), "w1b": np.ascontiguousarray(np.stack(w1b)),
         "convw": np.ascontiguousarray(np.stack(convw)), "convb": np.ascontiguousarray(np.stack(convb)),
         "dtb": np.stack(dtb), "alog": np.stack(alog), "dsk": np.stack(dsk), "sng": np.stack(sng),
         "gq": bc(inputs["q_norm_g"][0]), "gk": bc(inputs["k_norm_g"][0]), "g1": gain_layout(inputs["ln1_g"][0]),
         "tokmask": np.ascontiguousarray(tokmask.reshape(-1, 128).T).astype(np.float32), "blkbias": bc(blkbias)}
    m.update(host_consts1(NTOKW))
    return m


def p2_inputs(inputs, core, e_att=None, e_yb=None):
    b, t = core // 4, core % 4
    sl = slice(t * 2048, (t + 1) * 2048)
    m = {
        "x2": np.ascontiguousarray(inputs["x"][b, sl]),
        "p2": np.ascontiguousarray(inputs["p"][0, b, sl]),
        "wg": np.ascontiguousarray(inputs["w_in"][0][:, 8224:10272]),
        "woa": inputs["w_o_attn"][0], "wos": inputs["w_o_ssm"][0], "wout": inputs["w_out"][0],
        "wgu": inputs["w_gate_up"][0], "wd": inputs["w_down"][0], "wpg": inputs["w_ple_gate"][0],
        "wpp": inputs["w_ple_proj"][0],
        "g1": gain_layout(inputs["ln1_g"][0]), "g2": gain_layout(inputs["ln2_g"][0]),
        "g3": gain_layout(inputs["ln3_g"][0]),
    }
    m.update(host_consts())
    if e_att is not None:
        m["e_att"] = e_att
        m["e_yb"] = e_yb
    return m


MODE = "fused"


def kernel(**inputs):
    inputs = {k: np.asarray(v) for k, v in inputs.items()}
    x = inputs["x"]
    out = np.zeros(x.shape, np.float32)
    if MODE == "two":
        nc1 = build_program("p1", NTOKW=8192, OWN0=0, NG=1)
        maps1 = []
        for core in range(8):
            b, g = core // 4, core % 4
            maps1.append(p1_inputs(inputs, b, [g], x[b], np.ones(8192, np.float32), np.zeros(32, np.float32), 8192))
        r1 = run_bass_kernel_spmd(nc1, maps1, core_ids=list(range(8))).results
        nc2 = build_program("p2")
        maps2 = []
        for core in range(8):
            b, t = core // 4, core % 4
            sl = slice(t * 2048, (t + 1) * 2048)
            ea = np.stack([np.asarray(r1[b * 4 + g]["e_att"])[0][:, :, sl] for g in range(4)])
            ey = np.stack([np.asarray(r1[b * 4 + g]["e_yb"])[0][:, :, sl] for g in range(4)])
            maps2.append(p2_inputs(inputs, core, np.ascontiguousarray(ea), np.ascontiguousarray(ey)))
        r2 = run_bass_kernel_spmd(nc2, maps2, core_ids=list(range(8))).results
        for core in range(8):
            b, t = core // 4, core % 4
            out[b, t * 2048:(t + 1) * 2048] = np.asarray(r2[core]["out"])
        return out
    nc = build_program("fused", NTOKW=8192, OWN0=6144, NG=4)
    maps = []
    for core in range(8):
        b, t = core // 4, core % 4
        npad = (3 - t) * 2048
        xw = np.concatenate([np.zeros((npad, 1024), np.float32), x[b, :(t + 1) * 2048]], 0)
        tokmask = np.concatenate([np.zeros(npad, np.float32), np.ones(8192 - npad, np.float32)])
        blkbias = np.where(np.arange(32) < npad // 256, NEG, 0.0).astype(np.float32)
        m = p1_inputs(inputs, b, [0, 1, 2, 3], xw, tokmask, blkbias, 8192)
        m.update(p2_inputs(inputs, core))
        maps.append(m)
    r = run_bass_kernel_spmd(nc, maps, core_ids=list(range(8))).results
    for core in range(8):
        b, t = core // 4, core % 4
        out[b, t * 2048:(t + 1) * 2048] = np.asarray(r[core]["out"])
    return out
```
